# Optimizing a Trainium2 kernel written in Bass

```python
import math
import jax, jax.numpy as jnp
from jax import lax
import numpy as np

D_MODEL = 1024
BATCH = 8
SEQ = 4096
DEPTH = 1

HEAD_DIM = 64
ATTN_GROUPS = ((128, 1), (512, 4), (2048, 16))
ATTN_HEADS_PER_GROUP = 4
ATTN_HEADS = ATTN_HEADS_PER_GROUP * len(ATTN_GROUPS)
ATTN_WIDTH = ATTN_HEADS * HEAD_DIM
BAND_BLOCK = 128
RET_HEADS = 6
RET_QK_DIM = 64
RET_V_DIM = 128
RET_QK_WIDTH = RET_HEADS * RET_QK_DIM
RET_V_WIDTH = RET_HEADS * RET_V_DIM
RET_CHUNK = 128
MEM_LEN = 256
MEM_HEADS = 4
MEM_HEAD_DIM = 128
MEM_WIDTH = MEM_HEADS * MEM_HEAD_DIM
N_BRANCHES = 3
IN_WIDTH = 3 * ATTN_WIDTH + 2 * RET_QK_WIDTH + 2 * RET_V_WIDTH + MEM_WIDTH
D_FF = 2816
CONV_WIDTH = 3
ROPE_THETA = 10000.0
EPS = 1e-6
NEG_INF = -1e30

kernel_name = 'hybrid_dilated_retention_memory_encoder'


def rms_norm(x, g):
    xf = x.astype(jnp.float32)
    y = xf * lax.rsqrt(jnp.mean(xf * xf, axis=-1, keepdims=True) + EPS)
    return (y * g.astype(jnp.float32)).astype(x.dtype)


def rotary(x, pos):
    d = x.shape[-1]
    inv = ROPE_THETA ** (-jnp.arange(0, d, 2, dtype=jnp.float32) / d)
    ang = pos.astype(jnp.float32)[:, None] * inv[None, :]
    cos, sin = jnp.cos(ang), jnp.sin(ang)
    x1, x2 = jnp.split(x.astype(jnp.float32), 2, axis=-1)
    return jnp.concatenate([x1 * cos - x2 * sin, x1 * sin + x2 * cos], axis=-1).astype(x.dtype)


def _heads(t, n):
    B, S, _ = t.shape
    return t.reshape(B, S, n, -1).transpose(0, 2, 1, 3)


def _merge_heads(t):
    B, H, S, d = t.shape
    return t.transpose(0, 2, 1, 3).reshape(B, S, H * d)


def banded_attention(q, k, v, half):
    B, H, L, hd = q.shape
    nb = -(-L // BAND_BLOCK)
    Lp = nb * BAND_BLOCK
    span = BAND_BLOCK + 2 * half
    qb = jnp.pad(q, ((0, 0), (0, 0), (0, Lp - L), (0, 0))).reshape(B, H, nb, BAND_BLOCK, hd)
    pad_kv = ((0, 0), (0, 0), (half, Lp - L + half), (0, 0))
    kp = jnp.pad(k, pad_kv)
    vp = jnp.pad(v, pad_kv)
    key_idx = jnp.arange(nb)[:, None] * BAND_BLOCK + jnp.arange(span)[None, :]
    kb = jnp.take(kp, key_idx, axis=2)
    vb = jnp.take(vp, key_idx, axis=2)
    q_pos = jnp.arange(Lp).reshape(nb, BAND_BLOCK)
    k_pos = key_idx - half
    valid = ((jnp.abs(q_pos[:, :, None] - k_pos[:, None, :]) <= half)
             & (k_pos[:, None, :] >= 0) & (k_pos[:, None, :] < L))
    s = jnp.einsum('bhnqd,bhnkd->bhnqk', qb, kb).astype(jnp.float32) * (hd ** -0.5)
    s = jnp.where(valid, s, NEG_INF)
    lse = jax.nn.logsumexp(s, axis=-1)
    p = jnp.exp(s - lse[..., None]).astype(v.dtype)
    o = jnp.einsum('bhnqk,bhnkd->bhnqd', p, vb)
    o = o.reshape(B, H, Lp, hd)[:, :, :L]
    lse = lse.reshape(B, H, Lp)[:, :, :L]
    return o, lse


def dilated_group_attention(q, k, v, window, dilation):
    B, H, S, hd = q.shape
    half = window // (2 * dilation)
    L = S // dilation

    def to_residue(t):
        return t.reshape(B, H, L, dilation, hd).transpose(0, 1, 3, 2, 4).reshape(B, H * dilation, L, hd)

    o, lse = banded_attention(to_residue(q), to_residue(k), to_residue(v), half)
    o = o.reshape(B, H, dilation, L, hd).transpose(0, 1, 3, 2, 4).reshape(B, H, S, hd)
    lse = lse.reshape(B, H, dilation, L).transpose(0, 1, 3, 2).reshape(B, H, S)
    return o, lse


def dilated_attention_branch(q, k, v):
    B, _, S, _ = q.shape
    outs, lses = [], []
    for g, (window, dilation) in enumerate(ATTN_GROUPS):
        hs = slice(g * ATTN_HEADS_PER_GROUP, (g + 1) * ATTN_HEADS_PER_GROUP)
        o, lse = dilated_group_attention(q[:, hs], k[:, hs], v[:, hs], window, dilation)
        outs.append(o)
        lses.append(lse)
    alpha = jax.nn.softmax(jnp.stack(lses, axis=0), axis=0)
    y = jnp.stack(outs, axis=0) * alpha[..., None].astype(q.dtype)
    return y.transpose(1, 3, 0, 2, 4).reshape(B, S, ATTN_WIDTH)


def retention_chunkwise(q, k, v, log_gamma, include_diag):
    B, H, S, dk = q.shape
    dv = v.shape[-1]
    C = RET_CHUNK
    N = S // C
    qc = q.reshape(B, H, N, C, dk)
    kc = k.reshape(B, H, N, C, dk)
    vc = v.reshape(B, H, N, C, dv)
    idx = jnp.arange(C, dtype=jnp.float32)
    diff = idx[:, None] - idx[None, :]
    mask = diff >= 0 if include_diag else diff > 0
    decay_in = jnp.where(mask[None], jnp.exp(log_gamma[:, None, None] * jnp.where(mask, diff, 0.0)[None]), 0.0)
    s = jnp.einsum('bhncd,bhnmd->bhncm', qc, kc).astype(jnp.float32) * decay_in[None, :, None]
    inner = jnp.einsum('bhncm,bhnme->bhnce', s, vc.astype(jnp.float32))
    zeta = jnp.exp(log_gamma[:, None] * (C - 1.0 - idx)[None, :])
    kv = jnp.einsum('bhncd,bhnce->bhnde', kc.astype(jnp.float32) * zeta[None, :, None, :, None],
                    vc.astype(jnp.float32))
    chunk_decay = jnp.exp(log_gamma * C)[None, :, None, None]

    def step(state, kv_n):
        return state * chunk_decay + kv_n, state

    _, prev = lax.scan(step, jnp.zeros((B, H, dk, dv), jnp.float32), kv.transpose(2, 0, 1, 3, 4))
    prev = prev.transpose(1, 2, 0, 3, 4)
    xi = jnp.exp(log_gamma[:, None] * (idx + 1.0)[None, :])
    cross = jnp.einsum('bhncd,bhnde->bhnce', qc.astype(jnp.float32), prev) * xi[None, :, None, :, None]
    return (inner + cross).reshape(B, H, S, dv)


def retention_branch(q, k, v, gate, decay_fwd, decay_bwd, g_ret):
    lg_f = jax.nn.log_sigmoid(decay_fwd.astype(jnp.float32))
    lg_b = jax.nn.log_sigmoid(decay_bwd.astype(jnp.float32))
    y_f = retention_chunkwise(q, k, v, lg_f, True)
    flip = lambda t: jnp.flip(t, axis=2)
    y_b = flip(retention_chunkwise(flip(q), flip(k), flip(v), lg_b, False))
    y = y_f + y_b
    mu = jnp.mean(y, axis=-1, keepdims=True)
    var = jnp.mean(jnp.square(y - mu), axis=-1, keepdims=True)
    y = (y - mu) * lax.rsqrt(var + EPS)
    y = _merge_heads(y) * g_ret.astype(jnp.float32)
    return (y * jax.nn.silu(gate.astype(jnp.float32))).astype(gate.dtype)


def memory_cross_attention(q, mem_n, w_mem_kv):
    km, vm = jnp.split(mem_n @ w_mem_kv, 2, axis=-1)
    km = _heads(km, MEM_HEADS)
    vm = _heads(vm, MEM_HEADS)
    s = jnp.einsum('bhsd,bhmd->bhsm', q, km).astype(jnp.float32) * (MEM_HEAD_DIM ** -0.5)
    p = jax.nn.softmax(s, axis=-1).astype(vm.dtype)
    return _merge_heads(jnp.einsum('bhsm,bhmd->bhsd', p, vm))


def depthwise_conv(u, w, b):
    C = u.shape[-1]
    out = lax.conv_general_dilated(
        u, w[:, None, :].astype(u.dtype), window_strides=(1,),
        padding=((CONV_WIDTH // 2, CONV_WIDTH // 2),),
        dimension_numbers=('NWC', 'WIO', 'NWC'), feature_group_count=C)
    return out + b


def hybrid_layer(h, mem, g_mix, w_in, w_mem_kv, g_mem, decay_fwd, decay_bwd, g_ret,
                 w_proj_attn, w_proj_ret, w_proj_mem, w_gate, b_gate, w_out,
                 g_ffn, w_up, conv_w, conv_b, w_down):
    B, S, _ = h.shape
    pos = jnp.arange(S)
    n = rms_norm(h, g_mix)
    proj = n @ w_in
    sizes = [ATTN_WIDTH, ATTN_WIDTH, ATTN_WIDTH, RET_QK_WIDTH, RET_QK_WIDTH, RET_V_WIDTH, RET_V_WIDTH]
    cuts = []
    acc = 0
    for s_ in sizes:
        acc += s_
        cuts.append(acc)
    qa, ka, va, qr, kr, vr, gr, qm = jnp.split(proj, cuts, axis=-1)

    qa = rotary(_heads(qa, ATTN_HEADS), pos)
    ka = rotary(_heads(ka, ATTN_HEADS), pos)
    y_a = dilated_attention_branch(qa, ka, _heads(va, ATTN_HEADS))

    qr = rotary(_heads(qr, RET_HEADS), pos)
    kr = rotary(_heads(kr, RET_HEADS), pos) * (RET_QK_DIM ** -0.5)
    y_r = retention_branch(qr, kr, _heads(vr, RET_HEADS), gr, decay_fwd, decay_bwd, g_ret)

    y_m = memory_cross_attention(_heads(qm, MEM_HEADS), rms_norm(mem, g_mem), w_mem_kv)

    gates = jax.nn.sigmoid(n @ w_gate + b_gate).reshape(B, S, N_BRANCHES, D_MODEL)
    merged = (gates[:, :, 0] * (y_a @ w_proj_attn)
              + gates[:, :, 1] * (y_r @ w_proj_ret)
              + gates[:, :, 2] * (y_m @ w_proj_mem))
    h = h + merged @ w_out

    u = depthwise_conv(rms_norm(h, g_ffn) @ w_up, conv_w, conv_b)
    a, b = jnp.split(u, 2, axis=-1)
    return h + (jax.nn.silu(a) * b) @ w_down


def setup_inputs(seed: int = 0) -> dict:
    key = jax.random.key(seed)
    ks = jax.random.split(key, 24)
    f32 = jnp.float32

    def nrm(k, shape, fan_in):
        return jax.random.normal(k, shape, f32) * (fan_in ** -0.5)

    def gain(k, shape):
        return 1.0 + 0.01 * jax.random.normal(k, shape, f32)

    a = 5.0 + jnp.arange(RET_HEADS, dtype=f32)
    decay_logit = jnp.log(2.0 ** a - 1.0)
    return {
        'x': jax.random.normal(ks[0], (BATCH, SEQ, D_MODEL), f32),
        'mem': jax.random.normal(ks[1], (BATCH, MEM_LEN, D_MODEL), f32),
        'g_mix': gain(ks[2], (DEPTH, D_MODEL)),
        'w_in': nrm(ks[3], (DEPTH, D_MODEL, IN_WIDTH), D_MODEL),
        'w_mem_kv': nrm(ks[4], (DEPTH, D_MODEL, 2 * MEM_WIDTH), D_MODEL),
        'g_mem': gain(ks[5], (DEPTH, D_MODEL)),
        'ret_decay_fwd': decay_logit[None] + 0.1 * jax.random.normal(ks[6], (DEPTH, RET_HEADS), f32),
        'ret_decay_bwd': decay_logit[None] + 0.1 * jax.random.normal(ks[7], (DEPTH, RET_HEADS), f32),
        'g_ret': gain(ks[8], (DEPTH, RET_V_WIDTH)),
        'w_proj_attn': nrm(ks[9], (DEPTH, ATTN_WIDTH, D_MODEL), ATTN_WIDTH),
        'w_proj_ret': nrm(ks[10], (DEPTH, RET_V_WIDTH, D_MODEL), RET_V_WIDTH),
        'w_proj_mem': nrm(ks[11], (DEPTH, MEM_WIDTH, D_MODEL), MEM_WIDTH),
        'w_gate': nrm(ks[12], (DEPTH, D_MODEL, N_BRANCHES * D_MODEL), D_MODEL),
        'b_gate': 0.01 * jax.random.normal(ks[13], (DEPTH, N_BRANCHES * D_MODEL), f32),
        'w_out': nrm(ks[14], (DEPTH, D_MODEL, D_MODEL), D_MODEL),
        'g_ffn': gain(ks[15], (DEPTH, D_MODEL)),
        'w_up': nrm(ks[16], (DEPTH, D_MODEL, 2 * D_FF), D_MODEL),
        'conv_w': nrm(ks[17], (DEPTH, CONV_WIDTH, 2 * D_FF), CONV_WIDTH),
        'conv_b': 0.01 * jax.random.normal(ks[18], (DEPTH, 2 * D_FF), f32),
        'w_down': nrm(ks[19], (DEPTH, D_FF, D_MODEL), D_FF),
        'g_final': gain(ks[20], (D_MODEL,)),
    }


def reference(x, mem, g_mix, w_in, w_mem_kv, g_mem, ret_decay_fwd, ret_decay_bwd, g_ret,
              w_proj_attn, w_proj_ret, w_proj_mem, w_gate, b_gate, w_out,
              g_ffn, w_up, conv_w, conv_b, w_down, g_final):
    h = x
    for l in range(DEPTH):
        h = hybrid_layer(h, mem, g_mix[l], w_in[l], w_mem_kv[l], g_mem[l],
                         ret_decay_fwd[l], ret_decay_bwd[l], g_ret[l],
                         w_proj_attn[l], w_proj_ret[l], w_proj_mem[l], w_gate[l], b_gate[l], w_out[l],
                         g_ffn[l], w_up[l], conv_w[l], conv_b[l], w_down[l])
    return rms_norm(h, g_final)
```

```python
import contextlib
import numpy as np
import concourse.bass as bass
import concourse.mybir as mybir
from concourse.bass_utils import run_bass_kernel_spmd

F32 = mybir.dt.float32
BF16 = mybir.dt.bfloat16
AF = mybir.ActivationFunctionType
ALU = mybir.AluOpType
AX = mybir.AxisListType

S = 4096
D = 1024
NT = S // 128
NB = S // 512
IN_W = 5120
DFF = 2816
EPS = 1e-6
C_QA, C_KA, C_VA, C_QR, C_KR, C_VR, C_GR, C_QM = 0, 768, 1536, 2304, 2688, 3072, 3840, 4608


class Buf:
    __slots__ = ("name", "w", "r")

    def __init__(self, name):
        self.name = name
        self.w = None
        self.r = []


class Sched:
    ENG = ("tensor", "vector", "scalar", "gpsimd", "sync")

    def __init__(self, nc, n_dma_sems=32, prefix=""):
        self.nc = nc
        self.prefix = prefix
        self.lists = {e: [] for e in self.ENG}
        self.cnt = {e: 0 for e in self.ENG}
        self.known = {e: {} for e in self.ENG}
        self.ndma = n_dma_sems
        self.dma_issued = [0] * n_dma_sems
        self.dma_rr = 0

    def _need(self, eng, ev, waits):
        if ev is None:
            return
        key, val = ev
        if key == eng and eng == "tensor":
            return
        if self.known[eng].get(key, 0) >= val:
            return
        if waits.get(key, 0) < val:
            waits[key] = val

    def _deps(self, eng, reads, writes):
        waits = {}
        for b in reads:
            self._need(eng, b.w, waits)
        for b in writes:
            self._need(eng, b.w, waits)
            for ev in b.r:
                self._need(eng, ev, waits)
        for k, v in waits.items():
            self.known[eng][k] = v
        return list(waits.items())

    def op(self, eng, fn, reads=(), writes=()):
        waits = self._deps(eng, reads, writes)
        self.cnt[eng] += 1
        ev = (eng, self.cnt[eng])
        self.lists[eng].append((waits, fn, eng, 1))
        for b in reads:
            b.r.append(ev)
        for b in writes:
            b.w = ev
            b.r = []
        return ev

    def dma(self, eng, fn, reads=(), writes=()):
        i = self.dma_rr
        self.dma_rr = (self.dma_rr + 1) % self.ndma
        key = ("dma", i)
        waits = dict(self._deps(eng, reads, writes))
        prev = self.dma_issued[i]
        if prev > 0 and self.known[eng].get(key, 0) < prev:
            waits[key] = prev
            self.known[eng][key] = prev
        self.dma_issued[i] = prev + 16
        ev = (key, prev + 16)
        self.lists[eng].append((list(waits.items()), fn, key, 16))
        for b in reads:
            b.r.append(ev)
        for b in writes:
            b.w = ev
            b.r = []
        return ev

    def barrier(self):
        for e in self.ENG:
            waits = {}
            for o in self.ENG:
                if o != e and self.cnt[o] > 0:
                    self._need(e, (o, self.cnt[o]), waits)
            if e != "tensor" and self.cnt[e] > 0:
                self._need(e, (e, self.cnt[e]), waits)
            for i in range(self.ndma):
                if self.dma_issued[i] > 0:
                    self._need(e, (("dma", i), self.dma_issued[i]), waits)
            for k, v in waits.items():
                self.known[e][k] = v
            self.lists[e].append((list(waits.items()), None, None, 0))

    def emit(self, stack):
        nc = self.nc
        semmap = {}
        handles = []
        for e in self.ENG:
            semmap[e] = nc.alloc_semaphore(name=self.prefix + "s_" + e)
            handles.append(semmap[e])
        for i in range(self.ndma):
            semmap[("dma", i)] = nc.alloc_semaphore(name=self.prefix + "s_dma%d" % i)
            handles.append(semmap[("dma", i)])

        def runner(items):
            def f(e):
                for waits, fn, key, inc in items:
                    for k, v in waits:
                        e.wait_ge(semmap[k], v)
                    if fn is not None:
                        fn(e).then_inc(semmap[key], inc)
            return f
        with nc.Block() as block:
            block.tensor(runner(self.lists["tensor"]))
            block.vector(runner(self.lists["vector"]))
            block.scalar(runner(self.lists["scalar"]))
            block.gpsimd(runner(self.lists["gpsimd"]))
            block.sync(runner(self.lists["sync"]))
        nc.clear_and_free_semaphores(handles)
        nc.all_engine_barrier()


class Ring:
    def __init__(self, items):
        self.items = items
        self.i = 0

    def next(self):
        it = self.items[self.i]
        self.i = (self.i + 1) % len(self.items)
        return it


class Prog:
    def __init__(self, debug=False, stop_after=None, phases=None, ext_in=()):
        self.debug = debug
        self.stop_after = stop_after
        self.phases = phases
        self.ext_in = set(ext_in)
        nc = self.nc = bass.Bass("TRN2", target_bir_lowering=False)
        ein = lambda n, s: nc.dram_tensor(n, s, F32, kind="ExternalInput").ap()
        self.x = ein("x", [S, D])
        self.mem = ein("mem", [256, D])
        self.g_mix = ein("g_mix", [1, D])
        self.w_in = ein("w_in", [1, D, IN_W])
        self.w_mem_kv = ein("w_mem_kv", [1, D, 1024])
        self.g_mem = ein("g_mem", [1, D])
        self.dec_f = ein("ret_decay_fwd", [1, 6])
        self.dec_b = ein("ret_decay_bwd", [1, 6])
        self.g_ret = ein("g_ret", [1, 768])
        self.w_pa = ein("w_proj_attn", [1, 768, D])
        self.w_pr = ein("w_proj_ret", [1, 768, D])
        self.w_pm = ein("w_proj_mem", [1, 512, D])
        self.w_gate = ein("w_gate", [1, D, 3 * D])
        self.b_gate = ein("b_gate", [1, 3 * D])
        self.w_out = ein("w_out", [1, D, D])
        self.g_ffn = ein("g_ffn", [1, D])
        self.w_up = ein("w_up", [1, D, 2 * DFF])
        self.conv_w = ein("conv_w", [1, 3, 2 * DFF])
        self.conv_b = ein("conv_b", [1, 2 * DFF])
        self.w_down = ein("w_down", [1, DFF, D])
        self.g_final = ein("g_final", [D])
        self.c_cos = ein("c_cos", [S, 32])
        self.c_sin = ein("c_sin", [S, 32])
        self.c_ident = ein("c_ident", [128, 128])
        self.c_amask = ein("c_amask", [128, 1024])
        self.c_ret = ein("c_ret", [128, 8, 128])
        self.out = nc.dram_tensor("out", [S, D], F32, kind="ExternalOutput").ap()
        kind = "ExternalOutput" if debug else "Internal"
        scr = lambda n, s, d: nc.dram_tensor(n, s, d, kind=("ExternalInput" if n in self.ext_in else kind)).ap()
        self.PROJ = scr("PROJ", [S, IN_W], BF16)
        self.GTT = scr("GTT", [3 * D, S], BF16)
        self.UD = scr("UD", [12 * 128, S], F32)
        self.YAT = scr("YAT", [768, S], BF16)
        self.YR = scr("YR", [S, 768], BF16)
        self.YMT = scr("YMT", [512, S], BF16)
        self.H = scr("H", [S, D], F32)
        self.HNT = scr("HNT", [D, S], BF16)
        self.GT2 = scr("GT2", [DFF, S], BF16)

    def phase_a(self):
        nc = self.nc
        with contextlib.ExitStack() as st:
            sb = lambda n, s, d: st.enter_context(nc.sbuf_tensor("a_" + n, s, d))
            ps = lambda n, s, d: st.enter_context(nc.psum_tensor("a_" + n, s, d))
            Sc = Sched(nc, prefix="a_")
            xT = sb("xT", [128, 8, S], BF16)
            xT_b = [Buf("xT%d" % t) for t in range(NT)]
            xr = Ring([(sb("xr%d" % i, [128, D], F32), Buf("xr%d" % i)) for i in range(3)])
            xg = Ring([(sb("xg%d" % i, [128, D], BF16), Buf("xg%d" % i)) for i in range(2)])
            junk = sb("junk", [128, D], BF16)
            junk_b = Buf("junk")
            gmix = sb("gmix", [128, D], F32)
            gmix_b = Buf("gmix")
            ssq = sb("ssq", [128, NT], F32)
            msq = sb("msq", [128, NT], F32)
            rstd = sb("rstd", [128, NT], F32)
            negh = sb("negh", [128, 1], F32)
            st_b = [Buf("st%d" % t) for t in range(NT)]
            const_b = Buf("const")
            identf = sb("identf", [128, 128], F32)
            ident = sb("ident", [128, 128], BF16)
            cos_t = sb("cos_t", [128, NT, 32], F32)
            sin_t = sb("sin_t", [128, NT, 32], F32)
            hb = sb("hb", [128, 24], F32)
            pT = Ring([(ps("pT%d" % i, [128, 1024], BF16), Buf("pT%d" % i)) for i in range(2)])
            pA = Ring([(ps("pA%d" % i, [128, 512], F32), Buf("pA%d" % i)) for i in range(6)])
            wr = Ring([(sb("w%d" % i, [128, 8, 512], BF16), Buf("w%d" % i)) for i in range(3)])
            ob = Ring([(sb("ob%d" % i, [128, 512], BF16), Buf("ob%d" % i)) for i in range(4)])
            tA = Ring([(sb("tA%d" % i, [128, 512], F32), Buf("tA%d" % i)) for i in range(3)])
            tB = Ring([(sb("tB%d" % i, [128, 512], F32), Buf("tB%d" % i)) for i in range(3)])
            proj_b = Buf("PROJ")
            gtt_b = Buf("GTT")

            Sc.dma("sync", lambda e: e.dma_start(out=gmix[:], in_=self.g_mix[0, :].partition_broadcast(128)), writes=[gmix_b])
            Sc.dma("sync", lambda e: e.dma_start(out=identf[:], in_=self.c_ident[:, :]), writes=[const_b])
            Sc.dma("sync", lambda e: e.dma_start(out=cos_t[:], in_=self.c_cos.rearrange("(t p) c -> p t c", p=128)), writes=[const_b])
            Sc.dma("sync", lambda e: e.dma_start(out=sin_t[:], in_=self.c_sin.rearrange("(t p) c -> p t c", p=128)), writes=[const_b])
            Sc.dma("sync", lambda e: e.dma_start(out=hb[:], in_=self.b_gate[0, :].rearrange("(f p) -> p f", p=128), allow_slow_non_contiguous=True), writes=[const_b])
            Sc.op("vector", lambda e: e.tensor_copy(out=ident[:], in_=identf[:]), reads=[const_b], writes=[const_b])
            Sc.op("vector", lambda e: e.tensor_scalar(out=hb[:], in0=hb[:], scalar1=0.5, scalar2=None, op0=ALU.mult), reads=[const_b], writes=[const_b])
            Sc.op("vector", lambda e: e.memset(ssq[:], 0.0), writes=st_b)
            Sc.op("vector", lambda e: e.memset(negh[:], -0.5), writes=[const_b])

            for t in range(NT):
                xt, xt_b = xr.next()
                Sc.dma("sync", lambda e, xt=xt, t=t: e.dma_start(out=xt[:], in_=self.x[t * 128:(t + 1) * 128, :]), writes=[xt_b])
                Sc.op("scalar", lambda e, xt=xt, t=t: e.activation(out=junk[:], in_=xt[:], func=AF.Square, accum_out=ssq[:, t:t + 1]),
                      reads=[xt_b], writes=[junk_b, st_b[t]])
                Sc.op("gpsimd", lambda e, t=t: e.tensor_scalar(out=msq[:, t:t + 1], in0=ssq[:, t:t + 1], scalar1=1.0 / D, scalar2=EPS, op0=ALU.mult, op1=ALU.add),
                      reads=[st_b[t]], writes=[st_b[t]])
                Sc.op("gpsimd", lambda e, t=t: e.tensor_tensor(out=rstd[:, t:t + 1], in0=msq[:, t:t + 1], in1=negh[:, 0:1], op=ALU.pow),
                      reads=[st_b[t], const_b], writes=[st_b[t]])
                g, g_b = xg.next()
                Sc.op("vector", lambda e, g=g, xt=xt, t=t: e.scalar_tensor_tensor(out=g[:], in0=xt[:], scalar=rstd[:, t:t + 1], in1=gmix[:], op0=ALU.mult, op1=ALU.mult),
                      reads=[xt_b, st_b[t], gmix_b], writes=[g_b])
                p, p_b = pT.next()
                for k in range(8):
                    Sc.op("tensor", lambda e, p=p, g=g, k=k: e.transpose(out=p[:, k * 128:(k + 1) * 128], in_=g[:, k * 128:(k + 1) * 128], identity=ident[:]),
                          reads=[g_b, const_b], writes=[p_b])
                eng = "scalar" if t % 2 == 0 else "vector"
                dst = xT[:, :, t * 128:(t + 1) * 128]
                src = p[:, :].rearrange("p (k c) -> p k c", k=8)
                if eng == "scalar":
                    Sc.op("scalar", lambda e, dst=dst, src=src: e.activation(out=dst, in_=src, func=AF.Copy), reads=[p_b], writes=[xT_b[t]])
                else:
                    Sc.op("vector", lambda e, dst=dst, src=src: e.tensor_copy(out=dst, in_=src), reads=[p_b], writes=[xT_b[t]])

            blocks = [(C_QA, 512, "rot"), (C_QA + 512, 256, "rot"), (C_KA, 512, "rot"), (C_KA + 512, 256, "rot"),
                      (C_VA, 512, "copy"), (C_VA + 512, 256, "copy"), (C_QR, 384, "rot"), (C_KR, 384, "rot"),
                      (C_VR, 512, "copy"), (C_VR + 512, 256, "copy"), (C_GR, 512, "silu2"), (C_GR + 512, 256, "silu2"),
                      (C_QM, 512, "copy")]
            for (c0, N, kind) in blocks:
                w, w_b = wr.next()
                Sc.dma("gpsimd", lambda e, w=w, c0=c0, N=N: e.dma_start(out=w[:, :, 0:N], in_=self.w_in[0, :, c0:c0 + N].rearrange("(k p) c -> p k c", p=128)), writes=[w_b])
                for t in range(NT):
                    p, p_b = pA.next()
                    for k in range(8):
                        Sc.op("tensor", lambda e, p=p, w=w, t=t, k=k, N=N: e.matmul(p[:, 0:N], lhsT=xT[:, k, t * 128:(t + 1) * 128], rhs=w[:, k, 0:N], start=(k == 0), stop=(k == 7)),
                              reads=[xT_b[t], w_b], writes=[p_b])
                    o, o_b = ob.next()
                    if kind == "copy":
                        Sc.op("scalar", lambda e, o=o, p=p, N=N: e.activation(out=o[:, 0:N], in_=p[:, 0:N], func=AF.Copy), reads=[p_b], writes=[o_b])
                    elif kind == "silu2":
                        a, a_b = tA.next()
                        Sc.op("scalar", lambda e, a=a, p=p, N=N: e.activation(out=a[:, 0:N], in_=p[:, 0:N], func=AF.Tanh, scale=0.5), reads=[p_b], writes=[a_b])
                        Sc.op("vector", lambda e, o=o, a=a, p=p, N=N: e.scalar_tensor_tensor(out=o[:, 0:N], in0=a[:, 0:N], scalar=1.0, in1=p[:, 0:N], op0=ALU.add, op1=ALU.mult),
                              reads=[a_b, p_b], writes=[o_b])
                    else:
                        H = N // 64
                        a, a_b = tA.next()
                        b, b_b = tB.next()
                        pv = p[:, 0:N].rearrange("p (h two f) -> p h two f", two=2, f=32)
                        av = a[:, 0:N].rearrange("p (h two f) -> p h two f", two=2, f=32)
                        bv = b[:, 0:N].rearrange("p (h two f) -> p h two f", two=2, f=32)
                        ov = o[:, 0:N].rearrange("p (h two f) -> p h two f", two=2, f=32)
                        cb = cos_t[:, t:t + 1, :].broadcast_to([128, H, 32])
                        sn = sin_t[:, t:t + 1, :].broadcast_to([128, H, 32])
                        x1, x2 = pv[:, :, 0, :], pv[:, :, 1, :]
                        Sc.op("vector", lambda e, av=av, x1=x1, cb=cb: e.tensor_tensor(out=av[:, :, 0, :], in0=x1, in1=cb, op=ALU.mult), reads=[p_b, const_b], writes=[a_b])
                        Sc.op("vector", lambda e, av=av, x2=x2, cb=cb: e.tensor_tensor(out=av[:, :, 1, :], in0=x2, in1=cb, op=ALU.mult), reads=[p_b, const_b], writes=[a_b])
                        Sc.op("vector", lambda e, bv=bv, x2=x2, sn=sn: e.tensor_tensor(out=bv[:, :, 0, :], in0=x2, in1=sn, op=ALU.mult), reads=[p_b, const_b], writes=[b_b])
                        Sc.op("vector", lambda e, bv=bv, x1=x1, sn=sn: e.tensor_tensor(out=bv[:, :, 1, :], in0=x1, in1=sn, op=ALU.mult), reads=[p_b, const_b], writes=[b_b])
                        Sc.op("gpsimd", lambda e, ov=ov, av=av, bv=bv: e.tensor_tensor(out=ov[:, :, 0, :], in0=av[:, :, 0, :], in1=bv[:, :, 0, :], op=ALU.subtract), reads=[a_b, b_b], writes=[o_b])
                        Sc.op("gpsimd", lambda e, ov=ov, av=av, bv=bv: e.tensor_tensor(out=ov[:, :, 1, :], in0=av[:, :, 1, :], in1=bv[:, :, 1, :], op=ALU.add), reads=[a_b, b_b], writes=[o_b])
                    Sc.dma("sync", lambda e, o=o, t=t, c0=c0, N=N: e.dma_start(out=self.PROJ[t * 128:(t + 1) * 128, c0:c0 + N], in_=o[:, 0:N]), reads=[o_b], writes=[proj_b])

            for fg in range(6):
                w, w_b = wr.next()
                Sc.dma("gpsimd", lambda e, w=w, fg=fg: e.dma_start(out=w[:, :, :], in_=self.w_gate[0, :, fg * 512:(fg + 1) * 512].rearrange("(k p) c -> p k c", p=128)), writes=[w_b])
                for j in range(4):
                    fc = fg * 4 + j
                    for tb in range(NB):
                        p, p_b = pA.next()
                        for k in range(8):
                            Sc.op("tensor", lambda e, p=p, w=w, tb=tb, k=k, j=j: e.matmul(p[:, :], lhsT=w[:, k, j * 128:(j + 1) * 128], rhs=xT[:, k, tb * 512:(tb + 1) * 512], start=(k == 0), stop=(k == 7)),
                                  reads=xT_b[tb * 4:tb * 4 + 4] + [w_b], writes=[p_b])
                        o, o_b = ob.next()
                        Sc.op("scalar", lambda e, o=o, p=p, fc=fc: e.activation(out=o[:, :], in_=p[:, :], func=AF.Tanh, bias=hb[:, fc:fc + 1], scale=0.5), reads=[p_b, const_b], writes=[o_b])
                        Sc.dma("sync", lambda e, o=o, fc=fc, tb=tb: e.dma_start(out=self.GTT[fc * 128:(fc + 1) * 128, tb * 512:(tb + 1) * 512], in_=o[:, :]), reads=[o_b], writes=[gtt_b])
            Sc.barrier()
            Sc.emit(st)


    def phase_b1(self):
        nc = self.nc
        with contextlib.ExitStack() as st:
            sb = lambda n, s, d: st.enter_context(nc.sbuf_tensor("b1_" + n, s, d))
            ps = lambda n, s, d: st.enter_context(nc.psum_tensor("b1_" + n, s, d))
            Sc = Sched(nc, prefix="b1_")
            const_b = Buf("const")
            amf = sb("amf", [128, 1024], F32)
            am = sb("am", [128, 1024], BF16)
            identf = sb("identf", [128, 128], F32)
            ident = sb("ident", [128, 128], BF16)
            Sc.dma("sync", lambda e: e.dma_start(out=amf[:], in_=self.c_amask[:, :]), writes=[const_b])
            Sc.dma("sync", lambda e: e.dma_start(out=identf[:], in_=self.c_ident[:, :]), writes=[const_b])
            Sc.op("vector", lambda e: e.tensor_copy(out=am[:], in_=amf[:]), reads=[const_b], writes=[const_b])
            Sc.op("vector", lambda e: e.tensor_copy(out=ident[:], in_=identf[:]), reads=[const_b], writes=[const_b])
            sets = []
            for i in range(3):
                Lc = S if i == 2 else 1024
                nbc = Lc // 128
                qT = [sb("qT%d_%d" % (i, pp), [128, Lc], BF16) for pp in range(2)]
                qZ = [[sb("qZ%d_%d_%d" % (i, pp, hh), [128, Lc], BF16) for hh in range(2)] for pp in range(2)]
                qz_b = [[[Buf("qz") for _ in range(nbc)] for hh in range(2)] for pp in range(2)]
                for pp in range(2):
                    Sc.op("gpsimd", lambda e, t=qZ[pp][0]: e.memset(t[64:128, :], 0.0), writes=qz_b[pp][0])
                    Sc.op("gpsimd", lambda e, t=qZ[pp][1]: e.memset(t[0:64, :], 0.0), writes=qz_b[pp][1])
                kT = [sb("kT%d_%d" % (i, pp), [128, Lc], BF16) for pp in range(2)]
                va = sb("va%d" % i, [128, nbc + 1, 4, 128], BF16)
                q_b = [[Buf("q") for _ in range(nbc)] for pp in range(2)]
                k_b = [[Buf("k") for _ in range(nbc)] for pp in range(2)]
                v_b = [Buf("v") for _ in range(nbc + 1)]
                Sc.op("gpsimd", lambda e, va=va: e.memset(va[:, :, :, 64:128], 1.0), writes=v_b)
                sets.append((qT, kT, va, q_b, k_b, v_b, qZ, qz_b))
            pS = Ring([(ps("pS%d" % i, [128, 512], F32), Buf("pS%d" % i)) for i in range(3)])
            pU = Ring([(ps("pU%d" % i, [128, 512], F32), Buf("pU%d" % i)) for i in range(4)])
            Er = Ring([(sb("E%d" % i, [128, 512], BF16), Buf("E%d" % i)) for i in range(6)])
            us = Ring([(sb("us%d" % i, [128, 512], F32), Buf("us%d" % i)) for i in range(4)])
            ud_b = Buf("UD")
            unit = 0
            for g, dl in enumerate((1, 4, 16)):
                if g not in getattr(self, "b1_groups", (0, 1, 2)):
                    continue
                L = S // dl
                nb = L // 128
                ucurs = {0: None, 1: None}
                for r in range(dl):
                    qT, kT, va, q_b, k_b, v_b, qZ, qz_b = sets[2] if g == 0 else sets[unit % 2]
                    unit += 1
                    rows = self.PROJ.rearrange("(i r) c -> r i c", r=dl)[r]
                    vc = C_VA + g * 256
                    Sc.dma("sync", lambda e, va=va, rows=rows, vc=vc: e.dma_start(out=va[0:64, 0, :, 0:64], in_=rows[0:64, vc:vc + 256].rearrange("k (h d) -> k h d", d=64)), writes=[v_b[0]])
                    for h4 in range(4):
                        Sc.dma("sync", lambda e, va=va, rows=rows, vc=vc, nb=nb, L=L, h4=h4: e.dma_start(out=va[:, 1:nb, h4, 0:64], in_=rows[64:L - 64, vc + h4 * 64:vc + h4 * 64 + 64].rearrange("(j k) d -> k j d", k=128)), writes=v_b[1:nb])
                    Sc.dma("sync", lambda e, va=va, rows=rows, vc=vc, nb=nb, L=L: e.dma_start(out=va[0:64, nb, :, 0:64], in_=rows[L - 64:L, vc:vc + 256].rearrange("k (h d) -> k h d", d=64)), writes=[v_b[nb]])
                    for pp in range(2):
                        qc = C_QA + (g * 4 + 2 * pp) * 64
                        kc = C_KA + (g * 4 + 2 * pp) * 64
                        for blk in range(nb):
                            Sc.dma("sync", lambda e, dst=kT[pp], rows=rows, kc=kc, blk=blk: e.dma_start_transpose(out=dst[:, blk * 128:(blk + 1) * 128], in_=rows[blk * 128:(blk + 1) * 128, kc:kc + 128]), writes=[k_b[pp][blk]])
                            Sc.dma("sync", lambda e, dst=qT[pp], rows=rows, qc=qc, blk=blk: e.dma_start_transpose(out=dst[:, blk * 128:(blk + 1) * 128], in_=rows[blk * 128:(blk + 1) * 128, qc:qc + 128]), writes=[q_b[pp][blk]])
                            sl = slice(blk * 128, (blk + 1) * 128)
                            Sc.op("vector", lambda e, d=qZ[pp][0], s_=qT[pp], sl=sl: e.tensor_copy(out=d[0:64, sl], in_=s_[0:64, sl]), reads=[q_b[pp][blk]], writes=[qz_b[pp][0][blk]])
                            Sc.op("gpsimd", lambda e, d=qZ[pp][1], s_=qT[pp], sl=sl: e.tensor_copy(out=d[64:128, sl], in_=s_[64:128, sl]), reads=[q_b[pp][blk]], writes=[qz_b[pp][1][blk]])
                    for pp in range(2):
                        if getattr(self, "b1_stage", 9) < 1:
                            continue
                        Es = {}
                        ucur = ucurs[pp]
                        for j in range(nb + 1):
                            k0, k1 = max(0, 128 * j - 64), min(L, 128 * j + 64)
                            M = k1 - k0
                            q0, q1 = max(0, 128 * (j - 1)), min(L, 128 * (j + 1))
                            Nq = q1 - q0
                            if j == 0:
                                mk = am[0:64, 512:768]
                            elif j == nb:
                                mk = am[0:64, 768:1024]
                            else:
                                mk = am[:, 0:512]
                            kblks = sorted(set([k0 // 128, (k1 - 1) // 128]))
                            qblks = sorted(set([q0 // 128, (q1 - 1) // 128]))
                            p, p_b = pS.next()
                            for hh in range(2):
                                rd = [k_b[pp][b] for b in kblks] + [qz_b[pp][hh][b] for b in qblks]
                                Sc.op("tensor", lambda e, p=p, kt=kT[pp], qt=qZ[pp][hh], hh=hh, k0=k0, k1=k1, q0=q0, q1=q1, M=M, Nq=Nq:
                                      e.matmul(p[0:M, hh * Nq:(hh + 1) * Nq], lhsT=kt[:, k0:k1], rhs=qt[:, q0:q1], start=True, stop=False),
                                      reads=rd, writes=[p_b])
                                Sc.op("tensor", lambda e, p=p, mk=mk, M=M, Nq=Nq, hh=hh: e.matmul(p[0:M, hh * Nq:(hh + 1) * Nq], lhsT=ident[0:M, 0:M], rhs=mk[:, 0:Nq], start=False, stop=True),
                                      reads=[const_b], writes=[p_b])
                            E, E_b = Er.next()
                            Sc.op("scalar", lambda e, E=E, p=p, M=M, Nq=Nq: e.activation(out=E[0:M, 0:2 * Nq], in_=p[0:M, 0:2 * Nq], func=AF.Exp, scale=0.125), reads=[p_b], writes=[E_b])
                            Es[j] = (E, E_b, M, Nq)
                            if j == 0 or getattr(self, "b1_stage", 9) < 2:
                                continue
                            b = j - 1
                            G = r * nb + b
                            if G % 4 == 0:
                                ucur = ucurs[pp] = [pU.next(), pU.next()]
                            for hh in range(2):
                                (u, u_b) = ucur[hh]
                                col = (G % 4) * 128
                                for n_, jj in enumerate((b, b + 1)):
                                    Ej, Ej_b, Mj, Nqj = Es[jj]
                                    if jj == b:
                                        c = hh * Nqj + (128 if b >= 1 else 0)
                                    else:
                                        c = hh * Nqj
                                    hv = 2 * pp + hh
                                    Sc.op("tensor", lambda e, u=u, va=va, Ej=Ej, jj=jj, hv=hv, Mj=Mj, c=c, col=col, n_=n_:
                                          e.matmul(u[:, col:col + 128], lhsT=va[0:Mj, jj, hv, :], rhs=Ej[0:Mj, c:c + 128], start=(n_ == 0), stop=(n_ == 1)),
                                          reads=[v_b[jj], Ej_b], writes=[u_b])
                            del Es[b]
                            if G % 4 == 3:
                                for hh in range(2):
                                    (u, u_b) = ucur[hh]
                                    o, o_b = us.next()
                                    if hh == 0:
                                        Sc.op("vector", lambda e, o=o, u=u: e.tensor_copy(out=o[:], in_=u[:]), reads=[u_b], writes=[o_b])
                                    else:
                                        Sc.op("scalar", lambda e, o=o, u=u: e.activation(out=o[:], in_=u[:], func=AF.Copy), reads=[u_b], writes=[o_b])
                                    hrow = (g * 4 + 2 * pp + hh) * 128
                                    c0 = (G - 3) * 128
                                    Sc.dma("gpsimd", lambda e, o=o, hrow=hrow, c0=c0: e.dma_start(out=self.UD[hrow:hrow + 128, c0:c0 + 512], in_=o[:]), reads=[o_b], writes=[ud_b])
            Sc.barrier()
            Sc.emit(st)

    def phase_b1c(self):
        nc = self.nc
        with contextlib.ExitStack() as st:
            sb = lambda n, s, d: st.enter_context(nc.sbuf_tensor("b1c_" + n, s, d))
            Sc = Sched(nc, prefix="b1c_")
            CH = 2048
            Ut = [Ring([(sb("U%d_%d" % (g, i), [128, CH], F32), Buf("U")) for i in range(2)]) for g in range(3)]
            Dt = [Ring([(sb("D%d_%d" % (g, i), [128, CH], F32), Buf("D")) for i in range(2)]) for g in range(3)]
            Rr = Ring([(sb("R%d" % i, [128, CH], F32), Buf("R")) for i in range(2)])
            Yr = Ring([(sb("Y%d" % i, [128, CH], BF16), Buf("Y")) for i in range(3)])
            yat_b = Buf("YAT")
            for c2 in range(S // CH):
                for sp in range(2):
                    tiles = []
                    for g, dl in enumerate((1, 4, 16)):
                        L = S // dl
                        il = CH // dl
                        u, u_b = Ut[g].next()
                        d_, d_b = Dt[g].next()
                        for hh in range(2):
                            h = g * 4 + 2 * sp + hh
                            srcU = self.UD[h * 128:h * 128 + 64, :].rearrange("p (r i) -> p r i", r=dl)[:, :, c2 * il:(c2 + 1) * il]
                            srcD = self.UD[h * 128 + 64:h * 128 + 128, :].rearrange("p (r i) -> p r i", r=dl)[:, :, c2 * il:(c2 + 1) * il]
                            Sc.dma("sync", lambda e, u=u, hh=hh, srcU=srcU, dl=dl: e.dma_start(out=u[hh * 64:(hh + 1) * 64, :].rearrange("p (r i) -> p r i", r=dl), in_=srcU), writes=[u_b])
                            Sc.dma("gpsimd", lambda e, d_=d_, hh=hh, srcD=srcD, dl=dl: e.dma_start(out=d_[hh * 64:(hh + 1) * 64, :].rearrange("p (r i) -> p r i", r=dl), in_=srcD), writes=[d_b])
                        tiles.append((u, u_b, d_, d_b, dl))
                    R, R_b = Rr.next()
                    nat = lambda t, dl: t[:, :].rearrange("p (i r) -> p i r", r=dl)
                    res = lambda t, dl: t[:, :].rearrange("p (r i) -> p i r", r=dl)
                    (u0, u0_b, d0, d0_b, _), (u1, u1_b, d1, d1_b, _), (u2, u2_b, d2, d2_b, _) = tiles
                    Sc.op("gpsimd", lambda e, R=R, d0=d0, d1=d1: e.tensor_tensor(out=nat(R, 4), in0=nat(d0, 4), in1=res(d1, 4), op=ALU.add), reads=[d0_b, d1_b], writes=[R_b])
                    Sc.op("gpsimd", lambda e, R=R, d2=d2: e.tensor_tensor(out=nat(R, 16), in0=nat(R, 16), in1=res(d2, 16), op=ALU.add), reads=[d2_b, R_b], writes=[R_b])
                    Sc.op("vector", lambda e, R=R: e.reciprocal(out=R[:, :], in_=R[:, :]), reads=[R_b], writes=[R_b])
                    for g, (u, u_b, d_, d_b, dl) in enumerate(tiles):
                        y, y_b = Yr.next()
                        eng = "vector" if g != 1 else "gpsimd"
                        Sc.op(eng, lambda e, y=y, u=u, dl=dl, R=R: e.tensor_tensor(out=nat(y, dl), in0=res(u, dl), in1=nat(R, dl), op=ALU.mult), reads=[u_b, R_b], writes=[y_b])
                        row = (g * 4 + 2 * sp) * 64
                        Sc.dma("sync", lambda e, y=y, row=row, c2=c2: e.dma_start(out=self.YAT[row:row + 128, c2 * CH:(c2 + 1) * CH], in_=y[:, :]), reads=[y_b], writes=[yat_b])
            Sc.barrier()
            Sc.emit(st)

    def phase_b2(self):
        nc = self.nc
        with contextlib.ExitStack() as st:
            sb = lambda n, s, d: st.enter_context(nc.sbuf_tensor("b2_" + n, s, d))
            ps = lambda n, s, d: st.enter_context(nc.psum_tensor("b2_" + n, s, d))
            Sc = Sched(nc, prefix="b2_")
            cb_ = Buf("const")
            cret = sb("cret", [128, 8, 128], F32)
            dec = sb("dec", [128, 12], F32)
            lg = sb("lg", [128, 12], F32)
            lgp = sb("lgp", [128, 6], F32)
            Mall = sb("Mall", [128, 6, 128], F32)
            tmpm = sb("tmpm", [128, 128], F32)
            zeta = sb("zeta", [128, 12], F32)
            XiF = sb("XiF", [128, 3, 128], F32)
            XiB = sb("XiB", [128, 3, 128], F32)
            g128 = sb("g128", [128, 6], F32)
            gret = sb("gret", [128, 768], F32)
            negh = sb("negh", [128, 6], F32)
            Sc.dma("sync", lambda e: e.dma_start(out=cret[:], in_=self.c_ret[:, :, :]), writes=[cb_])
            Sc.dma("sync", lambda e: e.dma_start(out=dec[:, 0:6], in_=self.dec_f[0, :].partition_broadcast(128)), writes=[cb_])
            Sc.dma("sync", lambda e: e.dma_start(out=dec[:, 6:12], in_=self.dec_b[0, :].partition_broadcast(128)), writes=[cb_])
            Sc.dma("sync", lambda e: e.dma_start(out=gret[:], in_=self.g_ret[0, :].partition_broadcast(128)), writes=[cb_])
            C = lambda eng, fn: Sc.op(eng, fn, reads=[cb_], writes=[cb_])
            C("vector", lambda e: e.memset(negh[:], -0.5))
            C("vector", lambda e: e.tensor_scalar(out=gret[:], in0=gret[:], scalar1=0.5, scalar2=None, op0=ALU.mult))
            C("scalar", lambda e: e.activation(out=lg[:], in_=dec[:], func=AF.Exp, scale=-1.0))
            C("vector", lambda e: e.tensor_scalar(out=lg[:], in0=lg[:], scalar1=1.0, scalar2=None, op0=ALU.add))
            C("scalar", lambda e: e.activation(out=lg[:], in_=lg[:], func=AF.Ln))
            C("vector", lambda e: e.tensor_scalar(out=lg[:], in0=lg[:], scalar1=-1.0, scalar2=None, op0=ALU.mult))
            for dr in range(2):
                for hh in range(2):
                    src = lg[hh * 64:(hh + 1) * 64, dr * 6:(dr + 1) * 6].rearrange("p (a b) -> p a b", b=2)[:, :, hh]
                    C("vector", lambda e, dr=dr, hh=hh, src=src: e.tensor_copy(out=lgp[hh * 64:(hh + 1) * 64, dr * 3:(dr + 1) * 3], in_=src))
            for h in range(6):
                C("scalar", lambda e, h=h: e.activation(out=Mall[:, h, :], in_=cret[:, 0, :], func=AF.Exp, scale=lg[:, h:h + 1]))
                C("vector", lambda e, h=h: e.tensor_tensor(out=Mall[:, h, :], in0=Mall[:, h, :], in1=cret[:, 2, :], op=ALU.mult))
                C("scalar", lambda e, h=h: e.activation(out=tmpm[:], in_=cret[:, 1, :], func=AF.Exp, scale=lg[:, 6 + h:7 + h]))
                C("vector", lambda e, h=h: e.tensor_tensor(out=tmpm[:], in0=tmpm[:], in1=cret[:, 3, :], op=ALU.mult))
                C("vector", lambda e, h=h: e.tensor_tensor(out=Mall[:, h, :], in0=Mall[:, h, :], in1=tmpm[:], op=ALU.add))
                C("scalar", lambda e, h=h: e.activation(out=zeta[:, h:h + 1], in_=cret[:, 6, 0:1], func=AF.Exp, scale=lg[:, h:h + 1]))
                C("scalar", lambda e, h=h: e.activation(out=zeta[:, 6 + h:7 + h], in_=cret[:, 6, 1:2], func=AF.Exp, scale=lg[:, 6 + h:7 + h]))
            C("vector", lambda e: e.tensor_scalar(out=zeta[:], in0=zeta[:], scalar1=0.125, scalar2=None, op0=ALU.mult))
            for pp in range(3):
                C("scalar", lambda e, pp=pp: e.activation(out=XiF[:, pp, :], in_=cret[:, 4, :], func=AF.Exp, scale=lgp[:, pp:pp + 1]))
                C("scalar", lambda e, pp=pp: e.activation(out=XiB[:, pp, :], in_=cret[:, 5, :], func=AF.Exp, scale=lgp[:, 3 + pp:4 + pp]))
            C("scalar", lambda e: e.activation(out=g128[:], in_=lgp[:], func=AF.Exp, scale=128.0))

            Sall = [sb("SallF", [128, NT, 3, 128], BF16), sb("SallB", [128, NT, 3, 128], BF16)]
            sall_b = [[Buf("sf") for _ in range(NT)], [Buf("sb") for _ in range(NT)]]
            Scur = [sb("ScurF", [128, 3, 128], F32), sb("ScurB", [128, 3, 128], F32)]
            scur_b = [Buf("scf"), Buf("scb")]
            kt_r = Ring([(sb("ktok%d" % i, [128, 384], BF16), Buf("ktok")) for i in range(3)])
            vt_r = Ring([(sb("vtok%d" % i, [128, 768], BF16), Buf("vtok")) for i in range(3)])
            kz_r = Ring([(sb("kz%d" % i, [128, 6, 64], BF16), Buf("kz")) for i in range(3)])
            pkv = Ring([(ps("pkv%d" % i, [128, 512], F32), Buf("pkv")) for i in range(2)])
            for dr in range(2):
                Sc.op("vector", lambda e, dr=dr: e.memset(Scur[dr][:], 0.0), writes=[scur_b[dr]])
            for step in range(2 * NT):
                dr = step % 2
                c = (step // 2) if dr == 0 else (NT - 1 - step // 2)
                if True:
                    kt, kt_b = kt_r.next()
                    vt, vt_b = vt_r.next()
                    Sc.dma("sync", lambda e, kt=kt, c=c: e.dma_start(out=kt[:], in_=self.PROJ[c * 128:(c + 1) * 128, C_KR:C_KR + 384]), writes=[kt_b])
                    Sc.dma("sync", lambda e, vt=vt, c=c: e.dma_start(out=vt[:], in_=self.PROJ[c * 128:(c + 1) * 128, C_VR:C_VR + 768]), writes=[vt_b])
                    kz, kz_b = kz_r.next()
                    zb = zeta[:, dr * 6:(dr + 1) * 6].unsqueeze(2).broadcast_to([128, 6, 64])
                    Sc.op("gpsimd", lambda e, kz=kz, kt=kt, zb=zb: e.tensor_tensor(out=kz[:], in0=kt[:, :].rearrange("p (h d) -> p h d", d=64), in1=zb, op=ALU.mult), reads=[kt_b, cb_], writes=[kz_b])
                    p, p_b = pkv.next()
                    for h in range(6):
                        pp, hh = h // 2, h % 2
                        Sc.op("tensor", lambda e, p=p, kz=kz, vt=vt, h=h, pp=pp, hh=hh: e.matmul(p[hh * 64:(hh + 1) * 64, pp * 128:(pp + 1) * 128], lhsT=kz[:, h, :], rhs=vt[:, h * 128:(h + 1) * 128], start=True, stop=True),
                              reads=[kz_b, vt_b], writes=[p_b])
                    Sc.op("scalar", lambda e, dr=dr, c=c: e.activation(out=Sall[dr][:, c, :, :], in_=Scur[dr][:], func=AF.Copy), reads=[scur_b[dr]], writes=[sall_b[dr][c]])
                    for pp in range(3):
                        Sc.op("vector", lambda e, dr=dr, pp=pp, p=p: e.scalar_tensor_tensor(out=Scur[dr][:, pp, :], in0=Scur[dr][:, pp, :], scalar=g128[:, dr * 3 + pp:dr * 3 + pp + 1], in1=p[:, pp * 128:(pp + 1) * 128], op0=ALU.mult, op1=ALU.add),
                              reads=[p_b, scur_b[dr], cb_], writes=[scur_b[dr]])

            qt_r = Ring([(sb("qTp%d" % i, [128, 3, 128], BF16), Buf("qTp")) for i in range(2)])
            ktp_r = Ring([(sb("kTp%d" % i, [128, 3, 128], BF16), Buf("kTp")) for i in range(2)])
            gr_r = Ring([(sb("gr%d" % i, [128, 768], BF16), Buf("gr")) for i in range(2)])
            qz_r = []
            for i in range(2):
                t3 = [sb("qz%d_%d" % (i, j), [128, 6, 128], BF16) for j in range(3)]
                b3 = [Buf("qz") for j in range(3)]
                for j in range(3):
                    Sc.op("gpsimd", lambda e, t=t3[j]: e.memset(t[:], 0.0), writes=[b3[j]])
                qz_r.append((t3, b3))
            qz_r = Ring(qz_r)
            pst = Ring([(ps("pst%d" % i, [128, 512], F32), Buf("pst")) for i in range(2)])
            pyy = Ring([(ps("pyy%d" % i, [128, 512], F32), Buf("pyy")) for i in range(4)])
            A_r = Ring([(sb("A%d" % i, [128, 6, 128], BF16), Buf("A")) for i in range(2)])
            ysb_r = Ring([(sb("ysb%d" % i, [128, 768], F32), Buf("ysb")) for i in range(2)])
            ysq_r = Ring([(sb("ysq%d" % i, [128, 768], F32), Buf("ysq")) for i in range(2)])
            st_r = Ring([(sb("stat%d" % i, [128, 4, 6], F32), Buf("stat")) for i in range(2)])
            yo_r = Ring([(sb("yo%d" % i, [128, 768], BF16), Buf("yo")) for i in range(2)])
            yr_b = Buf("YR")
            def stage1(c):
                qt, qt_b = qt_r.next()
                ktp, ktp_b = ktp_r.next()
                vt, vt_b = vt_r.next()
                gr, gr_b = gr_r.next()
                for pp in range(3):
                    Sc.dma("sync", lambda e, qt=qt, c=c, pp=pp: e.dma_start_transpose(out=qt[:, pp, :], in_=self.PROJ[c * 128:(c + 1) * 128, C_QR + pp * 128:C_QR + (pp + 1) * 128]), writes=[qt_b])
                    Sc.dma("sync", lambda e, ktp=ktp, c=c, pp=pp: e.dma_start_transpose(out=ktp[:, pp, :], in_=self.PROJ[c * 128:(c + 1) * 128, C_KR + pp * 128:C_KR + (pp + 1) * 128]), writes=[ktp_b])
                Sc.dma("sync", lambda e, vt=vt, c=c: e.dma_start(out=vt[:], in_=self.PROJ[c * 128:(c + 1) * 128, C_VR:C_VR + 768]), writes=[vt_b])
                Sc.dma("sync", lambda e, gr=gr, c=c: e.dma_start(out=gr[:], in_=self.PROJ[c * 128:(c + 1) * 128, C_GR:C_GR + 768]), writes=[gr_b])
                (qz, qxf, qxb), (qz_b, qxf_b, qxb_b) = qz_r.next()
                for h in range(6):
                    pp, hh = h // 2, h % 2
                    sl = slice(hh * 64, (hh + 1) * 64)
                    Sc.op("vector", lambda e, qz=qz, qt=qt, h=h, pp=pp, sl=sl: e.tensor_copy(out=qz[sl, h, :], in_=qt[sl, pp, :]), reads=[qt_b], writes=[qz_b])
                    Sc.op("gpsimd", lambda e, qxf=qxf, qt=qt, h=h, pp=pp, sl=sl: e.tensor_tensor(out=qxf[sl, h, :], in0=qt[sl, pp, :], in1=XiF[sl, pp, :], op=ALU.mult), reads=[qt_b, cb_], writes=[qxf_b])
                    Sc.op("vector", lambda e, qxb=qxb, qt=qt, h=h, pp=pp, sl=sl: e.tensor_tensor(out=qxb[sl, h, :], in0=qt[sl, pp, :], in1=XiB[sl, pp, :], op=ALU.mult), reads=[qt_b, cb_], writes=[qxb_b])
                (s0, s0_b), (s1, s1_b) = pst.next(), pst.next()
                for h in range(6):
                    pp = h // 2
                    tgt, tb_ = (s0, s0_b) if h < 4 else (s1, s1_b)
                    col = (h % 4) * 128
                    Sc.op("tensor", lambda e, tgt=tgt, ktp=ktp, qz=qz, h=h, pp=pp, col=col: e.matmul(tgt[:, col:col + 128], lhsT=ktp[:, pp, :], rhs=qz[:, h, :], start=True, stop=True), reads=[ktp_b, qz_b], writes=[tb_])
                A, A_b = A_r.next()
                Sc.op("vector", lambda e, A=A, s0=s0: e.tensor_tensor(out=A[:, 0:4, :], in0=s0[:, :].rearrange("p (h n) -> p h n", n=128), in1=Mall[:, 0:4, :], op=ALU.mult), reads=[s0_b, cb_], writes=[A_b])
                Sc.op("vector", lambda e, A=A, s1=s1: e.tensor_tensor(out=A[:, 4:6, :], in0=s1[:, 0:256].rearrange("p (h n) -> p h n", n=128), in1=Mall[:, 4:6, :], op=ALU.mult), reads=[s1_b, cb_], writes=[A_b])
                return dict(c=c, qxf=qxf, qxb=qxb, qxf_b=qxf_b, qxb_b=qxb_b, A=A, A_b=A_b, vt=vt, vt_b=vt_b, gr=gr, gr_b=gr_b)

            def stage2(cx):
                c, qxf, qxb, qxf_b, qxb_b, A, A_b, vt, vt_b, gr, gr_b = (cx[k] for k in ("c", "qxf", "qxb", "qxf_b", "qxb_b", "A", "A_b", "vt", "vt_b", "gr", "gr_b"))
                (y0, y0_b), (y1, y1_b) = pyy.next(), pyy.next()
                for h in range(6):
                    pp = h // 2
                    tgt, tb_ = (y0, y0_b) if h < 4 else (y1, y1_b)
                    col = (h % 4) * 128
                    Sc.op("tensor", lambda e, tgt=tgt, A=A, vt=vt, h=h, col=col: e.matmul(tgt[:, col:col + 128], lhsT=A[:, h, :], rhs=vt[:, h * 128:(h + 1) * 128], start=True, stop=False), reads=[A_b, vt_b], writes=[tb_])
                    Sc.op("tensor", lambda e, tgt=tgt, qxf=qxf, h=h, pp=pp, c=c, col=col: e.matmul(tgt[:, col:col + 128], lhsT=qxf[:, h, :], rhs=Sall[0][:, c, pp, :], start=False, stop=False), reads=[qxf_b, sall_b[0][c]], writes=[tb_])
                    Sc.op("tensor", lambda e, tgt=tgt, qxb=qxb, h=h, pp=pp, c=c, col=col: e.matmul(tgt[:, col:col + 128], lhsT=qxb[:, h, :], rhs=Sall[1][:, c, pp, :], start=False, stop=True), reads=[qxb_b, sall_b[1][c]], writes=[tb_])
                ysb, ysb_b = ysb_r.next()
                ysq, ysq_b = ysq_r.next()
                stt_, stt_b = st_r.next()
                Sc.op("scalar", lambda e, ysb=ysb, y0=y0: e.activation(out=ysb[:, 0:512], in_=y0[:, :], func=AF.Copy), reads=[y0_b], writes=[ysb_b])
                Sc.op("scalar", lambda e, ysb=ysb, y1=y1: e.activation(out=ysb[:, 512:768], in_=y1[:, 0:256], func=AF.Copy), reads=[y1_b], writes=[ysb_b])
                y3 = ysb[:, :].rearrange("p (h e) -> p h e", e=128)
                Sc.op("gpsimd", lambda e, ysq=ysq, ysb=ysb: e.tensor_tensor(out=ysq[:], in0=ysb[:], in1=ysb[:], op=ALU.mult), reads=[ysb_b], writes=[ysq_b])
                Sc.op("vector", lambda e, stt_=stt_, y3=y3: e.tensor_reduce(out=stt_[:, 0, :], in_=y3, axis=AX.X, op=ALU.add), reads=[ysb_b], writes=[stt_b])
                Sc.op("vector", lambda e, stt_=stt_, ysq=ysq: e.tensor_reduce(out=stt_[:, 1, :], in_=ysq[:, :].rearrange("p (h e) -> p h e", e=128), axis=AX.X, op=ALU.add), reads=[ysq_b], writes=[stt_b])
                Sc.op("gpsimd", lambda e, stt_=stt_: e.tensor_scalar(out=stt_[:, 0, :], in0=stt_[:, 0, :], scalar1=1.0 / 128, scalar2=None, op0=ALU.mult), reads=[stt_b], writes=[stt_b])
                Sc.op("gpsimd", lambda e, stt_=stt_: e.tensor_tensor(out=stt_[:, 2, :], in0=stt_[:, 0, :], in1=stt_[:, 0, :], op=ALU.mult), reads=[stt_b], writes=[stt_b])
                Sc.op("gpsimd", lambda e, stt_=stt_: e.tensor_scalar(out=stt_[:, 1, :], in0=stt_[:, 1, :], scalar1=1.0 / 128, scalar2=EPS, op0=ALU.mult, op1=ALU.add), reads=[stt_b], writes=[stt_b])
                Sc.op("gpsimd", lambda e, stt_=stt_: e.tensor_tensor(out=stt_[:, 1, :], in0=stt_[:, 1, :], in1=stt_[:, 2, :], op=ALU.subtract), reads=[stt_b], writes=[stt_b])
                Sc.op("gpsimd", lambda e, stt_=stt_: e.tensor_tensor(out=stt_[:, 3, :], in0=stt_[:, 1, :], in1=negh[:, :], op=ALU.pow), reads=[stt_b, cb_], writes=[stt_b])
                mb = stt_[:, 0, :].unsqueeze(2).broadcast_to([128, 6, 128])
                rb = stt_[:, 3, :].unsqueeze(2).broadcast_to([128, 6, 128])
                Sc.op("vector", lambda e, y3=y3, mb=mb: e.tensor_tensor(out=y3, in0=y3, in1=mb, op=ALU.subtract), reads=[stt_b, ysb_b], writes=[ysb_b])
                Sc.op("vector", lambda e, y3=y3, rb=rb: e.tensor_tensor(out=y3, in0=y3, in1=rb, op=ALU.mult), reads=[stt_b, ysb_b], writes=[ysb_b])
                Sc.op("gpsimd", lambda e, ysb=ysb: e.tensor_tensor(out=ysb[:], in0=ysb[:], in1=gret[:], op=ALU.mult), reads=[ysb_b, cb_], writes=[ysb_b])
                yo, yo_b = yo_r.next()
                Sc.op("gpsimd", lambda e, yo=yo, ysb=ysb, gr=gr: e.tensor_tensor(out=yo[:], in0=ysb[:], in1=gr[:], op=ALU.mult), reads=[ysb_b, gr_b], writes=[yo_b])
                Sc.dma("sync", lambda e, yo=yo, c=c: e.dma_start(out=self.YR[c * 128:(c + 1) * 128, :], in_=yo[:]), reads=[yo_b], writes=[yr_b])

            prev = None
            for c in range(NT):
                cx = stage1(c)
                if prev is not None:
                    stage2(prev)
                prev = cx
            stage2(prev)
            Sc.barrier()
            Sc.emit(st)

    def phase_b3(self):
        nc = self.nc
        with contextlib.ExitStack() as st:
            sb = lambda n, s, d: st.enter_context(nc.sbuf_tensor("b3_" + n, s, d))
            ps = lambda n, s, d: st.enter_context(nc.psum_tensor("b3_" + n, s, d))
            Sc = Sched(nc, prefix="b3_")
            cb_ = Buf("const")
            identf = sb("identf", [128, 128], F32)
            ident = sb("ident", [128, 128], BF16)
            ones = sb("ones", [128, 128], BF16)
            gmem = sb("gmem", [128, D], F32)
            negh = sb("negh", [128, 1], F32)
            wkv = sb("wkv", [128, 8, 1024], BF16)
            wkv_b = Buf("wkv")
            memT = sb("memT", [128, 8, 256], BF16)
            memT_b = Buf("memT")
            kmT = sb("kmT", [128, 4, 256], BF16)
            vm = sb("vm", [128, 2, 512], BF16)
            kv_b = Buf("kv")
            Sc.dma("sync", lambda e: e.dma_start(out=identf[:], in_=self.c_ident[:, :]), writes=[cb_])
            Sc.dma("sync", lambda e: e.dma_start(out=gmem[:], in_=self.g_mem[0, :].partition_broadcast(128)), writes=[cb_])
            Sc.dma("gpsimd", lambda e: e.dma_start(out=wkv[:], in_=self.w_mem_kv[0, :, :].rearrange("(k p) c -> p k c", p=128)), writes=[wkv_b])
            Sc.op("vector", lambda e: e.tensor_copy(out=ident[:], in_=identf[:]), reads=[cb_], writes=[cb_])
            Sc.op("vector", lambda e: e.memset(ones[:], 1.0), reads=[cb_], writes=[cb_])
            Sc.op("vector", lambda e: e.memset(negh[:], -0.5), reads=[cb_], writes=[cb_])
            mt_r = Ring([(sb("mt%d" % i, [128, D], F32), Buf("mt")) for i in range(2)])
            mg_r = Ring([(sb("mg%d" % i, [128, D], BF16), Buf("mg")) for i in range(2)])
            junk = sb("junk", [128, D], BF16)
            junk_b = Buf("junk")
            mst = sb("mst", [128, 4], F32)
            mst_b = Buf("mst")
            pT = Ring([(ps("pT%d" % i, [128, 1024], BF16), Buf("pT")) for i in range(1)])
            pA = Ring([(ps("pA%d" % i, [128, 512], F32), Buf("pA")) for i in range(7)])
            Sc.op("vector", lambda e: e.memset(mst[:], 0.0), writes=[mst_b])
            for t in range(2):
                m, m_b = mt_r.next()
                Sc.dma("sync", lambda e, m=m, t=t: e.dma_start(out=m[:], in_=self.mem[t * 128:(t + 1) * 128, :]), writes=[m_b])
                Sc.op("scalar", lambda e, m=m, t=t: e.activation(out=junk[:], in_=m[:], func=AF.Square, accum_out=mst[:, t:t + 1]), reads=[m_b, mst_b], writes=[junk_b, mst_b])
                Sc.op("gpsimd", lambda e, t=t: e.tensor_scalar(out=mst[:, t:t + 1], in0=mst[:, t:t + 1], scalar1=1.0 / D, scalar2=EPS, op0=ALU.mult, op1=ALU.add), reads=[mst_b], writes=[mst_b])
                Sc.op("gpsimd", lambda e, t=t: e.tensor_tensor(out=mst[:, 2 + t:3 + t], in0=mst[:, t:t + 1], in1=negh[:, 0:1], op=ALU.pow), reads=[mst_b, cb_], writes=[mst_b])
                g, g_b = mg_r.next()
                Sc.op("vector", lambda e, g=g, m=m, t=t: e.scalar_tensor_tensor(out=g[:], in0=m[:], scalar=mst[:, 2 + t:3 + t], in1=gmem[:], op0=ALU.mult, op1=ALU.mult), reads=[m_b, mst_b, cb_], writes=[g_b])
                p, p_b = pT.next()
                for k in range(8):
                    Sc.op("tensor", lambda e, p=p, g=g, k=k: e.transpose(out=p[:, k * 128:(k + 1) * 128], in_=g[:, k * 128:(k + 1) * 128], identity=ident[:]), reads=[g_b, cb_], writes=[p_b])
                Sc.op("vector", lambda e, p=p, t=t: e.tensor_copy(out=memT[:, :, t * 128:(t + 1) * 128], in_=p[:, :].rearrange("p (k c) -> p k c", k=8)), reads=[p_b], writes=[memT_b])
            for h in range(4):
                p, p_b = pA.next()
                for k in range(8):
                    Sc.op("tensor", lambda e, p=p, h=h, k=k: e.matmul(p[:, 0:256], lhsT=wkv[:, k, h * 128:(h + 1) * 128], rhs=memT[:, k, :], start=(k == 0), stop=(k == 7)), reads=[wkv_b, memT_b], writes=[p_b])
                Sc.op("scalar", lambda e, p=p, h=h: e.activation(out=kmT[:, h, :], in_=p[:, 0:256], func=AF.Copy), reads=[p_b], writes=[kv_b])
            for t in range(2):
                p, p_b = pA.next()
                for k in range(8):
                    Sc.op("tensor", lambda e, p=p, t=t, k=k: e.matmul(p[:, :], lhsT=memT[:, k, t * 128:(t + 1) * 128], rhs=wkv[:, k, 512:1024], start=(k == 0), stop=(k == 7)), reads=[wkv_b, memT_b], writes=[p_b])
                Sc.op("scalar", lambda e, p=p, t=t: e.activation(out=vm[:, t, :], in_=p[:, :], func=AF.Copy), reads=[p_b], writes=[kv_b])
            qm_r = Ring([(sb("qm%d" % i, [128, 512], BF16), Buf("qm")) for i in range(3)])
            E_r = Ring([(sb("E%d" % i, [128, 2, 512], BF16), Buf("E")) for i in range(2)])
            R_r = Ring([(sb("R%d" % i, [128, 512], F32), Buf("R")) for i in range(2)])
            y_r = Ring([(sb("y%d" % i, [128, 512], BF16), Buf("y")) for i in range(3)])
            ymt_b = Buf("YMT")
            sc = 1.0 / float(np.sqrt(128.0))
            for tb in range(NB):
                for h in range(4):
                    q, q_b = qm_r.next()
                    for tt in range(4):
                        r0 = tb * 512 + tt * 128
                        Sc.dma("sync", lambda e, q=q, r0=r0, h=h, tt=tt: e.dma_start_transpose(out=q[:, tt * 128:(tt + 1) * 128], in_=self.PROJ[r0:r0 + 128, C_QM + h * 128:C_QM + (h + 1) * 128]), writes=[q_b])
                    E, E_b = E_r.next()
                    for mc in range(2):
                        p, p_b = pA.next()
                        Sc.op("tensor", lambda e, p=p, h=h, mc=mc, q=q: e.matmul(p[:, :], lhsT=kmT[:, h, mc * 128:(mc + 1) * 128], rhs=q[:, :], start=True, stop=True), reads=[kv_b, q_b], writes=[p_b])
                        Sc.op("scalar", lambda e, E=E, p=p, mc=mc: e.activation(out=E[:, mc, :], in_=p[:, :], func=AF.Exp, scale=sc), reads=[p_b], writes=[E_b])
                    pu, pu_b = pA.next()
                    pd, pd_b = pA.next()
                    for mc in range(2):
                        Sc.op("tensor", lambda e, pu=pu, E=E, mc=mc, h=h: e.matmul(pu[:, :], lhsT=vm[:, mc, h * 128:(h + 1) * 128], rhs=E[:, mc, :], start=(mc == 0), stop=(mc == 1)), reads=[kv_b, E_b], writes=[pu_b])
                    for mc in range(2):
                        Sc.op("tensor", lambda e, pd=pd, E=E, mc=mc: e.matmul(pd[:, :], lhsT=ones[:, :], rhs=E[:, mc, :], start=(mc == 0), stop=(mc == 1)), reads=[cb_, E_b], writes=[pd_b])
                    R, R_b = R_r.next()
                    Sc.op("vector", lambda e, R=R, pd=pd: e.reciprocal(out=R[:, :], in_=pd[:, :]), reads=[pd_b], writes=[R_b])
                    y, y_b = y_r.next()
                    Sc.op("vector", lambda e, y=y, R=R, pu=pu: e.tensor_tensor(out=y[:, :], in0=pu[:, :], in1=R[:, :], op=ALU.mult), reads=[pu_b, R_b], writes=[y_b])
                    Sc.dma("gpsimd", lambda e, y=y, h=h, tb=tb: e.dma_start(out=self.YMT[h * 128:(h + 1) * 128, tb * 512:(tb + 1) * 512], in_=y[:, :]), reads=[y_b], writes=[ymt_b])
            Sc.barrier()
            Sc.emit(st)

    def phase_c(self):
        nc = self.nc
        with contextlib.ExitStack() as st:
            sb = lambda n, s, d: st.enter_context(nc.sbuf_tensor("pc_" + n, s, d))
            ps = lambda n, s, d: st.enter_context(nc.psum_tensor("pc_" + n, s, d))
            Sc = Sched(nc, prefix="pc_")
            cb_ = Buf("const")
            w_b = Buf("w")
            identf = sb("identf", [128, 128], F32)
            ident = sb("ident", [128, 128], BF16)
            gffn = sb("gffn", [128, D], F32)
            negh = sb("negh", [128, 1], F32)
            wpa = sb("wpa", [128, 6, D], BF16)
            wpr = sb("wpr", [128, 6, D], BF16)
            wpm = sb("wpm", [128, 4, D], BF16)
            wo = sb("wo", [128, 8, D], BF16)
            Sc.dma("sync", lambda e: e.dma_start(out=identf[:], in_=self.c_ident[:, :]), writes=[cb_])
            Sc.dma("sync", lambda e: e.dma_start(out=gffn[:], in_=self.g_ffn[0, :].partition_broadcast(128)), writes=[cb_])
            Sc.dma("gpsimd", lambda e: e.dma_start(out=wpa[:], in_=self.w_pa[0, :, :].rearrange("(k p) c -> p k c", p=128)), writes=[w_b])
            Sc.dma("gpsimd", lambda e: e.dma_start(out=wpr[:], in_=self.w_pr[0, :, :].rearrange("(k p) c -> p k c", p=128)), writes=[w_b])
            Sc.dma("gpsimd", lambda e: e.dma_start(out=wpm[:], in_=self.w_pm[0, :, :].rearrange("(k p) c -> p k c", p=128)), writes=[w_b])
            Sc.dma("gpsimd", lambda e: e.dma_start(out=wo[:], in_=self.w_out[0, :, :].rearrange("(k p) c -> p k c", p=128)), writes=[w_b])
            Sc.op("vector", lambda e: e.tensor_copy(out=ident[:], in_=identf[:]), reads=[cb_], writes=[cb_])
            Sc.op("vector", lambda e: e.memset(negh[:], -0.5), reads=[cb_], writes=[cb_])
            ya_r = Ring([(sb("ya%d" % i, [128, 6, 512], BF16), Buf("ya")) for i in range(2)])
            yr_r = Ring([(sb("yr%d" % i, [128, 6, 512], BF16), Buf("yr")) for i in range(2)])
            ym_r = Ring([(sb("ym%d" % i, [128, 4, 512], BF16), Buf("ym")) for i in range(2)])
            gt_r = Ring([(sb("gt%d" % i, [128, 3, 512], BF16), Buf("gt")) for i in range(3)])
            mg_r = Ring([(sb("mg%d" % i, [128, 8, 512], BF16), Buf("mg")) for i in range(2)])
            m1_r = Ring([(sb("m1_%d" % i, [128, 512], F32), Buf("m1")) for i in range(2)])
            m2_r = Ring([(sb("m2_%d" % i, [128, 512], F32), Buf("m2")) for i in range(2)])
            m3_r = Ring([(sb("m3_%d" % i, [128, 512], F32), Buf("m3")) for i in range(2)])
            x_r = Ring([(sb("x%d" % i, [128, D], F32), Buf("x")) for i in range(2)])
            h_r = Ring([(sb("h%d" % i, [128, D], F32), Buf("h")) for i in range(2)])
            hn_r = Ring([(sb("hn%d" % i, [128, D], BF16), Buf("hn")) for i in range(2)])
            ht_r = Ring([(sb("ht%d" % i, [128, 8, 128], BF16), Buf("ht")) for i in range(2)])
            junk = sb("junk", [128, D], BF16)
            junk_b = Buf("junk")
            stt_ = sb("stat", [128, 3, NT], F32)
            st_b = [Buf("st") for _ in range(NT)]
            Sc.op("vector", lambda e: e.memset(stt_[:], 0.0), writes=st_b)
            pP = Ring([(ps("pP%d" % i, [128, 512], F32), Buf("pP")) for i in range(5)])
            pO = Ring([(ps("pO%d" % i, [128, 512], F32), Buf("pO")) for i in range(2)])
            pT = Ring([(ps("pT%d" % i, [128, 1024], BF16), Buf("pT")) for i in range(1)])
            pend = []
            h_out_b = Buf("H")
            hnt_b = Buf("HNT")
            for tb in range(NB):
                cs = slice(tb * 512, (tb + 1) * 512)
                ya, ya_b = ya_r.next()
                yr, yr_b = yr_r.next()
                ym, ym_b = ym_r.next()
                Sc.dma("sync", lambda e, ya=ya, cs=cs: e.dma_start(out=ya[:], in_=self.YAT[:, cs].rearrange("(k p) s -> p k s", p=128)), writes=[ya_b])
                Sc.dma("sync", lambda e, ym=ym, cs=cs: e.dma_start(out=ym[:], in_=self.YMT[:, cs].rearrange("(k p) s -> p k s", p=128)), writes=[ym_b])
                for tt in range(4):
                    r0 = tb * 512 + tt * 128
                    for k in range(6):
                        Sc.dma("sync", lambda e, yr=yr, r0=r0, k=k, tt=tt: e.dma_start_transpose(out=yr[:, k, tt * 128:(tt + 1) * 128], in_=self.YR[r0:r0 + 128, k * 128:(k + 1) * 128]), writes=[yr_b])
                mg, mg_b = mg_r.next()
                for fc in range(8):
                    gt, gt_b = gt_r.next()
                    Sc.dma("sync", lambda e, gt=gt, fc=fc, cs=cs: e.dma_start(out=gt[:], in_=self.GTT.rearrange("(i f) s -> f i s", i=3)[fc * 128:(fc + 1) * 128, :, cs]), writes=[gt_b])
                    prs = []
                    for (w, src, src_b, nk) in ((wpa, ya, ya_b, 6), (wpr, yr, yr_b, 6), (wpm, ym, ym_b, 4)):
                        p, p_b = pP.next()
                        for k in range(nk):
                            Sc.op("tensor", lambda e, p=p, w=w, src=src, k=k, fc=fc, nk=nk: e.matmul(p[:, :], lhsT=w[:, k, fc * 128:(fc + 1) * 128], rhs=src[:, k, :], start=(k == 0), stop=(k == nk - 1)), reads=[w_b, src_b], writes=[p_b])
                        prs.append((p, p_b))
                    m1, m1_b = m1_r.next()
                    m2, m2_b = m2_r.next()
                    m3, m3_b = m3_r.next()
                    for i, (m, m_b) in enumerate(((m1, m1_b), (m2, m2_b), (m3, m3_b))):
                        p, p_b = prs[i]
                        Sc.op("vector", lambda e, m=m, gt=gt, i=i, p=p: e.scalar_tensor_tensor(out=m[:, :], in0=gt[:, i, :], scalar=1.0, in1=p[:, :], op0=ALU.add, op1=ALU.mult), reads=[gt_b, p_b], writes=[m_b])
                    Sc.op("gpsimd", lambda e, m1=m1, m2=m2: e.tensor_tensor(out=m1[:, :], in0=m1[:, :], in1=m2[:, :], op=ALU.add), reads=[m1_b, m2_b], writes=[m1_b])
                    Sc.op("gpsimd", lambda e, mg=mg, m1=m1, m3=m3, fc=fc: e.tensor_tensor(out=mg[:, fc, :], in0=m1[:, :], in1=m3[:, :], op=ALU.add), reads=[m1_b, m3_b], writes=[mg_b])
                for tt in range(4):
                    t = tb * 4 + tt
                    x, x_b = x_r.next()
                    Sc.dma("sync", lambda e, x=x, t=t: e.dma_start(out=x[:], in_=self.x[t * 128:(t + 1) * 128, :]), writes=[x_b])
                    h, h_b = h_r.next()
                    for nh in range(2):
                        p, p_b = pO.next()
                        for k in range(8):
                            Sc.op("tensor", lambda e, p=p, mg=mg, k=k, tt=tt, nh=nh: e.matmul(p[:, :], lhsT=mg[:, k, tt * 128:(tt + 1) * 128], rhs=wo[:, k, nh * 512:(nh + 1) * 512], start=(k == 0), stop=(k == 7)), reads=[mg_b, w_b], writes=[p_b])
                        Sc.op("vector", lambda e, h=h, p=p, x=x, nh=nh: e.scalar_tensor_tensor(out=h[:, nh * 512:(nh + 1) * 512], in0=p[:, :], scalar=0.5, in1=x[:, nh * 512:(nh + 1) * 512], op0=ALU.mult, op1=ALU.add), reads=[p_b, x_b], writes=[h_b])
                    while pend:
                        pend.pop(0)()
                    Sc.dma("gpsimd", lambda e, h=h, t=t: e.dma_start(out=self.H[t * 128:(t + 1) * 128, :], in_=h[:]), reads=[h_b], writes=[h_out_b])
                    Sc.op("scalar", lambda e, h=h, t=t: e.activation(out=junk[:], in_=h[:], func=AF.Square, accum_out=stt_[:, 0, t:t + 1]), reads=[h_b, st_b[t]], writes=[junk_b, st_b[t]])
                    Sc.op("gpsimd", lambda e, t=t: e.tensor_scalar(out=stt_[:, 1, t:t + 1], in0=stt_[:, 0, t:t + 1], scalar1=1.0 / D, scalar2=EPS, op0=ALU.mult, op1=ALU.add), reads=[st_b[t]], writes=[st_b[t]])
                    Sc.op("gpsimd", lambda e, t=t: e.tensor_tensor(out=stt_[:, 2, t:t + 1], in0=stt_[:, 1, t:t + 1], in1=negh[:, 0:1], op=ALU.pow), reads=[st_b[t], cb_], writes=[st_b[t]])
                    hn, hn_b = hn_r.next()
                    Sc.op("vector", lambda e, hn=hn, h=h, t=t: e.scalar_tensor_tensor(out=hn[:], in0=h[:], scalar=stt_[:, 2, t:t + 1], in1=gffn[:], op0=ALU.mult, op1=ALU.mult), reads=[h_b, st_b[t], cb_], writes=[hn_b])
                    def tail(hn=hn, hn_b=hn_b, t=t):
                        p, p_b = pT.next()
                        for k in range(8):
                            Sc.op("tensor", lambda e, p=p, hn=hn, k=k: e.transpose(out=p[:, k * 128:(k + 1) * 128], in_=hn[:, k * 128:(k + 1) * 128], identity=ident[:]), reads=[hn_b, cb_], writes=[p_b])
                        ht, ht_b = ht_r.next()
                        Sc.op("scalar", lambda e, ht=ht, p=p: e.activation(out=ht[:], in_=p[:, :].rearrange("p (k c) -> p k c", k=8), func=AF.Copy), reads=[p_b], writes=[ht_b])
                        Sc.dma("gpsimd", lambda e, ht=ht, t=t: e.dma_start(out=self.HNT[:, t * 128:(t + 1) * 128].rearrange("(k p) s -> p k s", p=128), in_=ht[:]), reads=[ht_b], writes=[hnt_b])
                    pend.append(tail)
            while pend:
                pend.pop(0)()
            Sc.barrier()
            Sc.emit(st)

    def phase_d(self):
        nc = self.nc
        NF = DFF // 128
        with contextlib.ExitStack() as st:
            sb = lambda n, s, d: st.enter_context(nc.sbuf_tensor("d_" + n, s, d))
            ps = lambda n, s, d: st.enter_context(nc.psum_tensor("d_" + n, s, d))
            Sc = Sched(nc, prefix="d_")
            cb_ = Buf("const")
            hnT = sb("hnT", [128, 8, S], BF16)
            hn_b = [Buf("hnT") for _ in range(NB)]
            for tb in range(NB):
                Sc.dma("sync", lambda e, tb=tb: e.dma_start(out=hnT[:, :, tb * 512:(tb + 1) * 512], in_=self.HNT[:, tb * 512:(tb + 1) * 512].rearrange("(k p) s -> p k s", p=128)), writes=[hn_b[tb]])
            cw = sb("cw", [128, 2, 3, NF], F32)
            cbias = sb("cbias", [128, 2, NF], F32)
            for ab in range(2):
                for j in range(3):
                    Sc.dma("sync", lambda e, ab=ab, j=j: e.dma_start(out=cw[:, ab, j, :], in_=self.conv_w[0, j, ab * DFF:(ab + 1) * DFF].rearrange("(f p) -> p f", p=128), allow_slow_non_contiguous=True), writes=[cb_])
                Sc.dma("sync", lambda e, ab=ab: e.dma_start(out=cbias[:, ab, :], in_=self.conv_b[0, ab * DFF:(ab + 1) * DFF].rearrange("(f p) -> p f", p=128), allow_slow_non_contiguous=True), writes=[cb_])
            w_r = Ring([(sb("w%d" % i, [128, 8, 256], BF16), Buf("w")) for i in range(2)])
            u_r = []
            for i in range(2):
                ua = sb("ua%d" % i, [128, S + 2], F32)
                ub = sb("ub%d" % i, [128, S + 2], F32)
                bl = [Buf("u") for _ in range(NB + 1)]
                for t_ in (ua, ub):
                    Sc.op("gpsimd", lambda e, t_=t_: e.memset(t_[:, 0:1], 0.0), writes=[bl[NB]])
                    Sc.op("gpsimd", lambda e, t_=t_: e.memset(t_[:, S + 1:S + 2], 0.0), writes=[bl[NB]])
                u_r.append(((ua, ub), bl))
            u_r = Ring(u_r)
            ca = sb("ca", [128, S], F32)
            cbb = sb("cb", [128, S], F32)
            th = sb("th", [128, S], F32)
            ca_b, cbb_b, th_b = Buf("ca"), Buf("cb"), Buf("th")
            g_r = Ring([(sb("g%d" % i, [128, S], BF16), Buf("g")) for i in range(2)])
            pA = Ring([(ps("pA%d" % i, [128, 512], F32), Buf("pA")) for i in range(8)])
            gt2_b = Buf("GT2")
            for fc in range(NF):
                w, w_b = w_r.next()
                Sc.dma("gpsimd", lambda e, w=w, fc=fc: e.dma_start(out=w[:, :, 0:128], in_=self.w_up[0, :, fc * 128:(fc + 1) * 128].rearrange("(k p) c -> p k c", p=128)), writes=[w_b])
                Sc.dma("gpsimd", lambda e, w=w, fc=fc: e.dma_start(out=w[:, :, 128:256], in_=self.w_up[0, :, DFF + fc * 128:DFF + (fc + 1) * 128].rearrange("(k p) c -> p k c", p=128)), writes=[w_b])
                (ua, ub), ubl = u_r.next()
                for tb in range(NB):
                    for ab, ut in ((0, ua), (1, ub)):
                        p, p_b = pA.next()
                        for k in range(8):
                            Sc.op("tensor", lambda e, p=p, w=w, k=k, ab=ab, tb=tb: e.matmul(p[:, :], lhsT=w[:, k, ab * 128:(ab + 1) * 128], rhs=hnT[:, k, tb * 512:(tb + 1) * 512], start=(k == 0), stop=(k == 7)), reads=[w_b, hn_b[tb]], writes=[p_b])
                        Sc.op("scalar", lambda e, ut=ut, p=p, tb=tb: e.activation(out=ut[:, 1 + tb * 512:1 + (tb + 1) * 512], in_=p[:, :], func=AF.Copy), reads=[p_b], writes=[ubl[tb]])
                for ab, ut, ct, ct_b, eng in ((0, ua, ca, ca_b, "vector"), (1, ub, cbb, cbb_b, "vector")):
                    Sc.op("scalar", lambda e, ut=ut, ct=ct, ab=ab, fc=fc: e.activation(out=ct[:, :], in_=ut[:, 0:S], func=AF.Identity, bias=cbias[:, ab, fc:fc + 1], scale=cw[:, ab, 0, fc:fc + 1]), reads=ubl + [cb_], writes=[ct_b])
                    Sc.op(eng, lambda e, ut=ut, ct=ct, ab=ab, fc=fc: e.scalar_tensor_tensor(out=ct[:, :], in0=ut[:, 1:S + 1], scalar=cw[:, ab, 1, fc:fc + 1], in1=ct[:, :], op0=ALU.mult, op1=ALU.add), reads=ubl + [cb_, ct_b], writes=[ct_b])
                    Sc.op(eng, lambda e, ut=ut, ct=ct, ab=ab, fc=fc: e.scalar_tensor_tensor(out=ct[:, :], in0=ut[:, 2:S + 2], scalar=cw[:, ab, 2, fc:fc + 1], in1=ct[:, :], op0=ALU.mult, op1=ALU.add), reads=ubl + [cb_, ct_b], writes=[ct_b])
                Sc.op("scalar", lambda e: e.activation(out=th[:, :], in_=ca[:, :], func=AF.Tanh, scale=0.5), reads=[ca_b], writes=[th_b])
                Sc.op("vector", lambda e: e.scalar_tensor_tensor(out=th[:, :], in0=th[:, :], scalar=1.0, in1=ca[:, :], op0=ALU.add, op1=ALU.mult), reads=[ca_b, th_b], writes=[th_b])
                g, g_b = g_r.next()
                Sc.op("vector", lambda e, g=g: e.tensor_tensor(out=g[:, :], in0=th[:, :], in1=cbb[:, :], op=ALU.mult), reads=[th_b, cbb_b], writes=[g_b])
                Sc.dma("sync", lambda e, g=g, fc=fc: e.dma_start(out=self.GT2[fc * 128:(fc + 1) * 128, :], in_=g[:, :]), reads=[g_b], writes=[gt2_b])
            Sc.barrier()
            Sc.emit(st)

    def phase_e(self):
        nc = self.nc
        NF = DFF // 128
        with contextlib.ExitStack() as st:
            sb = lambda n, s, d: st.enter_context(nc.sbuf_tensor("e_" + n, s, d))
            ps = lambda n, s, d: st.enter_context(nc.psum_tensor("e_" + n, s, d))
            Sc = Sched(nc, prefix="e_")
            cb_ = Buf("const")
            w_b = Buf("w")
            wd = sb("wd", [128, NF, D], BF16)
            gfin = sb("gfin", [128, D], F32)
            negh = sb("negh", [128, 1], F32)
            for q4 in range(2):
                Sc.dma("gpsimd", lambda e, q4=q4: e.dma_start(out=wd[:, q4 * 11:(q4 + 1) * 11, :], in_=self.w_down[0, q4 * 11 * 128:(q4 + 1) * 11 * 128, :].rearrange("(k p) c -> p k c", p=128)), writes=[w_b])
            Sc.dma("sync", lambda e: e.dma_start(out=gfin[:], in_=self.g_final.partition_broadcast(128)), writes=[cb_])
            Sc.op("vector", lambda e: e.memset(negh[:], -0.5), reads=[cb_], writes=[cb_])
            g_r = Ring([(sb("g%d" % i, [128, NF, 512], BF16), Buf("g")) for i in range(2)])
            h_r = Ring([(sb("h%d" % i, [128, D], F32), Buf("h")) for i in range(3)])
            o_r = Ring([(sb("o%d" % i, [128, D], F32), Buf("o")) for i in range(2)])
            junk = sb("junk", [128, D], BF16)
            junk_b = Buf("junk")
            stt_ = sb("stat", [128, 3, NT], F32)
            st_b = [Buf("st") for _ in range(NT)]
            Sc.op("vector", lambda e: e.memset(stt_[:], 0.0), writes=st_b)
            pO = Ring([(ps("pO%d" % i, [128, 512], F32), Buf("pO")) for i in range(6)])
            out_b = Buf("out")
            for tb in range(NB):
                g, g_b = g_r.next()
                Sc.dma("sync", lambda e, g=g, tb=tb: e.dma_start(out=g[:], in_=self.GT2[:, tb * 512:(tb + 1) * 512].rearrange("(k p) s -> p k s", p=128)), writes=[g_b])
                for tt in range(4):
                    t = tb * 4 + tt
                    h, h_b = h_r.next()
                    Sc.dma("sync", lambda e, h=h, t=t: e.dma_start(out=h[:], in_=self.H[t * 128:(t + 1) * 128, :]), writes=[h_b])
                    for nh in range(2):
                        p, p_b = pO.next()
                        for k in range(NF):
                            Sc.op("tensor", lambda e, p=p, g=g, k=k, tt=tt, nh=nh: e.matmul(p[:, :], lhsT=g[:, k, tt * 128:(tt + 1) * 128], rhs=wd[:, k, nh * 512:(nh + 1) * 512], start=(k == 0), stop=(k == NF - 1)), reads=[g_b, w_b], writes=[p_b])
                        Sc.op("vector", lambda e, h=h, p=p, nh=nh: e.scalar_tensor_tensor(out=h[:, nh * 512:(nh + 1) * 512], in0=p[:, :], scalar=0.5, in1=h[:, nh * 512:(nh + 1) * 512], op0=ALU.mult, op1=ALU.add), reads=[p_b, h_b], writes=[h_b])
                    Sc.op("scalar", lambda e, h=h, t=t: e.activation(out=junk[:], in_=h[:], func=AF.Square, accum_out=stt_[:, 0, t:t + 1]), reads=[h_b, st_b[t]], writes=[junk_b, st_b[t]])
                    Sc.op("gpsimd", lambda e, t=t: e.tensor_scalar(out=stt_[:, 1, t:t + 1], in0=stt_[:, 0, t:t + 1], scalar1=1.0 / D, scalar2=EPS, op0=ALU.mult, op1=ALU.add), reads=[st_b[t]], writes=[st_b[t]])
                    Sc.op("gpsimd", lambda e, t=t: e.tensor_tensor(out=stt_[:, 2, t:t + 1], in0=stt_[:, 1, t:t + 1], in1=negh[:, 0:1], op=ALU.pow), reads=[st_b[t], cb_], writes=[st_b[t]])
                    o, o_b = o_r.next()
                    Sc.op("vector", lambda e, o=o, h=h, t=t: e.scalar_tensor_tensor(out=o[:], in0=h[:], scalar=stt_[:, 2, t:t + 1], in1=gfin[:], op0=ALU.mult, op1=ALU.mult), reads=[h_b, st_b[t], cb_], writes=[o_b])
                    Sc.dma("sync", lambda e, o=o, t=t: e.dma_start(out=self.out[t * 128:(t + 1) * 128, :], in_=o[:]), reads=[o_b], writes=[out_b])
            Sc.barrier()
            Sc.emit(st)

    def build(self):
        for ph in ("a", "b1", "b1c", "b2", "b3", "c", "d", "e"):
            if self.phases is not None and ph not in self.phases:
                continue
            fn = getattr(self, "phase_" + ph, None)
            if fn is not None:
                fn()
            if self.stop_after == ph:
                break
        return self.nc


def host_consts():
    inv = 10000.0 ** (-np.arange(0, 64, 2, dtype=np.float32) / 64.0)
    ang = np.arange(S, dtype=np.float32)[:, None] * inv[None, :].astype(np.float32)
    c = {
        "c_cos": np.cos(ang).astype(np.float32),
        "c_sin": np.sin(ang).astype(np.float32),
        "c_ident": np.eye(128, dtype=np.float32),
    }
    kk = np.arange(128)[:, None]
    qq = np.arange(256)[None, :]
    band = ((qq >= kk) & (qq <= kk + 128))
    m = np.where(band, 0.0, -30000.0).astype(np.float32)
    am = np.zeros((128, 1024), np.float32)
    am[:, 0:256] = m
    am[:, 256:512] = m
    mf = m[64:128, 128:256]
    ml = m[0:64, 0:128]
    am[0:64, 512:640] = mf
    am[0:64, 640:768] = mf
    am[0:64, 768:896] = ml
    am[0:64, 896:1024] = ml
    c["c_amask"] = am
    r = np.zeros((128, 8, 128), np.float32)
    mm = np.arange(128)[:, None].astype(np.float32)
    nn = np.arange(128)[None, :].astype(np.float32)
    r[:, 0, :] = np.maximum(nn - mm, 0)
    r[:, 1, :] = np.maximum(mm - nn, 0)
    r[:, 2, :] = (nn >= mm) * 0.125
    r[:, 3, :] = (mm > nn) * 0.125
    r[:, 4, :] = nn + 1.0
    r[:, 5, :] = 128.0 - nn
    r[:, 6, 0] = 127.0 - np.arange(128)
    r[:, 6, 1] = np.arange(128)
    c["c_ret"] = r
    return c


def make_in_maps(inputs, n_cores=8):
    consts = host_consts()
    maps = []
    for b in range(n_cores):
        m = {"x": np.ascontiguousarray(inputs["x"][b]), "mem": np.ascontiguousarray(inputs["mem"][b])}
        for k, v in inputs.items():
            if k in ("x", "mem"):
                continue
            m[k] = np.ascontiguousarray(v)
        m.update(consts)
        maps.append(m)
    return maps


def kernel(**inputs):
    inputs = {k: np.asarray(v) for k, v in inputs.items()}
    prog = Prog()
    nc = prog.build()
    res = run_bass_kernel_spmd(nc, make_in_maps(inputs), core_ids=list(range(8)))
    return np.stack([r["out"] for r in res.results], axis=0)
```

```python
import contextlib
import numpy as np
import concourse.bass as bass
import concourse.mybir as mybir
from concourse.bass_utils import run_bass_kernel_spmd

F32 = mybir.dt.float32
BF16 = mybir.dt.bfloat16
AF = mybir.ActivationFunctionType
ALU = mybir.AluOpType
AX = mybir.AxisListType

S = 4096
D = 1024
NT = S // 128
NB = S // 512
IN_W = 5120
DFF = 2816
EPS = 1e-6
C_QA, C_KA, C_VA, C_QR, C_KR, C_VR, C_GR, C_QM = 0, 768, 1536, 2304, 2688, 3072, 3840, 4608


class Buf:
    __slots__ = ("name", "w", "r")

    def __init__(self, name):
        self.name = name
        self.w = None
        self.r = []


class Sched:
    ENG = ("tensor", "vector", "scalar", "gpsimd", "sync")

    def __init__(self, nc, n_dma_sems=32, prefix=""):
        self.nc = nc
        self.prefix = prefix
        self.lists = {e: [] for e in self.ENG}
        self.cnt = {e: 0 for e in self.ENG}
        self.known = {e: {} for e in self.ENG}
        self.ndma = n_dma_sems
        self.dma_issued = [0] * n_dma_sems
        self.dma_rr = 0

    def _need(self, eng, ev, waits):
        if ev is None:
            return
        key, val = ev
        if key == eng and eng == "tensor":
            return
        if self.known[eng].get(key, 0) >= val:
            return
        if waits.get(key, 0) < val:
            waits[key] = val

    def _deps(self, eng, reads, writes):
        waits = {}
        for b in reads:
            self._need(eng, b.w, waits)
        for b in writes:
            self._need(eng, b.w, waits)
            for ev in b.r:
                self._need(eng, ev, waits)
        for k, v in waits.items():
            self.known[eng][k] = v
        return list(waits.items())

    def op(self, eng, fn, reads=(), writes=()):
        waits = self._deps(eng, reads, writes)
        self.cnt[eng] += 1
        ev = (eng, self.cnt[eng])
        self.lists[eng].append((waits, fn, eng, 1))
        for b in reads:
            b.r.append(ev)
        for b in writes:
            b.w = ev
            b.r = []
        return ev

    def dma(self, eng, fn, reads=(), writes=()):
        i = self.dma_rr
        self.dma_rr = (self.dma_rr + 1) % self.ndma
        key = ("dma", i)
        waits = dict(self._deps(eng, reads, writes))
        prev = self.dma_issued[i]
        if prev > 0 and self.known[eng].get(key, 0) < prev:
            waits[key] = prev
            self.known[eng][key] = prev
        self.dma_issued[i] = prev + 16
        ev = (key, prev + 16)
        self.lists[eng].append((list(waits.items()), fn, key, 16))
        for b in reads:
            b.r.append(ev)
        for b in writes:
            b.w = ev
            b.r = []
        return ev

    def barrier(self):
        for e in self.ENG:
            waits = {}
            for o in self.ENG:
                if o != e and self.cnt[o] > 0:
                    self._need(e, (o, self.cnt[o]), waits)
            if e != "tensor" and self.cnt[e] > 0:
                self._need(e, (e, self.cnt[e]), waits)
            for i in range(self.ndma):
                if self.dma_issued[i] > 0:
                    self._need(e, (("dma", i), self.dma_issued[i]), waits)
            for k, v in waits.items():
                self.known[e][k] = v
            self.lists[e].append((list(waits.items()), None, None, 0))

    def emit(self, stack):
        nc = self.nc
        semmap = {}
        handles = []
        for e in self.ENG:
            semmap[e] = nc.alloc_semaphore(name=self.prefix + "s_" + e)
            handles.append(semmap[e])
        for i in range(self.ndma):
            semmap[("dma", i)] = nc.alloc_semaphore(name=self.prefix + "s_dma%d" % i)
            handles.append(semmap[("dma", i)])

        def runner(items):
            def f(e):
                for waits, fn, key, inc in items:
                    for k, v in waits:
                        e.wait_ge(semmap[k], v)
                    if fn is not None:
                        fn(e).then_inc(semmap[key], inc)
            return f
        with nc.Block() as block:
            block.tensor(runner(self.lists["tensor"]))
            block.vector(runner(self.lists["vector"]))
            block.scalar(runner(self.lists["scalar"]))
            block.gpsimd(runner(self.lists["gpsimd"]))
            block.sync(runner(self.lists["sync"]))
        nc.clear_and_free_semaphores(handles)
        nc.all_engine_barrier()


class Ring:
    def __init__(self, items):
        self.items = items
        self.i = 0

    def next(self):
        it = self.items[self.i]
        self.i = (self.i + 1) % len(self.items)
        return it


class Prog:
    def __init__(self, debug=False, stop_after=None, phases=None, ext_in=()):
        self.debug = debug
        self.stop_after = stop_after
        self.phases = phases
        self.ext_in = set(ext_in)
        nc = self.nc = bass.Bass("TRN2", target_bir_lowering=False)
        ein = lambda n, s: nc.dram_tensor(n, s, F32, kind="ExternalInput").ap()
        self.x = ein("x", [S, D])
        self.mem = ein("mem", [256, D])
        self.g_mix = ein("g_mix", [1, D])
        self.w_in = ein("w_in", [1, D, IN_W])
        self.w_mem_kv = ein("w_mem_kv", [1, D, 1024])
        self.g_mem = ein("g_mem", [1, D])
        self.dec_f = ein("ret_decay_fwd", [1, 6])
        self.dec_b = ein("ret_decay_bwd", [1, 6])
        self.g_ret = ein("g_ret", [1, 768])
        self.w_pa = ein("w_proj_attn", [1, 768, D])
        self.w_pr = ein("w_proj_ret", [1, 768, D])
        self.w_pm = ein("w_proj_mem", [1, 512, D])
        self.w_gate = ein("w_gate", [1, D, 3 * D])
        self.b_gate = ein("b_gate", [1, 3 * D])
        self.w_out = ein("w_out", [1, D, D])
        self.g_ffn = ein("g_ffn", [1, D])
        self.w_up = ein("w_up", [1, D, 2 * DFF])
        self.conv_w = ein("conv_w", [1, 3, 2 * DFF])
        self.conv_b = ein("conv_b", [1, 2 * DFF])
        self.w_down = ein("w_down", [1, DFF, D])
        self.g_final = ein("g_final", [D])
        self.c_cos = ein("c_cos", [S, 32])
        self.c_sin = ein("c_sin", [S, 32])
        self.c_ident = ein("c_ident", [128, 128])
        self.c_amask = ein("c_amask", [128, 1024])
        self.c_ret = ein("c_ret", [128, 8, 128])
        self.out = nc.dram_tensor("out", [S, D], F32, kind="ExternalOutput").ap()
        kind = "ExternalOutput" if debug else "Internal"
        scr = lambda n, s, d: nc.dram_tensor(n, s, d, kind=("ExternalInput" if n in self.ext_in else kind)).ap()
        self.PROJ = scr("PROJ", [S, IN_W], BF16)
        self.GTT = scr("GTT", [3 * D, S], BF16)
        self.UD = scr("UD", [12 * 128, S], F32)
        self.YAT = scr("YAT", [768, S], BF16)
        self.YR = scr("YR", [S, 768], BF16)
        self.YMT = scr("YMT", [512, S], BF16)
        self.H = scr("H", [S, D], F32)
        self.HNT = scr("HNT", [D, S], BF16)
        self.GT2 = scr("GT2", [DFF, S], BF16)

    def phase_a(self):
        nc = self.nc
        with contextlib.ExitStack() as st:
            sb = lambda n, s, d: st.enter_context(nc.sbuf_tensor("a_" + n, s, d))
            ps = lambda n, s, d: st.enter_context(nc.psum_tensor("a_" + n, s, d))
            Sc = Sched(nc, prefix="a_")
            xT = sb("xT", [128, 8, S], BF16)
            xT_b = [Buf("xT%d" % t) for t in range(NT)]
            xr = Ring([(sb("xr%d" % i, [128, D], F32), Buf("xr%d" % i)) for i in range(3)])
            xg = Ring([(sb("xg%d" % i, [128, D], BF16), Buf("xg%d" % i)) for i in range(2)])
            junk = sb("junk", [128, D], BF16)
            junk_b = Buf("junk")
            gmix = sb("gmix", [128, D], F32)
            gmix_b = Buf("gmix")
            ssq = sb("ssq", [128, NT], F32)
            msq = sb("msq", [128, NT], F32)
            rstd = sb("rstd", [128, NT], F32)
            negh = sb("negh", [128, 1], F32)
            st_b = [Buf("st%d" % t) for t in range(NT)]
            const_b = Buf("const")
            identf = sb("identf", [128, 128], F32)
            ident = sb("ident", [128, 128], BF16)
            cos_t = sb("cos_t", [128, NT, 32], F32)
            sin_t = sb("sin_t", [128, NT, 32], F32)
            hb = sb("hb", [128, 24], F32)
            pT = Ring([(ps("pT%d" % i, [128, 1024], BF16), Buf("pT%d" % i)) for i in range(2)])
            pA = Ring([(ps("pA%d" % i, [128, 512], F32), Buf("pA%d" % i)) for i in range(6)])
            wr = Ring([(sb("w%d" % i, [128, 8, 512], BF16), Buf("w%d" % i)) for i in range(3)])
            ob = Ring([(sb("ob%d" % i, [128, 512], BF16), Buf("ob%d" % i)) for i in range(4)])
            tA = Ring([(sb("tA%d" % i, [128, 512], F32), Buf("tA%d" % i)) for i in range(3)])
            tB = Ring([(sb("tB%d" % i, [128, 512], F32), Buf("tB%d" % i)) for i in range(3)])
            proj_b = Buf("PROJ")
            gtt_b = Buf("GTT")

            Sc.dma("sync", lambda e: e.dma_start(out=gmix[:], in_=self.g_mix[0, :].partition_broadcast(128)), writes=[gmix_b])
            Sc.dma("sync", lambda e: e.dma_start(out=identf[:], in_=self.c_ident[:, :]), writes=[const_b])
            Sc.dma("sync", lambda e: e.dma_start(out=cos_t[:], in_=self.c_cos.rearrange("(t p) c -> p t c", p=128)), writes=[const_b])
            Sc.dma("sync", lambda e: e.dma_start(out=sin_t[:], in_=self.c_sin.rearrange("(t p) c -> p t c", p=128)), writes=[const_b])
            Sc.dma("sync", lambda e: e.dma_start(out=hb[:], in_=self.b_gate[0, :].rearrange("(f p) -> p f", p=128), allow_slow_non_contiguous=True), writes=[const_b])
            Sc.op("vector", lambda e: e.tensor_copy(out=ident[:], in_=identf[:]), reads=[const_b], writes=[const_b])
            Sc.op("vector", lambda e: e.tensor_scalar(out=hb[:], in0=hb[:], scalar1=0.5, scalar2=None, op0=ALU.mult), reads=[const_b], writes=[const_b])
            Sc.op("vector", lambda e: e.memset(ssq[:], 0.0), writes=st_b)
            Sc.op("vector", lambda e: e.memset(negh[:], -0.5), writes=[const_b])

            for t in range(NT):
                xt, xt_b = xr.next()
                Sc.dma("sync", lambda e, xt=xt, t=t: e.dma_start(out=xt[:], in_=self.x[t * 128:(t + 1) * 128, :]), writes=[xt_b])
                Sc.op("scalar", lambda e, xt=xt, t=t: e.activation(out=junk[:], in_=xt[:], func=AF.Square, accum_out=ssq[:, t:t + 1]),
                      reads=[xt_b], writes=[junk_b, st_b[t]])
                Sc.op("gpsimd", lambda e, t=t: e.tensor_scalar(out=msq[:, t:t + 1], in0=ssq[:, t:t + 1], scalar1=1.0 / D, scalar2=EPS, op0=ALU.mult, op1=ALU.add),
                      reads=[st_b[t]], writes=[st_b[t]])
                Sc.op("gpsimd", lambda e, t=t: e.tensor_tensor(out=rstd[:, t:t + 1], in0=msq[:, t:t + 1], in1=negh[:, 0:1], op=ALU.pow),
                      reads=[st_b[t], const_b], writes=[st_b[t]])
                g, g_b = xg.next()
                Sc.op("vector", lambda e, g=g, xt=xt, t=t: e.scalar_tensor_tensor(out=g[:], in0=xt[:], scalar=rstd[:, t:t + 1], in1=gmix[:], op0=ALU.mult, op1=ALU.mult),
                      reads=[xt_b, st_b[t], gmix_b], writes=[g_b])
                p, p_b = pT.next()
                for k in range(8):
                    Sc.op("tensor", lambda e, p=p, g=g, k=k: e.transpose(out=p[:, k * 128:(k + 1) * 128], in_=g[:, k * 128:(k + 1) * 128], identity=ident[:]),
                          reads=[g_b, const_b], writes=[p_b])
                eng = "scalar" if t % 2 == 0 else "vector"
                dst = xT[:, :, t * 128:(t + 1) * 128]
                src = p[:, :].rearrange("p (k c) -> p k c", k=8)
                if eng == "scalar":
                    Sc.op("scalar", lambda e, dst=dst, src=src: e.activation(out=dst, in_=src, func=AF.Copy), reads=[p_b], writes=[xT_b[t]])
                else:
                    Sc.op("vector", lambda e, dst=dst, src=src: e.tensor_copy(out=dst, in_=src), reads=[p_b], writes=[xT_b[t]])

            blocks = [(C_QA, 512, "rot"), (C_QA + 512, 256, "rot"), (C_KA, 512, "rot"), (C_KA + 512, 256, "rot"),
                      (C_VA, 512, "copy"), (C_VA + 512, 256, "copy"), (C_QR, 384, "rot"), (C_KR, 384, "rot"),
                      (C_VR, 512, "copy"), (C_VR + 512, 256, "copy"), (C_GR, 512, "silu2"), (C_GR + 512, 256, "silu2"),
                      (C_QM, 512, "copy")]
            wspecs = [(self.w_in[0, :, c0:c0 + N], N) for (c0, N, kind) in blocks] + [(self.w_gate[0, :, fg * 512:(fg + 1) * 512], 512) for fg in range(6)]
            wloaded = {}

            def ensure_w(i):
                if i < len(wspecs) and i not in wloaded:
                    w, w_b = wr.next()
                    src, N = wspecs[i]
                    Sc.dma("gpsimd", lambda e, w=w, src=src, N=N: e.dma_start(out=w[:, :, 0:N], in_=src.rearrange("(k p) c -> p k c", p=128)), writes=[w_b])
                    wloaded[i] = (w, w_b)
                return wloaded.get(i)
            for bi, (c0, N, kind) in enumerate(blocks):
                w, w_b = ensure_w(bi)
                ensure_w(bi + 1)
                for t in range(NT):
                    p, p_b = pA.next()
                    for k in range(8):
                        Sc.op("tensor", lambda e, p=p, w=w, t=t, k=k, N=N: e.matmul(p[:, 0:N], lhsT=xT[:, k, t * 128:(t + 1) * 128], rhs=w[:, k, 0:N], start=(k == 0), stop=(k == 7)),
                              reads=[xT_b[t], w_b], writes=[p_b])
                    o, o_b = ob.next()
                    if kind == "copy":
                        Sc.op("scalar", lambda e, o=o, p=p, N=N: e.activation(out=o[:, 0:N], in_=p[:, 0:N], func=AF.Copy), reads=[p_b], writes=[o_b])
                    elif kind == "silu2":
                        a, a_b = tA.next()
                        Sc.op("scalar", lambda e, a=a, p=p, N=N: e.activation(out=a[:, 0:N], in_=p[:, 0:N], func=AF.Tanh, scale=0.5), reads=[p_b], writes=[a_b])
                        Sc.op("vector", lambda e, o=o, a=a, p=p, N=N: e.scalar_tensor_tensor(out=o[:, 0:N], in0=a[:, 0:N], scalar=1.0, in1=p[:, 0:N], op0=ALU.add, op1=ALU.mult),
                              reads=[a_b, p_b], writes=[o_b])
                    else:
                        H = N // 64
                        a, a_b = tA.next()
                        b, b_b = tB.next()
                        pv = p[:, 0:N].rearrange("p (h two f) -> p h two f", two=2, f=32)
                        av = a[:, 0:N].rearrange("p (h two f) -> p h two f", two=2, f=32)
                        bv = b[:, 0:N].rearrange("p (h two f) -> p h two f", two=2, f=32)
                        ov = o[:, 0:N].rearrange("p (h two f) -> p h two f", two=2, f=32)
                        cb = cos_t[:, t:t + 1, :].broadcast_to([128, H, 32])
                        sn = sin_t[:, t:t + 1, :].broadcast_to([128, H, 32])
                        x1, x2 = pv[:, :, 0, :], pv[:, :, 1, :]
                        Sc.op("vector", lambda e, av=av, x1=x1, cb=cb: e.tensor_tensor(out=av[:, :, 0, :], in0=x1, in1=cb, op=ALU.mult), reads=[p_b, const_b], writes=[a_b])
                        Sc.op("vector", lambda e, av=av, x2=x2, cb=cb: e.tensor_tensor(out=av[:, :, 1, :], in0=x2, in1=cb, op=ALU.mult), reads=[p_b, const_b], writes=[a_b])
                        Sc.op("vector", lambda e, bv=bv, x2=x2, sn=sn: e.tensor_tensor(out=bv[:, :, 0, :], in0=x2, in1=sn, op=ALU.mult), reads=[p_b, const_b], writes=[b_b])
                        Sc.op("vector", lambda e, bv=bv, x1=x1, sn=sn: e.tensor_tensor(out=bv[:, :, 1, :], in0=x1, in1=sn, op=ALU.mult), reads=[p_b, const_b], writes=[b_b])
                        Sc.op("gpsimd", lambda e, ov=ov, av=av, bv=bv: e.tensor_tensor(out=ov[:, :, 0, :], in0=av[:, :, 0, :], in1=bv[:, :, 0, :], op=ALU.subtract), reads=[a_b, b_b], writes=[o_b])
                        Sc.op("gpsimd", lambda e, ov=ov, av=av, bv=bv: e.tensor_tensor(out=ov[:, :, 1, :], in0=av[:, :, 1, :], in1=bv[:, :, 1, :], op=ALU.add), reads=[a_b, b_b], writes=[o_b])
                    Sc.dma("sync", lambda e, o=o, t=t, c0=c0, N=N: e.dma_start(out=self.PROJ[t * 128:(t + 1) * 128, c0:c0 + N], in_=o[:, 0:N]), reads=[o_b], writes=[proj_b])

            for fg in range(6):
                w, w_b = ensure_w(len(blocks) + fg)
                ensure_w(len(blocks) + fg + 1)
                for j in range(4):
                    fc = fg * 4 + j
                    for tb in range(NB):
                        p, p_b = pA.next()
                        for k in range(8):
                            Sc.op("tensor", lambda e, p=p, w=w, tb=tb, k=k, j=j: e.matmul(p[:, :], lhsT=w[:, k, j * 128:(j + 1) * 128], rhs=xT[:, k, tb * 512:(tb + 1) * 512], start=(k == 0), stop=(k == 7)),
                                  reads=xT_b[tb * 4:tb * 4 + 4] + [w_b], writes=[p_b])
                        o, o_b = ob.next()
                        Sc.op("scalar", lambda e, o=o, p=p, fc=fc: e.activation(out=o[:, :], in_=p[:, :], func=AF.Tanh, bias=hb[:, fc:fc + 1], scale=0.5), reads=[p_b, const_b], writes=[o_b])
                        Sc.dma("sync", lambda e, o=o, fc=fc, tb=tb: e.dma_start(out=self.GTT[fc * 128:(fc + 1) * 128, tb * 512:(tb + 1) * 512], in_=o[:, :]), reads=[o_b], writes=[gtt_b])
            Sc.barrier()
            Sc.emit(st)


    def phase_b1(self):
        nc = self.nc
        with contextlib.ExitStack() as st:
            sb = lambda n, s, d: st.enter_context(nc.sbuf_tensor("b1_" + n, s, d))
            ps = lambda n, s, d: st.enter_context(nc.psum_tensor("b1_" + n, s, d))
            Sc = Sched(nc, prefix="b1_")
            const_b = Buf("const")
            amf = sb("amf", [128, 1024], F32)
            am = sb("am", [128, 1024], BF16)
            identf = sb("identf", [128, 128], F32)
            ident = sb("ident", [128, 128], BF16)
            Sc.dma("sync", lambda e: e.dma_start(out=amf[:], in_=self.c_amask[:, :]), writes=[const_b])
            Sc.dma("sync", lambda e: e.dma_start(out=identf[:], in_=self.c_ident[:, :]), writes=[const_b])
            Sc.op("vector", lambda e: e.tensor_copy(out=am[:], in_=amf[:]), reads=[const_b], writes=[const_b])
            Sc.op("vector", lambda e: e.tensor_copy(out=ident[:], in_=identf[:]), reads=[const_b], writes=[const_b])
            sets = []
            for i in range(3):
                Lc = S if i == 2 else 1024
                nbc = Lc // 128
                qT = [sb("qT%d_%d" % (i, pp), [128, Lc], BF16) for pp in range(2)]
                qZ = [[sb("qZ%d_%d_%d" % (i, pp, hh), [128, Lc], BF16) for hh in range(2)] for pp in range(2)]
                qz_b = [[[Buf("qz") for _ in range(nbc)] for hh in range(2)] for pp in range(2)]
                for pp in range(2):
                    Sc.op("gpsimd", lambda e, t=qZ[pp][0]: e.memset(t[64:128, :], 0.0), writes=qz_b[pp][0])
                    Sc.op("gpsimd", lambda e, t=qZ[pp][1]: e.memset(t[0:64, :], 0.0), writes=qz_b[pp][1])
                kT = [sb("kT%d_%d" % (i, pp), [128, Lc], BF16) for pp in range(2)]
                va = sb("va%d" % i, [128, nbc + 1, 4, 128], BF16)
                q_b = [[Buf("q") for _ in range(nbc)] for pp in range(2)]
                k_b = [[Buf("k") for _ in range(nbc)] for pp in range(2)]
                v_b = [Buf("v") for _ in range(nbc + 1)]
                Sc.op("gpsimd", lambda e, va=va: e.memset(va[:, :, :, 64:128], 1.0), writes=v_b)
                sets.append((qT, kT, va, q_b, k_b, v_b, qZ, qz_b))
            pS = Ring([(ps("pS%d" % i, [128, 512], F32), Buf("pS%d" % i)) for i in range(3)])
            pU = Ring([(ps("pU%d" % i, [128, 512], F32), Buf("pU%d" % i)) for i in range(4)])
            Er = Ring([(sb("E%d" % i, [128, 512], BF16), Buf("E%d" % i)) for i in range(6)])
            us = Ring([(sb("us%d" % i, [128, 512], F32), Buf("us%d" % i)) for i in range(4)])
            ud_b = Buf("UD")
            unit = 0
            for g, dl in enumerate((1, 4, 16)):
                if g not in getattr(self, "b1_groups", (0, 1, 2)):
                    continue
                L = S // dl
                nb = L // 128
                ucurs = {0: None, 1: None}
                for r in range(dl):
                    qT, kT, va, q_b, k_b, v_b, qZ, qz_b = sets[2] if g == 0 else sets[unit % 2]
                    unit += 1
                    rows = self.PROJ.rearrange("(i r) c -> r i c", r=dl)[r]
                    vc = C_VA + g * 256
                    Sc.dma("sync", lambda e, va=va, rows=rows, vc=vc: e.dma_start(out=va[0:64, 0, :, 0:64], in_=rows[0:64, vc:vc + 256].rearrange("k (h d) -> k h d", d=64)), writes=[v_b[0]])
                    for h4 in range(4):
                        Sc.dma("sync", lambda e, va=va, rows=rows, vc=vc, nb=nb, L=L, h4=h4: e.dma_start(out=va[:, 1:nb, h4, 0:64], in_=rows[64:L - 64, vc + h4 * 64:vc + h4 * 64 + 64].rearrange("(j k) d -> k j d", k=128)), writes=v_b[1:nb])
                    Sc.dma("sync", lambda e, va=va, rows=rows, vc=vc, nb=nb, L=L: e.dma_start(out=va[0:64, nb, :, 0:64], in_=rows[L - 64:L, vc:vc + 256].rearrange("k (h d) -> k h d", d=64)), writes=[v_b[nb]])
                    for pp in range(2):
                        qc = C_QA + (g * 4 + 2 * pp) * 64
                        kc = C_KA + (g * 4 + 2 * pp) * 64
                        for blk in range(nb):
                            Sc.dma("sync", lambda e, dst=kT[pp], rows=rows, kc=kc, blk=blk: e.dma_start_transpose(out=dst[:, blk * 128:(blk + 1) * 128], in_=rows[blk * 128:(blk + 1) * 128, kc:kc + 128]), writes=[k_b[pp][blk]])
                            Sc.dma("sync", lambda e, dst=qT[pp], rows=rows, qc=qc, blk=blk: e.dma_start_transpose(out=dst[:, blk * 128:(blk + 1) * 128], in_=rows[blk * 128:(blk + 1) * 128, qc:qc + 128]), writes=[q_b[pp][blk]])
                            sl = slice(blk * 128, (blk + 1) * 128)
                            Sc.op("vector", lambda e, d=qZ[pp][0], s_=qT[pp], sl=sl: e.tensor_copy(out=d[0:64, sl], in_=s_[0:64, sl]), reads=[q_b[pp][blk]], writes=[qz_b[pp][0][blk]])
                            Sc.op("gpsimd", lambda e, d=qZ[pp][1], s_=qT[pp], sl=sl: e.tensor_copy(out=d[64:128, sl], in_=s_[64:128, sl]), reads=[q_b[pp][blk]], writes=[qz_b[pp][1][blk]])
                    for pp in range(2):
                        if getattr(self, "b1_stage", 9) < 1:
                            continue
                        Es = {}
                        ucur = ucurs[pp]
                        for j in range(nb + 1):
                            k0, k1 = max(0, 128 * j - 64), min(L, 128 * j + 64)
                            M = k1 - k0
                            q0, q1 = max(0, 128 * (j - 1)), min(L, 128 * (j + 1))
                            Nq = q1 - q0
                            if j == 0:
                                mk = am[0:64, 512:768]
                            elif j == nb:
                                mk = am[0:64, 768:1024]
                            else:
                                mk = am[:, 0:512]
                            kblks = sorted(set([k0 // 128, (k1 - 1) // 128]))
                            qblks = sorted(set([q0 // 128, (q1 - 1) // 128]))
                            p, p_b = pS.next()
                            for hh in range(2):
                                rd = [k_b[pp][b] for b in kblks] + [qz_b[pp][hh][b] for b in qblks]
                                Sc.op("tensor", lambda e, p=p, kt=kT[pp], qt=qZ[pp][hh], hh=hh, k0=k0, k1=k1, q0=q0, q1=q1, M=M, Nq=Nq:
                                      e.matmul(p[0:M, hh * Nq:(hh + 1) * Nq], lhsT=kt[:, k0:k1], rhs=qt[:, q0:q1], start=True, stop=False),
                                      reads=rd, writes=[p_b])
                                Sc.op("tensor", lambda e, p=p, mk=mk, M=M, Nq=Nq, hh=hh: e.matmul(p[0:M, hh * Nq:(hh + 1) * Nq], lhsT=ident[0:M, 0:M], rhs=mk[:, 0:Nq], start=False, stop=True),
                                      reads=[const_b], writes=[p_b])
                            E, E_b = Er.next()
                            Sc.op("scalar", lambda e, E=E, p=p, M=M, Nq=Nq: e.activation(out=E[0:M, 0:2 * Nq], in_=p[0:M, 0:2 * Nq], func=AF.Exp, scale=0.125), reads=[p_b], writes=[E_b])
                            Es[j] = (E, E_b, M, Nq)
                            if j == 0 or getattr(self, "b1_stage", 9) < 2:
                                continue
                            b = j - 1
                            G = r * nb + b
                            if G % 4 == 0:
                                ucur = ucurs[pp] = [pU.next(), pU.next()]
                            for hh in range(2):
                                (u, u_b) = ucur[hh]
                                col = (G % 4) * 128
                                for n_, jj in enumerate((b, b + 1)):
                                    Ej, Ej_b, Mj, Nqj = Es[jj]
                                    if jj == b:
                                        c = hh * Nqj + (128 if b >= 1 else 0)
                                    else:
                                        c = hh * Nqj
                                    hv = 2 * pp + hh
                                    Sc.op("tensor", lambda e, u=u, va=va, Ej=Ej, jj=jj, hv=hv, Mj=Mj, c=c, col=col, n_=n_:
                                          e.matmul(u[:, col:col + 128], lhsT=va[0:Mj, jj, hv, :], rhs=Ej[0:Mj, c:c + 128], start=(n_ == 0), stop=(n_ == 1)),
                                          reads=[v_b[jj], Ej_b], writes=[u_b])
                            del Es[b]
                            if G % 4 == 3:
                                for hh in range(2):
                                    (u, u_b) = ucur[hh]
                                    o, o_b = us.next()
                                    if hh == 0:
                                        Sc.op("vector", lambda e, o=o, u=u: e.tensor_copy(out=o[:], in_=u[:]), reads=[u_b], writes=[o_b])
                                    else:
                                        Sc.op("scalar", lambda e, o=o, u=u: e.activation(out=o[:], in_=u[:], func=AF.Copy), reads=[u_b], writes=[o_b])
                                    hrow = (g * 4 + 2 * pp + hh) * 128
                                    c0 = (G - 3) * 128
                                    Sc.dma("gpsimd", lambda e, o=o, hrow=hrow, c0=c0: e.dma_start(out=self.UD[hrow:hrow + 128, c0:c0 + 512], in_=o[:]), reads=[o_b], writes=[ud_b])
            Sc.barrier()
            Sc.emit(st)

    def phase_b1c(self):
        nc = self.nc
        with contextlib.ExitStack() as st:
            sb = lambda n, s, d: st.enter_context(nc.sbuf_tensor("b1c_" + n, s, d))
            Sc = Sched(nc, prefix="b1c_")
            CH = 2048
            Ut = [Ring([(sb("U%d_%d" % (g, i), [128, CH], F32), Buf("U")) for i in range(2)]) for g in range(3)]
            Dt = [Ring([(sb("D%d_%d" % (g, i), [128, CH], F32), Buf("D")) for i in range(2)]) for g in range(3)]
            Rr = Ring([(sb("R%d" % i, [128, CH], F32), Buf("R")) for i in range(2)])
            Yr = Ring([(sb("Y%d" % i, [128, CH], BF16), Buf("Y")) for i in range(3)])
            yat_b = Buf("YAT")
            for c2 in range(S // CH):
                for sp in range(2):
                    tiles = []
                    for g, dl in enumerate((1, 4, 16)):
                        L = S // dl
                        il = CH // dl
                        u, u_b = Ut[g].next()
                        d_, d_b = Dt[g].next()
                        for hh in range(2):
                            h = g * 4 + 2 * sp + hh
                            srcU = self.UD[h * 128:h * 128 + 64, :].rearrange("p (r i) -> p r i", r=dl)[:, :, c2 * il:(c2 + 1) * il]
                            srcD = self.UD[h * 128 + 64:h * 128 + 128, :].rearrange("p (r i) -> p r i", r=dl)[:, :, c2 * il:(c2 + 1) * il]
                            Sc.dma("sync", lambda e, u=u, hh=hh, srcU=srcU, dl=dl: e.dma_start(out=u[hh * 64:(hh + 1) * 64, :].rearrange("p (r i) -> p r i", r=dl), in_=srcU), writes=[u_b])
                            Sc.dma("gpsimd", lambda e, d_=d_, hh=hh, srcD=srcD, dl=dl: e.dma_start(out=d_[hh * 64:(hh + 1) * 64, :].rearrange("p (r i) -> p r i", r=dl), in_=srcD), writes=[d_b])
                        tiles.append((u, u_b, d_, d_b, dl))
                    R, R_b = Rr.next()
                    nat = lambda t, dl: t[:, :].rearrange("p (i r) -> p i r", r=dl)
                    res = lambda t, dl: t[:, :].rearrange("p (r i) -> p i r", r=dl)
                    (u0, u0_b, d0, d0_b, _), (u1, u1_b, d1, d1_b, _), (u2, u2_b, d2, d2_b, _) = tiles
                    Sc.op("gpsimd", lambda e, R=R, d0=d0, d1=d1: e.tensor_tensor(out=nat(R, 4), in0=nat(d0, 4), in1=res(d1, 4), op=ALU.add), reads=[d0_b, d1_b], writes=[R_b])
                    Sc.op("gpsimd", lambda e, R=R, d2=d2: e.tensor_tensor(out=nat(R, 16), in0=nat(R, 16), in1=res(d2, 16), op=ALU.add), reads=[d2_b, R_b], writes=[R_b])
                    Sc.op("vector", lambda e, R=R: e.reciprocal(out=R[:, :], in_=R[:, :]), reads=[R_b], writes=[R_b])
                    for g, (u, u_b, d_, d_b, dl) in enumerate(tiles):
                        y, y_b = Yr.next()
                        eng = "vector" if g != 1 else "gpsimd"
                        Sc.op(eng, lambda e, y=y, u=u, dl=dl, R=R: e.tensor_tensor(out=nat(y, dl), in0=res(u, dl), in1=nat(R, dl), op=ALU.mult), reads=[u_b, R_b], writes=[y_b])
                        row = (g * 4 + 2 * sp) * 64
                        Sc.dma("sync", lambda e, y=y, row=row, c2=c2: e.dma_start(out=self.YAT[row:row + 128, c2 * CH:(c2 + 1) * CH], in_=y[:, :]), reads=[y_b], writes=[yat_b])
            Sc.barrier()
            Sc.emit(st)

    def phase_b2(self):
        nc = self.nc
        with contextlib.ExitStack() as st:
            sb = lambda n, s, d: st.enter_context(nc.sbuf_tensor("b2_" + n, s, d))
            ps = lambda n, s, d: st.enter_context(nc.psum_tensor("b2_" + n, s, d))
            Sc = Sched(nc, prefix="b2_")
            cb_ = Buf("const")
            cret = sb("cret", [128, 8, 128], F32)
            dec = sb("dec", [128, 12], F32)
            lg = sb("lg", [128, 12], F32)
            lgp = sb("lgp", [128, 6], F32)
            Mall = sb("Mall", [128, 6, 128], F32)
            tmpm = sb("tmpm", [128, 128], F32)
            zeta = sb("zeta", [128, 12], F32)
            XiF = sb("XiF", [128, 3, 128], F32)
            XiB = sb("XiB", [128, 3, 128], F32)
            g128 = sb("g128", [128, 6], F32)
            gret = sb("gret", [128, 768], F32)
            negh = sb("negh", [128, 6], F32)
            Sc.dma("sync", lambda e: e.dma_start(out=cret[:], in_=self.c_ret[:, :, :]), writes=[cb_])
            Sc.dma("sync", lambda e: e.dma_start(out=dec[:, 0:6], in_=self.dec_f[0, :].partition_broadcast(128)), writes=[cb_])
            Sc.dma("sync", lambda e: e.dma_start(out=dec[:, 6:12], in_=self.dec_b[0, :].partition_broadcast(128)), writes=[cb_])
            Sc.dma("sync", lambda e: e.dma_start(out=gret[:], in_=self.g_ret[0, :].partition_broadcast(128)), writes=[cb_])
            C = lambda eng, fn: Sc.op(eng, fn, reads=[cb_], writes=[cb_])
            C("vector", lambda e: e.memset(negh[:], -0.5))
            C("vector", lambda e: e.tensor_scalar(out=gret[:], in0=gret[:], scalar1=0.5, scalar2=None, op0=ALU.mult))
            C("scalar", lambda e: e.activation(out=lg[:], in_=dec[:], func=AF.Exp, scale=-1.0))
            C("vector", lambda e: e.tensor_scalar(out=lg[:], in0=lg[:], scalar1=1.0, scalar2=None, op0=ALU.add))
            C("scalar", lambda e: e.activation(out=lg[:], in_=lg[:], func=AF.Ln))
            C("vector", lambda e: e.tensor_scalar(out=lg[:], in0=lg[:], scalar1=-1.0, scalar2=None, op0=ALU.mult))
            for dr in range(2):
                for hh in range(2):
                    src = lg[hh * 64:(hh + 1) * 64, dr * 6:(dr + 1) * 6].rearrange("p (a b) -> p a b", b=2)[:, :, hh]
                    C("vector", lambda e, dr=dr, hh=hh, src=src: e.tensor_copy(out=lgp[hh * 64:(hh + 1) * 64, dr * 3:(dr + 1) * 3], in_=src))
            for h in range(6):
                C("scalar", lambda e, h=h: e.activation(out=Mall[:, h, :], in_=cret[:, 0, :], func=AF.Exp, scale=lg[:, h:h + 1]))
                C("vector", lambda e, h=h: e.tensor_tensor(out=Mall[:, h, :], in0=Mall[:, h, :], in1=cret[:, 2, :], op=ALU.mult))
                C("scalar", lambda e, h=h: e.activation(out=tmpm[:], in_=cret[:, 1, :], func=AF.Exp, scale=lg[:, 6 + h:7 + h]))
                C("vector", lambda e, h=h: e.tensor_tensor(out=tmpm[:], in0=tmpm[:], in1=cret[:, 3, :], op=ALU.mult))
                C("vector", lambda e, h=h: e.tensor_tensor(out=Mall[:, h, :], in0=Mall[:, h, :], in1=tmpm[:], op=ALU.add))
                C("scalar", lambda e, h=h: e.activation(out=zeta[:, h:h + 1], in_=cret[:, 6, 0:1], func=AF.Exp, scale=lg[:, h:h + 1]))
                C("scalar", lambda e, h=h: e.activation(out=zeta[:, 6 + h:7 + h], in_=cret[:, 6, 1:2], func=AF.Exp, scale=lg[:, 6 + h:7 + h]))
            C("vector", lambda e: e.tensor_scalar(out=zeta[:], in0=zeta[:], scalar1=0.125, scalar2=None, op0=ALU.mult))
            for pp in range(3):
                C("scalar", lambda e, pp=pp: e.activation(out=XiF[:, pp, :], in_=cret[:, 4, :], func=AF.Exp, scale=lgp[:, pp:pp + 1]))
                C("scalar", lambda e, pp=pp: e.activation(out=XiB[:, pp, :], in_=cret[:, 5, :], func=AF.Exp, scale=lgp[:, 3 + pp:4 + pp]))
            C("scalar", lambda e: e.activation(out=g128[:], in_=lgp[:], func=AF.Exp, scale=128.0))

            Sall = [sb("SallF", [128, NT, 3, 128], BF16), sb("SallB", [128, NT, 3, 128], BF16)]
            sall_b = [[Buf("sf") for _ in range(NT)], [Buf("sb") for _ in range(NT)]]
            Scur = [sb("ScurF", [128, 3, 128], F32), sb("ScurB", [128, 3, 128], F32)]
            scur_b = [Buf("scf"), Buf("scb")]
            kt_r = Ring([(sb("ktok%d" % i, [128, 384], BF16), Buf("ktok")) for i in range(3)])
            vt_r = Ring([(sb("vtok%d" % i, [128, 768], BF16), Buf("vtok")) for i in range(3)])
            kz_r = Ring([(sb("kz%d" % i, [128, 6, 64], BF16), Buf("kz")) for i in range(3)])
            pkv = Ring([(ps("pkv%d" % i, [128, 512], F32), Buf("pkv")) for i in range(2)])
            for dr in range(2):
                Sc.op("vector", lambda e, dr=dr: e.memset(Scur[dr][:], 0.0), writes=[scur_b[dr]])
            for step in range(2 * NT):
                dr = step % 2
                c = (step // 2) if dr == 0 else (NT - 1 - step // 2)
                if True:
                    kt, kt_b = kt_r.next()
                    vt, vt_b = vt_r.next()
                    Sc.dma("sync", lambda e, kt=kt, c=c: e.dma_start(out=kt[:], in_=self.PROJ[c * 128:(c + 1) * 128, C_KR:C_KR + 384]), writes=[kt_b])
                    Sc.dma("sync", lambda e, vt=vt, c=c: e.dma_start(out=vt[:], in_=self.PROJ[c * 128:(c + 1) * 128, C_VR:C_VR + 768]), writes=[vt_b])
                    kz, kz_b = kz_r.next()
                    zb = zeta[:, dr * 6:(dr + 1) * 6].unsqueeze(2).broadcast_to([128, 6, 64])
                    Sc.op("gpsimd", lambda e, kz=kz, kt=kt, zb=zb: e.tensor_tensor(out=kz[:], in0=kt[:, :].rearrange("p (h d) -> p h d", d=64), in1=zb, op=ALU.mult), reads=[kt_b, cb_], writes=[kz_b])
                    p, p_b = pkv.next()
                    for h in range(6):
                        pp, hh = h // 2, h % 2
                        Sc.op("tensor", lambda e, p=p, kz=kz, vt=vt, h=h, pp=pp, hh=hh: e.matmul(p[hh * 64:(hh + 1) * 64, pp * 128:(pp + 1) * 128], lhsT=kz[:, h, :], rhs=vt[:, h * 128:(h + 1) * 128], start=True, stop=True),
                              reads=[kz_b, vt_b], writes=[p_b])
                    Sc.op("scalar", lambda e, dr=dr, c=c: e.activation(out=Sall[dr][:, c, :, :], in_=Scur[dr][:], func=AF.Copy), reads=[scur_b[dr]], writes=[sall_b[dr][c]])
                    for pp in range(3):
                        Sc.op("vector", lambda e, dr=dr, pp=pp, p=p: e.scalar_tensor_tensor(out=Scur[dr][:, pp, :], in0=Scur[dr][:, pp, :], scalar=g128[:, dr * 3 + pp:dr * 3 + pp + 1], in1=p[:, pp * 128:(pp + 1) * 128], op0=ALU.mult, op1=ALU.add),
                              reads=[p_b, scur_b[dr], cb_], writes=[scur_b[dr]])

            qt_r = Ring([(sb("qTp%d" % i, [128, 3, 128], BF16), Buf("qTp")) for i in range(2)])
            ktp_r = Ring([(sb("kTp%d" % i, [128, 3, 128], BF16), Buf("kTp")) for i in range(2)])
            gr_r = Ring([(sb("gr%d" % i, [128, 768], BF16), Buf("gr")) for i in range(2)])
            qz_r = []
            for i in range(2):
                t3 = [sb("qz%d_%d" % (i, j), [128, 6, 128], BF16) for j in range(3)]
                b3 = [Buf("qz") for j in range(3)]
                for j in range(3):
                    Sc.op("gpsimd", lambda e, t=t3[j]: e.memset(t[:], 0.0), writes=[b3[j]])
                qz_r.append((t3, b3))
            qz_r = Ring(qz_r)
            pst = Ring([(ps("pst%d" % i, [128, 512], F32), Buf("pst")) for i in range(2)])
            pyy = Ring([(ps("pyy%d" % i, [128, 512], F32), Buf("pyy")) for i in range(4)])
            A_r = Ring([(sb("A%d" % i, [128, 6, 128], BF16), Buf("A")) for i in range(2)])
            ysb_r = Ring([(sb("ysb%d" % i, [128, 768], F32), Buf("ysb")) for i in range(2)])
            ysq_r = Ring([(sb("ysq%d" % i, [128, 768], F32), Buf("ysq")) for i in range(2)])
            st_r = Ring([(sb("stat%d" % i, [128, 4, 6], F32), Buf("stat")) for i in range(2)])
            yo_r = Ring([(sb("yo%d" % i, [128, 768], BF16), Buf("yo")) for i in range(2)])
            yr_b = Buf("YR")
            def stage1(c):
                qt, qt_b = qt_r.next()
                ktp, ktp_b = ktp_r.next()
                vt, vt_b = vt_r.next()
                gr, gr_b = gr_r.next()
                for pp in range(3):
                    Sc.dma("sync", lambda e, qt=qt, c=c, pp=pp: e.dma_start_transpose(out=qt[:, pp, :], in_=self.PROJ[c * 128:(c + 1) * 128, C_QR + pp * 128:C_QR + (pp + 1) * 128]), writes=[qt_b])
                    Sc.dma("sync", lambda e, ktp=ktp, c=c, pp=pp: e.dma_start_transpose(out=ktp[:, pp, :], in_=self.PROJ[c * 128:(c + 1) * 128, C_KR + pp * 128:C_KR + (pp + 1) * 128]), writes=[ktp_b])
                Sc.dma("sync", lambda e, vt=vt, c=c: e.dma_start(out=vt[:], in_=self.PROJ[c * 128:(c + 1) * 128, C_VR:C_VR + 768]), writes=[vt_b])
                Sc.dma("sync", lambda e, gr=gr, c=c: e.dma_start(out=gr[:], in_=self.PROJ[c * 128:(c + 1) * 128, C_GR:C_GR + 768]), writes=[gr_b])
                (qz, qxf, qxb), (qz_b, qxf_b, qxb_b) = qz_r.next()
                for h in range(6):
                    pp, hh = h // 2, h % 2
                    sl = slice(hh * 64, (hh + 1) * 64)
                    Sc.op("vector", lambda e, qz=qz, qt=qt, h=h, pp=pp, sl=sl: e.tensor_copy(out=qz[sl, h, :], in_=qt[sl, pp, :]), reads=[qt_b], writes=[qz_b])
                    Sc.op("gpsimd", lambda e, qxf=qxf, qt=qt, h=h, pp=pp, sl=sl: e.tensor_tensor(out=qxf[sl, h, :], in0=qt[sl, pp, :], in1=XiF[sl, pp, :], op=ALU.mult), reads=[qt_b, cb_], writes=[qxf_b])
                    Sc.op("vector", lambda e, qxb=qxb, qt=qt, h=h, pp=pp, sl=sl: e.tensor_tensor(out=qxb[sl, h, :], in0=qt[sl, pp, :], in1=XiB[sl, pp, :], op=ALU.mult), reads=[qt_b, cb_], writes=[qxb_b])
                (s0, s0_b), (s1, s1_b) = pst.next(), pst.next()
                for h in range(6):
                    pp = h // 2
                    tgt, tb_ = (s0, s0_b) if h < 4 else (s1, s1_b)
                    col = (h % 4) * 128
                    Sc.op("tensor", lambda e, tgt=tgt, ktp=ktp, qz=qz, h=h, pp=pp, col=col: e.matmul(tgt[:, col:col + 128], lhsT=ktp[:, pp, :], rhs=qz[:, h, :], start=True, stop=True), reads=[ktp_b, qz_b], writes=[tb_])
                A, A_b = A_r.next()
                Sc.op("vector", lambda e, A=A, s0=s0: e.tensor_tensor(out=A[:, 0:4, :], in0=s0[:, :].rearrange("p (h n) -> p h n", n=128), in1=Mall[:, 0:4, :], op=ALU.mult), reads=[s0_b, cb_], writes=[A_b])
                Sc.op("vector", lambda e, A=A, s1=s1: e.tensor_tensor(out=A[:, 4:6, :], in0=s1[:, 0:256].rearrange("p (h n) -> p h n", n=128), in1=Mall[:, 4:6, :], op=ALU.mult), reads=[s1_b, cb_], writes=[A_b])
                return dict(c=c, qxf=qxf, qxb=qxb, qxf_b=qxf_b, qxb_b=qxb_b, A=A, A_b=A_b, vt=vt, vt_b=vt_b, gr=gr, gr_b=gr_b)

            def stage2(cx):
                c, qxf, qxb, qxf_b, qxb_b, A, A_b, vt, vt_b, gr, gr_b = (cx[k] for k in ("c", "qxf", "qxb", "qxf_b", "qxb_b", "A", "A_b", "vt", "vt_b", "gr", "gr_b"))
                (y0, y0_b), (y1, y1_b) = pyy.next(), pyy.next()
                for h in range(6):
                    pp = h // 2
                    tgt, tb_ = (y0, y0_b) if h < 4 else (y1, y1_b)
                    col = (h % 4) * 128
                    Sc.op("tensor", lambda e, tgt=tgt, A=A, vt=vt, h=h, col=col: e.matmul(tgt[:, col:col + 128], lhsT=A[:, h, :], rhs=vt[:, h * 128:(h + 1) * 128], start=True, stop=False), reads=[A_b, vt_b], writes=[tb_])
                    Sc.op("tensor", lambda e, tgt=tgt, qxf=qxf, h=h, pp=pp, c=c, col=col: e.matmul(tgt[:, col:col + 128], lhsT=qxf[:, h, :], rhs=Sall[0][:, c, pp, :], start=False, stop=False), reads=[qxf_b, sall_b[0][c]], writes=[tb_])
                    Sc.op("tensor", lambda e, tgt=tgt, qxb=qxb, h=h, pp=pp, c=c, col=col: e.matmul(tgt[:, col:col + 128], lhsT=qxb[:, h, :], rhs=Sall[1][:, c, pp, :], start=False, stop=True), reads=[qxb_b, sall_b[1][c]], writes=[tb_])
                ysb, ysb_b = ysb_r.next()
                ysq, ysq_b = ysq_r.next()
                stt_, stt_b = st_r.next()
                Sc.op("scalar", lambda e, ysb=ysb, y0=y0: e.activation(out=ysb[:, 0:512], in_=y0[:, :], func=AF.Copy), reads=[y0_b], writes=[ysb_b])
                Sc.op("scalar", lambda e, ysb=ysb, y1=y1: e.activation(out=ysb[:, 512:768], in_=y1[:, 0:256], func=AF.Copy), reads=[y1_b], writes=[ysb_b])
                y3 = ysb[:, :].rearrange("p (h e) -> p h e", e=128)
                Sc.op("gpsimd", lambda e, ysq=ysq, ysb=ysb: e.tensor_tensor(out=ysq[:], in0=ysb[:], in1=ysb[:], op=ALU.mult), reads=[ysb_b], writes=[ysq_b])
                Sc.op("vector", lambda e, stt_=stt_, y3=y3: e.tensor_reduce(out=stt_[:, 0, :], in_=y3, axis=AX.X, op=ALU.add), reads=[ysb_b], writes=[stt_b])
                Sc.op("vector", lambda e, stt_=stt_, ysq=ysq: e.tensor_reduce(out=stt_[:, 1, :], in_=ysq[:, :].rearrange("p (h e) -> p h e", e=128), axis=AX.X, op=ALU.add), reads=[ysq_b], writes=[stt_b])
                Sc.op("gpsimd", lambda e, stt_=stt_: e.tensor_scalar(out=stt_[:, 0, :], in0=stt_[:, 0, :], scalar1=1.0 / 128, scalar2=None, op0=ALU.mult), reads=[stt_b], writes=[stt_b])
                Sc.op("gpsimd", lambda e, stt_=stt_: e.tensor_tensor(out=stt_[:, 2, :], in0=stt_[:, 0, :], in1=stt_[:, 0, :], op=ALU.mult), reads=[stt_b], writes=[stt_b])
                Sc.op("gpsimd", lambda e, stt_=stt_: e.tensor_scalar(out=stt_[:, 1, :], in0=stt_[:, 1, :], scalar1=1.0 / 128, scalar2=EPS, op0=ALU.mult, op1=ALU.add), reads=[stt_b], writes=[stt_b])
                Sc.op("gpsimd", lambda e, stt_=stt_: e.tensor_tensor(out=stt_[:, 1, :], in0=stt_[:, 1, :], in1=stt_[:, 2, :], op=ALU.subtract), reads=[stt_b], writes=[stt_b])
                Sc.op("gpsimd", lambda e, stt_=stt_: e.tensor_tensor(out=stt_[:, 3, :], in0=stt_[:, 1, :], in1=negh[:, :], op=ALU.pow), reads=[stt_b, cb_], writes=[stt_b])
                mb = stt_[:, 0, :].unsqueeze(2).broadcast_to([128, 6, 128])
                rb = stt_[:, 3, :].unsqueeze(2).broadcast_to([128, 6, 128])
                Sc.op("vector", lambda e, y3=y3, mb=mb: e.tensor_tensor(out=y3, in0=y3, in1=mb, op=ALU.subtract), reads=[stt_b, ysb_b], writes=[ysb_b])
                Sc.op("vector", lambda e, y3=y3, rb=rb: e.tensor_tensor(out=y3, in0=y3, in1=rb, op=ALU.mult), reads=[stt_b, ysb_b], writes=[ysb_b])
                Sc.op("gpsimd", lambda e, ysb=ysb: e.tensor_tensor(out=ysb[:], in0=ysb[:], in1=gret[:], op=ALU.mult), reads=[ysb_b, cb_], writes=[ysb_b])
                yo, yo_b = yo_r.next()
                Sc.op("gpsimd", lambda e, yo=yo, ysb=ysb, gr=gr: e.tensor_tensor(out=yo[:], in0=ysb[:], in1=gr[:], op=ALU.mult), reads=[ysb_b, gr_b], writes=[yo_b])
                Sc.dma("sync", lambda e, yo=yo, c=c: e.dma_start(out=self.YR[c * 128:(c + 1) * 128, :], in_=yo[:]), reads=[yo_b], writes=[yr_b])

            prev = None
            for c in range(NT):
                cx = stage1(c)
                if prev is not None:
                    stage2(prev)
                prev = cx
            stage2(prev)
            Sc.barrier()
            Sc.emit(st)

    def phase_b3(self):
        nc = self.nc
        with contextlib.ExitStack() as st:
            sb = lambda n, s, d: st.enter_context(nc.sbuf_tensor("b3_" + n, s, d))
            ps = lambda n, s, d: st.enter_context(nc.psum_tensor("b3_" + n, s, d))
            Sc = Sched(nc, prefix="b3_")
            cb_ = Buf("const")
            identf = sb("identf", [128, 128], F32)
            ident = sb("ident", [128, 128], BF16)
            ones = sb("ones", [128, 128], BF16)
            gmem = sb("gmem", [128, D], F32)
            negh = sb("negh", [128, 1], F32)
            wkv = sb("wkv", [128, 8, 1024], BF16)
            wkv_b = Buf("wkv")
            memT = sb("memT", [128, 8, 256], BF16)
            memT_b = Buf("memT")
            kmT = sb("kmT", [128, 4, 256], BF16)
            vm = sb("vm", [128, 2, 512], BF16)
            kv_b = Buf("kv")
            Sc.dma("sync", lambda e: e.dma_start(out=identf[:], in_=self.c_ident[:, :]), writes=[cb_])
            Sc.dma("sync", lambda e: e.dma_start(out=gmem[:], in_=self.g_mem[0, :].partition_broadcast(128)), writes=[cb_])
            Sc.dma("gpsimd", lambda e: e.dma_start(out=wkv[:], in_=self.w_mem_kv[0, :, :].rearrange("(k p) c -> p k c", p=128)), writes=[wkv_b])
            Sc.op("vector", lambda e: e.tensor_copy(out=ident[:], in_=identf[:]), reads=[cb_], writes=[cb_])
            Sc.op("vector", lambda e: e.memset(ones[:], 1.0), reads=[cb_], writes=[cb_])
            Sc.op("vector", lambda e: e.memset(negh[:], -0.5), reads=[cb_], writes=[cb_])
            mt_r = Ring([(sb("mt%d" % i, [128, D], F32), Buf("mt")) for i in range(2)])
            mg_r = Ring([(sb("mg%d" % i, [128, D], BF16), Buf("mg")) for i in range(2)])
            junk = sb("junk", [128, D], BF16)
            junk_b = Buf("junk")
            mst = sb("mst", [128, 4], F32)
            mst_b = Buf("mst")
            pT = Ring([(ps("pT%d" % i, [128, 1024], BF16), Buf("pT")) for i in range(1)])
            pA = Ring([(ps("pA%d" % i, [128, 512], F32), Buf("pA")) for i in range(7)])
            Sc.op("vector", lambda e: e.memset(mst[:], 0.0), writes=[mst_b])
            for t in range(2):
                m, m_b = mt_r.next()
                Sc.dma("sync", lambda e, m=m, t=t: e.dma_start(out=m[:], in_=self.mem[t * 128:(t + 1) * 128, :]), writes=[m_b])
                Sc.op("scalar", lambda e, m=m, t=t: e.activation(out=junk[:], in_=m[:], func=AF.Square, accum_out=mst[:, t:t + 1]), reads=[m_b, mst_b], writes=[junk_b, mst_b])
                Sc.op("gpsimd", lambda e, t=t: e.tensor_scalar(out=mst[:, t:t + 1], in0=mst[:, t:t + 1], scalar1=1.0 / D, scalar2=EPS, op0=ALU.mult, op1=ALU.add), reads=[mst_b], writes=[mst_b])
                Sc.op("gpsimd", lambda e, t=t: e.tensor_tensor(out=mst[:, 2 + t:3 + t], in0=mst[:, t:t + 1], in1=negh[:, 0:1], op=ALU.pow), reads=[mst_b, cb_], writes=[mst_b])
                g, g_b = mg_r.next()
                Sc.op("vector", lambda e, g=g, m=m, t=t: e.scalar_tensor_tensor(out=g[:], in0=m[:], scalar=mst[:, 2 + t:3 + t], in1=gmem[:], op0=ALU.mult, op1=ALU.mult), reads=[m_b, mst_b, cb_], writes=[g_b])
                p, p_b = pT.next()
                for k in range(8):
                    Sc.op("tensor", lambda e, p=p, g=g, k=k: e.transpose(out=p[:, k * 128:(k + 1) * 128], in_=g[:, k * 128:(k + 1) * 128], identity=ident[:]), reads=[g_b, cb_], writes=[p_b])
                Sc.op("vector", lambda e, p=p, t=t: e.tensor_copy(out=memT[:, :, t * 128:(t + 1) * 128], in_=p[:, :].rearrange("p (k c) -> p k c", k=8)), reads=[p_b], writes=[memT_b])
            for h in range(4):
                p, p_b = pA.next()
                for k in range(8):
                    Sc.op("tensor", lambda e, p=p, h=h, k=k: e.matmul(p[:, 0:256], lhsT=wkv[:, k, h * 128:(h + 1) * 128], rhs=memT[:, k, :], start=(k == 0), stop=(k == 7)), reads=[wkv_b, memT_b], writes=[p_b])
                Sc.op("scalar", lambda e, p=p, h=h: e.activation(out=kmT[:, h, :], in_=p[:, 0:256], func=AF.Copy), reads=[p_b], writes=[kv_b])
            for t in range(2):
                p, p_b = pA.next()
                for k in range(8):
                    Sc.op("tensor", lambda e, p=p, t=t, k=k: e.matmul(p[:, :], lhsT=memT[:, k, t * 128:(t + 1) * 128], rhs=wkv[:, k, 512:1024], start=(k == 0), stop=(k == 7)), reads=[wkv_b, memT_b], writes=[p_b])
                Sc.op("scalar", lambda e, p=p, t=t: e.activation(out=vm[:, t, :], in_=p[:, :], func=AF.Copy), reads=[p_b], writes=[kv_b])
            qm_r = Ring([(sb("qm%d" % i, [128, 512], BF16), Buf("qm")) for i in range(3)])
            E_r = Ring([(sb("E%d" % i, [128, 2, 512], BF16), Buf("E")) for i in range(2)])
            R_r = Ring([(sb("R%d" % i, [128, 512], F32), Buf("R")) for i in range(2)])
            y_r = Ring([(sb("y%d" % i, [128, 512], BF16), Buf("y")) for i in range(3)])
            ymt_b = Buf("YMT")
            sc = 1.0 / float(np.sqrt(128.0))
            for tb in range(NB):
                for h in range(4):
                    q, q_b = qm_r.next()
                    for tt in range(4):
                        r0 = tb * 512 + tt * 128
                        Sc.dma("sync", lambda e, q=q, r0=r0, h=h, tt=tt: e.dma_start_transpose(out=q[:, tt * 128:(tt + 1) * 128], in_=self.PROJ[r0:r0 + 128, C_QM + h * 128:C_QM + (h + 1) * 128]), writes=[q_b])
                    E, E_b = E_r.next()
                    for mc in range(2):
                        p, p_b = pA.next()
                        Sc.op("tensor", lambda e, p=p, h=h, mc=mc, q=q: e.matmul(p[:, :], lhsT=kmT[:, h, mc * 128:(mc + 1) * 128], rhs=q[:, :], start=True, stop=True), reads=[kv_b, q_b], writes=[p_b])
                        Sc.op("scalar", lambda e, E=E, p=p, mc=mc: e.activation(out=E[:, mc, :], in_=p[:, :], func=AF.Exp, scale=sc), reads=[p_b], writes=[E_b])
                    pu, pu_b = pA.next()
                    pd, pd_b = pA.next()
                    for mc in range(2):
                        Sc.op("tensor", lambda e, pu=pu, E=E, mc=mc, h=h: e.matmul(pu[:, :], lhsT=vm[:, mc, h * 128:(h + 1) * 128], rhs=E[:, mc, :], start=(mc == 0), stop=(mc == 1)), reads=[kv_b, E_b], writes=[pu_b])
                    for mc in range(2):
                        Sc.op("tensor", lambda e, pd=pd, E=E, mc=mc: e.matmul(pd[:, :], lhsT=ones[:, :], rhs=E[:, mc, :], start=(mc == 0), stop=(mc == 1)), reads=[cb_, E_b], writes=[pd_b])
                    R, R_b = R_r.next()
                    Sc.op("vector", lambda e, R=R, pd=pd: e.reciprocal(out=R[:, :], in_=pd[:, :]), reads=[pd_b], writes=[R_b])
                    y, y_b = y_r.next()
                    Sc.op("vector", lambda e, y=y, R=R, pu=pu: e.tensor_tensor(out=y[:, :], in0=pu[:, :], in1=R[:, :], op=ALU.mult), reads=[pu_b, R_b], writes=[y_b])
                    Sc.dma("gpsimd", lambda e, y=y, h=h, tb=tb: e.dma_start(out=self.YMT[h * 128:(h + 1) * 128, tb * 512:(tb + 1) * 512], in_=y[:, :]), reads=[y_b], writes=[ymt_b])
            Sc.barrier()
            Sc.emit(st)

    def phase_c(self):
        nc = self.nc
        with contextlib.ExitStack() as st:
            sb = lambda n, s, d: st.enter_context(nc.sbuf_tensor("pc_" + n, s, d))
            ps = lambda n, s, d: st.enter_context(nc.psum_tensor("pc_" + n, s, d))
            Sc = Sched(nc, prefix="pc_")
            cb_ = Buf("const")
            w_b = Buf("w")
            identf = sb("identf", [128, 128], F32)
            ident = sb("ident", [128, 128], BF16)
            gffn = sb("gffn", [128, D], F32)
            negh = sb("negh", [128, 1], F32)
            wpa = sb("wpa", [128, 6, D], BF16)
            wpr = sb("wpr", [128, 6, D], BF16)
            wpm = sb("wpm", [128, 4, D], BF16)
            wo = sb("wo", [128, 8, D], BF16)
            Sc.dma("sync", lambda e: e.dma_start(out=identf[:], in_=self.c_ident[:, :]), writes=[cb_])
            Sc.dma("sync", lambda e: e.dma_start(out=gffn[:], in_=self.g_ffn[0, :].partition_broadcast(128)), writes=[cb_])
            Sc.dma("gpsimd", lambda e: e.dma_start(out=wpa[:], in_=self.w_pa[0, :, :].rearrange("(k p) c -> p k c", p=128)), writes=[w_b])
            Sc.dma("gpsimd", lambda e: e.dma_start(out=wpr[:], in_=self.w_pr[0, :, :].rearrange("(k p) c -> p k c", p=128)), writes=[w_b])
            Sc.dma("gpsimd", lambda e: e.dma_start(out=wpm[:], in_=self.w_pm[0, :, :].rearrange("(k p) c -> p k c", p=128)), writes=[w_b])
            Sc.dma("gpsimd", lambda e: e.dma_start(out=wo[:], in_=self.w_out[0, :, :].rearrange("(k p) c -> p k c", p=128)), writes=[w_b])
            Sc.op("vector", lambda e: e.tensor_copy(out=ident[:], in_=identf[:]), reads=[cb_], writes=[cb_])
            Sc.op("vector", lambda e: e.memset(negh[:], -0.5), reads=[cb_], writes=[cb_])
            ya_r = Ring([(sb("ya%d" % i, [128, 6, 512], BF16), Buf("ya")) for i in range(2)])
            yr_r = Ring([(sb("yr%d" % i, [128, 6, 512], BF16), Buf("yr")) for i in range(2)])
            ym_r = Ring([(sb("ym%d" % i, [128, 4, 512], BF16), Buf("ym")) for i in range(2)])
            gt_r = Ring([(sb("gt%d" % i, [128, 3, 512], BF16), Buf("gt")) for i in range(3)])
            mg_r = Ring([(sb("mg%d" % i, [128, 8, 512], BF16), Buf("mg")) for i in range(2)])
            m1_r = Ring([(sb("m1_%d" % i, [128, 512], F32), Buf("m1")) for i in range(2)])
            m2_r = Ring([(sb("m2_%d" % i, [128, 512], F32), Buf("m2")) for i in range(2)])
            m3_r = Ring([(sb("m3_%d" % i, [128, 512], F32), Buf("m3")) for i in range(2)])
            x_r = Ring([(sb("x%d" % i, [128, D], F32), Buf("x")) for i in range(2)])
            h_r = Ring([(sb("h%d" % i, [128, D], F32), Buf("h")) for i in range(2)])
            hn_r = Ring([(sb("hn%d" % i, [128, D], BF16), Buf("hn")) for i in range(2)])
            ht_r = Ring([(sb("ht%d" % i, [128, 8, 128], BF16), Buf("ht")) for i in range(2)])
            junk = sb("junk", [128, D], BF16)
            junk_b = Buf("junk")
            stt_ = sb("stat", [128, 3, NT], F32)
            st_b = [Buf("st") for _ in range(NT)]
            Sc.op("vector", lambda e: e.memset(stt_[:], 0.0), writes=st_b)
            pP = Ring([(ps("pP%d" % i, [128, 512], F32), Buf("pP")) for i in range(5)])
            pO = Ring([(ps("pO%d" % i, [128, 512], F32), Buf("pO")) for i in range(2)])
            pT = Ring([(ps("pT%d" % i, [128, 1024], BF16), Buf("pT")) for i in range(1)])
            pend = []
            h_out_b = Buf("H")
            hnt_b = Buf("HNT")
            def load_y(tb):
                cs = slice(tb * 512, (tb + 1) * 512)
                ya, ya_b = ya_r.next()
                yr, yr_b = yr_r.next()
                ym, ym_b = ym_r.next()
                Sc.dma("sync", lambda e, ya=ya, cs=cs: e.dma_start(out=ya[:], in_=self.YAT[:, cs].rearrange("(k p) s -> p k s", p=128)), writes=[ya_b])
                Sc.dma("sync", lambda e, ym=ym, cs=cs: e.dma_start(out=ym[:], in_=self.YMT[:, cs].rearrange("(k p) s -> p k s", p=128)), writes=[ym_b])
                for tt in range(4):
                    r0 = tb * 512 + tt * 128
                    for k in range(6):
                        Sc.dma("sync", lambda e, yr=yr, r0=r0, k=k, tt=tt: e.dma_start_transpose(out=yr[:, k, tt * 128:(tt + 1) * 128], in_=self.YR[r0:r0 + 128, k * 128:(k + 1) * 128]), writes=[yr_b])
                return (ya, ya_b, yr, yr_b, ym, ym_b)
            ynext = load_y(0)
            for tb in range(NB):
                cs = slice(tb * 512, (tb + 1) * 512)
                ya, ya_b, yr, yr_b, ym, ym_b = ynext
                if tb + 1 < NB:
                    ynext = load_y(tb + 1)
                mg, mg_b = mg_r.next()
                for fc in range(8):
                    gt, gt_b = gt_r.next()
                    Sc.dma("sync", lambda e, gt=gt, fc=fc, cs=cs: e.dma_start(out=gt[:], in_=self.GTT.rearrange("(i f) s -> f i s", i=3)[fc * 128:(fc + 1) * 128, :, cs]), writes=[gt_b])
                    prs = []
                    for (w, src, src_b, nk) in ((wpa, ya, ya_b, 6), (wpr, yr, yr_b, 6), (wpm, ym, ym_b, 4)):
                        p, p_b = pP.next()
                        for k in range(nk):
                            Sc.op("tensor", lambda e, p=p, w=w, src=src, k=k, fc=fc, nk=nk: e.matmul(p[:, :], lhsT=w[:, k, fc * 128:(fc + 1) * 128], rhs=src[:, k, :], start=(k == 0), stop=(k == nk - 1)), reads=[w_b, src_b], writes=[p_b])
                        prs.append((p, p_b))
                    m1, m1_b = m1_r.next()
                    m2, m2_b = m2_r.next()
                    m3, m3_b = m3_r.next()
                    for i, (m, m_b) in enumerate(((m1, m1_b), (m2, m2_b), (m3, m3_b))):
                        p, p_b = prs[i]
                        Sc.op("vector", lambda e, m=m, gt=gt, i=i, p=p: e.scalar_tensor_tensor(out=m[:, :], in0=gt[:, i, :], scalar=1.0, in1=p[:, :], op0=ALU.add, op1=ALU.mult), reads=[gt_b, p_b], writes=[m_b])
                    Sc.op("gpsimd", lambda e, m1=m1, m2=m2: e.tensor_tensor(out=m1[:, :], in0=m1[:, :], in1=m2[:, :], op=ALU.add), reads=[m1_b, m2_b], writes=[m1_b])
                    Sc.op("gpsimd", lambda e, mg=mg, m1=m1, m3=m3, fc=fc: e.tensor_tensor(out=mg[:, fc, :], in0=m1[:, :], in1=m3[:, :], op=ALU.add), reads=[m1_b, m3_b], writes=[mg_b])
                for tt in range(4):
                    t = tb * 4 + tt
                    x, x_b = x_r.next()
                    Sc.dma("sync", lambda e, x=x, t=t: e.dma_start(out=x[:], in_=self.x[t * 128:(t + 1) * 128, :]), writes=[x_b])
                    h, h_b = h_r.next()
                    for nh in range(2):
                        p, p_b = pO.next()
                        for k in range(8):
                            Sc.op("tensor", lambda e, p=p, mg=mg, k=k, tt=tt, nh=nh: e.matmul(p[:, :], lhsT=mg[:, k, tt * 128:(tt + 1) * 128], rhs=wo[:, k, nh * 512:(nh + 1) * 512], start=(k == 0), stop=(k == 7)), reads=[mg_b, w_b], writes=[p_b])
                        Sc.op("vector", lambda e, h=h, p=p, x=x, nh=nh: e.scalar_tensor_tensor(out=h[:, nh * 512:(nh + 1) * 512], in0=p[:, :], scalar=0.5, in1=x[:, nh * 512:(nh + 1) * 512], op0=ALU.mult, op1=ALU.add), reads=[p_b, x_b], writes=[h_b])
                    while pend:
                        pend.pop(0)()
                    Sc.dma("gpsimd", lambda e, h=h, t=t: e.dma_start(out=self.H[t * 128:(t + 1) * 128, :], in_=h[:]), reads=[h_b], writes=[h_out_b])
                    Sc.op("scalar", lambda e, h=h, t=t: e.activation(out=junk[:], in_=h[:], func=AF.Square, accum_out=stt_[:, 0, t:t + 1]), reads=[h_b, st_b[t]], writes=[junk_b, st_b[t]])
                    Sc.op("gpsimd", lambda e, t=t: e.tensor_scalar(out=stt_[:, 1, t:t + 1], in0=stt_[:, 0, t:t + 1], scalar1=1.0 / D, scalar2=EPS, op0=ALU.mult, op1=ALU.add), reads=[st_b[t]], writes=[st_b[t]])
                    Sc.op("gpsimd", lambda e, t=t: e.tensor_tensor(out=stt_[:, 2, t:t + 1], in0=stt_[:, 1, t:t + 1], in1=negh[:, 0:1], op=ALU.pow), reads=[st_b[t], cb_], writes=[st_b[t]])
                    hn, hn_b = hn_r.next()
                    Sc.op("vector", lambda e, hn=hn, h=h, t=t: e.scalar_tensor_tensor(out=hn[:], in0=h[:], scalar=stt_[:, 2, t:t + 1], in1=gffn[:], op0=ALU.mult, op1=ALU.mult), reads=[h_b, st_b[t], cb_], writes=[hn_b])
                    def tail(hn=hn, hn_b=hn_b, t=t):
                        p, p_b = pT.next()
                        for k in range(8):
                            Sc.op("tensor", lambda e, p=p, hn=hn, k=k: e.transpose(out=p[:, k * 128:(k + 1) * 128], in_=hn[:, k * 128:(k + 1) * 128], identity=ident[:]), reads=[hn_b, cb_], writes=[p_b])
                        ht, ht_b = ht_r.next()
                        Sc.op("scalar", lambda e, ht=ht, p=p: e.activation(out=ht[:], in_=p[:, :].rearrange("p (k c) -> p k c", k=8), func=AF.Copy), reads=[p_b], writes=[ht_b])
                        Sc.dma("gpsimd", lambda e, ht=ht, t=t: e.dma_start(out=self.HNT[:, t * 128:(t + 1) * 128].rearrange("(k p) s -> p k s", p=128), in_=ht[:]), reads=[ht_b], writes=[hnt_b])
                    pend.append(tail)
            while pend:
                pend.pop(0)()
            Sc.barrier()
            Sc.emit(st)

    def phase_d(self):
        nc = self.nc
        NF = DFF // 128
        with contextlib.ExitStack() as st:
            sb = lambda n, s, d: st.enter_context(nc.sbuf_tensor("d_" + n, s, d))
            ps = lambda n, s, d: st.enter_context(nc.psum_tensor("d_" + n, s, d))
            Sc = Sched(nc, prefix="d_")
            cb_ = Buf("const")
            hnT = sb("hnT", [128, 8, S], BF16)
            hn_b = [Buf("hnT") for _ in range(NB)]
            for tb in range(NB):
                Sc.dma("sync", lambda e, tb=tb: e.dma_start(out=hnT[:, :, tb * 512:(tb + 1) * 512], in_=self.HNT[:, tb * 512:(tb + 1) * 512].rearrange("(k p) s -> p k s", p=128)), writes=[hn_b[tb]])
            cw = sb("cw", [128, 2, 3, NF], F32)
            cbias = sb("cbias", [128, 2, NF], F32)
            for ab in range(2):
                for j in range(3):
                    Sc.dma("sync", lambda e, ab=ab, j=j: e.dma_start(out=cw[:, ab, j, :], in_=self.conv_w[0, j, ab * DFF:(ab + 1) * DFF].rearrange("(f p) -> p f", p=128), allow_slow_non_contiguous=True), writes=[cb_])
                Sc.dma("sync", lambda e, ab=ab: e.dma_start(out=cbias[:, ab, :], in_=self.conv_b[0, ab * DFF:(ab + 1) * DFF].rearrange("(f p) -> p f", p=128), allow_slow_non_contiguous=True), writes=[cb_])
            w_r = Ring([(sb("w%d" % i, [128, 8, 256], BF16), Buf("w")) for i in range(2)])
            u_r = []
            for i in range(2):
                ua = sb("ua%d" % i, [128, S + 2], F32)
                ub = sb("ub%d" % i, [128, S + 2], F32)
                bl = [Buf("u") for _ in range(NB + 1)]
                for t_ in (ua, ub):
                    Sc.op("gpsimd", lambda e, t_=t_: e.memset(t_[:, 0:1], 0.0), writes=[bl[NB]])
                    Sc.op("gpsimd", lambda e, t_=t_: e.memset(t_[:, S + 1:S + 2], 0.0), writes=[bl[NB]])
                u_r.append(((ua, ub), bl))
            u_r = Ring(u_r)
            ca = sb("ca", [128, S], F32)
            cbb = sb("cb", [128, S], F32)
            th = sb("th", [128, S], F32)
            ca_b, cbb_b, th_b = Buf("ca"), Buf("cb"), Buf("th")
            g_r = Ring([(sb("g%d" % i, [128, S], BF16), Buf("g")) for i in range(2)])
            pA = Ring([(ps("pA%d" % i, [128, 512], F32), Buf("pA")) for i in range(8)])
            gt2_b = Buf("GT2")
            for fc in range(NF):
                w, w_b = w_r.next()
                Sc.dma("gpsimd", lambda e, w=w, fc=fc: e.dma_start(out=w[:, :, 0:128], in_=self.w_up[0, :, fc * 128:(fc + 1) * 128].rearrange("(k p) c -> p k c", p=128)), writes=[w_b])
                Sc.dma("gpsimd", lambda e, w=w, fc=fc: e.dma_start(out=w[:, :, 128:256], in_=self.w_up[0, :, DFF + fc * 128:DFF + (fc + 1) * 128].rearrange("(k p) c -> p k c", p=128)), writes=[w_b])
                (ua, ub), ubl = u_r.next()
                for tb in range(NB):
                    for ab, ut in ((0, ua), (1, ub)):
                        p, p_b = pA.next()
                        for k in range(8):
                            Sc.op("tensor", lambda e, p=p, w=w, k=k, ab=ab, tb=tb: e.matmul(p[:, :], lhsT=w[:, k, ab * 128:(ab + 1) * 128], rhs=hnT[:, k, tb * 512:(tb + 1) * 512], start=(k == 0), stop=(k == 7)), reads=[w_b, hn_b[tb]], writes=[p_b])
                        Sc.op("scalar", lambda e, ut=ut, p=p, tb=tb: e.activation(out=ut[:, 1 + tb * 512:1 + (tb + 1) * 512], in_=p[:, :], func=AF.Copy), reads=[p_b], writes=[ubl[tb]])
                for ab, ut, ct, ct_b, eng in ((0, ua, ca, ca_b, "vector"), (1, ub, cbb, cbb_b, "vector")):
                    Sc.op("scalar", lambda e, ut=ut, ct=ct, ab=ab, fc=fc: e.activation(out=ct[:, :], in_=ut[:, 0:S], func=AF.Identity, bias=cbias[:, ab, fc:fc + 1], scale=cw[:, ab, 0, fc:fc + 1]), reads=ubl + [cb_], writes=[ct_b])
                    Sc.op(eng, lambda e, ut=ut, ct=ct, ab=ab, fc=fc: e.scalar_tensor_tensor(out=ct[:, :], in0=ut[:, 1:S + 1], scalar=cw[:, ab, 1, fc:fc + 1], in1=ct[:, :], op0=ALU.mult, op1=ALU.add), reads=ubl + [cb_, ct_b], writes=[ct_b])
                    Sc.op(eng, lambda e, ut=ut, ct=ct, ab=ab, fc=fc: e.scalar_tensor_tensor(out=ct[:, :], in0=ut[:, 2:S + 2], scalar=cw[:, ab, 2, fc:fc + 1], in1=ct[:, :], op0=ALU.mult, op1=ALU.add), reads=ubl + [cb_, ct_b], writes=[ct_b])
                Sc.op("scalar", lambda e: e.activation(out=th[:, :], in_=ca[:, :], func=AF.Tanh, scale=0.5), reads=[ca_b], writes=[th_b])
                Sc.op("vector", lambda e: e.scalar_tensor_tensor(out=th[:, :], in0=th[:, :], scalar=1.0, in1=ca[:, :], op0=ALU.add, op1=ALU.mult), reads=[ca_b, th_b], writes=[th_b])
                g, g_b = g_r.next()
                Sc.op("vector", lambda e, g=g: e.tensor_tensor(out=g[:, :], in0=th[:, :], in1=cbb[:, :], op=ALU.mult), reads=[th_b, cbb_b], writes=[g_b])
                Sc.dma("sync", lambda e, g=g, fc=fc: e.dma_start(out=self.GT2[fc * 128:(fc + 1) * 128, :], in_=g[:, :]), reads=[g_b], writes=[gt2_b])
            Sc.barrier()
            Sc.emit(st)

    def phase_e(self):
        nc = self.nc
        NF = DFF // 128
        with contextlib.ExitStack() as st:
            sb = lambda n, s, d: st.enter_context(nc.sbuf_tensor("e_" + n, s, d))
            ps = lambda n, s, d: st.enter_context(nc.psum_tensor("e_" + n, s, d))
            Sc = Sched(nc, prefix="e_")
            cb_ = Buf("const")
            w_b = Buf("w")
            wd = sb("wd", [128, NF, D], BF16)
            gfin = sb("gfin", [128, D], F32)
            negh = sb("negh", [128, 1], F32)
            for q4 in range(2):
                Sc.dma("gpsimd", lambda e, q4=q4: e.dma_start(out=wd[:, q4 * 11:(q4 + 1) * 11, :], in_=self.w_down[0, q4 * 11 * 128:(q4 + 1) * 11 * 128, :].rearrange("(k p) c -> p k c", p=128)), writes=[w_b])
            Sc.dma("sync", lambda e: e.dma_start(out=gfin[:], in_=self.g_final.partition_broadcast(128)), writes=[cb_])
            Sc.op("vector", lambda e: e.memset(negh[:], -0.5), reads=[cb_], writes=[cb_])
            g_r = Ring([(sb("g%d" % i, [128, NF, 512], BF16), Buf("g")) for i in range(2)])
            h_r = Ring([(sb("h%d" % i, [128, D], F32), Buf("h")) for i in range(3)])
            o_r = Ring([(sb("o%d" % i, [128, D], F32), Buf("o")) for i in range(2)])
            junk = sb("junk", [128, D], BF16)
            junk_b = Buf("junk")
            stt_ = sb("stat", [128, 3, NT], F32)
            st_b = [Buf("st") for _ in range(NT)]
            Sc.op("vector", lambda e: e.memset(stt_[:], 0.0), writes=st_b)
            pO = Ring([(ps("pO%d" % i, [128, 512], F32), Buf("pO")) for i in range(6)])
            out_b = Buf("out")
            def load_g(tb):
                g, g_b = g_r.next()
                Sc.dma("sync", lambda e, g=g, tb=tb: e.dma_start(out=g[:], in_=self.GT2[:, tb * 512:(tb + 1) * 512].rearrange("(k p) s -> p k s", p=128)), writes=[g_b])
                return g, g_b
            gnext = load_g(0)
            for tb in range(NB):
                g, g_b = gnext
                if tb + 1 < NB:
                    gnext = load_g(tb + 1)
                for tt in range(4):
                    t = tb * 4 + tt
                    h, h_b = h_r.next()
                    Sc.dma("sync", lambda e, h=h, t=t: e.dma_start(out=h[:], in_=self.H[t * 128:(t + 1) * 128, :]), writes=[h_b])
                    for nh in range(2):
                        p, p_b = pO.next()
                        for k in range(NF):
                            Sc.op("tensor", lambda e, p=p, g=g, k=k, tt=tt, nh=nh: e.matmul(p[:, :], lhsT=g[:, k, tt * 128:(tt + 1) * 128], rhs=wd[:, k, nh * 512:(nh + 1) * 512], start=(k == 0), stop=(k == NF - 1)), reads=[g_b, w_b], writes=[p_b])
                        Sc.op("vector", lambda e, h=h, p=p, nh=nh: e.scalar_tensor_tensor(out=h[:, nh * 512:(nh + 1) * 512], in0=p[:, :], scalar=0.5, in1=h[:, nh * 512:(nh + 1) * 512], op0=ALU.mult, op1=ALU.add), reads=[p_b, h_b], writes=[h_b])
                    Sc.op("scalar", lambda e, h=h, t=t: e.activation(out=junk[:], in_=h[:], func=AF.Square, accum_out=stt_[:, 0, t:t + 1]), reads=[h_b, st_b[t]], writes=[junk_b, st_b[t]])
                    Sc.op("gpsimd", lambda e, t=t: e.tensor_scalar(out=stt_[:, 1, t:t + 1], in0=stt_[:, 0, t:t + 1], scalar1=1.0 / D, scalar2=EPS, op0=ALU.mult, op1=ALU.add), reads=[st_b[t]], writes=[st_b[t]])
                    Sc.op("gpsimd", lambda e, t=t: e.tensor_tensor(out=stt_[:, 2, t:t + 1], in0=stt_[:, 1, t:t + 1], in1=negh[:, 0:1], op=ALU.pow), reads=[st_b[t], cb_], writes=[st_b[t]])
                    o, o_b = o_r.next()
                    Sc.op("vector", lambda e, o=o, h=h, t=t: e.scalar_tensor_tensor(out=o[:], in0=h[:], scalar=stt_[:, 2, t:t + 1], in1=gfin[:], op0=ALU.mult, op1=ALU.mult), reads=[h_b, st_b[t], cb_], writes=[o_b])
                    Sc.dma("sync", lambda e, o=o, t=t: e.dma_start(out=self.out[t * 128:(t + 1) * 128, :], in_=o[:]), reads=[o_b], writes=[out_b])
            Sc.barrier()
            Sc.emit(st)

    def build(self):
        for ph in ("a", "b1", "b1c", "b2", "b3", "c", "d", "e"):
            if self.phases is not None and ph not in self.phases:
                continue
            fn = getattr(self, "phase_" + ph, None)
            if fn is not None:
                fn()
            if self.stop_after == ph:
                break
        return self.nc


def host_consts():
    inv = 10000.0 ** (-np.arange(0, 64, 2, dtype=np.float32) / 64.0)
    ang = np.arange(S, dtype=np.float32)[:, None] * inv[None, :].astype(np.float32)
    c = {
        "c_cos": np.cos(ang).astype(np.float32),
        "c_sin": np.sin(ang).astype(np.float32),
        "c_ident": np.eye(128, dtype=np.float32),
    }
    kk = np.arange(128)[:, None]
    qq = np.arange(256)[None, :]
    band = ((qq >= kk) & (qq <= kk + 128))
    m = np.where(band, 0.0, -30000.0).astype(np.float32)
    am = np.zeros((128, 1024), np.float32)
    am[:, 0:256] = m
    am[:, 256:512] = m
    mf = m[64:128, 128:256]
    ml = m[0:64, 0:128]
    am[0:64, 512:640] = mf
    am[0:64, 640:768] = mf
    am[0:64, 768:896] = ml
    am[0:64, 896:1024] = ml
    c["c_amask"] = am
    r = np.zeros((128, 8, 128), np.float32)
    mm = np.arange(128)[:, None].astype(np.float32)
    nn = np.arange(128)[None, :].astype(np.float32)
    r[:, 0, :] = np.maximum(nn - mm, 0)
    r[:, 1, :] = np.maximum(mm - nn, 0)
    r[:, 2, :] = (nn >= mm) * 0.125
    r[:, 3, :] = (mm > nn) * 0.125
    r[:, 4, :] = nn + 1.0
    r[:, 5, :] = 128.0 - nn
    r[:, 6, 0] = 127.0 - np.arange(128)
    r[:, 6, 1] = np.arange(128)
    c["c_ret"] = r
    return c


def make_in_maps(inputs, n_cores=8):
    consts = host_consts()
    maps = []
    for b in range(n_cores):
        m = {"x": np.ascontiguousarray(inputs["x"][b]), "mem": np.ascontiguousarray(inputs["mem"][b])}
        for k, v in inputs.items():
            if k in ("x", "mem"):
                continue
            m[k] = np.ascontiguousarray(v)
        m.update(consts)
        maps.append(m)
    return maps


def kernel(**inputs):
    inputs = {k: np.asarray(v) for k, v in inputs.items()}
    prog = Prog()
    nc = prog.build()
    res = run_bass_kernel_spmd(nc, make_in_maps(inputs), core_ids=list(range(8)))
    return np.stack([r["out"] for r in res.results], axis=0)
```

```python
import contextlib
import numpy as np
import concourse.bass as bass
import concourse.mybir as mybir
from concourse.bass_utils import run_bass_kernel_spmd

F32 = mybir.dt.float32
BF16 = mybir.dt.bfloat16
AF = mybir.ActivationFunctionType
ALU = mybir.AluOpType
AX = mybir.AxisListType

S = 4096
D = 1024
NT = S // 128
NB = S // 512
IN_W = 5120
DFF = 2816
EPS = 1e-6
C_QA, C_KA, C_VA, C_QR, C_KR, C_VR, C_GR, C_QM = 0, 768, 1536, 2304, 2688, 3072, 3840, 4608


class Buf:
    __slots__ = ("name", "w", "r")

    def __init__(self, name):
        self.name = name
        self.w = None
        self.r = []


class Sched:
    ENG = ("tensor", "vector", "scalar", "gpsimd", "sync")

    def __init__(self, nc, n_dma_sems=32, prefix=""):
        self.nc = nc
        self.prefix = prefix
        self.lists = {e: [] for e in self.ENG}
        self.cnt = {e: 0 for e in self.ENG}
        self.known = {e: {} for e in self.ENG}
        self.ndma = n_dma_sems
        self.dma_issued = [0] * n_dma_sems
        self.dma_rr = 0

    def _need(self, eng, ev, waits):
        if ev is None:
            return
        key, val = ev
        if key == eng and eng == "tensor":
            return
        if self.known[eng].get(key, 0) >= val:
            return
        if waits.get(key, 0) < val:
            waits[key] = val

    def _deps(self, eng, reads, writes):
        waits = {}
        for b in reads:
            self._need(eng, b.w, waits)
        for b in writes:
            self._need(eng, b.w, waits)
            for ev in b.r:
                self._need(eng, ev, waits)
        for k, v in waits.items():
            self.known[eng][k] = v
        return list(waits.items())

    def op(self, eng, fn, reads=(), writes=()):
        waits = self._deps(eng, reads, writes)
        self.cnt[eng] += 1
        ev = (eng, self.cnt[eng])
        self.lists[eng].append((waits, fn, eng, 1))
        for b in reads:
            b.r.append(ev)
        for b in writes:
            b.w = ev
            b.r = []
        return ev

    def dma(self, eng, fn, reads=(), writes=()):
        i = self.dma_rr
        self.dma_rr = (self.dma_rr + 1) % self.ndma
        key = ("dma", i)
        waits = dict(self._deps(eng, reads, writes))
        prev = self.dma_issued[i]
        if prev > 0 and self.known[eng].get(key, 0) < prev:
            waits[key] = prev
            self.known[eng][key] = prev
        self.dma_issued[i] = prev + 16
        ev = (key, prev + 16)
        self.lists[eng].append((list(waits.items()), fn, key, 16))
        for b in reads:
            b.r.append(ev)
        for b in writes:
            b.w = ev
            b.r = []
        return ev

    def barrier(self):
        for e in self.ENG:
            waits = {}
            for o in self.ENG:
                if o != e and self.cnt[o] > 0:
                    self._need(e, (o, self.cnt[o]), waits)
            if e != "tensor" and self.cnt[e] > 0:
                self._need(e, (e, self.cnt[e]), waits)
            for i in range(self.ndma):
                if self.dma_issued[i] > 0:
                    self._need(e, (("dma", i), self.dma_issued[i]), waits)
            for k, v in waits.items():
                self.known[e][k] = v
            self.lists[e].append((list(waits.items()), None, None, 0))

    def emit(self, stack):
        nc = self.nc
        semmap = {}
        handles = []
        for e in self.ENG:
            semmap[e] = nc.alloc_semaphore(name=self.prefix + "s_" + e)
            handles.append(semmap[e])
        for i in range(self.ndma):
            semmap[("dma", i)] = nc.alloc_semaphore(name=self.prefix + "s_dma%d" % i)
            handles.append(semmap[("dma", i)])

        def runner(items):
            def f(e):
                for waits, fn, key, inc in items:
                    for k, v in waits:
                        e.wait_ge(semmap[k], v)
                    if fn is not None:
                        fn(e).then_inc(semmap[key], inc)
            return f
        with nc.Block() as block:
            block.tensor(runner(self.lists["tensor"]))
            block.vector(runner(self.lists["vector"]))
            block.scalar(runner(self.lists["scalar"]))
            block.gpsimd(runner(self.lists["gpsimd"]))
            block.sync(runner(self.lists["sync"]))
        nc.clear_and_free_semaphores(handles)
        nc.all_engine_barrier()


class Ring:
    def __init__(self, items):
        self.items = items
        self.i = 0

    def next(self):
        it = self.items[self.i]
        self.i = (self.i + 1) % len(self.items)
        return it


class Prog:
    def __init__(self, debug=False, stop_after=None, phases=None, ext_in=()):
        self.debug = debug
        self.stop_after = stop_after
        self.phases = phases
        self.ext_in = set(ext_in)
        nc = self.nc = bass.Bass("TRN2", target_bir_lowering=False)
        ein = lambda n, s: nc.dram_tensor(n, s, F32, kind="ExternalInput").ap()
        self.x = ein("x", [S, D])
        self.mem = ein("mem", [256, D])
        self.g_mix = ein("g_mix", [1, D])
        self.w_in = ein("w_in", [1, D, IN_W])
        self.w_mem_kv = ein("w_mem_kv", [1, D, 1024])
        self.g_mem = ein("g_mem", [1, D])
        self.dec_f = ein("ret_decay_fwd", [1, 6])
        self.dec_b = ein("ret_decay_bwd", [1, 6])
        self.g_ret = ein("g_ret", [1, 768])
        self.w_pa = ein("w_proj_attn", [1, 768, D])
        self.w_pr = ein("w_proj_ret", [1, 768, D])
        self.w_pm = ein("w_proj_mem", [1, 512, D])
        self.w_gate = ein("w_gate", [1, D, 3 * D])
        self.b_gate = ein("b_gate", [1, 3 * D])
        self.w_out = ein("w_out", [1, D, D])
        self.g_ffn = ein("g_ffn", [1, D])
        self.w_up = ein("w_up", [1, D, 2 * DFF])
        self.conv_w = ein("conv_w", [1, 3, 2 * DFF])
        self.conv_b = ein("conv_b", [1, 2 * DFF])
        self.w_down = ein("w_down", [1, DFF, D])
        self.g_final = ein("g_final", [D])
        self.c_cos = ein("c_cos", [S, 32])
        self.c_sin = ein("c_sin", [S, 32])
        self.c_ident = ein("c_ident", [128, 128])
        self.c_amask = ein("c_amask", [128, 1024])
        self.c_ret = ein("c_ret", [128, 8, 128])
        self.out = nc.dram_tensor("out", [S, D], F32, kind="ExternalOutput").ap()
        kind = "ExternalOutput" if debug else "Internal"
        scr = lambda n, s, d: nc.dram_tensor(n, s, d, kind=("ExternalInput" if n in self.ext_in else kind)).ap()
        self.PROJ = scr("PROJ", [S, IN_W], BF16)
        self.GTT = scr("GTT", [3 * D, S], BF16)
        self.UD = scr("UD", [12 * 128, S], F32)
        self.YAT = scr("YAT", [768, S], BF16)
        self.YR = scr("YR", [S, 768], BF16)
        self.YMT = scr("YMT", [512, S], BF16)
        self.H = scr("H", [S, D], F32)
        self.HNT = scr("HNT", [D, S], BF16)
        self.GT2 = scr("GT2", [DFF, S], BF16)

    def phase_a(self):
        nc = self.nc
        with contextlib.ExitStack() as st:
            sb = lambda n, s, d: st.enter_context(nc.sbuf_tensor("a_" + n, s, d))
            ps = lambda n, s, d: st.enter_context(nc.psum_tensor("a_" + n, s, d))
            Sc = Sched(nc, prefix="a_")
            xT = sb("xT", [128, 8, S], BF16)
            xT_b = [Buf("xT%d" % t) for t in range(NT)]
            xr = Ring([(sb("xr%d" % i, [128, D], F32), Buf("xr%d" % i)) for i in range(3)])
            xg = Ring([(sb("xg%d" % i, [128, D], BF16), Buf("xg%d" % i)) for i in range(2)])
            junk = sb("junk", [128, D], BF16)
            junk_b = Buf("junk")
            gmix = sb("gmix", [128, D], F32)
            gmix_b = Buf("gmix")
            ssq = sb("ssq", [128, NT], F32)
            msq = sb("msq", [128, NT], F32)
            rstd = sb("rstd", [128, NT], F32)
            negh = sb("negh", [128, 1], F32)
            st_b = [Buf("st%d" % t) for t in range(NT)]
            const_b = Buf("const")
            identf = sb("identf", [128, 128], F32)
            ident = sb("ident", [128, 128], BF16)
            cos_t = sb("cos_t", [128, NT, 32], F32)
            sin_t = sb("sin_t", [128, NT, 32], F32)
            hb = sb("hb", [128, 24], F32)
            pT = Ring([(ps("pT%d" % i, [128, 1024], BF16), Buf("pT%d" % i)) for i in range(2)])
            pA = Ring([(ps("pA%d" % i, [128, 512], F32), Buf("pA%d" % i)) for i in range(6)])
            wr = Ring([(sb("w%d" % i, [128, 8, 512], BF16), Buf("w%d" % i)) for i in range(3)])
            ob = Ring([(sb("ob%d" % i, [128, 512], BF16), Buf("ob%d" % i)) for i in range(4)])
            tA = Ring([(sb("tA%d" % i, [128, 512], F32), Buf("tA%d" % i)) for i in range(3)])
            tB = Ring([(sb("tB%d" % i, [128, 512], F32), Buf("tB%d" % i)) for i in range(3)])
            proj_b = Buf("PROJ")
            gtt_b = Buf("GTT")

            Sc.dma("sync", lambda e: e.dma_start(out=gmix[:], in_=self.g_mix[0, :].partition_broadcast(128)), writes=[gmix_b])
            Sc.dma("sync", lambda e: e.dma_start(out=identf[:], in_=self.c_ident[:, :]), writes=[const_b])
            Sc.dma("sync", lambda e: e.dma_start(out=cos_t[:], in_=self.c_cos.rearrange("(t p) c -> p t c", p=128)), writes=[const_b])
            Sc.dma("sync", lambda e: e.dma_start(out=sin_t[:], in_=self.c_sin.rearrange("(t p) c -> p t c", p=128)), writes=[const_b])
            Sc.dma("sync", lambda e: e.dma_start(out=hb[:], in_=self.b_gate[0, :].rearrange("(f p) -> p f", p=128), allow_slow_non_contiguous=True), writes=[const_b])
            Sc.op("vector", lambda e: e.tensor_copy(out=ident[:], in_=identf[:]), reads=[const_b], writes=[const_b])
            Sc.op("vector", lambda e: e.tensor_scalar(out=hb[:], in0=hb[:], scalar1=0.5, scalar2=None, op0=ALU.mult), reads=[const_b], writes=[const_b])
            Sc.op("vector", lambda e: e.memset(ssq[:], 0.0), writes=st_b)
            Sc.op("vector", lambda e: e.memset(negh[:], -0.5), writes=[const_b])

            for t in range(NT):
                xt, xt_b = xr.next()
                Sc.dma("sync", lambda e, xt=xt, t=t: e.dma_start(out=xt[:], in_=self.x[t * 128:(t + 1) * 128, :]), writes=[xt_b])
                Sc.op("scalar", lambda e, xt=xt, t=t: e.activation(out=junk[:], in_=xt[:], func=AF.Square, accum_out=ssq[:, t:t + 1]),
                      reads=[xt_b], writes=[junk_b, st_b[t]])
                Sc.op("gpsimd", lambda e, t=t: e.tensor_scalar(out=msq[:, t:t + 1], in0=ssq[:, t:t + 1], scalar1=1.0 / D, scalar2=EPS, op0=ALU.mult, op1=ALU.add),
                      reads=[st_b[t]], writes=[st_b[t]])
                Sc.op("gpsimd", lambda e, t=t: e.tensor_tensor(out=rstd[:, t:t + 1], in0=msq[:, t:t + 1], in1=negh[:, 0:1], op=ALU.pow),
                      reads=[st_b[t], const_b], writes=[st_b[t]])
                g, g_b = xg.next()
                Sc.op("vector", lambda e, g=g, xt=xt, t=t: e.scalar_tensor_tensor(out=g[:], in0=xt[:], scalar=rstd[:, t:t + 1], in1=gmix[:], op0=ALU.mult, op1=ALU.mult),
                      reads=[xt_b, st_b[t], gmix_b], writes=[g_b])
                p, p_b = pT.next()
                for k in range(8):
                    Sc.op("tensor", lambda e, p=p, g=g, k=k: e.transpose(out=p[:, k * 128:(k + 1) * 128], in_=g[:, k * 128:(k + 1) * 128], identity=ident[:]),
                          reads=[g_b, const_b], writes=[p_b])
                eng = "scalar" if t % 2 == 0 else "vector"
                dst = xT[:, :, t * 128:(t + 1) * 128]
                src = p[:, :].rearrange("p (k c) -> p k c", k=8)
                if eng == "scalar":
                    Sc.op("scalar", lambda e, dst=dst, src=src: e.activation(out=dst, in_=src, func=AF.Copy), reads=[p_b], writes=[xT_b[t]])
                else:
                    Sc.op("vector", lambda e, dst=dst, src=src: e.tensor_copy(out=dst, in_=src), reads=[p_b], writes=[xT_b[t]])

            blocks = [(C_QA, 512, "rot"), (C_QA + 512, 256, "rot"), (C_KA, 512, "rot"), (C_KA + 512, 256, "rot"),
                      (C_VA, 512, "copy"), (C_VA + 512, 256, "copy"), (C_QR, 384, "rot"), (C_KR, 384, "rot"),
                      (C_VR, 512, "copy"), (C_VR + 512, 256, "copy"), (C_GR, 512, "silu2"), (C_GR + 512, 256, "silu2"),
                      (C_QM, 512, "copy")]
            wspecs = [(self.w_in[0, :, c0:c0 + N], N) for (c0, N, kind) in blocks] + [(self.w_gate[0, :, fg * 512:(fg + 1) * 512], 512) for fg in range(6)]
            wloaded = {}

            def ensure_w(i):
                if i < len(wspecs) and i not in wloaded:
                    w, w_b = wr.next()
                    src, N = wspecs[i]
                    Sc.dma("gpsimd", lambda e, w=w, src=src, N=N: e.dma_start(out=w[:, :, 0:N], in_=src.rearrange("(k p) c -> p k c", p=128)), writes=[w_b])
                    wloaded[i] = (w, w_b)
                return wloaded.get(i)
            for bi, (c0, N, kind) in enumerate(blocks):
                w, w_b = ensure_w(bi)
                ensure_w(bi + 1)
                for t in range(NT):
                    p, p_b = pA.next()
                    for k in range(8):
                        Sc.op("tensor", lambda e, p=p, w=w, t=t, k=k, N=N: e.matmul(p[:, 0:N], lhsT=xT[:, k, t * 128:(t + 1) * 128], rhs=w[:, k, 0:N], start=(k == 0), stop=(k == 7)),
                              reads=[xT_b[t], w_b], writes=[p_b])
                    o, o_b = ob.next()
                    if kind == "copy":
                        Sc.op("scalar", lambda e, o=o, p=p, N=N: e.activation(out=o[:, 0:N], in_=p[:, 0:N], func=AF.Copy), reads=[p_b], writes=[o_b])
                    elif kind == "silu2":
                        a, a_b = tA.next()
                        Sc.op("scalar", lambda e, a=a, p=p, N=N: e.activation(out=a[:, 0:N], in_=p[:, 0:N], func=AF.Tanh, scale=0.5), reads=[p_b], writes=[a_b])
                        Sc.op("vector", lambda e, o=o, a=a, p=p, N=N: e.scalar_tensor_tensor(out=o[:, 0:N], in0=a[:, 0:N], scalar=1.0, in1=p[:, 0:N], op0=ALU.add, op1=ALU.mult),
                              reads=[a_b, p_b], writes=[o_b])
                    else:
                        H = N // 64
                        a, a_b = tA.next()
                        b, b_b = tB.next()
                        pv = p[:, 0:N].rearrange("p (h two f) -> p h two f", two=2, f=32)
                        av = a[:, 0:N].rearrange("p (h two f) -> p h two f", two=2, f=32)
                        bv = b[:, 0:N].rearrange("p (h two f) -> p h two f", two=2, f=32)
                        ov = o[:, 0:N].rearrange("p (h two f) -> p h two f", two=2, f=32)
                        cb = cos_t[:, t:t + 1, :].broadcast_to([128, H, 32])
                        sn = sin_t[:, t:t + 1, :].broadcast_to([128, H, 32])
                        x1, x2 = pv[:, :, 0, :], pv[:, :, 1, :]
                        Sc.op("vector", lambda e, av=av, x1=x1, cb=cb: e.tensor_tensor(out=av[:, :, 0, :], in0=x1, in1=cb, op=ALU.mult), reads=[p_b, const_b], writes=[a_b])
                        Sc.op("vector", lambda e, av=av, x2=x2, cb=cb: e.tensor_tensor(out=av[:, :, 1, :], in0=x2, in1=cb, op=ALU.mult), reads=[p_b, const_b], writes=[a_b])
                        Sc.op("vector", lambda e, bv=bv, x2=x2, sn=sn: e.tensor_tensor(out=bv[:, :, 0, :], in0=x2, in1=sn, op=ALU.mult), reads=[p_b, const_b], writes=[b_b])
                        Sc.op("vector", lambda e, bv=bv, x1=x1, sn=sn: e.tensor_tensor(out=bv[:, :, 1, :], in0=x1, in1=sn, op=ALU.mult), reads=[p_b, const_b], writes=[b_b])
                        Sc.op("gpsimd", lambda e, ov=ov, av=av, bv=bv: e.tensor_tensor(out=ov[:, :, 0, :], in0=av[:, :, 0, :], in1=bv[:, :, 0, :], op=ALU.subtract), reads=[a_b, b_b], writes=[o_b])
                        Sc.op("gpsimd", lambda e, ov=ov, av=av, bv=bv: e.tensor_tensor(out=ov[:, :, 1, :], in0=av[:, :, 1, :], in1=bv[:, :, 1, :], op=ALU.add), reads=[a_b, b_b], writes=[o_b])
                    Sc.dma("sync", lambda e, o=o, t=t, c0=c0, N=N: e.dma_start(out=self.PROJ[t * 128:(t + 1) * 128, c0:c0 + N], in_=o[:, 0:N]), reads=[o_b], writes=[proj_b])

            for fg in range(6):
                w, w_b = ensure_w(len(blocks) + fg)
                ensure_w(len(blocks) + fg + 1)
                for j in range(4):
                    fc = fg * 4 + j
                    for tb in range(NB):
                        p, p_b = pA.next()
                        for k in range(8):
                            Sc.op("tensor", lambda e, p=p, w=w, tb=tb, k=k, j=j: e.matmul(p[:, :], lhsT=w[:, k, j * 128:(j + 1) * 128], rhs=xT[:, k, tb * 512:(tb + 1) * 512], start=(k == 0), stop=(k == 7)),
                                  reads=xT_b[tb * 4:tb * 4 + 4] + [w_b], writes=[p_b])
                        o, o_b = ob.next()
                        Sc.op("scalar", lambda e, o=o, p=p, fc=fc: e.activation(out=o[:, :], in_=p[:, :], func=AF.Tanh, bias=hb[:, fc:fc + 1], scale=0.5), reads=[p_b, const_b], writes=[o_b])
                        Sc.dma("sync", lambda e, o=o, fc=fc, tb=tb: e.dma_start(out=self.GTT[fc * 128:(fc + 1) * 128, tb * 512:(tb + 1) * 512], in_=o[:, :]), reads=[o_b], writes=[gtt_b])
            Sc.barrier()
            Sc.emit(st)


    def phase_b1(self):
        nc = self.nc
        with contextlib.ExitStack() as st:
            sb = lambda n, s, d: st.enter_context(nc.sbuf_tensor("b1_" + n, s, d))
            ps = lambda n, s, d: st.enter_context(nc.psum_tensor("b1_" + n, s, d))
            Sc = Sched(nc, prefix="b1_")
            const_b = Buf("const")
            amf = sb("amf", [128, 1024], F32)
            am = sb("am", [128, 1024], BF16)
            identf = sb("identf", [128, 128], F32)
            ident = sb("ident", [128, 128], BF16)
            Sc.dma("sync", lambda e: e.dma_start(out=amf[:], in_=self.c_amask[:, :]), writes=[const_b])
            Sc.dma("sync", lambda e: e.dma_start(out=identf[:], in_=self.c_ident[:, :]), writes=[const_b])
            Sc.op("vector", lambda e: e.tensor_copy(out=am[:], in_=amf[:]), reads=[const_b], writes=[const_b])
            Sc.op("vector", lambda e: e.tensor_copy(out=ident[:], in_=identf[:]), reads=[const_b], writes=[const_b])
            sets = []
            for i in range(3):
                Lc = S if i == 2 else 1024
                nbc = Lc // 128
                qT = [sb("qT%d_%d" % (i, pp), [128, Lc], BF16) for pp in range(2)]
                qZ = [[sb("qZ%d_%d_%d" % (i, pp, hh), [128, Lc], BF16) for hh in range(2)] for pp in range(2)]
                qz_b = [[[Buf("qz") for _ in range(nbc)] for hh in range(2)] for pp in range(2)]
                for pp in range(2):
                    Sc.op("gpsimd", lambda e, t=qZ[pp][0]: e.memset(t[64:128, :], 0.0), writes=qz_b[pp][0])
                    Sc.op("gpsimd", lambda e, t=qZ[pp][1]: e.memset(t[0:64, :], 0.0), writes=qz_b[pp][1])
                kT = [sb("kT%d_%d" % (i, pp), [128, Lc], BF16) for pp in range(2)]
                va = sb("va%d" % i, [128, nbc + 1, 4, 128], BF16)
                q_b = [[Buf("q") for _ in range(nbc)] for pp in range(2)]
                k_b = [[Buf("k") for _ in range(nbc)] for pp in range(2)]
                v_b = [Buf("v") for _ in range(nbc + 1)]
                Sc.op("gpsimd", lambda e, va=va: e.memset(va[:, :, :, 64:128], 1.0), writes=v_b)
                sets.append((qT, kT, va, q_b, k_b, v_b, qZ, qz_b))
            pS = Ring([(ps("pS%d" % i, [128, 512], F32), Buf("pS%d" % i)) for i in range(3)])
            pU = Ring([(ps("pU%d" % i, [128, 512], F32), Buf("pU%d" % i)) for i in range(4)])
            Er = Ring([(sb("E%d" % i, [128, 512], BF16), Buf("E%d" % i)) for i in range(6)])
            us = Ring([(sb("us%d" % i, [128, 512], F32), Buf("us%d" % i)) for i in range(4)])
            ud_b = Buf("UD")
            unit = 0
            for g, dl in enumerate((1, 4, 16)):
                if g not in getattr(self, "b1_groups", (0, 1, 2)):
                    continue
                L = S // dl
                nb = L // 128
                ucurs = {0: None, 1: None}
                for r in range(dl):
                    qT, kT, va, q_b, k_b, v_b, qZ, qz_b = sets[2] if g == 0 else sets[unit % 2]
                    unit += 1
                    rows = self.PROJ.rearrange("(i r) c -> r i c", r=dl)[r]
                    vc = C_VA + g * 256
                    Sc.dma("sync", lambda e, va=va, rows=rows, vc=vc: e.dma_start(out=va[0:64, 0, :, 0:64], in_=rows[0:64, vc:vc + 256].rearrange("k (h d) -> k h d", d=64)), writes=[v_b[0]])
                    for h4 in range(4):
                        Sc.dma("sync", lambda e, va=va, rows=rows, vc=vc, nb=nb, L=L, h4=h4: e.dma_start(out=va[:, 1:nb, h4, 0:64], in_=rows[64:L - 64, vc + h4 * 64:vc + h4 * 64 + 64].rearrange("(j k) d -> k j d", k=128)), writes=v_b[1:nb])
                    Sc.dma("sync", lambda e, va=va, rows=rows, vc=vc, nb=nb, L=L: e.dma_start(out=va[0:64, nb, :, 0:64], in_=rows[L - 64:L, vc:vc + 256].rearrange("k (h d) -> k h d", d=64)), writes=[v_b[nb]])
                    for pp in range(2):
                        qc = C_QA + (g * 4 + 2 * pp) * 64
                        kc = C_KA + (g * 4 + 2 * pp) * 64
                        nbc = min(4, nb)
                        for b4 in range(0, nb, nbc):
                            rs = slice(b4 * 128, (b4 + nbc) * 128)
                            Sc.dma("sync", lambda e, dst=kT[pp], rows=rows, kc=kc, rs=rs: e.dma_start_transpose(out=dst[:, rs], in_=rows[rs, kc:kc + 128]), writes=k_b[pp][b4:b4 + nbc])
                            Sc.dma("sync", lambda e, dst=qT[pp], rows=rows, qc=qc, rs=rs: e.dma_start_transpose(out=dst[:, rs], in_=rows[rs, qc:qc + 128]), writes=q_b[pp][b4:b4 + nbc])
                        for blk in range(nb):
                            sl = slice(blk * 128, (blk + 1) * 128)
                            Sc.op("vector", lambda e, d=qZ[pp][0], s_=qT[pp], sl=sl: e.tensor_copy(out=d[0:64, sl], in_=s_[0:64, sl]), reads=[q_b[pp][blk]], writes=[qz_b[pp][0][blk]])
                            Sc.op("gpsimd", lambda e, d=qZ[pp][1], s_=qT[pp], sl=sl: e.tensor_copy(out=d[64:128, sl], in_=s_[64:128, sl]), reads=[q_b[pp][blk]], writes=[qz_b[pp][1][blk]])
                    for pp in range(2):
                        if getattr(self, "b1_stage", 9) < 1:
                            continue
                        Es = {}
                        ucur = ucurs[pp]
                        for j in range(nb + 1):
                            k0, k1 = max(0, 128 * j - 64), min(L, 128 * j + 64)
                            M = k1 - k0
                            q0, q1 = max(0, 128 * (j - 1)), min(L, 128 * (j + 1))
                            Nq = q1 - q0
                            if j == 0:
                                mk = am[0:64, 512:768]
                            elif j == nb:
                                mk = am[0:64, 768:1024]
                            else:
                                mk = am[:, 0:512]
                            kblks = sorted(set([k0 // 128, (k1 - 1) // 128]))
                            qblks = sorted(set([q0 // 128, (q1 - 1) // 128]))
                            p, p_b = pS.next()
                            for hh in range(2):
                                rd = [k_b[pp][b] for b in kblks] + [qz_b[pp][hh][b] for b in qblks]
                                Sc.op("tensor", lambda e, p=p, kt=kT[pp], qt=qZ[pp][hh], hh=hh, k0=k0, k1=k1, q0=q0, q1=q1, M=M, Nq=Nq:
                                      e.matmul(p[0:M, hh * Nq:(hh + 1) * Nq], lhsT=kt[:, k0:k1], rhs=qt[:, q0:q1], start=True, stop=False),
                                      reads=rd, writes=[p_b])
                                Sc.op("tensor", lambda e, p=p, mk=mk, M=M, Nq=Nq, hh=hh: e.matmul(p[0:M, hh * Nq:(hh + 1) * Nq], lhsT=ident[0:M, 0:M], rhs=mk[:, 0:Nq], start=False, stop=True),
                                      reads=[const_b], writes=[p_b])
                            E, E_b = Er.next()
                            Sc.op("scalar", lambda e, E=E, p=p, M=M, Nq=Nq: e.activation(out=E[0:M, 0:2 * Nq], in_=p[0:M, 0:2 * Nq], func=AF.Exp, scale=0.125), reads=[p_b], writes=[E_b])
                            Es[j] = (E, E_b, M, Nq)
                            if j == 0 or getattr(self, "b1_stage", 9) < 2:
                                continue
                            b = j - 1
                            G = r * nb + b
                            if G % 4 == 0:
                                ucur = ucurs[pp] = [pU.next(), pU.next()]
                            for hh in range(2):
                                (u, u_b) = ucur[hh]
                                col = (G % 4) * 128
                                for n_, jj in enumerate((b, b + 1)):
                                    Ej, Ej_b, Mj, Nqj = Es[jj]
                                    if jj == b:
                                        c = hh * Nqj + (128 if b >= 1 else 0)
                                    else:
                                        c = hh * Nqj
                                    hv = 2 * pp + hh
                                    Sc.op("tensor", lambda e, u=u, va=va, Ej=Ej, jj=jj, hv=hv, Mj=Mj, c=c, col=col, n_=n_:
                                          e.matmul(u[:, col:col + 128], lhsT=va[0:Mj, jj, hv, :], rhs=Ej[0:Mj, c:c + 128], start=(n_ == 0), stop=(n_ == 1)),
                                          reads=[v_b[jj], Ej_b], writes=[u_b])
                            del Es[b]
                            if G % 4 == 3:
                                for hh in range(2):
                                    (u, u_b) = ucur[hh]
                                    o, o_b = us.next()
                                    if hh == 0:
                                        Sc.op("vector", lambda e, o=o, u=u: e.tensor_copy(out=o[:], in_=u[:]), reads=[u_b], writes=[o_b])
                                    else:
                                        Sc.op("scalar", lambda e, o=o, u=u: e.activation(out=o[:], in_=u[:], func=AF.Copy), reads=[u_b], writes=[o_b])
                                    hrow = (g * 4 + 2 * pp + hh) * 128
                                    c0 = (G - 3) * 128
                                    Sc.dma("gpsimd", lambda e, o=o, hrow=hrow, c0=c0: e.dma_start(out=self.UD[hrow:hrow + 128, c0:c0 + 512], in_=o[:]), reads=[o_b], writes=[ud_b])
            Sc.barrier()
            Sc.emit(st)

    def phase_b1c(self):
        nc = self.nc
        with contextlib.ExitStack() as st:
            sb = lambda n, s, d: st.enter_context(nc.sbuf_tensor("b1c_" + n, s, d))
            Sc = Sched(nc, prefix="b1c_")
            CH = 2048
            Ut = [Ring([(sb("U%d_%d" % (g, i), [128, CH], F32), Buf("U")) for i in range(2)]) for g in range(3)]
            Dt = [Ring([(sb("D%d_%d" % (g, i), [128, CH], F32), Buf("D")) for i in range(2)]) for g in range(3)]
            Rr = Ring([(sb("R%d" % i, [128, CH], F32), Buf("R")) for i in range(2)])
            Yr = Ring([(sb("Y%d" % i, [128, CH], BF16), Buf("Y")) for i in range(3)])
            yat_b = Buf("YAT")
            for c2 in range(S // CH):
                for sp in range(2):
                    tiles = []
                    for g, dl in enumerate((1, 4, 16)):
                        L = S // dl
                        il = CH // dl
                        u, u_b = Ut[g].next()
                        d_, d_b = Dt[g].next()
                        for hh in range(2):
                            h = g * 4 + 2 * sp + hh
                            srcU = self.UD[h * 128:h * 128 + 64, :].rearrange("p (r i) -> p r i", r=dl)[:, :, c2 * il:(c2 + 1) * il]
                            srcD = self.UD[h * 128 + 64:h * 128 + 128, :].rearrange("p (r i) -> p r i", r=dl)[:, :, c2 * il:(c2 + 1) * il]
                            Sc.dma("sync", lambda e, u=u, hh=hh, srcU=srcU, dl=dl: e.dma_start(out=u[hh * 64:(hh + 1) * 64, :].rearrange("p (r i) -> p r i", r=dl), in_=srcU), writes=[u_b])
                            Sc.dma("gpsimd", lambda e, d_=d_, hh=hh, srcD=srcD, dl=dl: e.dma_start(out=d_[hh * 64:(hh + 1) * 64, :].rearrange("p (r i) -> p r i", r=dl), in_=srcD), writes=[d_b])
                        tiles.append((u, u_b, d_, d_b, dl))
                    R, R_b = Rr.next()
                    nat = lambda t, dl: t[:, :].rearrange("p (i r) -> p i r", r=dl)
                    res = lambda t, dl: t[:, :].rearrange("p (r i) -> p i r", r=dl)
                    (u0, u0_b, d0, d0_b, _), (u1, u1_b, d1, d1_b, _), (u2, u2_b, d2, d2_b, _) = tiles
                    Sc.op("gpsimd", lambda e, R=R, d0=d0, d1=d1: e.tensor_tensor(out=nat(R, 4), in0=nat(d0, 4), in1=res(d1, 4), op=ALU.add), reads=[d0_b, d1_b], writes=[R_b])
                    Sc.op("gpsimd", lambda e, R=R, d2=d2: e.tensor_tensor(out=nat(R, 16), in0=nat(R, 16), in1=res(d2, 16), op=ALU.add), reads=[d2_b, R_b], writes=[R_b])
                    Sc.op("vector", lambda e, R=R: e.reciprocal(out=R[:, :], in_=R[:, :]), reads=[R_b], writes=[R_b])
                    for g, (u, u_b, d_, d_b, dl) in enumerate(tiles):
                        y, y_b = Yr.next()
                        eng = "vector" if g != 1 else "gpsimd"
                        Sc.op(eng, lambda e, y=y, u=u, dl=dl, R=R: e.tensor_tensor(out=nat(y, dl), in0=res(u, dl), in1=nat(R, dl), op=ALU.mult), reads=[u_b, R_b], writes=[y_b])
                        row = (g * 4 + 2 * sp) * 64
                        Sc.dma("sync", lambda e, y=y, row=row, c2=c2: e.dma_start(out=self.YAT[row:row + 128, c2 * CH:(c2 + 1) * CH], in_=y[:, :]), reads=[y_b], writes=[yat_b])
            Sc.barrier()
            Sc.emit(st)

    def phase_b2(self):
        nc = self.nc
        with contextlib.ExitStack() as st:
            sb = lambda n, s, d: st.enter_context(nc.sbuf_tensor("b2_" + n, s, d))
            ps = lambda n, s, d: st.enter_context(nc.psum_tensor("b2_" + n, s, d))
            Sc = Sched(nc, prefix="b2_")
            cb_ = Buf("const")
            cret = sb("cret", [128, 8, 128], F32)
            dec = sb("dec", [128, 12], F32)
            lg = sb("lg", [128, 12], F32)
            lgp = sb("lgp", [128, 6], F32)
            Mall = sb("Mall", [128, 6, 128], F32)
            tmpm = sb("tmpm", [128, 128], F32)
            zeta = sb("zeta", [128, 12], F32)
            XiF = sb("XiF", [128, 3, 128], F32)
            XiB = sb("XiB", [128, 3, 128], F32)
            g128 = sb("g128", [128, 6], F32)
            gret = sb("gret", [128, 768], F32)
            negh = sb("negh", [128, 6], F32)
            Sc.dma("sync", lambda e: e.dma_start(out=cret[:], in_=self.c_ret[:, :, :]), writes=[cb_])
            Sc.dma("sync", lambda e: e.dma_start(out=dec[:, 0:6], in_=self.dec_f[0, :].partition_broadcast(128)), writes=[cb_])
            Sc.dma("sync", lambda e: e.dma_start(out=dec[:, 6:12], in_=self.dec_b[0, :].partition_broadcast(128)), writes=[cb_])
            Sc.dma("sync", lambda e: e.dma_start(out=gret[:], in_=self.g_ret[0, :].partition_broadcast(128)), writes=[cb_])
            C = lambda eng, fn: Sc.op(eng, fn, reads=[cb_], writes=[cb_])
            C("vector", lambda e: e.memset(negh[:], -0.5))
            C("vector", lambda e: e.tensor_scalar(out=gret[:], in0=gret[:], scalar1=0.5, scalar2=None, op0=ALU.mult))
            C("scalar", lambda e: e.activation(out=lg[:], in_=dec[:], func=AF.Exp, scale=-1.0))
            C("vector", lambda e: e.tensor_scalar(out=lg[:], in0=lg[:], scalar1=1.0, scalar2=None, op0=ALU.add))
            C("scalar", lambda e: e.activation(out=lg[:], in_=lg[:], func=AF.Ln))
            C("vector", lambda e: e.tensor_scalar(out=lg[:], in0=lg[:], scalar1=-1.0, scalar2=None, op0=ALU.mult))
            for dr in range(2):
                for hh in range(2):
                    src = lg[hh * 64:(hh + 1) * 64, dr * 6:(dr + 1) * 6].rearrange("p (a b) -> p a b", b=2)[:, :, hh]
                    C("vector", lambda e, dr=dr, hh=hh, src=src: e.tensor_copy(out=lgp[hh * 64:(hh + 1) * 64, dr * 3:(dr + 1) * 3], in_=src))
            for h in range(6):
                C("scalar", lambda e, h=h: e.activation(out=Mall[:, h, :], in_=cret[:, 0, :], func=AF.Exp, scale=lg[:, h:h + 1]))
                C("vector", lambda e, h=h: e.tensor_tensor(out=Mall[:, h, :], in0=Mall[:, h, :], in1=cret[:, 2, :], op=ALU.mult))
                C("scalar", lambda e, h=h: e.activation(out=tmpm[:], in_=cret[:, 1, :], func=AF.Exp, scale=lg[:, 6 + h:7 + h]))
                C("vector", lambda e, h=h: e.tensor_tensor(out=tmpm[:], in0=tmpm[:], in1=cret[:, 3, :], op=ALU.mult))
                C("vector", lambda e, h=h: e.tensor_tensor(out=Mall[:, h, :], in0=Mall[:, h, :], in1=tmpm[:], op=ALU.add))
                C("scalar", lambda e, h=h: e.activation(out=zeta[:, h:h + 1], in_=cret[:, 6, 0:1], func=AF.Exp, scale=lg[:, h:h + 1]))
                C("scalar", lambda e, h=h: e.activation(out=zeta[:, 6 + h:7 + h], in_=cret[:, 6, 1:2], func=AF.Exp, scale=lg[:, 6 + h:7 + h]))
            C("vector", lambda e: e.tensor_scalar(out=zeta[:], in0=zeta[:], scalar1=0.125, scalar2=None, op0=ALU.mult))
            for pp in range(3):
                C("scalar", lambda e, pp=pp: e.activation(out=XiF[:, pp, :], in_=cret[:, 4, :], func=AF.Exp, scale=lgp[:, pp:pp + 1]))
                C("scalar", lambda e, pp=pp: e.activation(out=XiB[:, pp, :], in_=cret[:, 5, :], func=AF.Exp, scale=lgp[:, 3 + pp:4 + pp]))
            C("scalar", lambda e: e.activation(out=g128[:], in_=lgp[:], func=AF.Exp, scale=128.0))

            Sall = [sb("SallF", [128, NT, 3, 128], BF16), sb("SallB", [128, NT, 3, 128], BF16)]
            sall_b = [[Buf("sf") for _ in range(NT)], [Buf("sb") for _ in range(NT)]]
            Scur = [sb("ScurF", [128, 3, 128], F32), sb("ScurB", [128, 3, 128], F32)]
            scur_b = [Buf("scf"), Buf("scb")]
            kt_r = Ring([(sb("ktok%d" % i, [128, 384], BF16), Buf("ktok")) for i in range(3)])
            vt_r = Ring([(sb("vtok%d" % i, [128, 768], BF16), Buf("vtok")) for i in range(3)])
            kz_r = Ring([(sb("kz%d" % i, [128, 6, 64], BF16), Buf("kz")) for i in range(3)])
            pkv = Ring([(ps("pkv%d" % i, [128, 512], F32), Buf("pkv")) for i in range(2)])
            for dr in range(2):
                Sc.op("vector", lambda e, dr=dr: e.memset(Scur[dr][:], 0.0), writes=[scur_b[dr]])
            for step in range(2 * NT):
                dr = step % 2
                c = (step // 2) if dr == 0 else (NT - 1 - step // 2)
                if True:
                    kt, kt_b = kt_r.next()
                    vt, vt_b = vt_r.next()
                    Sc.dma("sync", lambda e, kt=kt, c=c: e.dma_start(out=kt[:], in_=self.PROJ[c * 128:(c + 1) * 128, C_KR:C_KR + 384]), writes=[kt_b])
                    Sc.dma("sync", lambda e, vt=vt, c=c: e.dma_start(out=vt[:], in_=self.PROJ[c * 128:(c + 1) * 128, C_VR:C_VR + 768]), writes=[vt_b])
                    kz, kz_b = kz_r.next()
                    zb = zeta[:, dr * 6:(dr + 1) * 6].unsqueeze(2).broadcast_to([128, 6, 64])
                    Sc.op("gpsimd", lambda e, kz=kz, kt=kt, zb=zb: e.tensor_tensor(out=kz[:], in0=kt[:, :].rearrange("p (h d) -> p h d", d=64), in1=zb, op=ALU.mult), reads=[kt_b, cb_], writes=[kz_b])
                    p, p_b = pkv.next()
                    for h in range(6):
                        pp, hh = h // 2, h % 2
                        Sc.op("tensor", lambda e, p=p, kz=kz, vt=vt, h=h, pp=pp, hh=hh: e.matmul(p[hh * 64:(hh + 1) * 64, pp * 128:(pp + 1) * 128], lhsT=kz[:, h, :], rhs=vt[:, h * 128:(h + 1) * 128], start=True, stop=True),
                              reads=[kz_b, vt_b], writes=[p_b])
                    Sc.op("scalar", lambda e, dr=dr, c=c: e.activation(out=Sall[dr][:, c, :, :], in_=Scur[dr][:], func=AF.Copy), reads=[scur_b[dr]], writes=[sall_b[dr][c]])
                    for pp in range(3):
                        Sc.op("vector", lambda e, dr=dr, pp=pp, p=p: e.scalar_tensor_tensor(out=Scur[dr][:, pp, :], in0=Scur[dr][:, pp, :], scalar=g128[:, dr * 3 + pp:dr * 3 + pp + 1], in1=p[:, pp * 128:(pp + 1) * 128], op0=ALU.mult, op1=ALU.add),
                              reads=[p_b, scur_b[dr], cb_], writes=[scur_b[dr]])

            qt_r = Ring([(sb("qTp%d" % i, [128, 3, 128], BF16), Buf("qTp")) for i in range(2)])
            ktp_r = Ring([(sb("kTp%d" % i, [128, 3, 128], BF16), Buf("kTp")) for i in range(2)])
            gr_r = Ring([(sb("gr%d" % i, [128, 768], BF16), Buf("gr")) for i in range(2)])
            qz_r = []
            for i in range(2):
                t3 = [sb("qz%d_%d" % (i, j), [128, 6, 128], BF16) for j in range(3)]
                b3 = [Buf("qz") for j in range(3)]
                for j in range(3):
                    Sc.op("gpsimd", lambda e, t=t3[j]: e.memset(t[:], 0.0), writes=[b3[j]])
                qz_r.append((t3, b3))
            qz_r = Ring(qz_r)
            pst = Ring([(ps("pst%d" % i, [128, 512], F32), Buf("pst")) for i in range(2)])
            pyy = Ring([(ps("pyy%d" % i, [128, 512], F32), Buf("pyy")) for i in range(4)])
            A_r = Ring([(sb("A%d" % i, [128, 6, 128], BF16), Buf("A")) for i in range(2)])
            ysb_r = Ring([(sb("ysb%d" % i, [128, 768], F32), Buf("ysb")) for i in range(2)])
            ysq_r = Ring([(sb("ysq%d" % i, [128, 768], F32), Buf("ysq")) for i in range(2)])
            st_r = Ring([(sb("stat%d" % i, [128, 4, 6], F32), Buf("stat")) for i in range(2)])
            yo_r = Ring([(sb("yo%d" % i, [128, 768], BF16), Buf("yo")) for i in range(2)])
            yr_b = Buf("YR")
            def stage1(c):
                qt, qt_b = qt_r.next()
                ktp, ktp_b = ktp_r.next()
                vt, vt_b = vt_r.next()
                gr, gr_b = gr_r.next()
                for pp in range(3):
                    Sc.dma("sync", lambda e, qt=qt, c=c, pp=pp: e.dma_start_transpose(out=qt[:, pp, :], in_=self.PROJ[c * 128:(c + 1) * 128, C_QR + pp * 128:C_QR + (pp + 1) * 128]), writes=[qt_b])
                    Sc.dma("sync", lambda e, ktp=ktp, c=c, pp=pp: e.dma_start_transpose(out=ktp[:, pp, :], in_=self.PROJ[c * 128:(c + 1) * 128, C_KR + pp * 128:C_KR + (pp + 1) * 128]), writes=[ktp_b])
                Sc.dma("sync", lambda e, vt=vt, c=c: e.dma_start(out=vt[:], in_=self.PROJ[c * 128:(c + 1) * 128, C_VR:C_VR + 768]), writes=[vt_b])
                Sc.dma("sync", lambda e, gr=gr, c=c: e.dma_start(out=gr[:], in_=self.PROJ[c * 128:(c + 1) * 128, C_GR:C_GR + 768]), writes=[gr_b])
                (qz, qxf, qxb), (qz_b, qxf_b, qxb_b) = qz_r.next()
                for h in range(6):
                    pp, hh = h // 2, h % 2
                    sl = slice(hh * 64, (hh + 1) * 64)
                    Sc.op("vector", lambda e, qz=qz, qt=qt, h=h, pp=pp, sl=sl: e.tensor_copy(out=qz[sl, h, :], in_=qt[sl, pp, :]), reads=[qt_b], writes=[qz_b])
                    Sc.op("gpsimd", lambda e, qxf=qxf, qt=qt, h=h, pp=pp, sl=sl: e.tensor_tensor(out=qxf[sl, h, :], in0=qt[sl, pp, :], in1=XiF[sl, pp, :], op=ALU.mult), reads=[qt_b, cb_], writes=[qxf_b])
                    Sc.op("vector", lambda e, qxb=qxb, qt=qt, h=h, pp=pp, sl=sl: e.tensor_tensor(out=qxb[sl, h, :], in0=qt[sl, pp, :], in1=XiB[sl, pp, :], op=ALU.mult), reads=[qt_b, cb_], writes=[qxb_b])
                (s0, s0_b), (s1, s1_b) = pst.next(), pst.next()
                for h in range(6):
                    pp = h // 2
                    tgt, tb_ = (s0, s0_b) if h < 4 else (s1, s1_b)
                    col = (h % 4) * 128
                    Sc.op("tensor", lambda e, tgt=tgt, ktp=ktp, qz=qz, h=h, pp=pp, col=col: e.matmul(tgt[:, col:col + 128], lhsT=ktp[:, pp, :], rhs=qz[:, h, :], start=True, stop=True), reads=[ktp_b, qz_b], writes=[tb_])
                A, A_b = A_r.next()
                Sc.op("vector", lambda e, A=A, s0=s0: e.tensor_tensor(out=A[:, 0:4, :], in0=s0[:, :].rearrange("p (h n) -> p h n", n=128), in1=Mall[:, 0:4, :], op=ALU.mult), reads=[s0_b, cb_], writes=[A_b])
                Sc.op("vector", lambda e, A=A, s1=s1: e.tensor_tensor(out=A[:, 4:6, :], in0=s1[:, 0:256].rearrange("p (h n) -> p h n", n=128), in1=Mall[:, 4:6, :], op=ALU.mult), reads=[s1_b, cb_], writes=[A_b])
                return dict(c=c, qxf=qxf, qxb=qxb, qxf_b=qxf_b, qxb_b=qxb_b, A=A, A_b=A_b, vt=vt, vt_b=vt_b, gr=gr, gr_b=gr_b)

            def stage2(cx):
                c, qxf, qxb, qxf_b, qxb_b, A, A_b, vt, vt_b, gr, gr_b = (cx[k] for k in ("c", "qxf", "qxb", "qxf_b", "qxb_b", "A", "A_b", "vt", "vt_b", "gr", "gr_b"))
                (y0, y0_b), (y1, y1_b) = pyy.next(), pyy.next()
                for h in range(6):
                    pp = h // 2
                    tgt, tb_ = (y0, y0_b) if h < 4 else (y1, y1_b)
                    col = (h % 4) * 128
                    Sc.op("tensor", lambda e, tgt=tgt, A=A, vt=vt, h=h, col=col: e.matmul(tgt[:, col:col + 128], lhsT=A[:, h, :], rhs=vt[:, h * 128:(h + 1) * 128], start=True, stop=False), reads=[A_b, vt_b], writes=[tb_])
                    Sc.op("tensor", lambda e, tgt=tgt, qxf=qxf, h=h, pp=pp, c=c, col=col: e.matmul(tgt[:, col:col + 128], lhsT=qxf[:, h, :], rhs=Sall[0][:, c, pp, :], start=False, stop=False), reads=[qxf_b, sall_b[0][c]], writes=[tb_])
                    Sc.op("tensor", lambda e, tgt=tgt, qxb=qxb, h=h, pp=pp, c=c, col=col: e.matmul(tgt[:, col:col + 128], lhsT=qxb[:, h, :], rhs=Sall[1][:, c, pp, :], start=False, stop=True), reads=[qxb_b, sall_b[1][c]], writes=[tb_])
                ysb, ysb_b = ysb_r.next()
                ysq, ysq_b = ysq_r.next()
                stt_, stt_b = st_r.next()
                Sc.op("scalar", lambda e, ysb=ysb, y0=y0: e.activation(out=ysb[:, 0:512], in_=y0[:, :], func=AF.Copy), reads=[y0_b], writes=[ysb_b])
                Sc.op("scalar", lambda e, ysb=ysb, y1=y1: e.activation(out=ysb[:, 512:768], in_=y1[:, 0:256], func=AF.Copy), reads=[y1_b], writes=[ysb_b])
                y3 = ysb[:, :].rearrange("p (h e) -> p h e", e=128)
                Sc.op("gpsimd", lambda e, ysq=ysq, ysb=ysb: e.tensor_tensor(out=ysq[:], in0=ysb[:], in1=ysb[:], op=ALU.mult), reads=[ysb_b], writes=[ysq_b])
                Sc.op("vector", lambda e, stt_=stt_, y3=y3: e.tensor_reduce(out=stt_[:, 0, :], in_=y3, axis=AX.X, op=ALU.add), reads=[ysb_b], writes=[stt_b])
                Sc.op("vector", lambda e, stt_=stt_, ysq=ysq: e.tensor_reduce(out=stt_[:, 1, :], in_=ysq[:, :].rearrange("p (h e) -> p h e", e=128), axis=AX.X, op=ALU.add), reads=[ysq_b], writes=[stt_b])
                Sc.op("gpsimd", lambda e, stt_=stt_: e.tensor_scalar(out=stt_[:, 0, :], in0=stt_[:, 0, :], scalar1=1.0 / 128, scalar2=None, op0=ALU.mult), reads=[stt_b], writes=[stt_b])
                Sc.op("gpsimd", lambda e, stt_=stt_: e.tensor_tensor(out=stt_[:, 2, :], in0=stt_[:, 0, :], in1=stt_[:, 0, :], op=ALU.mult), reads=[stt_b], writes=[stt_b])
                Sc.op("gpsimd", lambda e, stt_=stt_: e.tensor_scalar(out=stt_[:, 1, :], in0=stt_[:, 1, :], scalar1=1.0 / 128, scalar2=EPS, op0=ALU.mult, op1=ALU.add), reads=[stt_b], writes=[stt_b])
                Sc.op("gpsimd", lambda e, stt_=stt_: e.tensor_tensor(out=stt_[:, 1, :], in0=stt_[:, 1, :], in1=stt_[:, 2, :], op=ALU.subtract), reads=[stt_b], writes=[stt_b])
                Sc.op("gpsimd", lambda e, stt_=stt_: e.tensor_tensor(out=stt_[:, 3, :], in0=stt_[:, 1, :], in1=negh[:, :], op=ALU.pow), reads=[stt_b, cb_], writes=[stt_b])
                mb = stt_[:, 0, :].unsqueeze(2).broadcast_to([128, 6, 128])
                rb = stt_[:, 3, :].unsqueeze(2).broadcast_to([128, 6, 128])
                Sc.op("vector", lambda e, y3=y3, mb=mb: e.tensor_tensor(out=y3, in0=y3, in1=mb, op=ALU.subtract), reads=[stt_b, ysb_b], writes=[ysb_b])
                Sc.op("vector", lambda e, y3=y3, rb=rb: e.tensor_tensor(out=y3, in0=y3, in1=rb, op=ALU.mult), reads=[stt_b, ysb_b], writes=[ysb_b])
                Sc.op("gpsimd", lambda e, ysb=ysb: e.tensor_tensor(out=ysb[:], in0=ysb[:], in1=gret[:], op=ALU.mult), reads=[ysb_b, cb_], writes=[ysb_b])
                yo, yo_b = yo_r.next()
                Sc.op("gpsimd", lambda e, yo=yo, ysb=ysb, gr=gr: e.tensor_tensor(out=yo[:], in0=ysb[:], in1=gr[:], op=ALU.mult), reads=[ysb_b, gr_b], writes=[yo_b])
                Sc.dma("sync", lambda e, yo=yo, c=c: e.dma_start(out=self.YR[c * 128:(c + 1) * 128, :], in_=yo[:]), reads=[yo_b], writes=[yr_b])

            prev = None
            for c in range(NT):
                cx = stage1(c)
                if prev is not None:
                    stage2(prev)
                prev = cx
            stage2(prev)
            Sc.barrier()
            Sc.emit(st)

    def phase_b3(self):
        nc = self.nc
        with contextlib.ExitStack() as st:
            sb = lambda n, s, d: st.enter_context(nc.sbuf_tensor("b3_" + n, s, d))
            ps = lambda n, s, d: st.enter_context(nc.psum_tensor("b3_" + n, s, d))
            Sc = Sched(nc, prefix="b3_")
            cb_ = Buf("const")
            identf = sb("identf", [128, 128], F32)
            ident = sb("ident", [128, 128], BF16)
            ones = sb("ones", [128, 128], BF16)
            gmem = sb("gmem", [128, D], F32)
            negh = sb("negh", [128, 1], F32)
            wkv = sb("wkv", [128, 8, 1024], BF16)
            wkv_b = Buf("wkv")
            memT = sb("memT", [128, 8, 256], BF16)
            memT_b = Buf("memT")
            kmT = sb("kmT", [128, 4, 256], BF16)
            vm = sb("vm", [128, 2, 512], BF16)
            kv_b = Buf("kv")
            Sc.dma("sync", lambda e: e.dma_start(out=identf[:], in_=self.c_ident[:, :]), writes=[cb_])
            Sc.dma("sync", lambda e: e.dma_start(out=gmem[:], in_=self.g_mem[0, :].partition_broadcast(128)), writes=[cb_])
            Sc.dma("gpsimd", lambda e: e.dma_start(out=wkv[:], in_=self.w_mem_kv[0, :, :].rearrange("(k p) c -> p k c", p=128)), writes=[wkv_b])
            Sc.op("vector", lambda e: e.tensor_copy(out=ident[:], in_=identf[:]), reads=[cb_], writes=[cb_])
            Sc.op("vector", lambda e: e.memset(ones[:], 1.0), reads=[cb_], writes=[cb_])
            Sc.op("vector", lambda e: e.memset(negh[:], -0.5), reads=[cb_], writes=[cb_])
            mt_r = Ring([(sb("mt%d" % i, [128, D], F32), Buf("mt")) for i in range(2)])
            mg_r = Ring([(sb("mg%d" % i, [128, D], BF16), Buf("mg")) for i in range(2)])
            junk = sb("junk", [128, D], BF16)
            junk_b = Buf("junk")
            mst = sb("mst", [128, 4], F32)
            mst_b = Buf("mst")
            pT = Ring([(ps("pT%d" % i, [128, 1024], BF16), Buf("pT")) for i in range(1)])
            pA = Ring([(ps("pA%d" % i, [128, 512], F32), Buf("pA")) for i in range(7)])
            Sc.op("vector", lambda e: e.memset(mst[:], 0.0), writes=[mst_b])
            for t in range(2):
                m, m_b = mt_r.next()
                Sc.dma("sync", lambda e, m=m, t=t: e.dma_start(out=m[:], in_=self.mem[t * 128:(t + 1) * 128, :]), writes=[m_b])
                Sc.op("scalar", lambda e, m=m, t=t: e.activation(out=junk[:], in_=m[:], func=AF.Square, accum_out=mst[:, t:t + 1]), reads=[m_b, mst_b], writes=[junk_b, mst_b])
                Sc.op("gpsimd", lambda e, t=t: e.tensor_scalar(out=mst[:, t:t + 1], in0=mst[:, t:t + 1], scalar1=1.0 / D, scalar2=EPS, op0=ALU.mult, op1=ALU.add), reads=[mst_b], writes=[mst_b])
                Sc.op("gpsimd", lambda e, t=t: e.tensor_tensor(out=mst[:, 2 + t:3 + t], in0=mst[:, t:t + 1], in1=negh[:, 0:1], op=ALU.pow), reads=[mst_b, cb_], writes=[mst_b])
                g, g_b = mg_r.next()
                Sc.op("vector", lambda e, g=g, m=m, t=t: e.scalar_tensor_tensor(out=g[:], in0=m[:], scalar=mst[:, 2 + t:3 + t], in1=gmem[:], op0=ALU.mult, op1=ALU.mult), reads=[m_b, mst_b, cb_], writes=[g_b])
                p, p_b = pT.next()
                for k in range(8):
                    Sc.op("tensor", lambda e, p=p, g=g, k=k: e.transpose(out=p[:, k * 128:(k + 1) * 128], in_=g[:, k * 128:(k + 1) * 128], identity=ident[:]), reads=[g_b, cb_], writes=[p_b])
                Sc.op("vector", lambda e, p=p, t=t: e.tensor_copy(out=memT[:, :, t * 128:(t + 1) * 128], in_=p[:, :].rearrange("p (k c) -> p k c", k=8)), reads=[p_b], writes=[memT_b])
            for h in range(4):
                p, p_b = pA.next()
                for k in range(8):
                    Sc.op("tensor", lambda e, p=p, h=h, k=k: e.matmul(p[:, 0:256], lhsT=wkv[:, k, h * 128:(h + 1) * 128], rhs=memT[:, k, :], start=(k == 0), stop=(k == 7)), reads=[wkv_b, memT_b], writes=[p_b])
                Sc.op("scalar", lambda e, p=p, h=h: e.activation(out=kmT[:, h, :], in_=p[:, 0:256], func=AF.Copy), reads=[p_b], writes=[kv_b])
            for t in range(2):
                p, p_b = pA.next()
                for k in range(8):
                    Sc.op("tensor", lambda e, p=p, t=t, k=k: e.matmul(p[:, :], lhsT=memT[:, k, t * 128:(t + 1) * 128], rhs=wkv[:, k, 512:1024], start=(k == 0), stop=(k == 7)), reads=[wkv_b, memT_b], writes=[p_b])
                Sc.op("scalar", lambda e, p=p, t=t: e.activation(out=vm[:, t, :], in_=p[:, :], func=AF.Copy), reads=[p_b], writes=[kv_b])
            qm_r = Ring([(sb("qm%d" % i, [128, 512], BF16), Buf("qm")) for i in range(3)])
            E_r = Ring([(sb("E%d" % i, [128, 2, 512], BF16), Buf("E")) for i in range(2)])
            R_r = Ring([(sb("R%d" % i, [128, 512], F32), Buf("R")) for i in range(2)])
            y_r = Ring([(sb("y%d" % i, [128, 512], BF16), Buf("y")) for i in range(3)])
            ymt_b = Buf("YMT")
            sc = 1.0 / float(np.sqrt(128.0))
            for tb in range(NB):
                for h in range(4):
                    q, q_b = qm_r.next()
                    Sc.dma("sync", lambda e, q=q, tb=tb, h=h: e.dma_start_transpose(out=q[:, :], in_=self.PROJ[tb * 512:(tb + 1) * 512, C_QM + h * 128:C_QM + (h + 1) * 128]), writes=[q_b])
                    E, E_b = E_r.next()
                    for mc in range(2):
                        p, p_b = pA.next()
                        Sc.op("tensor", lambda e, p=p, h=h, mc=mc, q=q: e.matmul(p[:, :], lhsT=kmT[:, h, mc * 128:(mc + 1) * 128], rhs=q[:, :], start=True, stop=True), reads=[kv_b, q_b], writes=[p_b])
                        Sc.op("scalar", lambda e, E=E, p=p, mc=mc: e.activation(out=E[:, mc, :], in_=p[:, :], func=AF.Exp, scale=sc), reads=[p_b], writes=[E_b])
                    pu, pu_b = pA.next()
                    pd, pd_b = pA.next()
                    for mc in range(2):
                        Sc.op("tensor", lambda e, pu=pu, E=E, mc=mc, h=h: e.matmul(pu[:, :], lhsT=vm[:, mc, h * 128:(h + 1) * 128], rhs=E[:, mc, :], start=(mc == 0), stop=(mc == 1)), reads=[kv_b, E_b], writes=[pu_b])
                    for mc in range(2):
                        Sc.op("tensor", lambda e, pd=pd, E=E, mc=mc: e.matmul(pd[:, :], lhsT=ones[:, :], rhs=E[:, mc, :], start=(mc == 0), stop=(mc == 1)), reads=[cb_, E_b], writes=[pd_b])
                    R, R_b = R_r.next()
                    Sc.op("vector", lambda e, R=R, pd=pd: e.reciprocal(out=R[:, :], in_=pd[:, :]), reads=[pd_b], writes=[R_b])
                    y, y_b = y_r.next()
                    Sc.op("vector", lambda e, y=y, R=R, pu=pu: e.tensor_tensor(out=y[:, :], in0=pu[:, :], in1=R[:, :], op=ALU.mult), reads=[pu_b, R_b], writes=[y_b])
                    Sc.dma("gpsimd", lambda e, y=y, h=h, tb=tb: e.dma_start(out=self.YMT[h * 128:(h + 1) * 128, tb * 512:(tb + 1) * 512], in_=y[:, :]), reads=[y_b], writes=[ymt_b])
            Sc.barrier()
            Sc.emit(st)

    def phase_c(self):
        nc = self.nc
        with contextlib.ExitStack() as st:
            sb = lambda n, s, d: st.enter_context(nc.sbuf_tensor("pc_" + n, s, d))
            ps = lambda n, s, d: st.enter_context(nc.psum_tensor("pc_" + n, s, d))
            Sc = Sched(nc, prefix="pc_")
            cb_ = Buf("const")
            w_b = Buf("w")
            identf = sb("identf", [128, 128], F32)
            ident = sb("ident", [128, 128], BF16)
            gffn = sb("gffn", [128, D], F32)
            negh = sb("negh", [128, 1], F32)
            wpa = sb("wpa", [128, 6, D], BF16)
            wpr = sb("wpr", [128, 6, D], BF16)
            wpm = sb("wpm", [128, 4, D], BF16)
            wo = sb("wo", [128, 8, D], BF16)
            Sc.dma("sync", lambda e: e.dma_start(out=identf[:], in_=self.c_ident[:, :]), writes=[cb_])
            Sc.dma("sync", lambda e: e.dma_start(out=gffn[:], in_=self.g_ffn[0, :].partition_broadcast(128)), writes=[cb_])
            Sc.dma("gpsimd", lambda e: e.dma_start(out=wpa[:], in_=self.w_pa[0, :, :].rearrange("(k p) c -> p k c", p=128)), writes=[w_b])
            Sc.dma("gpsimd", lambda e: e.dma_start(out=wpr[:], in_=self.w_pr[0, :, :].rearrange("(k p) c -> p k c", p=128)), writes=[w_b])
            Sc.dma("gpsimd", lambda e: e.dma_start(out=wpm[:], in_=self.w_pm[0, :, :].rearrange("(k p) c -> p k c", p=128)), writes=[w_b])
            Sc.dma("gpsimd", lambda e: e.dma_start(out=wo[:], in_=self.w_out[0, :, :].rearrange("(k p) c -> p k c", p=128)), writes=[w_b])
            Sc.op("vector", lambda e: e.tensor_copy(out=ident[:], in_=identf[:]), reads=[cb_], writes=[cb_])
            Sc.op("vector", lambda e: e.memset(negh[:], -0.5), reads=[cb_], writes=[cb_])
            ya_r = Ring([(sb("ya%d" % i, [128, 6, 512], BF16), Buf("ya")) for i in range(2)])
            yr_r = Ring([(sb("yr%d" % i, [128, 6, 512], BF16), Buf("yr")) for i in range(2)])
            ym_r = Ring([(sb("ym%d" % i, [128, 4, 512], BF16), Buf("ym")) for i in range(2)])
            gt_r = Ring([(sb("gt%d" % i, [128, 3, 512], BF16), Buf("gt")) for i in range(3)])
            mg_r = Ring([(sb("mg%d" % i, [128, 8, 512], BF16), Buf("mg")) for i in range(2)])
            m1_r = Ring([(sb("m1_%d" % i, [128, 512], F32), Buf("m1")) for i in range(2)])
            m2_r = Ring([(sb("m2_%d" % i, [128, 512], F32), Buf("m2")) for i in range(2)])
            m3_r = Ring([(sb("m3_%d" % i, [128, 512], F32), Buf("m3")) for i in range(2)])
            x_r = Ring([(sb("x%d" % i, [128, D], F32), Buf("x")) for i in range(2)])
            h_r = Ring([(sb("h%d" % i, [128, D], F32), Buf("h")) for i in range(2)])
            hn_r = Ring([(sb("hn%d" % i, [128, D], BF16), Buf("hn")) for i in range(2)])
            ht_r = Ring([(sb("ht%d" % i, [128, 8, 128], BF16), Buf("ht")) for i in range(2)])
            junk = sb("junk", [128, D], BF16)
            junk_b = Buf("junk")
            stt_ = sb("stat", [128, 3, NT], F32)
            st_b = [Buf("st") for _ in range(NT)]
            Sc.op("vector", lambda e: e.memset(stt_[:], 0.0), writes=st_b)
            pP = Ring([(ps("pP%d" % i, [128, 512], F32), Buf("pP")) for i in range(5)])
            pO = Ring([(ps("pO%d" % i, [128, 512], F32), Buf("pO")) for i in range(2)])
            pT = Ring([(ps("pT%d" % i, [128, 1024], BF16), Buf("pT")) for i in range(1)])
            pend = []
            h_out_b = Buf("H")
            hnt_b = Buf("HNT")
            def load_y(tb):
                cs = slice(tb * 512, (tb + 1) * 512)
                ya, ya_b = ya_r.next()
                yr, yr_b = yr_r.next()
                ym, ym_b = ym_r.next()
                Sc.dma("sync", lambda e, ya=ya, cs=cs: e.dma_start(out=ya[:], in_=self.YAT[:, cs].rearrange("(k p) s -> p k s", p=128)), writes=[ya_b])
                Sc.dma("sync", lambda e, ym=ym, cs=cs: e.dma_start(out=ym[:], in_=self.YMT[:, cs].rearrange("(k p) s -> p k s", p=128)), writes=[ym_b])
                for k in range(6):
                    Sc.dma("sync", lambda e, yr=yr, tb=tb, k=k: e.dma_start_transpose(out=yr[:, k, :], in_=self.YR[tb * 512:(tb + 1) * 512, k * 128:(k + 1) * 128]), writes=[yr_b])
                return (ya, ya_b, yr, yr_b, ym, ym_b)
            ynext = load_y(0)
            for tb in range(NB):
                cs = slice(tb * 512, (tb + 1) * 512)
                ya, ya_b, yr, yr_b, ym, ym_b = ynext
                if tb + 1 < NB:
                    ynext = load_y(tb + 1)
                mg, mg_b = mg_r.next()
                for fc in range(8):
                    gt, gt_b = gt_r.next()
                    Sc.dma("sync", lambda e, gt=gt, fc=fc, cs=cs: e.dma_start(out=gt[:], in_=self.GTT.rearrange("(i f) s -> f i s", i=3)[fc * 128:(fc + 1) * 128, :, cs]), writes=[gt_b])
                    prs = []
                    for (w, src, src_b, nk) in ((wpa, ya, ya_b, 6), (wpr, yr, yr_b, 6), (wpm, ym, ym_b, 4)):
                        p, p_b = pP.next()
                        for k in range(nk):
                            Sc.op("tensor", lambda e, p=p, w=w, src=src, k=k, fc=fc, nk=nk: e.matmul(p[:, :], lhsT=w[:, k, fc * 128:(fc + 1) * 128], rhs=src[:, k, :], start=(k == 0), stop=(k == nk - 1)), reads=[w_b, src_b], writes=[p_b])
                        prs.append((p, p_b))
                    m1, m1_b = m1_r.next()
                    m2, m2_b = m2_r.next()
                    m3, m3_b = m3_r.next()
                    for i, (m, m_b) in enumerate(((m1, m1_b), (m2, m2_b), (m3, m3_b))):
                        p, p_b = prs[i]
                        Sc.op("vector", lambda e, m=m, gt=gt, i=i, p=p: e.scalar_tensor_tensor(out=m[:, :], in0=gt[:, i, :], scalar=1.0, in1=p[:, :], op0=ALU.add, op1=ALU.mult), reads=[gt_b, p_b], writes=[m_b])
                    Sc.op("gpsimd", lambda e, m1=m1, m2=m2: e.tensor_tensor(out=m1[:, :], in0=m1[:, :], in1=m2[:, :], op=ALU.add), reads=[m1_b, m2_b], writes=[m1_b])
                    Sc.op("gpsimd", lambda e, mg=mg, m1=m1, m3=m3, fc=fc: e.tensor_tensor(out=mg[:, fc, :], in0=m1[:, :], in1=m3[:, :], op=ALU.add), reads=[m1_b, m3_b], writes=[mg_b])
                for tt in range(4):
                    t = tb * 4 + tt
                    x, x_b = x_r.next()
                    Sc.dma("sync", lambda e, x=x, t=t: e.dma_start(out=x[:], in_=self.x[t * 128:(t + 1) * 128, :]), writes=[x_b])
                    h, h_b = h_r.next()
                    for nh in range(2):
                        p, p_b = pO.next()
                        for k in range(8):
                            Sc.op("tensor", lambda e, p=p, mg=mg, k=k, tt=tt, nh=nh: e.matmul(p[:, :], lhsT=mg[:, k, tt * 128:(tt + 1) * 128], rhs=wo[:, k, nh * 512:(nh + 1) * 512], start=(k == 0), stop=(k == 7)), reads=[mg_b, w_b], writes=[p_b])
                        Sc.op("vector", lambda e, h=h, p=p, x=x, nh=nh: e.scalar_tensor_tensor(out=h[:, nh * 512:(nh + 1) * 512], in0=p[:, :], scalar=0.5, in1=x[:, nh * 512:(nh + 1) * 512], op0=ALU.mult, op1=ALU.add), reads=[p_b, x_b], writes=[h_b])
                    while pend:
                        pend.pop(0)()
                    Sc.dma("gpsimd", lambda e, h=h, t=t: e.dma_start(out=self.H[t * 128:(t + 1) * 128, :], in_=h[:]), reads=[h_b], writes=[h_out_b])
                    Sc.op("scalar", lambda e, h=h, t=t: e.activation(out=junk[:], in_=h[:], func=AF.Square, accum_out=stt_[:, 0, t:t + 1]), reads=[h_b, st_b[t]], writes=[junk_b, st_b[t]])
                    Sc.op("gpsimd", lambda e, t=t: e.tensor_scalar(out=stt_[:, 1, t:t + 1], in0=stt_[:, 0, t:t + 1], scalar1=1.0 / D, scalar2=EPS, op0=ALU.mult, op1=ALU.add), reads=[st_b[t]], writes=[st_b[t]])
                    Sc.op("gpsimd", lambda e, t=t: e.tensor_tensor(out=stt_[:, 2, t:t + 1], in0=stt_[:, 1, t:t + 1], in1=negh[:, 0:1], op=ALU.pow), reads=[st_b[t], cb_], writes=[st_b[t]])
                    hn, hn_b = hn_r.next()
                    Sc.op("vector", lambda e, hn=hn, h=h, t=t: e.scalar_tensor_tensor(out=hn[:], in0=h[:], scalar=stt_[:, 2, t:t + 1], in1=gffn[:], op0=ALU.mult, op1=ALU.mult), reads=[h_b, st_b[t], cb_], writes=[hn_b])
                    def tail(hn=hn, hn_b=hn_b, t=t):
                        p, p_b = pT.next()
                        for k in range(8):
                            Sc.op("tensor", lambda e, p=p, hn=hn, k=k: e.transpose(out=p[:, k * 128:(k + 1) * 128], in_=hn[:, k * 128:(k + 1) * 128], identity=ident[:]), reads=[hn_b, cb_], writes=[p_b])
                        ht, ht_b = ht_r.next()
                        Sc.op("scalar", lambda e, ht=ht, p=p: e.activation(out=ht[:], in_=p[:, :].rearrange("p (k c) -> p k c", k=8), func=AF.Copy), reads=[p_b], writes=[ht_b])
                        Sc.dma("gpsimd", lambda e, ht=ht, t=t: e.dma_start(out=self.HNT[:, t * 128:(t + 1) * 128].rearrange("(k p) s -> p k s", p=128), in_=ht[:]), reads=[ht_b], writes=[hnt_b])
                    pend.append(tail)
            while pend:
                pend.pop(0)()
            Sc.barrier()
            Sc.emit(st)

    def phase_d(self):
        nc = self.nc
        NF = DFF // 128
        with contextlib.ExitStack() as st:
            sb = lambda n, s, d: st.enter_context(nc.sbuf_tensor("d_" + n, s, d))
            ps = lambda n, s, d: st.enter_context(nc.psum_tensor("d_" + n, s, d))
            Sc = Sched(nc, prefix="d_")
            cb_ = Buf("const")
            hnT = sb("hnT", [128, 8, S], BF16)
            hn_b = [Buf("hnT") for _ in range(NB)]
            for tb in range(NB):
                Sc.dma("sync", lambda e, tb=tb: e.dma_start(out=hnT[:, :, tb * 512:(tb + 1) * 512], in_=self.HNT[:, tb * 512:(tb + 1) * 512].rearrange("(k p) s -> p k s", p=128)), writes=[hn_b[tb]])
            cw = sb("cw", [128, 2, 3, NF], F32)
            cbias = sb("cbias", [128, 2, NF], F32)
            for ab in range(2):
                for j in range(3):
                    Sc.dma("sync", lambda e, ab=ab, j=j: e.dma_start(out=cw[:, ab, j, :], in_=self.conv_w[0, j, ab * DFF:(ab + 1) * DFF].rearrange("(f p) -> p f", p=128), allow_slow_non_contiguous=True), writes=[cb_])
                Sc.dma("sync", lambda e, ab=ab: e.dma_start(out=cbias[:, ab, :], in_=self.conv_b[0, ab * DFF:(ab + 1) * DFF].rearrange("(f p) -> p f", p=128), allow_slow_non_contiguous=True), writes=[cb_])
            w_r = Ring([(sb("w%d" % i, [128, 8, 256], BF16), Buf("w")) for i in range(2)])
            u_r = []
            for i in range(2):
                ua = sb("ua%d" % i, [128, S + 2], F32)
                ub = sb("ub%d" % i, [128, S + 2], F32)
                bl = [Buf("u") for _ in range(NB + 1)]
                for t_ in (ua, ub):
                    Sc.op("gpsimd", lambda e, t_=t_: e.memset(t_[:, 0:1], 0.0), writes=[bl[NB]])
                    Sc.op("gpsimd", lambda e, t_=t_: e.memset(t_[:, S + 1:S + 2], 0.0), writes=[bl[NB]])
                u_r.append(((ua, ub), bl))
            u_r = Ring(u_r)
            ca = sb("ca", [128, S], F32)
            cbb = sb("cb", [128, S], F32)
            th = sb("th", [128, S], F32)
            ca_b, cbb_b, th_b = Buf("ca"), Buf("cb"), Buf("th")
            g_r = Ring([(sb("g%d" % i, [128, S], BF16), Buf("g")) for i in range(2)])
            pA = Ring([(ps("pA%d" % i, [128, 512], F32), Buf("pA")) for i in range(8)])
            gt2_b = Buf("GT2")
            for fc in range(NF):
                w, w_b = w_r.next()
                Sc.dma("gpsimd", lambda e, w=w, fc=fc: e.dma_start(out=w[:, :, 0:128], in_=self.w_up[0, :, fc * 128:(fc + 1) * 128].rearrange("(k p) c -> p k c", p=128)), writes=[w_b])
                Sc.dma("gpsimd", lambda e, w=w, fc=fc: e.dma_start(out=w[:, :, 128:256], in_=self.w_up[0, :, DFF + fc * 128:DFF + (fc + 1) * 128].rearrange("(k p) c -> p k c", p=128)), writes=[w_b])
                (ua, ub), ubl = u_r.next()
                for tb in range(NB):
                    for ab, ut in ((0, ua), (1, ub)):
                        p, p_b = pA.next()
                        for k in range(8):
                            Sc.op("tensor", lambda e, p=p, w=w, k=k, ab=ab, tb=tb: e.matmul(p[:, :], lhsT=w[:, k, ab * 128:(ab + 1) * 128], rhs=hnT[:, k, tb * 512:(tb + 1) * 512], start=(k == 0), stop=(k == 7)), reads=[w_b, hn_b[tb]], writes=[p_b])
                        Sc.op("scalar", lambda e, ut=ut, p=p, tb=tb: e.activation(out=ut[:, 1 + tb * 512:1 + (tb + 1) * 512], in_=p[:, :], func=AF.Copy), reads=[p_b], writes=[ubl[tb]])
                for ab, ut, ct, ct_b, eng in ((0, ua, ca, ca_b, "vector"), (1, ub, cbb, cbb_b, "vector")):
                    Sc.op("scalar", lambda e, ut=ut, ct=ct, ab=ab, fc=fc: e.activation(out=ct[:, :], in_=ut[:, 0:S], func=AF.Identity, bias=cbias[:, ab, fc:fc + 1], scale=cw[:, ab, 0, fc:fc + 1]), reads=ubl + [cb_], writes=[ct_b])
                    Sc.op(eng, lambda e, ut=ut, ct=ct, ab=ab, fc=fc: e.scalar_tensor_tensor(out=ct[:, :], in0=ut[:, 1:S + 1], scalar=cw[:, ab, 1, fc:fc + 1], in1=ct[:, :], op0=ALU.mult, op1=ALU.add), reads=ubl + [cb_, ct_b], writes=[ct_b])
                    Sc.op(eng, lambda e, ut=ut, ct=ct, ab=ab, fc=fc: e.scalar_tensor_tensor(out=ct[:, :], in0=ut[:, 2:S + 2], scalar=cw[:, ab, 2, fc:fc + 1], in1=ct[:, :], op0=ALU.mult, op1=ALU.add), reads=ubl + [cb_, ct_b], writes=[ct_b])
                Sc.op("scalar", lambda e: e.activation(out=th[:, :], in_=ca[:, :], func=AF.Tanh, scale=0.5), reads=[ca_b], writes=[th_b])
                Sc.op("vector", lambda e: e.scalar_tensor_tensor(out=th[:, :], in0=th[:, :], scalar=1.0, in1=ca[:, :], op0=ALU.add, op1=ALU.mult), reads=[ca_b, th_b], writes=[th_b])
                g, g_b = g_r.next()
                Sc.op("vector", lambda e, g=g: e.tensor_tensor(out=g[:, :], in0=th[:, :], in1=cbb[:, :], op=ALU.mult), reads=[th_b, cbb_b], writes=[g_b])
                Sc.dma("sync", lambda e, g=g, fc=fc: e.dma_start(out=self.GT2[fc * 128:(fc + 1) * 128, :], in_=g[:, :]), reads=[g_b], writes=[gt2_b])
            Sc.barrier()
            Sc.emit(st)

    def phase_e(self):
        nc = self.nc
        NF = DFF // 128
        with contextlib.ExitStack() as st:
            sb = lambda n, s, d: st.enter_context(nc.sbuf_tensor("e_" + n, s, d))
            ps = lambda n, s, d: st.enter_context(nc.psum_tensor("e_" + n, s, d))
            Sc = Sched(nc, prefix="e_")
            cb_ = Buf("const")
            w_b = Buf("w")
            wd = sb("wd", [128, NF, D], BF16)
            gfin = sb("gfin", [128, D], F32)
            negh = sb("negh", [128, 1], F32)
            for q4 in range(2):
                Sc.dma("gpsimd", lambda e, q4=q4: e.dma_start(out=wd[:, q4 * 11:(q4 + 1) * 11, :], in_=self.w_down[0, q4 * 11 * 128:(q4 + 1) * 11 * 128, :].rearrange("(k p) c -> p k c", p=128)), writes=[w_b])
            Sc.dma("sync", lambda e: e.dma_start(out=gfin[:], in_=self.g_final.partition_broadcast(128)), writes=[cb_])
            Sc.op("vector", lambda e: e.memset(negh[:], -0.5), reads=[cb_], writes=[cb_])
            g_r = Ring([(sb("g%d" % i, [128, NF, 512], BF16), Buf("g")) for i in range(2)])
            h_r = Ring([(sb("h%d" % i, [128, D], F32), Buf("h")) for i in range(3)])
            o_r = Ring([(sb("o%d" % i, [128, D], F32), Buf("o")) for i in range(2)])
            junk = sb("junk", [128, D], BF16)
            junk_b = Buf("junk")
            stt_ = sb("stat", [128, 3, NT], F32)
            st_b = [Buf("st") for _ in range(NT)]
            Sc.op("vector", lambda e: e.memset(stt_[:], 0.0), writes=st_b)
            pO = Ring([(ps("pO%d" % i, [128, 512], F32), Buf("pO")) for i in range(6)])
            out_b = Buf("out")
            def load_g(tb):
                g, g_b = g_r.next()
                Sc.dma("sync", lambda e, g=g, tb=tb: e.dma_start(out=g[:], in_=self.GT2[:, tb * 512:(tb + 1) * 512].rearrange("(k p) s -> p k s", p=128)), writes=[g_b])
                return g, g_b
            gnext = load_g(0)
            for tb in range(NB):
                g, g_b = gnext
                if tb + 1 < NB:
                    gnext = load_g(tb + 1)
                for tt in range(4):
                    t = tb * 4 + tt
                    h, h_b = h_r.next()
                    Sc.dma("sync", lambda e, h=h, t=t: e.dma_start(out=h[:], in_=self.H[t * 128:(t + 1) * 128, :]), writes=[h_b])
                    for nh in range(2):
                        p, p_b = pO.next()
                        for k in range(NF):
                            Sc.op("tensor", lambda e, p=p, g=g, k=k, tt=tt, nh=nh: e.matmul(p[:, :], lhsT=g[:, k, tt * 128:(tt + 1) * 128], rhs=wd[:, k, nh * 512:(nh + 1) * 512], start=(k == 0), stop=(k == NF - 1)), reads=[g_b, w_b], writes=[p_b])
                        Sc.op("vector", lambda e, h=h, p=p, nh=nh: e.scalar_tensor_tensor(out=h[:, nh * 512:(nh + 1) * 512], in0=p[:, :], scalar=0.5, in1=h[:, nh * 512:(nh + 1) * 512], op0=ALU.mult, op1=ALU.add), reads=[p_b, h_b], writes=[h_b])
                    Sc.op("scalar", lambda e, h=h, t=t: e.activation(out=junk[:], in_=h[:], func=AF.Square, accum_out=stt_[:, 0, t:t + 1]), reads=[h_b, st_b[t]], writes=[junk_b, st_b[t]])
                    Sc.op("gpsimd", lambda e, t=t: e.tensor_scalar(out=stt_[:, 1, t:t + 1], in0=stt_[:, 0, t:t + 1], scalar1=1.0 / D, scalar2=EPS, op0=ALU.mult, op1=ALU.add), reads=[st_b[t]], writes=[st_b[t]])
                    Sc.op("gpsimd", lambda e, t=t: e.tensor_tensor(out=stt_[:, 2, t:t + 1], in0=stt_[:, 1, t:t + 1], in1=negh[:, 0:1], op=ALU.pow), reads=[st_b[t], cb_], writes=[st_b[t]])
                    o, o_b = o_r.next()
                    Sc.op("vector", lambda e, o=o, h=h, t=t: e.scalar_tensor_tensor(out=o[:], in0=h[:], scalar=stt_[:, 2, t:t + 1], in1=gfin[:], op0=ALU.mult, op1=ALU.mult), reads=[h_b, st_b[t], cb_], writes=[o_b])
                    Sc.dma("sync", lambda e, o=o, t=t: e.dma_start(out=self.out[t * 128:(t + 1) * 128, :], in_=o[:]), reads=[o_b], writes=[out_b])
            Sc.barrier()
            Sc.emit(st)

    def build(self):
        for ph in ("a", "b1", "b1c", "b2", "b3", "c", "d", "e"):
            if self.phases is not None and ph not in self.phases:
                continue
            fn = getattr(self, "phase_" + ph, None)
            if fn is not None:
                fn()
            if self.stop_after == ph:
                break
        return self.nc


def host_consts():
    inv = 10000.0 ** (-np.arange(0, 64, 2, dtype=np.float32) / 64.0)
    ang = np.arange(S, dtype=np.float32)[:, None] * inv[None, :].astype(np.float32)
    c = {
        "c_cos": np.cos(ang).astype(np.float32),
        "c_sin": np.sin(ang).astype(np.float32),
        "c_ident": np.eye(128, dtype=np.float32),
    }
    kk = np.arange(128)[:, None]
    qq = np.arange(256)[None, :]
    band = ((qq >= kk) & (qq <= kk + 128))
    m = np.where(band, 0.0, -30000.0).astype(np.float32)
    am = np.zeros((128, 1024), np.float32)
    am[:, 0:256] = m
    am[:, 256:512] = m
    mf = m[64:128, 128:256]
    ml = m[0:64, 0:128]
    am[0:64, 512:640] = mf
    am[0:64, 640:768] = mf
    am[0:64, 768:896] = ml
    am[0:64, 896:1024] = ml
    c["c_amask"] = am
    r = np.zeros((128, 8, 128), np.float32)
    mm = np.arange(128)[:, None].astype(np.float32)
    nn = np.arange(128)[None, :].astype(np.float32)
    r[:, 0, :] = np.maximum(nn - mm, 0)
    r[:, 1, :] = np.maximum(mm - nn, 0)
    r[:, 2, :] = (nn >= mm) * 0.125
    r[:, 3, :] = (mm > nn) * 0.125
    r[:, 4, :] = nn + 1.0
    r[:, 5, :] = 128.0 - nn
    r[:, 6, 0] = 127.0 - np.arange(128)
    r[:, 6, 1] = np.arange(128)
    c["c_ret"] = r
    return c


def make_in_maps(inputs, n_cores=8):
    consts = host_consts()
    maps = []
    for b in range(n_cores):
        m = {"x": np.ascontiguousarray(inputs["x"][b]), "mem": np.ascontiguousarray(inputs["mem"][b])}
        for k, v in inputs.items():
            if k in ("x", "mem"):
                continue
            m[k] = np.ascontiguousarray(v)
        m.update(consts)
        maps.append(m)
    return maps


def kernel(**inputs):
    inputs = {k: np.asarray(v) for k, v in inputs.items()}
    prog = Prog()
    nc = prog.build()
    res = run_bass_kernel_spmd(nc, make_in_maps(inputs), core_ids=list(range(8)))
    return np.stack([r["out"] for r in res.results], axis=0)
```

```python
import contextlib
import numpy as np
import concourse.bass as bass
import concourse.mybir as mybir
from concourse.bass_utils import run_bass_kernel_spmd

F32 = mybir.dt.float32
BF16 = mybir.dt.bfloat16
AF = mybir.ActivationFunctionType
ALU = mybir.AluOpType
AX = mybir.AxisListType

S = 4096
D = 1024
NT = S // 128
NB = S // 512
IN_W = 5120
DFF = 2816
EPS = 1e-6
C_QA, C_KA, C_VA, C_QR, C_KR, C_VR, C_GR, C_QM = 0, 768, 1536, 2304, 2688, 3072, 3840, 4608


class Buf:
    __slots__ = ("name", "w", "r")

    def __init__(self, name):
        self.name = name
        self.w = None
        self.r = []


class Sched:
    ENG = ("tensor", "vector", "scalar", "gpsimd", "sync")

    def __init__(self, nc, n_dma_sems=32, prefix=""):
        self.nc = nc
        self.prefix = prefix
        self.lists = {e: [] for e in self.ENG}
        self.cnt = {e: 0 for e in self.ENG}
        self.known = {e: {} for e in self.ENG}
        self.ndma = n_dma_sems
        self.dma_issued = [0] * n_dma_sems
        self.dma_rr = 0

    def _need(self, eng, ev, waits):
        if ev is None:
            return
        key, val = ev
        if key == eng and eng == "tensor":
            return
        if self.known[eng].get(key, 0) >= val:
            return
        if waits.get(key, 0) < val:
            waits[key] = val

    def _deps(self, eng, reads, writes):
        waits = {}
        for b in reads:
            self._need(eng, b.w, waits)
        for b in writes:
            self._need(eng, b.w, waits)
            for ev in b.r:
                self._need(eng, ev, waits)
        for k, v in waits.items():
            self.known[eng][k] = v
        return list(waits.items())

    def op(self, eng, fn, reads=(), writes=()):
        waits = self._deps(eng, reads, writes)
        self.cnt[eng] += 1
        ev = (eng, self.cnt[eng])
        self.lists[eng].append((waits, fn, eng, 1))
        for b in reads:
            b.r.append(ev)
        for b in writes:
            b.w = ev
            b.r = []
        return ev

    def dma(self, eng, fn, reads=(), writes=()):
        i = self.dma_rr
        self.dma_rr = (self.dma_rr + 1) % self.ndma
        key = ("dma", i)
        waits = dict(self._deps(eng, reads, writes))
        prev = self.dma_issued[i]
        if prev > 0 and self.known[eng].get(key, 0) < prev:
            waits[key] = prev
            self.known[eng][key] = prev
        self.dma_issued[i] = prev + 16
        ev = (key, prev + 16)
        self.lists[eng].append((list(waits.items()), fn, key, 16))
        for b in reads:
            b.r.append(ev)
        for b in writes:
            b.w = ev
            b.r = []
        return ev

    def barrier(self):
        for e in self.ENG:
            waits = {}
            for o in self.ENG:
                if o != e and self.cnt[o] > 0:
                    self._need(e, (o, self.cnt[o]), waits)
            if e != "tensor" and self.cnt[e] > 0:
                self._need(e, (e, self.cnt[e]), waits)
            for i in range(self.ndma):
                if self.dma_issued[i] > 0:
                    self._need(e, (("dma", i), self.dma_issued[i]), waits)
            for k, v in waits.items():
                self.known[e][k] = v
            self.lists[e].append((list(waits.items()), None, None, 0))

    def emit(self, stack):
        nc = self.nc
        semmap = {}
        handles = []
        for e in self.ENG:
            semmap[e] = nc.alloc_semaphore(name=self.prefix + "s_" + e)
            handles.append(semmap[e])
        for i in range(self.ndma):
            semmap[("dma", i)] = nc.alloc_semaphore(name=self.prefix + "s_dma%d" % i)
            handles.append(semmap[("dma", i)])

        def runner(items):
            def f(e):
                for waits, fn, key, inc in items:
                    for k, v in waits:
                        e.wait_ge(semmap[k], v)
                    if fn is not None:
                        fn(e).then_inc(semmap[key], inc)
            return f
        with nc.Block() as block:
            block.tensor(runner(self.lists["tensor"]))
            block.vector(runner(self.lists["vector"]))
            block.scalar(runner(self.lists["scalar"]))
            block.gpsimd(runner(self.lists["gpsimd"]))
            block.sync(runner(self.lists["sync"]))
        nc.clear_and_free_semaphores(handles)
        nc.all_engine_barrier()


class Ring:
    def __init__(self, items):
        self.items = items
        self.i = 0

    def next(self):
        it = self.items[self.i]
        self.i = (self.i + 1) % len(self.items)
        return it


class Prog:
    def __init__(self, debug=False, stop_after=None, phases=None, ext_in=()):
        self.debug = debug
        self.stop_after = stop_after
        self.phases = phases
        self.ext_in = set(ext_in)
        nc = self.nc = bass.Bass("TRN2", target_bir_lowering=False)
        ein = lambda n, s: nc.dram_tensor(n, s, F32, kind="ExternalInput").ap()
        self.x = ein("x", [S, D])
        self.mem = ein("mem", [256, D])
        self.g_mix = ein("g_mix", [1, D])
        self.w_in = ein("w_in", [1, D, IN_W])
        self.w_mem_kv = ein("w_mem_kv", [1, D, 1024])
        self.g_mem = ein("g_mem", [1, D])
        self.dec_f = ein("ret_decay_fwd", [1, 6])
        self.dec_b = ein("ret_decay_bwd", [1, 6])
        self.g_ret = ein("g_ret", [1, 768])
        self.w_pa = ein("w_proj_attn", [1, 768, D])
        self.w_pr = ein("w_proj_ret", [1, 768, D])
        self.w_pm = ein("w_proj_mem", [1, 512, D])
        self.w_gate = ein("w_gate", [1, D, 3 * D])
        self.b_gate = ein("b_gate", [1, 3 * D])
        self.w_out = ein("w_out", [1, D, D])
        self.g_ffn = ein("g_ffn", [1, D])
        self.w_up = ein("w_up", [1, D, 2 * DFF])
        self.conv_w = ein("conv_w", [1, 3, 2 * DFF])
        self.conv_b = ein("conv_b", [1, 2 * DFF])
        self.w_down = ein("w_down", [1, DFF, D])
        self.g_final = ein("g_final", [D])
        self.c_cos = ein("c_cos", [S, 32])
        self.c_sin = ein("c_sin", [S, 32])
        self.c_ident = ein("c_ident", [128, 128])
        self.c_amask = ein("c_amask", [128, 1024])
        self.c_ret = ein("c_ret", [128, 8, 128])
        self.out = nc.dram_tensor("out", [S, D], F32, kind="ExternalOutput").ap()
        kind = "ExternalOutput" if debug else "Internal"
        scr = lambda n, s, d: nc.dram_tensor(n, s, d, kind=("ExternalInput" if n in self.ext_in else kind)).ap()
        self.PROJ = scr("PROJ", [S, IN_W], BF16)
        self.GTT = scr("GTT", [3 * D, S], BF16)
        self.UD = scr("UD", [12 * 128, S], F32)
        self.YAT = scr("YAT", [768, S], BF16)
        self.YR = scr("YR", [S, 768], BF16)
        self.YMT = scr("YMT", [512, S], BF16)
        self.H = scr("H", [S, D], F32)
        self.HNT = scr("HNT", [D, S], BF16)
        self.GT2 = scr("GT2", [DFF, S], BF16)

    def phase_a(self):
        nc = self.nc
        with contextlib.ExitStack() as st:
            sb = lambda n, s, d: st.enter_context(nc.sbuf_tensor("a_" + n, s, d))
            ps = lambda n, s, d: st.enter_context(nc.psum_tensor("a_" + n, s, d))
            Sc = Sched(nc, prefix="a_")
            xT = sb("xT", [128, 8, S], BF16)
            xT_b = [Buf("xT%d" % t) for t in range(NT)]
            xr = Ring([(sb("xr%d" % i, [128, D], F32), Buf("xr%d" % i)) for i in range(3)])
            xg = Ring([(sb("xg%d" % i, [128, D], BF16), Buf("xg%d" % i)) for i in range(2)])
            junk = sb("junk", [128, D], BF16)
            junk_b = Buf("junk")
            gmix = sb("gmix", [128, D], F32)
            gmix_b = Buf("gmix")
            ssq = sb("ssq", [128, NT], F32)
            msq = sb("msq", [128, NT], F32)
            rstd = sb("rstd", [128, NT], F32)
            negh = sb("negh", [128, 1], F32)
            st_b = [Buf("st%d" % t) for t in range(NT)]
            const_b = Buf("const")
            identf = sb("identf", [128, 128], F32)
            ident = sb("ident", [128, 128], BF16)
            cos_t = sb("cos_t", [128, NT, 32], F32)
            sin_t = sb("sin_t", [128, NT, 32], F32)
            hb = sb("hb", [128, 24], F32)
            pT = Ring([(ps("pT%d" % i, [128, 1024], BF16), Buf("pT%d" % i)) for i in range(2)])
            pA = Ring([(ps("pA%d" % i, [128, 512], F32), Buf("pA%d" % i)) for i in range(6)])
            wr = Ring([(sb("w%d" % i, [128, 8, 512], BF16), Buf("w%d" % i)) for i in range(3)])
            ob = Ring([(sb("ob%d" % i, [128, 512], BF16), Buf("ob%d" % i)) for i in range(4)])
            tA = Ring([(sb("tA%d" % i, [128, 512], F32), Buf("tA%d" % i)) for i in range(3)])
            tB = Ring([(sb("tB%d" % i, [128, 512], F32), Buf("tB%d" % i)) for i in range(3)])
            proj_b = Buf("PROJ")
            gtt_b = Buf("GTT")

            Sc.dma("sync", lambda e: e.dma_start(out=gmix[:], in_=self.g_mix[0, :].partition_broadcast(128)), writes=[gmix_b])
            Sc.dma("sync", lambda e: e.dma_start(out=identf[:], in_=self.c_ident[:, :]), writes=[const_b])
            Sc.dma("sync", lambda e: e.dma_start(out=cos_t[:], in_=self.c_cos.rearrange("(t p) c -> p t c", p=128)), writes=[const_b])
            Sc.dma("sync", lambda e: e.dma_start(out=sin_t[:], in_=self.c_sin.rearrange("(t p) c -> p t c", p=128)), writes=[const_b])
            Sc.dma("sync", lambda e: e.dma_start(out=hb[:], in_=self.b_gate[0, :].rearrange("(f p) -> p f", p=128), allow_slow_non_contiguous=True), writes=[const_b])
            Sc.op("vector", lambda e: e.tensor_copy(out=ident[:], in_=identf[:]), reads=[const_b], writes=[const_b])
            Sc.op("vector", lambda e: e.tensor_scalar(out=hb[:], in0=hb[:], scalar1=0.5, scalar2=None, op0=ALU.mult), reads=[const_b], writes=[const_b])
            Sc.op("vector", lambda e: e.memset(ssq[:], 0.0), writes=st_b)
            Sc.op("vector", lambda e: e.memset(negh[:], -0.5), writes=[const_b])

            for t in range(NT):
                xt, xt_b = xr.next()
                Sc.dma("sync", lambda e, xt=xt, t=t: e.dma_start(out=xt[:], in_=self.x[t * 128:(t + 1) * 128, :]), writes=[xt_b])
                Sc.op("scalar", lambda e, xt=xt, t=t: e.activation(out=junk[:], in_=xt[:], func=AF.Square, accum_out=ssq[:, t:t + 1]),
                      reads=[xt_b], writes=[junk_b, st_b[t]])
                Sc.op("gpsimd", lambda e, t=t: e.tensor_scalar(out=msq[:, t:t + 1], in0=ssq[:, t:t + 1], scalar1=1.0 / D, scalar2=EPS, op0=ALU.mult, op1=ALU.add),
                      reads=[st_b[t]], writes=[st_b[t]])
                Sc.op("gpsimd", lambda e, t=t: e.tensor_tensor(out=rstd[:, t:t + 1], in0=msq[:, t:t + 1], in1=negh[:, 0:1], op=ALU.pow),
                      reads=[st_b[t], const_b], writes=[st_b[t]])
                g, g_b = xg.next()
                Sc.op("vector", lambda e, g=g, xt=xt, t=t: e.scalar_tensor_tensor(out=g[:], in0=xt[:], scalar=rstd[:, t:t + 1], in1=gmix[:], op0=ALU.mult, op1=ALU.mult),
                      reads=[xt_b, st_b[t], gmix_b], writes=[g_b])
                p, p_b = pT.next()
                for k in range(8):
                    Sc.op("tensor", lambda e, p=p, g=g, k=k: e.transpose(out=p[:, k * 128:(k + 1) * 128], in_=g[:, k * 128:(k + 1) * 128], identity=ident[:]),
                          reads=[g_b, const_b], writes=[p_b])
                eng = "scalar" if t % 2 == 0 else "vector"
                dst = xT[:, :, t * 128:(t + 1) * 128]
                src = p[:, :].rearrange("p (k c) -> p k c", k=8)
                if eng == "scalar":
                    Sc.op("scalar", lambda e, dst=dst, src=src: e.activation(out=dst, in_=src, func=AF.Copy), reads=[p_b], writes=[xT_b[t]])
                else:
                    Sc.op("vector", lambda e, dst=dst, src=src: e.tensor_copy(out=dst, in_=src), reads=[p_b], writes=[xT_b[t]])

            blocks = [(C_QA, 512, "rot"), (C_QA + 512, 256, "rot"), (C_KA, 512, "rot"), (C_KA + 512, 256, "rot"),
                      (C_VA, 512, "copy"), (C_VA + 512, 256, "copy"), (C_QR, 384, "rot"), (C_KR, 384, "rot"),
                      (C_VR, 512, "copy"), (C_VR + 512, 256, "copy"), (C_GR, 512, "silu2"), (C_GR + 512, 256, "silu2"),
                      (C_QM, 512, "copy")]
            wspecs = [(self.w_in[0, :, c0:c0 + N], N) for (c0, N, kind) in blocks] + [(self.w_gate[0, :, fg * 512:(fg + 1) * 512], 512) for fg in range(6)]
            wloaded = {}

            def ensure_w(i):
                if i < len(wspecs) and i not in wloaded:
                    w, w_b = wr.next()
                    src, N = wspecs[i]
                    Sc.dma("gpsimd", lambda e, w=w, src=src, N=N: e.dma_start(out=w[:, :, 0:N], in_=src.rearrange("(k p) c -> p k c", p=128)), writes=[w_b])
                    wloaded[i] = (w, w_b)
                return wloaded.get(i)
            for bi, (c0, N, kind) in enumerate(blocks):
                w, w_b = ensure_w(bi)
                ensure_w(bi + 1)
                for t in range(NT):
                    p, p_b = pA.next()
                    for k in range(8):
                        Sc.op("tensor", lambda e, p=p, w=w, t=t, k=k, N=N: e.matmul(p[:, 0:N], lhsT=xT[:, k, t * 128:(t + 1) * 128], rhs=w[:, k, 0:N], start=(k == 0), stop=(k == 7)),
                              reads=[xT_b[t], w_b], writes=[p_b])
                    o, o_b = ob.next()
                    if kind == "copy":
                        Sc.op("scalar", lambda e, o=o, p=p, N=N: e.activation(out=o[:, 0:N], in_=p[:, 0:N], func=AF.Copy), reads=[p_b], writes=[o_b])
                    elif kind == "silu2":
                        a, a_b = tA.next()
                        Sc.op("scalar", lambda e, a=a, p=p, N=N: e.activation(out=a[:, 0:N], in_=p[:, 0:N], func=AF.Tanh, scale=0.5), reads=[p_b], writes=[a_b])
                        Sc.op("vector", lambda e, o=o, a=a, p=p, N=N: e.scalar_tensor_tensor(out=o[:, 0:N], in0=a[:, 0:N], scalar=1.0, in1=p[:, 0:N], op0=ALU.add, op1=ALU.mult),
                              reads=[a_b, p_b], writes=[o_b])
                    else:
                        H = N // 64
                        a, a_b = tA.next()
                        b, b_b = tB.next()
                        pv = p[:, 0:N].rearrange("p (h two f) -> p h two f", two=2, f=32)
                        av = a[:, 0:N].rearrange("p (h two f) -> p h two f", two=2, f=32)
                        bv = b[:, 0:N].rearrange("p (h two f) -> p h two f", two=2, f=32)
                        ov = o[:, 0:N].rearrange("p (h two f) -> p h two f", two=2, f=32)
                        cb = cos_t[:, t:t + 1, :].broadcast_to([128, H, 32])
                        sn = sin_t[:, t:t + 1, :].broadcast_to([128, H, 32])
                        x1, x2 = pv[:, :, 0, :], pv[:, :, 1, :]
                        Sc.op("vector", lambda e, av=av, x1=x1, cb=cb: e.tensor_tensor(out=av[:, :, 0, :], in0=x1, in1=cb, op=ALU.mult), reads=[p_b, const_b], writes=[a_b])
                        Sc.op("vector", lambda e, av=av, x2=x2, cb=cb: e.tensor_tensor(out=av[:, :, 1, :], in0=x2, in1=cb, op=ALU.mult), reads=[p_b, const_b], writes=[a_b])
                        Sc.op("vector", lambda e, bv=bv, x2=x2, sn=sn: e.tensor_tensor(out=bv[:, :, 0, :], in0=x2, in1=sn, op=ALU.mult), reads=[p_b, const_b], writes=[b_b])
                        Sc.op("vector", lambda e, bv=bv, x1=x1, sn=sn: e.tensor_tensor(out=bv[:, :, 1, :], in0=x1, in1=sn, op=ALU.mult), reads=[p_b, const_b], writes=[b_b])
                        Sc.op("gpsimd", lambda e, ov=ov, av=av, bv=bv: e.tensor_tensor(out=ov[:, :, 0, :], in0=av[:, :, 0, :], in1=bv[:, :, 0, :], op=ALU.subtract), reads=[a_b, b_b], writes=[o_b])
                        Sc.op("gpsimd", lambda e, ov=ov, av=av, bv=bv: e.tensor_tensor(out=ov[:, :, 1, :], in0=av[:, :, 1, :], in1=bv[:, :, 1, :], op=ALU.add), reads=[a_b, b_b], writes=[o_b])
                    Sc.dma("sync", lambda e, o=o, t=t, c0=c0, N=N: e.dma_start(out=self.PROJ[t * 128:(t + 1) * 128, c0:c0 + N], in_=o[:, 0:N]), reads=[o_b], writes=[proj_b])

            for fg in range(6):
                w, w_b = ensure_w(len(blocks) + fg)
                ensure_w(len(blocks) + fg + 1)
                for j in range(4):
                    fc = fg * 4 + j
                    for tb in range(NB):
                        p, p_b = pA.next()
                        for k in range(8):
                            Sc.op("tensor", lambda e, p=p, w=w, tb=tb, k=k, j=j: e.matmul(p[:, :], lhsT=w[:, k, j * 128:(j + 1) * 128], rhs=xT[:, k, tb * 512:(tb + 1) * 512], start=(k == 0), stop=(k == 7)),
                                  reads=xT_b[tb * 4:tb * 4 + 4] + [w_b], writes=[p_b])
                        o, o_b = ob.next()
                        Sc.op("scalar", lambda e, o=o, p=p, fc=fc: e.activation(out=o[:, :], in_=p[:, :], func=AF.Tanh, bias=hb[:, fc:fc + 1], scale=0.5), reads=[p_b, const_b], writes=[o_b])
                        Sc.dma("sync", lambda e, o=o, fc=fc, tb=tb: e.dma_start(out=self.GTT[fc * 128:(fc + 1) * 128, tb * 512:(tb + 1) * 512], in_=o[:, :]), reads=[o_b], writes=[gtt_b])
            Sc.barrier()
            Sc.emit(st)


    def phase_b1(self):
        nc = self.nc
        with contextlib.ExitStack() as st:
            sb = lambda n, s, d: st.enter_context(nc.sbuf_tensor("b1_" + n, s, d))
            ps = lambda n, s, d: st.enter_context(nc.psum_tensor("b1_" + n, s, d))
            Sc = Sched(nc, prefix="b1_")
            const_b = Buf("const")
            amf = sb("amf", [128, 1024], F32)
            am = sb("am", [128, 1024], BF16)
            identf = sb("identf", [128, 128], F32)
            ident = sb("ident", [128, 128], BF16)
            Sc.dma("sync", lambda e: e.dma_start(out=amf[:], in_=self.c_amask[:, :]), writes=[const_b])
            Sc.dma("sync", lambda e: e.dma_start(out=identf[:], in_=self.c_ident[:, :]), writes=[const_b])
            Sc.op("vector", lambda e: e.tensor_copy(out=am[:], in_=amf[:]), reads=[const_b], writes=[const_b])
            Sc.op("vector", lambda e: e.tensor_copy(out=ident[:], in_=identf[:]), reads=[const_b], writes=[const_b])
            sets = []
            for i in range(3):
                Lc = S if i == 2 else 1024
                nbc = Lc // 128
                qT = [sb("qT%d_%d" % (i, pp), [128, Lc], BF16) for pp in range(2)]
                qZ = [[sb("qZ%d_%d_%d" % (i, pp, hh), [128, Lc], BF16) for hh in range(2)] for pp in range(2)]
                qz_b = [[[Buf("qz") for _ in range(nbc)] for hh in range(2)] for pp in range(2)]
                for pp in range(2):
                    Sc.op("gpsimd", lambda e, t=qZ[pp][0]: e.memset(t[64:128, :], 0.0), writes=qz_b[pp][0])
                    Sc.op("gpsimd", lambda e, t=qZ[pp][1]: e.memset(t[0:64, :], 0.0), writes=qz_b[pp][1])
                kT = [sb("kT%d_%d" % (i, pp), [128, Lc], BF16) for pp in range(2)]
                va = sb("va%d" % i, [128, nbc + 1, 4, 128], BF16)
                q_b = [[Buf("q") for _ in range(nbc)] for pp in range(2)]
                k_b = [[Buf("k") for _ in range(nbc)] for pp in range(2)]
                v_b = [Buf("v") for _ in range(nbc + 1)]
                Sc.op("gpsimd", lambda e, va=va: e.memset(va[:, :, :, 64:128], 1.0), writes=v_b)
                sets.append((qT, kT, va, q_b, k_b, v_b, qZ, qz_b))
            pS = Ring([(ps("pS%d" % i, [128, 512], F32), Buf("pS%d" % i)) for i in range(3)])
            pU = Ring([(ps("pU%d" % i, [128, 512], F32), Buf("pU%d" % i)) for i in range(4)])
            Er = Ring([(sb("E%d" % i, [128, 512], BF16), Buf("E%d" % i)) for i in range(6)])
            us = Ring([(sb("us%d" % i, [128, 512], F32), Buf("us%d" % i)) for i in range(4)])
            ud_b = Buf("UD")
            unit = 0
            for g, dl in enumerate((1, 4, 16)):
                if g not in getattr(self, "b1_groups", (0, 1, 2)):
                    continue
                L = S // dl
                nb = L // 128
                ucurs = {0: None, 1: None}
                for r in range(dl):
                    qT, kT, va, q_b, k_b, v_b, qZ, qz_b = sets[2] if g == 0 else sets[unit % 2]
                    unit += 1
                    rows = self.PROJ.rearrange("(i r) c -> r i c", r=dl)[r]
                    vc = C_VA + g * 256
                    Sc.dma("sync", lambda e, va=va, rows=rows, vc=vc: e.dma_start(out=va[0:64, 0, :, 0:64], in_=rows[0:64, vc:vc + 256].rearrange("k (h d) -> k h d", d=64)), writes=[v_b[0]])
                    for h4 in range(4):
                        Sc.dma("sync", lambda e, va=va, rows=rows, vc=vc, nb=nb, L=L, h4=h4: e.dma_start(out=va[:, 1:nb, h4, 0:64], in_=rows[64:L - 64, vc + h4 * 64:vc + h4 * 64 + 64].rearrange("(j k) d -> k j d", k=128)), writes=v_b[1:nb])
                    Sc.dma("sync", lambda e, va=va, rows=rows, vc=vc, nb=nb, L=L: e.dma_start(out=va[0:64, nb, :, 0:64], in_=rows[L - 64:L, vc:vc + 256].rearrange("k (h d) -> k h d", d=64)), writes=[v_b[nb]])
                    for pp in range(2):
                        qc = C_QA + (g * 4 + 2 * pp) * 64
                        kc = C_KA + (g * 4 + 2 * pp) * 64
                        nbc = min(4, nb)
                        for b4 in range(0, nb, nbc):
                            rs = slice(b4 * 128, (b4 + nbc) * 128)
                            Sc.dma("sync", lambda e, dst=kT[pp], rows=rows, kc=kc, rs=rs: e.dma_start_transpose(out=dst[:, rs], in_=rows[rs, kc:kc + 128]), writes=k_b[pp][b4:b4 + nbc])
                            Sc.dma("sync", lambda e, dst=qT[pp], rows=rows, qc=qc, rs=rs: e.dma_start_transpose(out=dst[:, rs], in_=rows[rs, qc:qc + 128]), writes=q_b[pp][b4:b4 + nbc])
                        for blk in range(nb):
                            sl = slice(blk * 128, (blk + 1) * 128)
                            Sc.op("vector", lambda e, d=qZ[pp][0], s_=qT[pp], sl=sl: e.tensor_copy(out=d[0:64, sl], in_=s_[0:64, sl]), reads=[q_b[pp][blk]], writes=[qz_b[pp][0][blk]])
                            Sc.op("gpsimd", lambda e, d=qZ[pp][1], s_=qT[pp], sl=sl: e.tensor_copy(out=d[64:128, sl], in_=s_[64:128, sl]), reads=[q_b[pp][blk]], writes=[qz_b[pp][1][blk]])
                    for pp in range(2):
                        if getattr(self, "b1_stage", 9) < 1:
                            continue
                        Es = {}
                        ucur = ucurs[pp]
                        for j in range(nb + 1):
                            k0, k1 = max(0, 128 * j - 64), min(L, 128 * j + 64)
                            M = k1 - k0
                            q0, q1 = max(0, 128 * (j - 1)), min(L, 128 * (j + 1))
                            Nq = q1 - q0
                            if j == 0:
                                mk = am[0:64, 512:768]
                            elif j == nb:
                                mk = am[0:64, 768:1024]
                            else:
                                mk = am[:, 0:512]
                            kblks = sorted(set([k0 // 128, (k1 - 1) // 128]))
                            qblks = sorted(set([q0 // 128, (q1 - 1) // 128]))
                            p, p_b = pS.next()
                            for hh in range(2):
                                rd = [k_b[pp][b] for b in kblks] + [qz_b[pp][hh][b] for b in qblks]
                                Sc.op("tensor", lambda e, p=p, kt=kT[pp], qt=qZ[pp][hh], hh=hh, k0=k0, k1=k1, q0=q0, q1=q1, M=M, Nq=Nq:
                                      e.matmul(p[0:M, hh * Nq:(hh + 1) * Nq], lhsT=kt[:, k0:k1], rhs=qt[:, q0:q1], start=True, stop=False),
                                      reads=rd, writes=[p_b])
                                Sc.op("tensor", lambda e, p=p, mk=mk, M=M, Nq=Nq, hh=hh: e.matmul(p[0:M, hh * Nq:(hh + 1) * Nq], lhsT=ident[0:M, 0:M], rhs=mk[:, 0:Nq], start=False, stop=True),
                                      reads=[const_b], writes=[p_b])
                            E, E_b = Er.next()
                            Sc.op("scalar", lambda e, E=E, p=p, M=M, Nq=Nq: e.activation(out=E[0:M, 0:2 * Nq], in_=p[0:M, 0:2 * Nq], func=AF.Exp, scale=0.125), reads=[p_b], writes=[E_b])
                            Es[j] = (E, E_b, M, Nq)
                            if j == 0 or getattr(self, "b1_stage", 9) < 2:
                                continue
                            b = j - 1
                            G = r * nb + b
                            if G % 4 == 0:
                                ucur = ucurs[pp] = [pU.next(), pU.next()]
                            for hh in range(2):
                                (u, u_b) = ucur[hh]
                                col = (G % 4) * 128
                                for n_, jj in enumerate((b, b + 1)):
                                    Ej, Ej_b, Mj, Nqj = Es[jj]
                                    if jj == b:
                                        c = hh * Nqj + (128 if b >= 1 else 0)
                                    else:
                                        c = hh * Nqj
                                    hv = 2 * pp + hh
                                    Sc.op("tensor", lambda e, u=u, va=va, Ej=Ej, jj=jj, hv=hv, Mj=Mj, c=c, col=col, n_=n_:
                                          e.matmul(u[:, col:col + 128], lhsT=va[0:Mj, jj, hv, :], rhs=Ej[0:Mj, c:c + 128], start=(n_ == 0), stop=(n_ == 1)),
                                          reads=[v_b[jj], Ej_b], writes=[u_b])
                            del Es[b]
                            if G % 4 == 3:
                                for hh in range(2):
                                    (u, u_b) = ucur[hh]
                                    o, o_b = us.next()
                                    if hh == 0:
                                        Sc.op("vector", lambda e, o=o, u=u: e.tensor_copy(out=o[:], in_=u[:]), reads=[u_b], writes=[o_b])
                                    else:
                                        Sc.op("scalar", lambda e, o=o, u=u: e.activation(out=o[:], in_=u[:], func=AF.Copy), reads=[u_b], writes=[o_b])
                                    hrow = (g * 4 + 2 * pp + hh) * 128
                                    c0 = (G - 3) * 128
                                    Sc.dma("gpsimd", lambda e, o=o, hrow=hrow, c0=c0: e.dma_start(out=self.UD[hrow:hrow + 128, c0:c0 + 512], in_=o[:]), reads=[o_b], writes=[ud_b])
            Sc.barrier()
            Sc.emit(st)

    def phase_b1c(self):
        nc = self.nc
        with contextlib.ExitStack() as st:
            sb = lambda n, s, d: st.enter_context(nc.sbuf_tensor("b1c_" + n, s, d))
            Sc = Sched(nc, prefix="b1c_")
            CH = 2048
            Ut = [Ring([(sb("U%d_%d" % (g, i), [128, CH], F32), Buf("U")) for i in range(2)]) for g in range(3)]
            Dt = [Ring([(sb("D%d_%d" % (g, i), [128, CH], F32), Buf("D")) for i in range(2)]) for g in range(3)]
            Rr = Ring([(sb("R%d" % i, [128, CH], F32), Buf("R")) for i in range(2)])
            Yr = Ring([(sb("Y%d" % i, [128, CH], BF16), Buf("Y")) for i in range(3)])
            yat_b = Buf("YAT")
            for c2 in range(S // CH):
                for sp in range(2):
                    tiles = []
                    for g, dl in enumerate((1, 4, 16)):
                        L = S // dl
                        il = CH // dl
                        u, u_b = Ut[g].next()
                        d_, d_b = Dt[g].next()
                        for hh in range(2):
                            h = g * 4 + 2 * sp + hh
                            srcU = self.UD[h * 128:h * 128 + 64, :].rearrange("p (r i) -> p r i", r=dl)[:, :, c2 * il:(c2 + 1) * il]
                            srcD = self.UD[h * 128 + 64:h * 128 + 128, :].rearrange("p (r i) -> p r i", r=dl)[:, :, c2 * il:(c2 + 1) * il]
                            Sc.dma("sync", lambda e, u=u, hh=hh, srcU=srcU, dl=dl: e.dma_start(out=u[hh * 64:(hh + 1) * 64, :].rearrange("p (r i) -> p r i", r=dl), in_=srcU), writes=[u_b])
                            Sc.dma("gpsimd", lambda e, d_=d_, hh=hh, srcD=srcD, dl=dl: e.dma_start(out=d_[hh * 64:(hh + 1) * 64, :].rearrange("p (r i) -> p r i", r=dl), in_=srcD), writes=[d_b])
                        tiles.append((u, u_b, d_, d_b, dl))
                    R, R_b = Rr.next()
                    nat = lambda t, dl: t[:, :].rearrange("p (i r) -> p i r", r=dl)
                    res = lambda t, dl: t[:, :].rearrange("p (r i) -> p i r", r=dl)
                    (u0, u0_b, d0, d0_b, _), (u1, u1_b, d1, d1_b, _), (u2, u2_b, d2, d2_b, _) = tiles
                    Sc.op("gpsimd", lambda e, R=R, d0=d0, d1=d1: e.tensor_tensor(out=nat(R, 4), in0=nat(d0, 4), in1=res(d1, 4), op=ALU.add), reads=[d0_b, d1_b], writes=[R_b])
                    Sc.op("gpsimd", lambda e, R=R, d2=d2: e.tensor_tensor(out=nat(R, 16), in0=nat(R, 16), in1=res(d2, 16), op=ALU.add), reads=[d2_b, R_b], writes=[R_b])
                    Sc.op("vector", lambda e, R=R: e.reciprocal(out=R[:, :], in_=R[:, :]), reads=[R_b], writes=[R_b])
                    for g, (u, u_b, d_, d_b, dl) in enumerate(tiles):
                        y, y_b = Yr.next()
                        eng = "vector" if g != 1 else "gpsimd"
                        Sc.op(eng, lambda e, y=y, u=u, dl=dl, R=R: e.tensor_tensor(out=nat(y, dl), in0=res(u, dl), in1=nat(R, dl), op=ALU.mult), reads=[u_b, R_b], writes=[y_b])
                        row = (g * 4 + 2 * sp) * 64
                        Sc.dma("sync", lambda e, y=y, row=row, c2=c2: e.dma_start(out=self.YAT[row:row + 128, c2 * CH:(c2 + 1) * CH], in_=y[:, :]), reads=[y_b], writes=[yat_b])
            Sc.barrier()
            Sc.emit(st)

    def phase_b2(self):
        nc = self.nc
        with contextlib.ExitStack() as st:
            sb = lambda n, s, d: st.enter_context(nc.sbuf_tensor("b2_" + n, s, d))
            ps = lambda n, s, d: st.enter_context(nc.psum_tensor("b2_" + n, s, d))
            Sc = Sched(nc, prefix="b2_")
            cb_ = Buf("const")
            cret = sb("cret", [128, 8, 128], F32)
            dec = sb("dec", [128, 12], F32)
            lg = sb("lg", [128, 12], F32)
            lgp = sb("lgp", [128, 6], F32)
            Mall = sb("Mall", [128, 6, 128], F32)
            tmpm = sb("tmpm", [128, 128], F32)
            zeta = sb("zeta", [128, 12], F32)
            XiF = sb("XiF", [128, 3, 128], F32)
            XiB = sb("XiB", [128, 3, 128], F32)
            g128 = sb("g128", [128, 6], F32)
            gret = sb("gret", [128, 768], F32)
            negh = sb("negh", [128, 6], F32)
            Sc.dma("sync", lambda e: e.dma_start(out=cret[:], in_=self.c_ret[:, :, :]), writes=[cb_])
            Sc.dma("sync", lambda e: e.dma_start(out=dec[:, 0:6], in_=self.dec_f[0, :].partition_broadcast(128)), writes=[cb_])
            Sc.dma("sync", lambda e: e.dma_start(out=dec[:, 6:12], in_=self.dec_b[0, :].partition_broadcast(128)), writes=[cb_])
            Sc.dma("sync", lambda e: e.dma_start(out=gret[:], in_=self.g_ret[0, :].partition_broadcast(128)), writes=[cb_])
            C = lambda eng, fn: Sc.op(eng, fn, reads=[cb_], writes=[cb_])
            C("vector", lambda e: e.memset(negh[:], -0.5))
            C("vector", lambda e: e.tensor_scalar(out=gret[:], in0=gret[:], scalar1=0.5, scalar2=None, op0=ALU.mult))
            C("scalar", lambda e: e.activation(out=lg[:], in_=dec[:], func=AF.Exp, scale=-1.0))
            C("vector", lambda e: e.tensor_scalar(out=lg[:], in0=lg[:], scalar1=1.0, scalar2=None, op0=ALU.add))
            C("scalar", lambda e: e.activation(out=lg[:], in_=lg[:], func=AF.Ln))
            C("vector", lambda e: e.tensor_scalar(out=lg[:], in0=lg[:], scalar1=-1.0, scalar2=None, op0=ALU.mult))
            for dr in range(2):
                for hh in range(2):
                    src = lg[hh * 64:(hh + 1) * 64, dr * 6:(dr + 1) * 6].rearrange("p (a b) -> p a b", b=2)[:, :, hh]
                    C("vector", lambda e, dr=dr, hh=hh, src=src: e.tensor_copy(out=lgp[hh * 64:(hh + 1) * 64, dr * 3:(dr + 1) * 3], in_=src))
            for h in range(6):
                C("scalar", lambda e, h=h: e.activation(out=Mall[:, h, :], in_=cret[:, 0, :], func=AF.Exp, scale=lg[:, h:h + 1]))
                C("vector", lambda e, h=h: e.tensor_tensor(out=Mall[:, h, :], in0=Mall[:, h, :], in1=cret[:, 2, :], op=ALU.mult))
                C("scalar", lambda e, h=h: e.activation(out=tmpm[:], in_=cret[:, 1, :], func=AF.Exp, scale=lg[:, 6 + h:7 + h]))
                C("vector", lambda e, h=h: e.tensor_tensor(out=tmpm[:], in0=tmpm[:], in1=cret[:, 3, :], op=ALU.mult))
                C("vector", lambda e, h=h: e.tensor_tensor(out=Mall[:, h, :], in0=Mall[:, h, :], in1=tmpm[:], op=ALU.add))
                C("scalar", lambda e, h=h: e.activation(out=zeta[:, h:h + 1], in_=cret[:, 6, 0:1], func=AF.Exp, scale=lg[:, h:h + 1]))
                C("scalar", lambda e, h=h: e.activation(out=zeta[:, 6 + h:7 + h], in_=cret[:, 6, 1:2], func=AF.Exp, scale=lg[:, 6 + h:7 + h]))
            C("vector", lambda e: e.tensor_scalar(out=zeta[:], in0=zeta[:], scalar1=0.125, scalar2=None, op0=ALU.mult))
            for pp in range(3):
                C("scalar", lambda e, pp=pp: e.activation(out=XiF[:, pp, :], in_=cret[:, 4, :], func=AF.Exp, scale=lgp[:, pp:pp + 1]))
                C("scalar", lambda e, pp=pp: e.activation(out=XiB[:, pp, :], in_=cret[:, 5, :], func=AF.Exp, scale=lgp[:, 3 + pp:4 + pp]))
            C("scalar", lambda e: e.activation(out=g128[:], in_=lgp[:], func=AF.Exp, scale=128.0))

            Sall = [sb("SallF", [128, NT, 3, 128], BF16), sb("SallB", [128, NT, 3, 128], BF16)]
            sall_b = [[Buf("sf") for _ in range(NT)], [Buf("sb") for _ in range(NT)]]
            Scur = [sb("ScurF", [128, 3, 128], F32), sb("ScurB", [128, 3, 128], F32)]
            scur_b = [Buf("scf"), Buf("scb")]
            kt_r = Ring([(sb("ktok%d" % i, [128, 384], BF16), Buf("ktok")) for i in range(3)])
            vt_r = Ring([(sb("vtok%d" % i, [128, 768], BF16), Buf("vtok")) for i in range(3)])
            kz_r = Ring([(sb("kz%d" % i, [128, 6, 64], BF16), Buf("kz")) for i in range(3)])
            pkv = Ring([(ps("pkv%d" % i, [128, 512], F32), Buf("pkv")) for i in range(2)])
            for dr in range(2):
                Sc.op("vector", lambda e, dr=dr: e.memset(Scur[dr][:], 0.0), writes=[scur_b[dr]])
            for step in range(2 * NT):
                dr = step % 2
                c = (step // 2) if dr == 0 else (NT - 1 - step // 2)
                if True:
                    kt, kt_b = kt_r.next()
                    vt, vt_b = vt_r.next()
                    Sc.dma("sync", lambda e, kt=kt, c=c: e.dma_start(out=kt[:], in_=self.PROJ[c * 128:(c + 1) * 128, C_KR:C_KR + 384]), writes=[kt_b])
                    Sc.dma("sync", lambda e, vt=vt, c=c: e.dma_start(out=vt[:], in_=self.PROJ[c * 128:(c + 1) * 128, C_VR:C_VR + 768]), writes=[vt_b])
                    kz, kz_b = kz_r.next()
                    zb = zeta[:, dr * 6:(dr + 1) * 6].unsqueeze(2).broadcast_to([128, 6, 64])
                    Sc.op("gpsimd", lambda e, kz=kz, kt=kt, zb=zb: e.tensor_tensor(out=kz[:], in0=kt[:, :].rearrange("p (h d) -> p h d", d=64), in1=zb, op=ALU.mult), reads=[kt_b, cb_], writes=[kz_b])
                    p, p_b = pkv.next()
                    for h in range(6):
                        pp, hh = h // 2, h % 2
                        Sc.op("tensor", lambda e, p=p, kz=kz, vt=vt, h=h, pp=pp, hh=hh: e.matmul(p[hh * 64:(hh + 1) * 64, pp * 128:(pp + 1) * 128], lhsT=kz[:, h, :], rhs=vt[:, h * 128:(h + 1) * 128], start=True, stop=True),
                              reads=[kz_b, vt_b], writes=[p_b])
                    Sc.op("scalar", lambda e, dr=dr, c=c: e.activation(out=Sall[dr][:, c, :, :], in_=Scur[dr][:], func=AF.Copy), reads=[scur_b[dr]], writes=[sall_b[dr][c]])
                    for pp in range(3):
                        Sc.op("vector", lambda e, dr=dr, pp=pp, p=p: e.scalar_tensor_tensor(out=Scur[dr][:, pp, :], in0=Scur[dr][:, pp, :], scalar=g128[:, dr * 3 + pp:dr * 3 + pp + 1], in1=p[:, pp * 128:(pp + 1) * 128], op0=ALU.mult, op1=ALU.add),
                              reads=[p_b, scur_b[dr], cb_], writes=[scur_b[dr]])

            qt_r = Ring([(sb("qTp%d" % i, [128, 3, 128], BF16), Buf("qTp")) for i in range(2)])
            ktp_r = Ring([(sb("kTp%d" % i, [128, 3, 128], BF16), Buf("kTp")) for i in range(2)])
            gr_r = Ring([(sb("gr%d" % i, [128, 768], BF16), Buf("gr")) for i in range(2)])
            qz_r = []
            for i in range(2):
                t3 = [sb("qz%d_%d" % (i, j), [128, 6, 128], BF16) for j in range(3)]
                b3 = [Buf("qz") for j in range(3)]
                for j in range(3):
                    Sc.op("gpsimd", lambda e, t=t3[j]: e.memset(t[:], 0.0), writes=[b3[j]])
                qz_r.append((t3, b3))
            qz_r = Ring(qz_r)
            pst = Ring([(ps("pst%d" % i, [128, 512], F32), Buf("pst")) for i in range(2)])
            pyy = Ring([(ps("pyy%d" % i, [128, 512], F32), Buf("pyy")) for i in range(4)])
            A_r = Ring([(sb("A%d" % i, [128, 6, 128], BF16), Buf("A")) for i in range(2)])
            ysb_r = Ring([(sb("ysb%d" % i, [128, 768], F32), Buf("ysb")) for i in range(2)])
            ysq_r = Ring([(sb("ysq%d" % i, [128, 768], F32), Buf("ysq")) for i in range(2)])
            st_r = Ring([(sb("stat%d" % i, [128, 4, 6], F32), Buf("stat")) for i in range(2)])
            yo_r = Ring([(sb("yo%d" % i, [128, 768], BF16), Buf("yo")) for i in range(2)])
            yr_b = Buf("YR")
            def stage1(c):
                qt, qt_b = qt_r.next()
                ktp, ktp_b = ktp_r.next()
                vt, vt_b = vt_r.next()
                gr, gr_b = gr_r.next()
                for pp in range(3):
                    Sc.dma("sync", lambda e, qt=qt, c=c, pp=pp: e.dma_start_transpose(out=qt[:, pp, :], in_=self.PROJ[c * 128:(c + 1) * 128, C_QR + pp * 128:C_QR + (pp + 1) * 128]), writes=[qt_b])
                    Sc.dma("sync", lambda e, ktp=ktp, c=c, pp=pp: e.dma_start_transpose(out=ktp[:, pp, :], in_=self.PROJ[c * 128:(c + 1) * 128, C_KR + pp * 128:C_KR + (pp + 1) * 128]), writes=[ktp_b])
                Sc.dma("sync", lambda e, vt=vt, c=c: e.dma_start(out=vt[:], in_=self.PROJ[c * 128:(c + 1) * 128, C_VR:C_VR + 768]), writes=[vt_b])
                Sc.dma("sync", lambda e, gr=gr, c=c: e.dma_start(out=gr[:], in_=self.PROJ[c * 128:(c + 1) * 128, C_GR:C_GR + 768]), writes=[gr_b])
                (qz, qxf, qxb), (qz_b, qxf_b, qxb_b) = qz_r.next()
                for hh in range(2):
                    sl = slice(hh * 64, (hh + 1) * 64)
                    hv = lambda t, sl=sl, hh=hh: t[sl, :, :].rearrange("p (pp two) n -> p pp two n", two=2)[:, :, hh, :]
                    Sc.op("vector", lambda e, qz=qz, qt=qt, sl=sl, hv=hv: e.tensor_copy(out=hv(qz), in_=qt[sl, :, :]), reads=[qt_b], writes=[qz_b])
                    Sc.op("gpsimd", lambda e, qxf=qxf, qt=qt, sl=sl, hv=hv: e.tensor_tensor(out=hv(qxf), in0=qt[sl, :, :], in1=XiF[sl, :, :], op=ALU.mult), reads=[qt_b, cb_], writes=[qxf_b])
                    Sc.op("vector", lambda e, qxb=qxb, qt=qt, sl=sl, hv=hv: e.tensor_tensor(out=hv(qxb), in0=qt[sl, :, :], in1=XiB[sl, :, :], op=ALU.mult), reads=[qt_b, cb_], writes=[qxb_b])
                (s0, s0_b), (s1, s1_b) = pst.next(), pst.next()
                for h in range(6):
                    pp = h // 2
                    tgt, tb_ = (s0, s0_b) if h < 4 else (s1, s1_b)
                    col = (h % 4) * 128
                    Sc.op("tensor", lambda e, tgt=tgt, ktp=ktp, qz=qz, h=h, pp=pp, col=col: e.matmul(tgt[:, col:col + 128], lhsT=ktp[:, pp, :], rhs=qz[:, h, :], start=True, stop=True), reads=[ktp_b, qz_b], writes=[tb_])
                A, A_b = A_r.next()
                Sc.op("vector", lambda e, A=A, s0=s0: e.tensor_tensor(out=A[:, 0:4, :], in0=s0[:, :].rearrange("p (h n) -> p h n", n=128), in1=Mall[:, 0:4, :], op=ALU.mult), reads=[s0_b, cb_], writes=[A_b])
                Sc.op("vector", lambda e, A=A, s1=s1: e.tensor_tensor(out=A[:, 4:6, :], in0=s1[:, 0:256].rearrange("p (h n) -> p h n", n=128), in1=Mall[:, 4:6, :], op=ALU.mult), reads=[s1_b, cb_], writes=[A_b])
                return dict(c=c, qxf=qxf, qxb=qxb, qxf_b=qxf_b, qxb_b=qxb_b, A=A, A_b=A_b, vt=vt, vt_b=vt_b, gr=gr, gr_b=gr_b)

            def stage2(cx):
                c, qxf, qxb, qxf_b, qxb_b, A, A_b, vt, vt_b, gr, gr_b = (cx[k] for k in ("c", "qxf", "qxb", "qxf_b", "qxb_b", "A", "A_b", "vt", "vt_b", "gr", "gr_b"))
                (y0, y0_b), (y1, y1_b) = pyy.next(), pyy.next()
                for h in range(6):
                    pp = h // 2
                    tgt, tb_ = (y0, y0_b) if h < 4 else (y1, y1_b)
                    col = (h % 4) * 128
                    Sc.op("tensor", lambda e, tgt=tgt, A=A, vt=vt, h=h, col=col: e.matmul(tgt[:, col:col + 128], lhsT=A[:, h, :], rhs=vt[:, h * 128:(h + 1) * 128], start=True, stop=False), reads=[A_b, vt_b], writes=[tb_])
                    Sc.op("tensor", lambda e, tgt=tgt, qxf=qxf, h=h, pp=pp, c=c, col=col: e.matmul(tgt[:, col:col + 128], lhsT=qxf[:, h, :], rhs=Sall[0][:, c, pp, :], start=False, stop=False), reads=[qxf_b, sall_b[0][c]], writes=[tb_])
                    Sc.op("tensor", lambda e, tgt=tgt, qxb=qxb, h=h, pp=pp, c=c, col=col: e.matmul(tgt[:, col:col + 128], lhsT=qxb[:, h, :], rhs=Sall[1][:, c, pp, :], start=False, stop=True), reads=[qxb_b, sall_b[1][c]], writes=[tb_])
                ysb, ysb_b = ysb_r.next()
                ysq, ysq_b = ysq_r.next()
                stt_, stt_b = st_r.next()
                Sc.op("scalar", lambda e, ysb=ysb, y0=y0: e.activation(out=ysb[:, 0:512], in_=y0[:, :], func=AF.Copy), reads=[y0_b], writes=[ysb_b])
                Sc.op("scalar", lambda e, ysb=ysb, y1=y1: e.activation(out=ysb[:, 512:768], in_=y1[:, 0:256], func=AF.Copy), reads=[y1_b], writes=[ysb_b])
                y3 = ysb[:, :].rearrange("p (h e) -> p h e", e=128)
                Sc.op("gpsimd", lambda e, ysq=ysq, ysb=ysb: e.tensor_tensor(out=ysq[:], in0=ysb[:], in1=ysb[:], op=ALU.mult), reads=[ysb_b], writes=[ysq_b])
                Sc.op("vector", lambda e, stt_=stt_, y3=y3: e.tensor_reduce(out=stt_[:, 0, :], in_=y3, axis=AX.X, op=ALU.add), reads=[ysb_b], writes=[stt_b])
                Sc.op("vector", lambda e, stt_=stt_, ysq=ysq: e.tensor_reduce(out=stt_[:, 1, :], in_=ysq[:, :].rearrange("p (h e) -> p h e", e=128), axis=AX.X, op=ALU.add), reads=[ysq_b], writes=[stt_b])
                Sc.op("gpsimd", lambda e, stt_=stt_: e.tensor_scalar(out=stt_[:, 0, :], in0=stt_[:, 0, :], scalar1=1.0 / 128, scalar2=None, op0=ALU.mult), reads=[stt_b], writes=[stt_b])
                Sc.op("gpsimd", lambda e, stt_=stt_: e.tensor_tensor(out=stt_[:, 2, :], in0=stt_[:, 0, :], in1=stt_[:, 0, :], op=ALU.mult), reads=[stt_b], writes=[stt_b])
                Sc.op("gpsimd", lambda e, stt_=stt_: e.tensor_scalar(out=stt_[:, 1, :], in0=stt_[:, 1, :], scalar1=1.0 / 128, scalar2=EPS, op0=ALU.mult, op1=ALU.add), reads=[stt_b], writes=[stt_b])
                Sc.op("gpsimd", lambda e, stt_=stt_: e.tensor_tensor(out=stt_[:, 1, :], in0=stt_[:, 1, :], in1=stt_[:, 2, :], op=ALU.subtract), reads=[stt_b], writes=[stt_b])
                Sc.op("gpsimd", lambda e, stt_=stt_: e.tensor_tensor(out=stt_[:, 3, :], in0=stt_[:, 1, :], in1=negh[:, :], op=ALU.pow), reads=[stt_b, cb_], writes=[stt_b])
                mb = stt_[:, 0, :].unsqueeze(2).broadcast_to([128, 6, 128])
                rb = stt_[:, 3, :].unsqueeze(2).broadcast_to([128, 6, 128])
                Sc.op("vector", lambda e, y3=y3, mb=mb: e.tensor_tensor(out=y3, in0=y3, in1=mb, op=ALU.subtract), reads=[stt_b, ysb_b], writes=[ysb_b])
                Sc.op("vector", lambda e, y3=y3, rb=rb: e.tensor_tensor(out=y3, in0=y3, in1=rb, op=ALU.mult), reads=[stt_b, ysb_b], writes=[ysb_b])
                Sc.op("gpsimd", lambda e, ysb=ysb: e.tensor_tensor(out=ysb[:], in0=ysb[:], in1=gret[:], op=ALU.mult), reads=[ysb_b, cb_], writes=[ysb_b])
                yo, yo_b = yo_r.next()
                Sc.op("gpsimd", lambda e, yo=yo, ysb=ysb, gr=gr: e.tensor_tensor(out=yo[:], in0=ysb[:], in1=gr[:], op=ALU.mult), reads=[ysb_b, gr_b], writes=[yo_b])
                Sc.dma("sync", lambda e, yo=yo, c=c: e.dma_start(out=self.YR[c * 128:(c + 1) * 128, :], in_=yo[:]), reads=[yo_b], writes=[yr_b])

            prev = None
            for c in range(NT):
                cx = stage1(c)
                if prev is not None:
                    stage2(prev)
                prev = cx
            stage2(prev)
            Sc.barrier()
            Sc.emit(st)

    def phase_b3(self):
        nc = self.nc
        with contextlib.ExitStack() as st:
            sb = lambda n, s, d: st.enter_context(nc.sbuf_tensor("b3_" + n, s, d))
            ps = lambda n, s, d: st.enter_context(nc.psum_tensor("b3_" + n, s, d))
            Sc = Sched(nc, prefix="b3_")
            cb_ = Buf("const")
            identf = sb("identf", [128, 128], F32)
            ident = sb("ident", [128, 128], BF16)
            ones = sb("ones", [128, 128], BF16)
            gmem = sb("gmem", [128, D], F32)
            negh = sb("negh", [128, 1], F32)
            wkv = sb("wkv", [128, 8, 1024], BF16)
            wkv_b = Buf("wkv")
            memT = sb("memT", [128, 8, 256], BF16)
            memT_b = Buf("memT")
            kmT = sb("kmT", [128, 4, 256], BF16)
            vm = sb("vm", [128, 2, 512], BF16)
            kv_b = Buf("kv")
            Sc.dma("sync", lambda e: e.dma_start(out=identf[:], in_=self.c_ident[:, :]), writes=[cb_])
            Sc.dma("sync", lambda e: e.dma_start(out=gmem[:], in_=self.g_mem[0, :].partition_broadcast(128)), writes=[cb_])
            Sc.dma("gpsimd", lambda e: e.dma_start(out=wkv[:], in_=self.w_mem_kv[0, :, :].rearrange("(k p) c -> p k c", p=128)), writes=[wkv_b])
            Sc.op("vector", lambda e: e.tensor_copy(out=ident[:], in_=identf[:]), reads=[cb_], writes=[cb_])
            Sc.op("vector", lambda e: e.memset(ones[:], 1.0), reads=[cb_], writes=[cb_])
            Sc.op("vector", lambda e: e.memset(negh[:], -0.5), reads=[cb_], writes=[cb_])
            mt_r = Ring([(sb("mt%d" % i, [128, D], F32), Buf("mt")) for i in range(2)])
            mg_r = Ring([(sb("mg%d" % i, [128, D], BF16), Buf("mg")) for i in range(2)])
            junk = sb("junk", [128, D], BF16)
            junk_b = Buf("junk")
            mst = sb("mst", [128, 4], F32)
            mst_b = Buf("mst")
            pT = Ring([(ps("pT%d" % i, [128, 1024], BF16), Buf("pT")) for i in range(1)])
            pA = Ring([(ps("pA%d" % i, [128, 512], F32), Buf("pA")) for i in range(7)])
            Sc.op("vector", lambda e: e.memset(mst[:], 0.0), writes=[mst_b])
            for t in range(2):
                m, m_b = mt_r.next()
                Sc.dma("sync", lambda e, m=m, t=t: e.dma_start(out=m[:], in_=self.mem[t * 128:(t + 1) * 128, :]), writes=[m_b])
                Sc.op("scalar", lambda e, m=m, t=t: e.activation(out=junk[:], in_=m[:], func=AF.Square, accum_out=mst[:, t:t + 1]), reads=[m_b, mst_b], writes=[junk_b, mst_b])
                Sc.op("gpsimd", lambda e, t=t: e.tensor_scalar(out=mst[:, t:t + 1], in0=mst[:, t:t + 1], scalar1=1.0 / D, scalar2=EPS, op0=ALU.mult, op1=ALU.add), reads=[mst_b], writes=[mst_b])
                Sc.op("gpsimd", lambda e, t=t: e.tensor_tensor(out=mst[:, 2 + t:3 + t], in0=mst[:, t:t + 1], in1=negh[:, 0:1], op=ALU.pow), reads=[mst_b, cb_], writes=[mst_b])
                g, g_b = mg_r.next()
                Sc.op("vector", lambda e, g=g, m=m, t=t: e.scalar_tensor_tensor(out=g[:], in0=m[:], scalar=mst[:, 2 + t:3 + t], in1=gmem[:], op0=ALU.mult, op1=ALU.mult), reads=[m_b, mst_b, cb_], writes=[g_b])
                p, p_b = pT.next()
                for k in range(8):
                    Sc.op("tensor", lambda e, p=p, g=g, k=k: e.transpose(out=p[:, k * 128:(k + 1) * 128], in_=g[:, k * 128:(k + 1) * 128], identity=ident[:]), reads=[g_b, cb_], writes=[p_b])
                Sc.op("vector", lambda e, p=p, t=t: e.tensor_copy(out=memT[:, :, t * 128:(t + 1) * 128], in_=p[:, :].rearrange("p (k c) -> p k c", k=8)), reads=[p_b], writes=[memT_b])
            for h in range(4):
                p, p_b = pA.next()
                for k in range(8):
                    Sc.op("tensor", lambda e, p=p, h=h, k=k: e.matmul(p[:, 0:256], lhsT=wkv[:, k, h * 128:(h + 1) * 128], rhs=memT[:, k, :], start=(k == 0), stop=(k == 7)), reads=[wkv_b, memT_b], writes=[p_b])
                Sc.op("scalar", lambda e, p=p, h=h: e.activation(out=kmT[:, h, :], in_=p[:, 0:256], func=AF.Copy), reads=[p_b], writes=[kv_b])
            for t in range(2):
                p, p_b = pA.next()
                for k in range(8):
                    Sc.op("tensor", lambda e, p=p, t=t, k=k: e.matmul(p[:, :], lhsT=memT[:, k, t * 128:(t + 1) * 128], rhs=wkv[:, k, 512:1024], start=(k == 0), stop=(k == 7)), reads=[wkv_b, memT_b], writes=[p_b])
                Sc.op("scalar", lambda e, p=p, t=t: e.activation(out=vm[:, t, :], in_=p[:, :], func=AF.Copy), reads=[p_b], writes=[kv_b])
            qm_r = Ring([(sb("qm%d" % i, [128, 512], BF16), Buf("qm")) for i in range(3)])
            E_r = Ring([(sb("E%d" % i, [128, 2, 512], BF16), Buf("E")) for i in range(2)])
            R_r = Ring([(sb("R%d" % i, [128, 512], F32), Buf("R")) for i in range(2)])
            y_r = Ring([(sb("y%d" % i, [128, 512], BF16), Buf("y")) for i in range(3)])
            ymt_b = Buf("YMT")
            sc = 1.0 / float(np.sqrt(128.0))
            for tb in range(NB):
                for h in range(4):
                    q, q_b = qm_r.next()
                    Sc.dma("sync", lambda e, q=q, tb=tb, h=h: e.dma_start_transpose(out=q[:, :], in_=self.PROJ[tb * 512:(tb + 1) * 512, C_QM + h * 128:C_QM + (h + 1) * 128]), writes=[q_b])
                    E, E_b = E_r.next()
                    for mc in range(2):
                        p, p_b = pA.next()
                        Sc.op("tensor", lambda e, p=p, h=h, mc=mc, q=q: e.matmul(p[:, :], lhsT=kmT[:, h, mc * 128:(mc + 1) * 128], rhs=q[:, :], start=True, stop=True), reads=[kv_b, q_b], writes=[p_b])
                        Sc.op("scalar", lambda e, E=E, p=p, mc=mc: e.activation(out=E[:, mc, :], in_=p[:, :], func=AF.Exp, scale=sc), reads=[p_b], writes=[E_b])
                    pu, pu_b = pA.next()
                    pd, pd_b = pA.next()
                    for mc in range(2):
                        Sc.op("tensor", lambda e, pu=pu, E=E, mc=mc, h=h: e.matmul(pu[:, :], lhsT=vm[:, mc, h * 128:(h + 1) * 128], rhs=E[:, mc, :], start=(mc == 0), stop=(mc == 1)), reads=[kv_b, E_b], writes=[pu_b])
                    for mc in range(2):
                        Sc.op("tensor", lambda e, pd=pd, E=E, mc=mc: e.matmul(pd[:, :], lhsT=ones[:, :], rhs=E[:, mc, :], start=(mc == 0), stop=(mc == 1)), reads=[cb_, E_b], writes=[pd_b])
                    R, R_b = R_r.next()
                    Sc.op("vector", lambda e, R=R, pd=pd: e.reciprocal(out=R[:, :], in_=pd[:, :]), reads=[pd_b], writes=[R_b])
                    y, y_b = y_r.next()
                    Sc.op("vector", lambda e, y=y, R=R, pu=pu: e.tensor_tensor(out=y[:, :], in0=pu[:, :], in1=R[:, :], op=ALU.mult), reads=[pu_b, R_b], writes=[y_b])
                    Sc.dma("gpsimd", lambda e, y=y, h=h, tb=tb: e.dma_start(out=self.YMT[h * 128:(h + 1) * 128, tb * 512:(tb + 1) * 512], in_=y[:, :]), reads=[y_b], writes=[ymt_b])
            Sc.barrier()
            Sc.emit(st)

    def phase_c(self):
        nc = self.nc
        with contextlib.ExitStack() as st:
            sb = lambda n, s, d: st.enter_context(nc.sbuf_tensor("pc_" + n, s, d))
            ps = lambda n, s, d: st.enter_context(nc.psum_tensor("pc_" + n, s, d))
            Sc = Sched(nc, prefix="pc_")
            cb_ = Buf("const")
            w_b = Buf("w")
            identf = sb("identf", [128, 128], F32)
            ident = sb("ident", [128, 128], BF16)
            gffn = sb("gffn", [128, D], F32)
            negh = sb("negh", [128, 1], F32)
            wpa = sb("wpa", [128, 6, D], BF16)
            wpr = sb("wpr", [128, 6, D], BF16)
            wpm = sb("wpm", [128, 4, D], BF16)
            wo = sb("wo", [128, 8, D], BF16)
            Sc.dma("sync", lambda e: e.dma_start(out=identf[:], in_=self.c_ident[:, :]), writes=[cb_])
            Sc.dma("sync", lambda e: e.dma_start(out=gffn[:], in_=self.g_ffn[0, :].partition_broadcast(128)), writes=[cb_])
            Sc.dma("gpsimd", lambda e: e.dma_start(out=wpa[:], in_=self.w_pa[0, :, :].rearrange("(k p) c -> p k c", p=128)), writes=[w_b])
            Sc.dma("gpsimd", lambda e: e.dma_start(out=wpr[:], in_=self.w_pr[0, :, :].rearrange("(k p) c -> p k c", p=128)), writes=[w_b])
            Sc.dma("gpsimd", lambda e: e.dma_start(out=wpm[:], in_=self.w_pm[0, :, :].rearrange("(k p) c -> p k c", p=128)), writes=[w_b])
            Sc.dma("gpsimd", lambda e: e.dma_start(out=wo[:], in_=self.w_out[0, :, :].rearrange("(k p) c -> p k c", p=128)), writes=[w_b])
            Sc.op("vector", lambda e: e.tensor_copy(out=ident[:], in_=identf[:]), reads=[cb_], writes=[cb_])
            Sc.op("vector", lambda e: e.memset(negh[:], -0.5), reads=[cb_], writes=[cb_])
            ya_r = Ring([(sb("ya%d" % i, [128, 6, 512], BF16), Buf("ya")) for i in range(2)])
            yr_r = Ring([(sb("yr%d" % i, [128, 6, 512], BF16), Buf("yr")) for i in range(2)])
            ym_r = Ring([(sb("ym%d" % i, [128, 4, 512], BF16), Buf("ym")) for i in range(2)])
            gt_r = Ring([(sb("gt%d" % i, [128, 3, 512], BF16), Buf("gt")) for i in range(3)])
            mg_r = Ring([(sb("mg%d" % i, [128, 8, 512], BF16), Buf("mg")) for i in range(2)])
            m1_r = Ring([(sb("m1_%d" % i, [128, 512], F32), Buf("m1")) for i in range(2)])
            m2_r = Ring([(sb("m2_%d" % i, [128, 512], F32), Buf("m2")) for i in range(2)])
            m3_r = Ring([(sb("m3_%d" % i, [128, 512], F32), Buf("m3")) for i in range(2)])
            x_r = Ring([(sb("x%d" % i, [128, D], F32), Buf("x")) for i in range(2)])
            h_r = Ring([(sb("h%d" % i, [128, D], F32), Buf("h")) for i in range(2)])
            hn_r = Ring([(sb("hn%d" % i, [128, D], BF16), Buf("hn")) for i in range(2)])
            ht_r = Ring([(sb("ht%d" % i, [128, 8, 128], BF16), Buf("ht")) for i in range(2)])
            junk = sb("junk", [128, D], BF16)
            junk_b = Buf("junk")
            stt_ = sb("stat", [128, 3, NT], F32)
            st_b = [Buf("st") for _ in range(NT)]
            Sc.op("vector", lambda e: e.memset(stt_[:], 0.0), writes=st_b)
            pP = Ring([(ps("pP%d" % i, [128, 512], F32), Buf("pP")) for i in range(5)])
            pO = Ring([(ps("pO%d" % i, [128, 512], F32), Buf("pO")) for i in range(2)])
            pT = Ring([(ps("pT%d" % i, [128, 1024], BF16), Buf("pT")) for i in range(1)])
            pend = []
            h_out_b = Buf("H")
            hnt_b = Buf("HNT")
            def load_y(tb):
                cs = slice(tb * 512, (tb + 1) * 512)
                ya, ya_b = ya_r.next()
                yr, yr_b = yr_r.next()
                ym, ym_b = ym_r.next()
                Sc.dma("sync", lambda e, ya=ya, cs=cs: e.dma_start(out=ya[:], in_=self.YAT[:, cs].rearrange("(k p) s -> p k s", p=128)), writes=[ya_b])
                Sc.dma("sync", lambda e, ym=ym, cs=cs: e.dma_start(out=ym[:], in_=self.YMT[:, cs].rearrange("(k p) s -> p k s", p=128)), writes=[ym_b])
                for k in range(6):
                    Sc.dma("sync", lambda e, yr=yr, tb=tb, k=k: e.dma_start_transpose(out=yr[:, k, :], in_=self.YR[tb * 512:(tb + 1) * 512, k * 128:(k + 1) * 128]), writes=[yr_b])
                return (ya, ya_b, yr, yr_b, ym, ym_b)
            ynext = load_y(0)
            for tb in range(NB):
                cs = slice(tb * 512, (tb + 1) * 512)
                ya, ya_b, yr, yr_b, ym, ym_b = ynext
                if tb + 1 < NB:
                    ynext = load_y(tb + 1)
                mg, mg_b = mg_r.next()
                for fc in range(8):
                    gt, gt_b = gt_r.next()
                    Sc.dma("sync", lambda e, gt=gt, fc=fc, cs=cs: e.dma_start(out=gt[:], in_=self.GTT.rearrange("(i f) s -> f i s", i=3)[fc * 128:(fc + 1) * 128, :, cs]), writes=[gt_b])
                    prs = []
                    for (w, src, src_b, nk) in ((wpa, ya, ya_b, 6), (wpr, yr, yr_b, 6), (wpm, ym, ym_b, 4)):
                        p, p_b = pP.next()
                        for k in range(nk):
                            Sc.op("tensor", lambda e, p=p, w=w, src=src, k=k, fc=fc, nk=nk: e.matmul(p[:, :], lhsT=w[:, k, fc * 128:(fc + 1) * 128], rhs=src[:, k, :], start=(k == 0), stop=(k == nk - 1)), reads=[w_b, src_b], writes=[p_b])
                        prs.append((p, p_b))
                    m1, m1_b = m1_r.next()
                    m2, m2_b = m2_r.next()
                    m3, m3_b = m3_r.next()
                    for i, (m, m_b) in enumerate(((m1, m1_b), (m2, m2_b), (m3, m3_b))):
                        p, p_b = prs[i]
                        Sc.op("vector", lambda e, m=m, gt=gt, i=i, p=p: e.scalar_tensor_tensor(out=m[:, :], in0=gt[:, i, :], scalar=1.0, in1=p[:, :], op0=ALU.add, op1=ALU.mult), reads=[gt_b, p_b], writes=[m_b])
                    Sc.op("gpsimd", lambda e, m1=m1, m2=m2: e.tensor_tensor(out=m1[:, :], in0=m1[:, :], in1=m2[:, :], op=ALU.add), reads=[m1_b, m2_b], writes=[m1_b])
                    Sc.op("gpsimd", lambda e, mg=mg, m1=m1, m3=m3, fc=fc: e.tensor_tensor(out=mg[:, fc, :], in0=m1[:, :], in1=m3[:, :], op=ALU.add), reads=[m1_b, m3_b], writes=[mg_b])
                for tt in range(4):
                    t = tb * 4 + tt
                    x, x_b = x_r.next()
                    Sc.dma("sync", lambda e, x=x, t=t: e.dma_start(out=x[:], in_=self.x[t * 128:(t + 1) * 128, :]), writes=[x_b])
                    h, h_b = h_r.next()
                    for nh in range(2):
                        p, p_b = pO.next()
                        for k in range(8):
                            Sc.op("tensor", lambda e, p=p, mg=mg, k=k, tt=tt, nh=nh: e.matmul(p[:, :], lhsT=mg[:, k, tt * 128:(tt + 1) * 128], rhs=wo[:, k, nh * 512:(nh + 1) * 512], start=(k == 0), stop=(k == 7)), reads=[mg_b, w_b], writes=[p_b])
                        Sc.op("vector", lambda e, h=h, p=p, x=x, nh=nh: e.scalar_tensor_tensor(out=h[:, nh * 512:(nh + 1) * 512], in0=p[:, :], scalar=0.5, in1=x[:, nh * 512:(nh + 1) * 512], op0=ALU.mult, op1=ALU.add), reads=[p_b, x_b], writes=[h_b])
                    while pend:
                        pend.pop(0)()
                    Sc.dma("gpsimd", lambda e, h=h, t=t: e.dma_start(out=self.H[t * 128:(t + 1) * 128, :], in_=h[:]), reads=[h_b], writes=[h_out_b])
                    Sc.op("scalar", lambda e, h=h, t=t: e.activation(out=junk[:], in_=h[:], func=AF.Square, accum_out=stt_[:, 0, t:t + 1]), reads=[h_b, st_b[t]], writes=[junk_b, st_b[t]])
                    Sc.op("gpsimd", lambda e, t=t: e.tensor_scalar(out=stt_[:, 1, t:t + 1], in0=stt_[:, 0, t:t + 1], scalar1=1.0 / D, scalar2=EPS, op0=ALU.mult, op1=ALU.add), reads=[st_b[t]], writes=[st_b[t]])
                    Sc.op("gpsimd", lambda e, t=t: e.tensor_tensor(out=stt_[:, 2, t:t + 1], in0=stt_[:, 1, t:t + 1], in1=negh[:, 0:1], op=ALU.pow), reads=[st_b[t], cb_], writes=[st_b[t]])
                    hn, hn_b = hn_r.next()
                    Sc.op("vector", lambda e, hn=hn, h=h, t=t: e.scalar_tensor_tensor(out=hn[:], in0=h[:], scalar=stt_[:, 2, t:t + 1], in1=gffn[:], op0=ALU.mult, op1=ALU.mult), reads=[h_b, st_b[t], cb_], writes=[hn_b])
                    def tail(hn=hn, hn_b=hn_b, t=t):
                        p, p_b = pT.next()
                        for k in range(8):
                            Sc.op("tensor", lambda e, p=p, hn=hn, k=k: e.transpose(out=p[:, k * 128:(k + 1) * 128], in_=hn[:, k * 128:(k + 1) * 128], identity=ident[:]), reads=[hn_b, cb_], writes=[p_b])
                        ht, ht_b = ht_r.next()
                        Sc.op("scalar", lambda e, ht=ht, p=p: e.activation(out=ht[:], in_=p[:, :].rearrange("p (k c) -> p k c", k=8), func=AF.Copy), reads=[p_b], writes=[ht_b])
                        Sc.dma("gpsimd", lambda e, ht=ht, t=t: e.dma_start(out=self.HNT[:, t * 128:(t + 1) * 128].rearrange("(k p) s -> p k s", p=128), in_=ht[:]), reads=[ht_b], writes=[hnt_b])
                    pend.append(tail)
            while pend:
                pend.pop(0)()
            Sc.barrier()
            Sc.emit(st)

    def phase_d(self):
        nc = self.nc
        NF = DFF // 128
        with contextlib.ExitStack() as st:
            sb = lambda n, s, d: st.enter_context(nc.sbuf_tensor("d_" + n, s, d))
            ps = lambda n, s, d: st.enter_context(nc.psum_tensor("d_" + n, s, d))
            Sc = Sched(nc, prefix="d_")
            cb_ = Buf("const")
            hnT = sb("hnT", [128, 8, S], BF16)
            hn_b = [Buf("hnT") for _ in range(NB)]
            for tb in range(NB):
                Sc.dma("sync", lambda e, tb=tb: e.dma_start(out=hnT[:, :, tb * 512:(tb + 1) * 512], in_=self.HNT[:, tb * 512:(tb + 1) * 512].rearrange("(k p) s -> p k s", p=128)), writes=[hn_b[tb]])
            cw = sb("cw", [128, 2, 3, NF], F32)
            cbias = sb("cbias", [128, 2, NF], F32)
            for ab in range(2):
                for j in range(3):
                    Sc.dma("sync", lambda e, ab=ab, j=j: e.dma_start(out=cw[:, ab, j, :], in_=self.conv_w[0, j, ab * DFF:(ab + 1) * DFF].rearrange("(f p) -> p f", p=128), allow_slow_non_contiguous=True), writes=[cb_])
                Sc.dma("sync", lambda e, ab=ab: e.dma_start(out=cbias[:, ab, :], in_=self.conv_b[0, ab * DFF:(ab + 1) * DFF].rearrange("(f p) -> p f", p=128), allow_slow_non_contiguous=True), writes=[cb_])
            w_r = Ring([(sb("w%d" % i, [128, 8, 256], BF16), Buf("w")) for i in range(2)])
            u_r = []
            for i in range(2):
                ua = sb("ua%d" % i, [128, S + 2], F32)
                ub = sb("ub%d" % i, [128, S + 2], F32)
                bl = [Buf("u") for _ in range(NB + 1)]
                for t_ in (ua, ub):
                    Sc.op("gpsimd", lambda e, t_=t_: e.memset(t_[:, 0:1], 0.0), writes=[bl[NB]])
                    Sc.op("gpsimd", lambda e, t_=t_: e.memset(t_[:, S + 1:S + 2], 0.0), writes=[bl[NB]])
                u_r.append(((ua, ub), bl))
            u_r = Ring(u_r)
            ca = sb("ca", [128, S], F32)
            cbb = sb("cb", [128, S], F32)
            th = sb("th", [128, S], F32)
            ca_b, cbb_b, th_b = Buf("ca"), Buf("cb"), Buf("th")
            g_r = Ring([(sb("g%d" % i, [128, S], BF16), Buf("g")) for i in range(2)])
            pA = Ring([(ps("pA%d" % i, [128, 512], F32), Buf("pA")) for i in range(8)])
            gt2_b = Buf("GT2")
            for fc in range(NF):
                w, w_b = w_r.next()
                Sc.dma("gpsimd", lambda e, w=w, fc=fc: e.dma_start(out=w[:, :, 0:128], in_=self.w_up[0, :, fc * 128:(fc + 1) * 128].rearrange("(k p) c -> p k c", p=128)), writes=[w_b])
                Sc.dma("gpsimd", lambda e, w=w, fc=fc: e.dma_start(out=w[:, :, 128:256], in_=self.w_up[0, :, DFF + fc * 128:DFF + (fc + 1) * 128].rearrange("(k p) c -> p k c", p=128)), writes=[w_b])
                (ua, ub), ubl = u_r.next()
                for tb in range(NB):
                    for ab, ut in ((0, ua), (1, ub)):
                        p, p_b = pA.next()
                        for k in range(8):
                            Sc.op("tensor", lambda e, p=p, w=w, k=k, ab=ab, tb=tb: e.matmul(p[:, :], lhsT=w[:, k, ab * 128:(ab + 1) * 128], rhs=hnT[:, k, tb * 512:(tb + 1) * 512], start=(k == 0), stop=(k == 7)), reads=[w_b, hn_b[tb]], writes=[p_b])
                        Sc.op("scalar", lambda e, ut=ut, p=p, tb=tb: e.activation(out=ut[:, 1 + tb * 512:1 + (tb + 1) * 512], in_=p[:, :], func=AF.Copy), reads=[p_b], writes=[ubl[tb]])
                for ab, ut, ct, ct_b, eng in ((0, ua, ca, ca_b, "vector"), (1, ub, cbb, cbb_b, "vector")):
                    Sc.op("scalar", lambda e, ut=ut, ct=ct, ab=ab, fc=fc: e.activation(out=ct[:, :], in_=ut[:, 0:S], func=AF.Identity, bias=cbias[:, ab, fc:fc + 1], scale=cw[:, ab, 0, fc:fc + 1]), reads=ubl + [cb_], writes=[ct_b])
                    Sc.op(eng, lambda e, ut=ut, ct=ct, ab=ab, fc=fc: e.scalar_tensor_tensor(out=ct[:, :], in0=ut[:, 1:S + 1], scalar=cw[:, ab, 1, fc:fc + 1], in1=ct[:, :], op0=ALU.mult, op1=ALU.add), reads=ubl + [cb_, ct_b], writes=[ct_b])
                    Sc.op(eng, lambda e, ut=ut, ct=ct, ab=ab, fc=fc: e.scalar_tensor_tensor(out=ct[:, :], in0=ut[:, 2:S + 2], scalar=cw[:, ab, 2, fc:fc + 1], in1=ct[:, :], op0=ALU.mult, op1=ALU.add), reads=ubl + [cb_, ct_b], writes=[ct_b])
                Sc.op("scalar", lambda e: e.activation(out=th[:, :], in_=ca[:, :], func=AF.Tanh, scale=0.5), reads=[ca_b], writes=[th_b])
                Sc.op("vector", lambda e: e.scalar_tensor_tensor(out=th[:, :], in0=th[:, :], scalar=1.0, in1=ca[:, :], op0=ALU.add, op1=ALU.mult), reads=[ca_b, th_b], writes=[th_b])
                g, g_b = g_r.next()
                Sc.op("vector", lambda e, g=g: e.tensor_tensor(out=g[:, :], in0=th[:, :], in1=cbb[:, :], op=ALU.mult), reads=[th_b, cbb_b], writes=[g_b])
                Sc.dma("sync", lambda e, g=g, fc=fc: e.dma_start(out=self.GT2[fc * 128:(fc + 1) * 128, :], in_=g[:, :]), reads=[g_b], writes=[gt2_b])
            Sc.barrier()
            Sc.emit(st)

    def phase_e(self):
        nc = self.nc
        NF = DFF // 128
        with contextlib.ExitStack() as st:
            sb = lambda n, s, d: st.enter_context(nc.sbuf_tensor("e_" + n, s, d))
            ps = lambda n, s, d: st.enter_context(nc.psum_tensor("e_" + n, s, d))
            Sc = Sched(nc, prefix="e_")
            cb_ = Buf("const")
            w_b = Buf("w")
            wd = sb("wd", [128, NF, D], BF16)
            gfin = sb("gfin", [128, D], F32)
            negh = sb("negh", [128, 1], F32)
            for q4 in range(2):
                Sc.dma("gpsimd", lambda e, q4=q4: e.dma_start(out=wd[:, q4 * 11:(q4 + 1) * 11, :], in_=self.w_down[0, q4 * 11 * 128:(q4 + 1) * 11 * 128, :].rearrange("(k p) c -> p k c", p=128)), writes=[w_b])
            Sc.dma("sync", lambda e: e.dma_start(out=gfin[:], in_=self.g_final.partition_broadcast(128)), writes=[cb_])
            Sc.op("vector", lambda e: e.memset(negh[:], -0.5), reads=[cb_], writes=[cb_])
            g_r = Ring([(sb("g%d" % i, [128, NF, 512], BF16), Buf("g")) for i in range(2)])
            h_r = Ring([(sb("h%d" % i, [128, D], F32), Buf("h")) for i in range(3)])
            o_r = Ring([(sb("o%d" % i, [128, D], F32), Buf("o")) for i in range(2)])
            junk = sb("junk", [128, D], BF16)
            junk_b = Buf("junk")
            stt_ = sb("stat", [128, 3, NT], F32)
            st_b = [Buf("st") for _ in range(NT)]
            Sc.op("vector", lambda e: e.memset(stt_[:], 0.0), writes=st_b)
            pO = Ring([(ps("pO%d" % i, [128, 512], F32), Buf("pO")) for i in range(6)])
            out_b = Buf("out")
            def load_g(tb):
                g, g_b = g_r.next()
                Sc.dma("sync", lambda e, g=g, tb=tb: e.dma_start(out=g[:], in_=self.GT2[:, tb * 512:(tb + 1) * 512].rearrange("(k p) s -> p k s", p=128)), writes=[g_b])
                return g, g_b
            gnext = load_g(0)
            for tb in range(NB):
                g, g_b = gnext
                if tb + 1 < NB:
                    gnext = load_g(tb + 1)
                for tt in range(4):
                    t = tb * 4 + tt
                    h, h_b = h_r.next()
                    Sc.dma("sync", lambda e, h=h, t=t: e.dma_start(out=h[:], in_=self.H[t * 128:(t + 1) * 128, :]), writes=[h_b])
                    for nh in range(2):
                        p, p_b = pO.next()
                        for k in range(NF):
                            Sc.op("tensor", lambda e, p=p, g=g, k=k, tt=tt, nh=nh: e.matmul(p[:, :], lhsT=g[:, k, tt * 128:(tt + 1) * 128], rhs=wd[:, k, nh * 512:(nh + 1) * 512], start=(k == 0), stop=(k == NF - 1)), reads=[g_b, w_b], writes=[p_b])
                        Sc.op("vector", lambda e, h=h, p=p, nh=nh: e.scalar_tensor_tensor(out=h[:, nh * 512:(nh + 1) * 512], in0=p[:, :], scalar=0.5, in1=h[:, nh * 512:(nh + 1) * 512], op0=ALU.mult, op1=ALU.add), reads=[p_b, h_b], writes=[h_b])
                    Sc.op("scalar", lambda e, h=h, t=t: e.activation(out=junk[:], in_=h[:], func=AF.Square, accum_out=stt_[:, 0, t:t + 1]), reads=[h_b, st_b[t]], writes=[junk_b, st_b[t]])
                    Sc.op("gpsimd", lambda e, t=t: e.tensor_scalar(out=stt_[:, 1, t:t + 1], in0=stt_[:, 0, t:t + 1], scalar1=1.0 / D, scalar2=EPS, op0=ALU.mult, op1=ALU.add), reads=[st_b[t]], writes=[st_b[t]])
                    Sc.op("gpsimd", lambda e, t=t: e.tensor_tensor(out=stt_[:, 2, t:t + 1], in0=stt_[:, 1, t:t + 1], in1=negh[:, 0:1], op=ALU.pow), reads=[st_b[t], cb_], writes=[st_b[t]])
                    o, o_b = o_r.next()
                    Sc.op("vector", lambda e, o=o, h=h, t=t: e.scalar_tensor_tensor(out=o[:], in0=h[:], scalar=stt_[:, 2, t:t + 1], in1=gfin[:], op0=ALU.mult, op1=ALU.mult), reads=[h_b, st_b[t], cb_], writes=[o_b])
                    Sc.dma("sync", lambda e, o=o, t=t: e.dma_start(out=self.out[t * 128:(t + 1) * 128, :], in_=o[:]), reads=[o_b], writes=[out_b])
            Sc.barrier()
            Sc.emit(st)

    def build(self):
        for ph in ("a", "b1", "b1c", "b2", "b3", "c", "d", "e"):
            if self.phases is not None and ph not in self.phases:
                continue
            fn = getattr(self, "phase_" + ph, None)
            if fn is not None:
                fn()
            if self.stop_after == ph:
                break
        return self.nc


def host_consts():
    inv = 10000.0 ** (-np.arange(0, 64, 2, dtype=np.float32) / 64.0)
    ang = np.arange(S, dtype=np.float32)[:, None] * inv[None, :].astype(np.float32)
    c = {
        "c_cos": np.cos(ang).astype(np.float32),
        "c_sin": np.sin(ang).astype(np.float32),
        "c_ident": np.eye(128, dtype=np.float32),
    }
    kk = np.arange(128)[:, None]
    qq = np.arange(256)[None, :]
    band = ((qq >= kk) & (qq <= kk + 128))
    m = np.where(band, 0.0, -30000.0).astype(np.float32)
    am = np.zeros((128, 1024), np.float32)
    am[:, 0:256] = m
    am[:, 256:512] = m
    mf = m[64:128, 128:256]
    ml = m[0:64, 0:128]
    am[0:64, 512:640] = mf
    am[0:64, 640:768] = mf
    am[0:64, 768:896] = ml
    am[0:64, 896:1024] = ml
    c["c_amask"] = am
    r = np.zeros((128, 8, 128), np.float32)
    mm = np.arange(128)[:, None].astype(np.float32)
    nn = np.arange(128)[None, :].astype(np.float32)
    r[:, 0, :] = np.maximum(nn - mm, 0)
    r[:, 1, :] = np.maximum(mm - nn, 0)
    r[:, 2, :] = (nn >= mm) * 0.125
    r[:, 3, :] = (mm > nn) * 0.125
    r[:, 4, :] = nn + 1.0
    r[:, 5, :] = 128.0 - nn
    r[:, 6, 0] = 127.0 - np.arange(128)
    r[:, 6, 1] = np.arange(128)
    c["c_ret"] = r
    return c


def make_in_maps(inputs, n_cores=8):
    consts = host_consts()
    maps = []
    for b in range(n_cores):
        m = {"x": np.ascontiguousarray(inputs["x"][b]), "mem": np.ascontiguousarray(inputs["mem"][b])}
        for k, v in inputs.items():
            if k in ("x", "mem"):
                continue
            m[k] = np.ascontiguousarray(v)
        m.update(consts)
        maps.append(m)
    return maps


def kernel(**inputs):
    inputs = {k: np.asarray(v) for k, v in inputs.items()}
    prog = Prog()
    nc = prog.build()
    res = run_bass_kernel_spmd(nc, make_in_maps(inputs), core_ids=list(range(8)))
    return np.stack([r["out"] for r in res.results], axis=0)
```

```python
import contextlib
import numpy as np
import concourse.bass as bass
import concourse.mybir as mybir
from concourse.bass_utils import run_bass_kernel_spmd

F32 = mybir.dt.float32
BF16 = mybir.dt.bfloat16
AF = mybir.ActivationFunctionType
ALU = mybir.AluOpType
AX = mybir.AxisListType

S = 4096
D = 1024
NT = S // 128
NB = S // 512
IN_W = 5120
DFF = 2816
EPS = 1e-6
C_QA, C_KA, C_VA, C_QR, C_KR, C_VR, C_GR, C_QM = 0, 768, 1536, 2304, 2688, 3072, 3840, 4608


class Buf:
    __slots__ = ("name", "w", "r")

    def __init__(self, name):
        self.name = name
        self.w = None
        self.r = []


class Sched:
    ENG = ("tensor", "vector", "scalar", "gpsimd", "sync")

    def __init__(self, nc, n_dma_sems=32, prefix=""):
        self.nc = nc
        self.prefix = prefix
        self.lists = {e: [] for e in self.ENG}
        self.cnt = {e: 0 for e in self.ENG}
        self.known = {e: {} for e in self.ENG}
        self.ndma = n_dma_sems
        self.dma_issued = [0] * n_dma_sems
        self.dma_rr = 0

    def _need(self, eng, ev, waits):
        if ev is None:
            return
        key, val = ev
        if key == eng and eng == "tensor":
            return
        if self.known[eng].get(key, 0) >= val:
            return
        if waits.get(key, 0) < val:
            waits[key] = val

    def _deps(self, eng, reads, writes):
        waits = {}
        for b in reads:
            self._need(eng, b.w, waits)
        for b in writes:
            self._need(eng, b.w, waits)
            for ev in b.r:
                self._need(eng, ev, waits)
        for k, v in waits.items():
            self.known[eng][k] = v
        return list(waits.items())

    def op(self, eng, fn, reads=(), writes=()):
        waits = self._deps(eng, reads, writes)
        self.cnt[eng] += 1
        ev = (eng, self.cnt[eng])
        self.lists[eng].append((waits, fn, eng, 1))
        for b in reads:
            b.r.append(ev)
        for b in writes:
            b.w = ev
            b.r = []
        return ev

    def dma(self, eng, fn, reads=(), writes=()):
        i = self.dma_rr
        self.dma_rr = (self.dma_rr + 1) % self.ndma
        key = ("dma", i)
        waits = dict(self._deps(eng, reads, writes))
        prev = self.dma_issued[i]
        if prev > 0 and self.known[eng].get(key, 0) < prev:
            waits[key] = prev
            self.known[eng][key] = prev
        self.dma_issued[i] = prev + 16
        ev = (key, prev + 16)
        self.lists[eng].append((list(waits.items()), fn, key, 16))
        for b in reads:
            b.r.append(ev)
        for b in writes:
            b.w = ev
            b.r = []
        return ev

    def barrier(self):
        for e in self.ENG:
            waits = {}
            for o in self.ENG:
                if o != e and self.cnt[o] > 0:
                    self._need(e, (o, self.cnt[o]), waits)
            if e != "tensor" and self.cnt[e] > 0:
                self._need(e, (e, self.cnt[e]), waits)
            for i in range(self.ndma):
                if self.dma_issued[i] > 0:
                    self._need(e, (("dma", i), self.dma_issued[i]), waits)
            for k, v in waits.items():
                self.known[e][k] = v
            self.lists[e].append((list(waits.items()), None, None, 0))

    def emit(self, stack):
        nc = self.nc
        semmap = {}
        handles = []
        for e in self.ENG:
            semmap[e] = nc.alloc_semaphore(name=self.prefix + "s_" + e)
            handles.append(semmap[e])
        for i in range(self.ndma):
            semmap[("dma", i)] = nc.alloc_semaphore(name=self.prefix + "s_dma%d" % i)
            handles.append(semmap[("dma", i)])

        def runner(items):
            def f(e):
                for waits, fn, key, inc in items:
                    for k, v in waits:
                        e.wait_ge(semmap[k], v)
                    if fn is not None:
                        fn(e).then_inc(semmap[key], inc)
            return f
        with nc.Block() as block:
            block.tensor(runner(self.lists["tensor"]))
            block.vector(runner(self.lists["vector"]))
            block.scalar(runner(self.lists["scalar"]))
            block.gpsimd(runner(self.lists["gpsimd"]))
            block.sync(runner(self.lists["sync"]))
        nc.clear_and_free_semaphores(handles)
        nc.all_engine_barrier()


class Ring:
    def __init__(self, items):
        self.items = items
        self.i = 0

    def next(self):
        it = self.items[self.i]
        self.i = (self.i + 1) % len(self.items)
        return it


class Prog:
    def __init__(self, debug=False, stop_after=None, phases=None, ext_in=()):
        self.debug = debug
        self.stop_after = stop_after
        self.phases = phases
        self.ext_in = set(ext_in)
        nc = self.nc = bass.Bass("TRN2", target_bir_lowering=False)
        ein = lambda n, s: nc.dram_tensor(n, s, F32, kind="ExternalInput").ap()
        self.x = ein("x", [S, D])
        self.mem = ein("mem", [256, D])
        self.g_mix = ein("g_mix", [1, D])
        self.w_in = ein("w_in", [1, D, IN_W])
        self.w_mem_kv = ein("w_mem_kv", [1, D, 1024])
        self.g_mem = ein("g_mem", [1, D])
        self.dec_f = ein("ret_decay_fwd", [1, 6])
        self.dec_b = ein("ret_decay_bwd", [1, 6])
        self.g_ret = ein("g_ret", [1, 768])
        self.w_pa = ein("w_proj_attn", [1, 768, D])
        self.w_pr = ein("w_proj_ret", [1, 768, D])
        self.w_pm = ein("w_proj_mem", [1, 512, D])
        self.w_gate = ein("w_gate", [1, D, 3 * D])
        self.b_gate = ein("b_gate", [1, 3 * D])
        self.w_out = ein("w_out", [1, D, D])
        self.g_ffn = ein("g_ffn", [1, D])
        self.w_up = ein("w_up", [1, D, 2 * DFF])
        self.conv_w = ein("conv_w", [1, 3, 2 * DFF])
        self.conv_b = ein("conv_b", [1, 2 * DFF])
        self.w_down = ein("w_down", [1, DFF, D])
        self.g_final = ein("g_final", [D])
        self.c_cos = ein("c_cos", [S, 32])
        self.c_sin = ein("c_sin", [S, 32])
        self.c_ident = ein("c_ident", [128, 128])
        self.c_amask = ein("c_amask", [128, 1024])
        self.c_ret = ein("c_ret", [128, 8, 128])
        self.out = nc.dram_tensor("out", [S, D], F32, kind="ExternalOutput").ap()
        kind = "ExternalOutput" if debug else "Internal"
        scr = lambda n, s, d: nc.dram_tensor(n, s, d, kind=("ExternalInput" if n in self.ext_in else kind)).ap()
        self.PROJ = scr("PROJ", [S, IN_W], BF16)
        self.GTT = scr("GTT", [3 * D, S], BF16)
        self.UD = scr("UD", [12 * 128, S], F32)
        self.YAT = scr("YAT", [768, S], BF16)
        self.YR = scr("YR", [S, 768], BF16)
        self.YMT = scr("YMT", [512, S], BF16)
        self.H = scr("H", [S, D], F32)
        self.HNT = scr("HNT", [D, S], BF16)
        self.GT2 = scr("GT2", [DFF, S], BF16)

    def phase_a(self):
        nc = self.nc
        with contextlib.ExitStack() as st:
            sb = lambda n, s, d: st.enter_context(nc.sbuf_tensor("a_" + n, s, d))
            ps = lambda n, s, d: st.enter_context(nc.psum_tensor("a_" + n, s, d))
            Sc = Sched(nc, prefix="a_")
            xT = sb("xT", [128, 8, S], BF16)
            xT_b = [Buf("xT%d" % t) for t in range(NT)]
            xr = Ring([(sb("xr%d" % i, [128, D], F32), Buf("xr%d" % i)) for i in range(3)])
            xg = Ring([(sb("xg%d" % i, [128, D], BF16), Buf("xg%d" % i)) for i in range(2)])
            junk = sb("junk", [128, D], BF16)
            junk_b = Buf("junk")
            gmix = sb("gmix", [128, D], F32)
            gmix_b = Buf("gmix")
            ssq = sb("ssq", [128, NT], F32)
            msq = sb("msq", [128, NT], F32)
            rstd = sb("rstd", [128, NT], F32)
            negh = sb("negh", [128, 1], F32)
            st_b = [Buf("st%d" % t) for t in range(NT)]
            const_b = Buf("const")
            identf = sb("identf", [128, 128], F32)
            ident = sb("ident", [128, 128], BF16)
            cos_t = sb("cos_t", [128, NT, 32], F32)
            sin_t = sb("sin_t", [128, NT, 32], F32)
            hb = sb("hb", [128, 24], F32)
            pT = Ring([(ps("pT%d" % i, [128, 1024], BF16), Buf("pT%d" % i)) for i in range(2)])
            pA = Ring([(ps("pA%d" % i, [128, 512], F32), Buf("pA%d" % i)) for i in range(6)])
            wr = Ring([(sb("w%d" % i, [128, 8, 512], BF16), Buf("w%d" % i)) for i in range(3)])
            ob = Ring([(sb("ob%d" % i, [128, 512], BF16), Buf("ob%d" % i)) for i in range(4)])
            tA = Ring([(sb("tA%d" % i, [128, 512], F32), Buf("tA%d" % i)) for i in range(3)])
            tB = Ring([(sb("tB%d" % i, [128, 512], F32), Buf("tB%d" % i)) for i in range(3)])
            proj_b = Buf("PROJ")
            gtt_b = Buf("GTT")

            Sc.dma("sync", lambda e: e.dma_start(out=gmix[:], in_=self.g_mix[0, :].partition_broadcast(128)), writes=[gmix_b])
            Sc.dma("sync", lambda e: e.dma_start(out=identf[:], in_=self.c_ident[:, :]), writes=[const_b])
            Sc.dma("sync", lambda e: e.dma_start(out=cos_t[:], in_=self.c_cos.rearrange("(t p) c -> p t c", p=128)), writes=[const_b])
            Sc.dma("sync", lambda e: e.dma_start(out=sin_t[:], in_=self.c_sin.rearrange("(t p) c -> p t c", p=128)), writes=[const_b])
            Sc.dma("sync", lambda e: e.dma_start(out=hb[:], in_=self.b_gate[0, :].rearrange("(f p) -> p f", p=128), allow_slow_non_contiguous=True), writes=[const_b])
            Sc.op("vector", lambda e: e.tensor_copy(out=ident[:], in_=identf[:]), reads=[const_b], writes=[const_b])
            Sc.op("vector", lambda e: e.tensor_scalar(out=hb[:], in0=hb[:], scalar1=0.5, scalar2=None, op0=ALU.mult), reads=[const_b], writes=[const_b])
            Sc.op("vector", lambda e: e.memset(ssq[:], 0.0), writes=st_b)
            Sc.op("vector", lambda e: e.memset(negh[:], -0.5), writes=[const_b])

            for t in range(NT):
                xt, xt_b = xr.next()
                Sc.dma("sync", lambda e, xt=xt, t=t: e.dma_start(out=xt[:], in_=self.x[t * 128:(t + 1) * 128, :]), writes=[xt_b])
                Sc.op("scalar", lambda e, xt=xt, t=t: e.activation(out=junk[:], in_=xt[:], func=AF.Square, accum_out=ssq[:, t:t + 1]),
                      reads=[xt_b], writes=[junk_b, st_b[t]])
                Sc.op("gpsimd", lambda e, t=t: e.tensor_scalar(out=msq[:, t:t + 1], in0=ssq[:, t:t + 1], scalar1=1.0 / D, scalar2=EPS, op0=ALU.mult, op1=ALU.add),
                      reads=[st_b[t]], writes=[st_b[t]])
                Sc.op("gpsimd", lambda e, t=t: e.tensor_tensor(out=rstd[:, t:t + 1], in0=msq[:, t:t + 1], in1=negh[:, 0:1], op=ALU.pow),
                      reads=[st_b[t], const_b], writes=[st_b[t]])
                g, g_b = xg.next()
                Sc.op("vector", lambda e, g=g, xt=xt, t=t: e.scalar_tensor_tensor(out=g[:], in0=xt[:], scalar=rstd[:, t:t + 1], in1=gmix[:], op0=ALU.mult, op1=ALU.mult),
                      reads=[xt_b, st_b[t], gmix_b], writes=[g_b])
                p, p_b = pT.next()
                for k in range(8):
                    Sc.op("tensor", lambda e, p=p, g=g, k=k: e.transpose(out=p[:, k * 128:(k + 1) * 128], in_=g[:, k * 128:(k + 1) * 128], identity=ident[:]),
                          reads=[g_b, const_b], writes=[p_b])
                eng = "scalar" if t % 2 == 0 else "vector"
                dst = xT[:, :, t * 128:(t + 1) * 128]
                src = p[:, :].rearrange("p (k c) -> p k c", k=8)
                if eng == "scalar":
                    Sc.op("scalar", lambda e, dst=dst, src=src: e.activation(out=dst, in_=src, func=AF.Copy), reads=[p_b], writes=[xT_b[t]])
                else:
                    Sc.op("vector", lambda e, dst=dst, src=src: e.tensor_copy(out=dst, in_=src), reads=[p_b], writes=[xT_b[t]])

            blocks = [(C_QA, 512, "rot"), (C_QA + 512, 256, "rot"), (C_KA, 512, "rot"), (C_KA + 512, 256, "rot"),
                      (C_VA, 512, "copy"), (C_VA + 512, 256, "copy"), (C_QR, 384, "rot"), (C_KR, 384, "rot"),
                      (C_VR, 512, "copy"), (C_VR + 512, 256, "copy"), (C_GR, 512, "silu2"), (C_GR + 512, 256, "silu2"),
                      (C_QM, 512, "copy")]
            wspecs = [(self.w_in[0, :, c0:c0 + N], N) for (c0, N, kind) in blocks] + [(self.w_gate[0, :, fg * 512:(fg + 1) * 512], 512) for fg in range(6)]
            wloaded = {}

            def ensure_w(i):
                if i < len(wspecs) and i not in wloaded:
                    w, w_b = wr.next()
                    src, N = wspecs[i]
                    Sc.dma("gpsimd", lambda e, w=w, src=src, N=N: e.dma_start(out=w[:, :, 0:N], in_=src.rearrange("(k p) c -> p k c", p=128)), writes=[w_b])
                    wloaded[i] = (w, w_b)
                return wloaded.get(i)
            for bi, (c0, N, kind) in enumerate(blocks):
                w, w_b = ensure_w(bi)
                ensure_w(bi + 1)
                for t in range(NT):
                    p, p_b = pA.next()
                    for k in range(8):
                        Sc.op("tensor", lambda e, p=p, w=w, t=t, k=k, N=N: e.matmul(p[:, 0:N], lhsT=xT[:, k, t * 128:(t + 1) * 128], rhs=w[:, k, 0:N], start=(k == 0), stop=(k == 7)),
                              reads=[xT_b[t], w_b], writes=[p_b])
                    o, o_b = ob.next()
                    if kind == "copy":
                        Sc.op("scalar", lambda e, o=o, p=p, N=N: e.activation(out=o[:, 0:N], in_=p[:, 0:N], func=AF.Copy), reads=[p_b], writes=[o_b])
                    elif kind == "silu2":
                        a, a_b = tA.next()
                        Sc.op("scalar", lambda e, a=a, p=p, N=N: e.activation(out=a[:, 0:N], in_=p[:, 0:N], func=AF.Tanh, scale=0.5), reads=[p_b], writes=[a_b])
                        Sc.op("vector", lambda e, o=o, a=a, p=p, N=N: e.scalar_tensor_tensor(out=o[:, 0:N], in0=a[:, 0:N], scalar=1.0, in1=p[:, 0:N], op0=ALU.add, op1=ALU.mult),
                              reads=[a_b, p_b], writes=[o_b])
                    else:
                        H = N // 64
                        a, a_b = tA.next()
                        b, b_b = tB.next()
                        pv = p[:, 0:N].rearrange("p (h two f) -> p h two f", two=2, f=32)
                        av = a[:, 0:N].rearrange("p (h two f) -> p h two f", two=2, f=32)
                        bv = b[:, 0:N].rearrange("p (h two f) -> p h two f", two=2, f=32)
                        ov = o[:, 0:N].rearrange("p (h two f) -> p h two f", two=2, f=32)
                        cb = cos_t[:, t:t + 1, :].broadcast_to([128, H, 32])
                        sn = sin_t[:, t:t + 1, :].broadcast_to([128, H, 32])
                        x1, x2 = pv[:, :, 0, :], pv[:, :, 1, :]
                        Sc.op("vector", lambda e, av=av, x1=x1, cb=cb: e.tensor_tensor(out=av[:, :, 0, :], in0=x1, in1=cb, op=ALU.mult), reads=[p_b, const_b], writes=[a_b])
                        Sc.op("vector", lambda e, av=av, x2=x2, cb=cb: e.tensor_tensor(out=av[:, :, 1, :], in0=x2, in1=cb, op=ALU.mult), reads=[p_b, const_b], writes=[a_b])
                        Sc.op("vector", lambda e, bv=bv, x2=x2, sn=sn: e.tensor_tensor(out=bv[:, :, 0, :], in0=x2, in1=sn, op=ALU.mult), reads=[p_b, const_b], writes=[b_b])
                        Sc.op("vector", lambda e, bv=bv, x1=x1, sn=sn: e.tensor_tensor(out=bv[:, :, 1, :], in0=x1, in1=sn, op=ALU.mult), reads=[p_b, const_b], writes=[b_b])
                        Sc.op("gpsimd", lambda e, ov=ov, av=av, bv=bv: e.tensor_tensor(out=ov[:, :, 0, :], in0=av[:, :, 0, :], in1=bv[:, :, 0, :], op=ALU.subtract), reads=[a_b, b_b], writes=[o_b])
                        Sc.op("gpsimd", lambda e, ov=ov, av=av, bv=bv: e.tensor_tensor(out=ov[:, :, 1, :], in0=av[:, :, 1, :], in1=bv[:, :, 1, :], op=ALU.add), reads=[a_b, b_b], writes=[o_b])
                    Sc.dma("sync", lambda e, o=o, t=t, c0=c0, N=N: e.dma_start(out=self.PROJ[t * 128:(t + 1) * 128, c0:c0 + N], in_=o[:, 0:N]), reads=[o_b], writes=[proj_b])

            for fg in range(6):
                w, w_b = ensure_w(len(blocks) + fg)
                ensure_w(len(blocks) + fg + 1)
                for j in range(4):
                    fc = fg * 4 + j
                    for tb in range(NB):
                        p, p_b = pA.next()
                        for k in range(8):
                            Sc.op("tensor", lambda e, p=p, w=w, tb=tb, k=k, j=j: e.matmul(p[:, :], lhsT=w[:, k, j * 128:(j + 1) * 128], rhs=xT[:, k, tb * 512:(tb + 1) * 512], start=(k == 0), stop=(k == 7)),
                                  reads=xT_b[tb * 4:tb * 4 + 4] + [w_b], writes=[p_b])
                        o, o_b = ob.next()
                        Sc.op("scalar", lambda e, o=o, p=p, fc=fc: e.activation(out=o[:, :], in_=p[:, :], func=AF.Tanh, bias=hb[:, fc:fc + 1], scale=0.5), reads=[p_b, const_b], writes=[o_b])
                        Sc.dma("sync", lambda e, o=o, fc=fc, tb=tb: e.dma_start(out=self.GTT[fc * 128:(fc + 1) * 128, tb * 512:(tb + 1) * 512], in_=o[:, :]), reads=[o_b], writes=[gtt_b])
            Sc.barrier()
            Sc.emit(st)


    def phase_b1(self):
        nc = self.nc
        with contextlib.ExitStack() as st:
            sb = lambda n, s, d: st.enter_context(nc.sbuf_tensor("b1_" + n, s, d))
            ps = lambda n, s, d: st.enter_context(nc.psum_tensor("b1_" + n, s, d))
            Sc = Sched(nc, prefix="b1_")
            const_b = Buf("const")
            amf = sb("amf", [128, 1024], F32)
            am = sb("am", [128, 1024], BF16)
            identf = sb("identf", [128, 128], F32)
            ident = sb("ident", [128, 128], BF16)
            Sc.dma("sync", lambda e: e.dma_start(out=amf[:], in_=self.c_amask[:, :]), writes=[const_b])
            Sc.dma("sync", lambda e: e.dma_start(out=identf[:], in_=self.c_ident[:, :]), writes=[const_b])
            Sc.op("vector", lambda e: e.tensor_copy(out=am[:], in_=amf[:]), reads=[const_b], writes=[const_b])
            Sc.op("vector", lambda e: e.tensor_copy(out=ident[:], in_=identf[:]), reads=[const_b], writes=[const_b])
            sets = []
            for i in range(3):
                Lc = S if i == 2 else 1024
                nbc = Lc // 128
                qT = [sb("qT%d_%d" % (i, pp), [128, Lc], BF16) for pp in range(2)]
                qZ = [[sb("qZ%d_%d_%d" % (i, pp, hh), [128, Lc], BF16) for hh in range(2)] for pp in range(2)]
                qz_b = [[[Buf("qz") for _ in range(nbc)] for hh in range(2)] for pp in range(2)]
                for pp in range(2):
                    Sc.op("gpsimd", lambda e, t=qZ[pp][0]: e.memset(t[64:128, :], 0.0), writes=qz_b[pp][0])
                    Sc.op("gpsimd", lambda e, t=qZ[pp][1]: e.memset(t[0:64, :], 0.0), writes=qz_b[pp][1])
                kT = [sb("kT%d_%d" % (i, pp), [128, Lc], BF16) for pp in range(2)]
                va = sb("va%d" % i, [128, nbc + 1, 4, 128], BF16)
                q_b = [[Buf("q") for _ in range(nbc)] for pp in range(2)]
                k_b = [[Buf("k") for _ in range(nbc)] for pp in range(2)]
                v_b = [Buf("v") for _ in range(nbc + 1)]
                Sc.op("gpsimd", lambda e, va=va: e.memset(va[:, :, :, 64:128], 1.0), writes=v_b)
                sets.append((qT, kT, va, q_b, k_b, v_b, qZ, qz_b))
            pS = Ring([(ps("pS%d" % i, [128, 512], F32), Buf("pS%d" % i)) for i in range(3)])
            pU = Ring([(ps("pU%d" % i, [128, 512], F32), Buf("pU%d" % i)) for i in range(4)])
            Er = Ring([(sb("E%d" % i, [128, 512], BF16), Buf("E%d" % i)) for i in range(6)])
            us = Ring([(sb("us%d" % i, [128, 512], F32), Buf("us%d" % i)) for i in range(4)])
            ud_b = Buf("UD")
            unit = 0
            for g, dl in enumerate((1, 4, 16)):
                if g not in getattr(self, "b1_groups", (0, 1, 2)):
                    continue
                L = S // dl
                nb = L // 128
                ucurs = {0: None, 1: None}
                for r in range(dl):
                    qT, kT, va, q_b, k_b, v_b, qZ, qz_b = sets[2] if g == 0 else sets[unit % 2]
                    unit += 1
                    rows = self.PROJ.rearrange("(i r) c -> r i c", r=dl)[r]
                    vc = C_VA + g * 256
                    Sc.dma("sync", lambda e, va=va, rows=rows, vc=vc: e.dma_start(out=va[0:64, 0, :, 0:64], in_=rows[0:64, vc:vc + 256].rearrange("k (h d) -> k h d", d=64)), writes=[v_b[0]])
                    for h4 in range(4):
                        Sc.dma("sync", lambda e, va=va, rows=rows, vc=vc, nb=nb, L=L, h4=h4: e.dma_start(out=va[:, 1:nb, h4, 0:64], in_=rows[64:L - 64, vc + h4 * 64:vc + h4 * 64 + 64].rearrange("(j k) d -> k j d", k=128)), writes=v_b[1:nb])
                    Sc.dma("sync", lambda e, va=va, rows=rows, vc=vc, nb=nb, L=L: e.dma_start(out=va[0:64, nb, :, 0:64], in_=rows[L - 64:L, vc:vc + 256].rearrange("k (h d) -> k h d", d=64)), writes=[v_b[nb]])
                    for pp in range(2):
                        qc = C_QA + (g * 4 + 2 * pp) * 64
                        kc = C_KA + (g * 4 + 2 * pp) * 64
                        nbc = min(4, nb)
                        for b4 in range(0, nb, nbc):
                            rs = slice(b4 * 128, (b4 + nbc) * 128)
                            Sc.dma("sync", lambda e, dst=kT[pp], rows=rows, kc=kc, rs=rs: e.dma_start_transpose(out=dst[:, rs], in_=rows[rs, kc:kc + 128]), writes=k_b[pp][b4:b4 + nbc])
                            Sc.dma("sync", lambda e, dst=qT[pp], rows=rows, qc=qc, rs=rs: e.dma_start_transpose(out=dst[:, rs], in_=rows[rs, qc:qc + 128]), writes=q_b[pp][b4:b4 + nbc])
                        for blk in range(nb):
                            sl = slice(blk * 128, (blk + 1) * 128)
                            Sc.op("vector", lambda e, d=qZ[pp][0], s_=qT[pp], sl=sl: e.tensor_copy(out=d[0:64, sl], in_=s_[0:64, sl]), reads=[q_b[pp][blk]], writes=[qz_b[pp][0][blk]])
                            Sc.op("gpsimd", lambda e, d=qZ[pp][1], s_=qT[pp], sl=sl: e.tensor_copy(out=d[64:128, sl], in_=s_[64:128, sl]), reads=[q_b[pp][blk]], writes=[qz_b[pp][1][blk]])
                    for pp in range(2):
                        if getattr(self, "b1_stage", 9) < 1:
                            continue
                        Es = {}
                        ucur = ucurs[pp]
                        for j in range(nb + 1):
                            k0, k1 = max(0, 128 * j - 64), min(L, 128 * j + 64)
                            M = k1 - k0
                            q0, q1 = max(0, 128 * (j - 1)), min(L, 128 * (j + 1))
                            Nq = q1 - q0
                            if j == 0:
                                mk = am[0:64, 512:768]
                            elif j == nb:
                                mk = am[0:64, 768:1024]
                            else:
                                mk = am[:, 0:512]
                            kblks = sorted(set([k0 // 128, (k1 - 1) // 128]))
                            qblks = sorted(set([q0 // 128, (q1 - 1) // 128]))
                            p, p_b = pS.next()
                            for hh in range(2):
                                rd = [k_b[pp][b] for b in kblks] + [qz_b[pp][hh][b] for b in qblks]
                                Sc.op("tensor", lambda e, p=p, kt=kT[pp], qt=qZ[pp][hh], hh=hh, k0=k0, k1=k1, q0=q0, q1=q1, M=M, Nq=Nq:
                                      e.matmul(p[0:M, hh * Nq:(hh + 1) * Nq], lhsT=kt[:, k0:k1], rhs=qt[:, q0:q1], start=True, stop=False),
                                      reads=rd, writes=[p_b])
                                Sc.op("tensor", lambda e, p=p, mk=mk, M=M, Nq=Nq, hh=hh: e.matmul(p[0:M, hh * Nq:(hh + 1) * Nq], lhsT=ident[0:M, 0:M], rhs=mk[:, 0:Nq], start=False, stop=True),
                                      reads=[const_b], writes=[p_b])
                            E, E_b = Er.next()
                            Sc.op("scalar", lambda e, E=E, p=p, M=M, Nq=Nq: e.activation(out=E[0:M, 0:2 * Nq], in_=p[0:M, 0:2 * Nq], func=AF.Exp, scale=0.125), reads=[p_b], writes=[E_b])
                            Es[j] = (E, E_b, M, Nq)
                            if j == 0 or getattr(self, "b1_stage", 9) < 2:
                                continue
                            b = j - 1
                            G = r * nb + b
                            if G % 4 == 0:
                                ucur = ucurs[pp] = [pU.next(), pU.next()]
                            for hh in range(2):
                                (u, u_b) = ucur[hh]
                                col = (G % 4) * 128
                                for n_, jj in enumerate((b, b + 1)):
                                    Ej, Ej_b, Mj, Nqj = Es[jj]
                                    if jj == b:
                                        c = hh * Nqj + (128 if b >= 1 else 0)
                                    else:
                                        c = hh * Nqj
                                    hv = 2 * pp + hh
                                    Sc.op("tensor", lambda e, u=u, va=va, Ej=Ej, jj=jj, hv=hv, Mj=Mj, c=c, col=col, n_=n_:
                                          e.matmul(u[:, col:col + 128], lhsT=va[0:Mj, jj, hv, :], rhs=Ej[0:Mj, c:c + 128], start=(n_ == 0), stop=(n_ == 1)),
                                          reads=[v_b[jj], Ej_b], writes=[u_b])
                            del Es[b]
                            if G % 4 == 3:
                                for hh in range(2):
                                    (u, u_b) = ucur[hh]
                                    o, o_b = us.next()
                                    if hh == 0:
                                        Sc.op("vector", lambda e, o=o, u=u: e.tensor_copy(out=o[:], in_=u[:]), reads=[u_b], writes=[o_b])
                                    else:
                                        Sc.op("scalar", lambda e, o=o, u=u: e.activation(out=o[:], in_=u[:], func=AF.Copy), reads=[u_b], writes=[o_b])
                                    hrow = (g * 4 + 2 * pp + hh) * 128
                                    c0 = (G - 3) * 128
                                    Sc.dma("gpsimd", lambda e, o=o, hrow=hrow, c0=c0: e.dma_start(out=self.UD[hrow:hrow + 128, c0:c0 + 512], in_=o[:]), reads=[o_b], writes=[ud_b])
            Sc.barrier()
            Sc.emit(st)

    def phase_b1c(self):
        nc = self.nc
        with contextlib.ExitStack() as st:
            sb = lambda n, s, d: st.enter_context(nc.sbuf_tensor("b1c_" + n, s, d))
            Sc = Sched(nc, prefix="b1c_")
            CH = 2048
            Ut = [Ring([(sb("U%d_%d" % (g, i), [128, CH], F32), Buf("U")) for i in range(2)]) for g in range(3)]
            Dt = [Ring([(sb("D%d_%d" % (g, i), [128, CH], F32), Buf("D")) for i in range(2)]) for g in range(3)]
            Rr = Ring([(sb("R%d" % i, [128, CH], F32), Buf("R")) for i in range(2)])
            Yr = Ring([(sb("Y%d" % i, [128, CH], BF16), Buf("Y")) for i in range(3)])
            yat_b = Buf("YAT")
            for c2 in range(S // CH):
                for sp in range(2):
                    tiles = []
                    for g, dl in enumerate((1, 4, 16)):
                        L = S // dl
                        il = CH // dl
                        u, u_b = Ut[g].next()
                        d_, d_b = Dt[g].next()
                        for hh in range(2):
                            h = g * 4 + 2 * sp + hh
                            srcU = self.UD[h * 128:h * 128 + 64, :].rearrange("p (r i) -> p r i", r=dl)[:, :, c2 * il:(c2 + 1) * il]
                            srcD = self.UD[h * 128 + 64:h * 128 + 128, :].rearrange("p (r i) -> p r i", r=dl)[:, :, c2 * il:(c2 + 1) * il]
                            Sc.dma("sync", lambda e, u=u, hh=hh, srcU=srcU, dl=dl: e.dma_start(out=u[hh * 64:(hh + 1) * 64, :].rearrange("p (r i) -> p r i", r=dl), in_=srcU), writes=[u_b])
                            Sc.dma("sync", lambda e, d_=d_, hh=hh, srcD=srcD, dl=dl: e.dma_start(out=d_[hh * 64:(hh + 1) * 64, :].rearrange("p (r i) -> p r i", r=dl), in_=srcD), writes=[d_b])
                        tiles.append((u, u_b, d_, d_b, dl))
                    R, R_b = Rr.next()
                    nat = lambda t, dl: t[:, :].rearrange("p (i r) -> p i r", r=dl)
                    res = lambda t, dl: t[:, :].rearrange("p (r i) -> p i r", r=dl)
                    (u0, u0_b, d0, d0_b, _), (u1, u1_b, d1, d1_b, _), (u2, u2_b, d2, d2_b, _) = tiles
                    Sc.op("gpsimd", lambda e, R=R, d0=d0, d1=d1: e.tensor_tensor(out=nat(R, 4), in0=nat(d0, 4), in1=res(d1, 4), op=ALU.add), reads=[d0_b, d1_b], writes=[R_b])
                    Sc.op("gpsimd", lambda e, R=R, d2=d2: e.tensor_tensor(out=nat(R, 16), in0=nat(R, 16), in1=res(d2, 16), op=ALU.add), reads=[d2_b, R_b], writes=[R_b])
                    Sc.op("vector", lambda e, R=R: e.reciprocal(out=R[:, :], in_=R[:, :]), reads=[R_b], writes=[R_b])
                    for g, (u, u_b, d_, d_b, dl) in enumerate(tiles):
                        y, y_b = Yr.next()
                        eng = "vector" if g != 1 else "gpsimd"
                        Sc.op(eng, lambda e, y=y, u=u, dl=dl, R=R: e.tensor_tensor(out=nat(y, dl), in0=res(u, dl), in1=nat(R, dl), op=ALU.mult), reads=[u_b, R_b], writes=[y_b])
                        row = (g * 4 + 2 * sp) * 64
                        Sc.dma("gpsimd", lambda e, y=y, row=row, c2=c2: e.dma_start(out=self.YAT[row:row + 128, c2 * CH:(c2 + 1) * CH], in_=y[:, :]), reads=[y_b], writes=[yat_b])
            Sc.barrier()
            Sc.emit(st)

    def phase_b2(self):
        nc = self.nc
        with contextlib.ExitStack() as st:
            sb = lambda n, s, d: st.enter_context(nc.sbuf_tensor("b2_" + n, s, d))
            ps = lambda n, s, d: st.enter_context(nc.psum_tensor("b2_" + n, s, d))
            Sc = Sched(nc, prefix="b2_")
            cb_ = Buf("const")
            cret = sb("cret", [128, 8, 128], F32)
            dec = sb("dec", [128, 12], F32)
            lg = sb("lg", [128, 12], F32)
            lgp = sb("lgp", [128, 6], F32)
            Mall = sb("Mall", [128, 6, 128], F32)
            tmpm = sb("tmpm", [128, 128], F32)
            zeta = sb("zeta", [128, 12], F32)
            XiF = sb("XiF", [128, 3, 128], F32)
            XiB = sb("XiB", [128, 3, 128], F32)
            g128 = sb("g128", [128, 6], F32)
            gret = sb("gret", [128, 768], F32)
            negh = sb("negh", [128, 6], F32)
            Sc.dma("sync", lambda e: e.dma_start(out=cret[:], in_=self.c_ret[:, :, :]), writes=[cb_])
            Sc.dma("sync", lambda e: e.dma_start(out=dec[:, 0:6], in_=self.dec_f[0, :].partition_broadcast(128)), writes=[cb_])
            Sc.dma("sync", lambda e: e.dma_start(out=dec[:, 6:12], in_=self.dec_b[0, :].partition_broadcast(128)), writes=[cb_])
            Sc.dma("sync", lambda e: e.dma_start(out=gret[:], in_=self.g_ret[0, :].partition_broadcast(128)), writes=[cb_])
            C = lambda eng, fn: Sc.op(eng, fn, reads=[cb_], writes=[cb_])
            C("vector", lambda e: e.memset(negh[:], -0.5))
            C("vector", lambda e: e.tensor_scalar(out=gret[:], in0=gret[:], scalar1=0.5, scalar2=None, op0=ALU.mult))
            C("scalar", lambda e: e.activation(out=lg[:], in_=dec[:], func=AF.Exp, scale=-1.0))
            C("vector", lambda e: e.tensor_scalar(out=lg[:], in0=lg[:], scalar1=1.0, scalar2=None, op0=ALU.add))
            C("scalar", lambda e: e.activation(out=lg[:], in_=lg[:], func=AF.Ln))
            C("vector", lambda e: e.tensor_scalar(out=lg[:], in0=lg[:], scalar1=-1.0, scalar2=None, op0=ALU.mult))
            for dr in range(2):
                for hh in range(2):
                    src = lg[hh * 64:(hh + 1) * 64, dr * 6:(dr + 1) * 6].rearrange("p (a b) -> p a b", b=2)[:, :, hh]
                    C("vector", lambda e, dr=dr, hh=hh, src=src: e.tensor_copy(out=lgp[hh * 64:(hh + 1) * 64, dr * 3:(dr + 1) * 3], in_=src))
            for h in range(6):
                C("scalar", lambda e, h=h: e.activation(out=Mall[:, h, :], in_=cret[:, 0, :], func=AF.Exp, scale=lg[:, h:h + 1]))
                C("vector", lambda e, h=h: e.tensor_tensor(out=Mall[:, h, :], in0=Mall[:, h, :], in1=cret[:, 2, :], op=ALU.mult))
                C("scalar", lambda e, h=h: e.activation(out=tmpm[:], in_=cret[:, 1, :], func=AF.Exp, scale=lg[:, 6 + h:7 + h]))
                C("vector", lambda e, h=h: e.tensor_tensor(out=tmpm[:], in0=tmpm[:], in1=cret[:, 3, :], op=ALU.mult))
                C("vector", lambda e, h=h: e.tensor_tensor(out=Mall[:, h, :], in0=Mall[:, h, :], in1=tmpm[:], op=ALU.add))
                C("scalar", lambda e, h=h: e.activation(out=zeta[:, h:h + 1], in_=cret[:, 6, 0:1], func=AF.Exp, scale=lg[:, h:h + 1]))
                C("scalar", lambda e, h=h: e.activation(out=zeta[:, 6 + h:7 + h], in_=cret[:, 6, 1:2], func=AF.Exp, scale=lg[:, 6 + h:7 + h]))
            C("vector", lambda e: e.tensor_scalar(out=zeta[:], in0=zeta[:], scalar1=0.125, scalar2=None, op0=ALU.mult))
            for pp in range(3):
                C("scalar", lambda e, pp=pp: e.activation(out=XiF[:, pp, :], in_=cret[:, 4, :], func=AF.Exp, scale=lgp[:, pp:pp + 1]))
                C("scalar", lambda e, pp=pp: e.activation(out=XiB[:, pp, :], in_=cret[:, 5, :], func=AF.Exp, scale=lgp[:, 3 + pp:4 + pp]))
            C("scalar", lambda e: e.activation(out=g128[:], in_=lgp[:], func=AF.Exp, scale=128.0))

            Sall = [sb("SallF", [128, NT, 3, 128], BF16), sb("SallB", [128, NT, 3, 128], BF16)]
            sall_b = [[Buf("sf") for _ in range(NT)], [Buf("sb") for _ in range(NT)]]
            Scur = [sb("ScurF", [128, 3, 128], F32), sb("ScurB", [128, 3, 128], F32)]
            scur_b = [Buf("scf"), Buf("scb")]
            kt_r = Ring([(sb("ktok%d" % i, [128, 384], BF16), Buf("ktok")) for i in range(3)])
            vt_r = Ring([(sb("vtok%d" % i, [128, 768], BF16), Buf("vtok")) for i in range(3)])
            kz_r = Ring([(sb("kz%d" % i, [128, 6, 64], BF16), Buf("kz")) for i in range(3)])
            pkv = Ring([(ps("pkv%d" % i, [128, 512], F32), Buf("pkv")) for i in range(2)])
            for dr in range(2):
                Sc.op("vector", lambda e, dr=dr: e.memset(Scur[dr][:], 0.0), writes=[scur_b[dr]])
            for step in range(2 * NT):
                dr = step % 2
                c = (step // 2) if dr == 0 else (NT - 1 - step // 2)
                if True:
                    kt, kt_b = kt_r.next()
                    vt, vt_b = vt_r.next()
                    Sc.dma("sync", lambda e, kt=kt, c=c: e.dma_start(out=kt[:], in_=self.PROJ[c * 128:(c + 1) * 128, C_KR:C_KR + 384]), writes=[kt_b])
                    Sc.dma("sync", lambda e, vt=vt, c=c: e.dma_start(out=vt[:], in_=self.PROJ[c * 128:(c + 1) * 128, C_VR:C_VR + 768]), writes=[vt_b])
                    kz, kz_b = kz_r.next()
                    zb = zeta[:, dr * 6:(dr + 1) * 6].unsqueeze(2).broadcast_to([128, 6, 64])
                    Sc.op("gpsimd", lambda e, kz=kz, kt=kt, zb=zb: e.tensor_tensor(out=kz[:], in0=kt[:, :].rearrange("p (h d) -> p h d", d=64), in1=zb, op=ALU.mult), reads=[kt_b, cb_], writes=[kz_b])
                    p, p_b = pkv.next()
                    for h in range(6):
                        pp, hh = h // 2, h % 2
                        Sc.op("tensor", lambda e, p=p, kz=kz, vt=vt, h=h, pp=pp, hh=hh: e.matmul(p[hh * 64:(hh + 1) * 64, pp * 128:(pp + 1) * 128], lhsT=kz[:, h, :], rhs=vt[:, h * 128:(h + 1) * 128], start=True, stop=True),
                              reads=[kz_b, vt_b], writes=[p_b])
                    Sc.op("scalar", lambda e, dr=dr, c=c: e.activation(out=Sall[dr][:, c, :, :], in_=Scur[dr][:], func=AF.Copy), reads=[scur_b[dr]], writes=[sall_b[dr][c]])
                    for pp in range(3):
                        Sc.op("vector", lambda e, dr=dr, pp=pp, p=p: e.scalar_tensor_tensor(out=Scur[dr][:, pp, :], in0=Scur[dr][:, pp, :], scalar=g128[:, dr * 3 + pp:dr * 3 + pp + 1], in1=p[:, pp * 128:(pp + 1) * 128], op0=ALU.mult, op1=ALU.add),
                              reads=[p_b, scur_b[dr], cb_], writes=[scur_b[dr]])

            qt_r = Ring([(sb("qTp%d" % i, [128, 3, 128], BF16), Buf("qTp")) for i in range(3)])
            ktp_r = Ring([(sb("kTp%d" % i, [128, 3, 128], BF16), Buf("kTp")) for i in range(3)])
            gr_r = Ring([(sb("gr%d" % i, [128, 768], BF16), Buf("gr")) for i in range(3)])
            qz_r = []
            for i in range(2):
                t3 = [sb("qz%d_%d" % (i, j), [128, 6, 128], BF16) for j in range(3)]
                b3 = [Buf("qz") for j in range(3)]
                for j in range(3):
                    Sc.op("gpsimd", lambda e, t=t3[j]: e.memset(t[:], 0.0), writes=[b3[j]])
                qz_r.append((t3, b3))
            qz_r = Ring(qz_r)
            pst = Ring([(ps("pst%d" % i, [128, 512], F32), Buf("pst")) for i in range(2)])
            pyy = Ring([(ps("pyy%d" % i, [128, 512], F32), Buf("pyy")) for i in range(4)])
            A_r = Ring([(sb("A%d" % i, [128, 6, 128], BF16), Buf("A")) for i in range(2)])
            ysb_r = Ring([(sb("ysb%d" % i, [128, 768], F32), Buf("ysb")) for i in range(2)])
            ysq_r = Ring([(sb("ysq%d" % i, [128, 768], F32), Buf("ysq")) for i in range(2)])
            st_r = Ring([(sb("stat%d" % i, [128, 4, 6], F32), Buf("stat")) for i in range(2)])
            yo_r = Ring([(sb("yo%d" % i, [128, 768], BF16), Buf("yo")) for i in range(2)])
            yr_b = Buf("YR")
            def stage1(c):
                qt, qt_b = qt_r.next()
                ktp, ktp_b = ktp_r.next()
                vt, vt_b = vt_r.next()
                gr, gr_b = gr_r.next()
                for pp in range(3):
                    Sc.dma("sync", lambda e, qt=qt, c=c, pp=pp: e.dma_start_transpose(out=qt[:, pp, :], in_=self.PROJ[c * 128:(c + 1) * 128, C_QR + pp * 128:C_QR + (pp + 1) * 128]), writes=[qt_b])
                    Sc.dma("sync", lambda e, ktp=ktp, c=c, pp=pp: e.dma_start_transpose(out=ktp[:, pp, :], in_=self.PROJ[c * 128:(c + 1) * 128, C_KR + pp * 128:C_KR + (pp + 1) * 128]), writes=[ktp_b])
                Sc.dma("sync", lambda e, vt=vt, c=c: e.dma_start(out=vt[:], in_=self.PROJ[c * 128:(c + 1) * 128, C_VR:C_VR + 768]), writes=[vt_b])
                Sc.dma("sync", lambda e, gr=gr, c=c: e.dma_start(out=gr[:], in_=self.PROJ[c * 128:(c + 1) * 128, C_GR:C_GR + 768]), writes=[gr_b])
                (qz, qxf, qxb), (qz_b, qxf_b, qxb_b) = qz_r.next()
                for hh in range(2):
                    sl = slice(hh * 64, (hh + 1) * 64)
                    hv = lambda t, sl=sl, hh=hh: t[sl, :, :].rearrange("p (pp two) n -> p pp two n", two=2)[:, :, hh, :]
                    Sc.op("vector", lambda e, qz=qz, qt=qt, sl=sl, hv=hv: e.tensor_copy(out=hv(qz), in_=qt[sl, :, :]), reads=[qt_b], writes=[qz_b])
                    Sc.op("gpsimd", lambda e, qxf=qxf, qt=qt, sl=sl, hv=hv: e.tensor_tensor(out=hv(qxf), in0=qt[sl, :, :], in1=XiF[sl, :, :], op=ALU.mult), reads=[qt_b, cb_], writes=[qxf_b])
                    Sc.op("vector", lambda e, qxb=qxb, qt=qt, sl=sl, hv=hv: e.tensor_tensor(out=hv(qxb), in0=qt[sl, :, :], in1=XiB[sl, :, :], op=ALU.mult), reads=[qt_b, cb_], writes=[qxb_b])
                (s0, s0_b), (s1, s1_b) = pst.next(), pst.next()
                for h in range(6):
                    pp = h // 2
                    tgt, tb_ = (s0, s0_b) if h < 4 else (s1, s1_b)
                    col = (h % 4) * 128
                    Sc.op("tensor", lambda e, tgt=tgt, ktp=ktp, qz=qz, h=h, pp=pp, col=col: e.matmul(tgt[:, col:col + 128], lhsT=ktp[:, pp, :], rhs=qz[:, h, :], start=True, stop=True), reads=[ktp_b, qz_b], writes=[tb_])
                A, A_b = A_r.next()
                Sc.op("vector", lambda e, A=A, s0=s0: e.tensor_tensor(out=A[:, 0:4, :], in0=s0[:, :].rearrange("p (h n) -> p h n", n=128), in1=Mall[:, 0:4, :], op=ALU.mult), reads=[s0_b, cb_], writes=[A_b])
                Sc.op("vector", lambda e, A=A, s1=s1: e.tensor_tensor(out=A[:, 4:6, :], in0=s1[:, 0:256].rearrange("p (h n) -> p h n", n=128), in1=Mall[:, 4:6, :], op=ALU.mult), reads=[s1_b, cb_], writes=[A_b])
                return dict(c=c, qxf=qxf, qxb=qxb, qxf_b=qxf_b, qxb_b=qxb_b, A=A, A_b=A_b, vt=vt, vt_b=vt_b, gr=gr, gr_b=gr_b)

            def stage2(cx):
                c, qxf, qxb, qxf_b, qxb_b, A, A_b, vt, vt_b, gr, gr_b = (cx[k] for k in ("c", "qxf", "qxb", "qxf_b", "qxb_b", "A", "A_b", "vt", "vt_b", "gr", "gr_b"))
                (y0, y0_b), (y1, y1_b) = pyy.next(), pyy.next()
                for h in range(6):
                    pp = h // 2
                    tgt, tb_ = (y0, y0_b) if h < 4 else (y1, y1_b)
                    col = (h % 4) * 128
                    Sc.op("tensor", lambda e, tgt=tgt, A=A, vt=vt, h=h, col=col: e.matmul(tgt[:, col:col + 128], lhsT=A[:, h, :], rhs=vt[:, h * 128:(h + 1) * 128], start=True, stop=False), reads=[A_b, vt_b], writes=[tb_])
                    Sc.op("tensor", lambda e, tgt=tgt, qxf=qxf, h=h, pp=pp, c=c, col=col: e.matmul(tgt[:, col:col + 128], lhsT=qxf[:, h, :], rhs=Sall[0][:, c, pp, :], start=False, stop=False), reads=[qxf_b, sall_b[0][c]], writes=[tb_])
                    Sc.op("tensor", lambda e, tgt=tgt, qxb=qxb, h=h, pp=pp, c=c, col=col: e.matmul(tgt[:, col:col + 128], lhsT=qxb[:, h, :], rhs=Sall[1][:, c, pp, :], start=False, stop=True), reads=[qxb_b, sall_b[1][c]], writes=[tb_])
                ysb, ysb_b = ysb_r.next()
                ysq, ysq_b = ysq_r.next()
                stt_, stt_b = st_r.next()
                Sc.op("scalar", lambda e, ysb=ysb, y0=y0: e.activation(out=ysb[:, 0:512], in_=y0[:, :], func=AF.Copy), reads=[y0_b], writes=[ysb_b])
                Sc.op("scalar", lambda e, ysb=ysb, y1=y1: e.activation(out=ysb[:, 512:768], in_=y1[:, 0:256], func=AF.Copy), reads=[y1_b], writes=[ysb_b])
                y3 = ysb[:, :].rearrange("p (h e) -> p h e", e=128)
                Sc.op("gpsimd", lambda e, ysq=ysq, ysb=ysb: e.tensor_tensor(out=ysq[:], in0=ysb[:], in1=ysb[:], op=ALU.mult), reads=[ysb_b], writes=[ysq_b])
                Sc.op("vector", lambda e, stt_=stt_, y3=y3: e.tensor_reduce(out=stt_[:, 0, :], in_=y3, axis=AX.X, op=ALU.add), reads=[ysb_b], writes=[stt_b])
                Sc.op("vector", lambda e, stt_=stt_, ysq=ysq: e.tensor_reduce(out=stt_[:, 1, :], in_=ysq[:, :].rearrange("p (h e) -> p h e", e=128), axis=AX.X, op=ALU.add), reads=[ysq_b], writes=[stt_b])
                Sc.op("gpsimd", lambda e, stt_=stt_: e.tensor_scalar(out=stt_[:, 0, :], in0=stt_[:, 0, :], scalar1=1.0 / 128, scalar2=None, op0=ALU.mult), reads=[stt_b], writes=[stt_b])
                Sc.op("gpsimd", lambda e, stt_=stt_: e.tensor_tensor(out=stt_[:, 2, :], in0=stt_[:, 0, :], in1=stt_[:, 0, :], op=ALU.mult), reads=[stt_b], writes=[stt_b])
                Sc.op("gpsimd", lambda e, stt_=stt_: e.tensor_scalar(out=stt_[:, 1, :], in0=stt_[:, 1, :], scalar1=1.0 / 128, scalar2=EPS, op0=ALU.mult, op1=ALU.add), reads=[stt_b], writes=[stt_b])
                Sc.op("gpsimd", lambda e, stt_=stt_: e.tensor_tensor(out=stt_[:, 1, :], in0=stt_[:, 1, :], in1=stt_[:, 2, :], op=ALU.subtract), reads=[stt_b], writes=[stt_b])
                Sc.op("gpsimd", lambda e, stt_=stt_: e.tensor_tensor(out=stt_[:, 3, :], in0=stt_[:, 1, :], in1=negh[:, :], op=ALU.pow), reads=[stt_b, cb_], writes=[stt_b])
                mb = stt_[:, 0, :].unsqueeze(2).broadcast_to([128, 6, 128])
                rb = stt_[:, 3, :].unsqueeze(2).broadcast_to([128, 6, 128])
                Sc.op("vector", lambda e, y3=y3, mb=mb: e.tensor_tensor(out=y3, in0=y3, in1=mb, op=ALU.subtract), reads=[stt_b, ysb_b], writes=[ysb_b])
                Sc.op("vector", lambda e, y3=y3, rb=rb: e.tensor_tensor(out=y3, in0=y3, in1=rb, op=ALU.mult), reads=[stt_b, ysb_b], writes=[ysb_b])
                Sc.op("gpsimd", lambda e, ysb=ysb: e.tensor_tensor(out=ysb[:], in0=ysb[:], in1=gret[:], op=ALU.mult), reads=[ysb_b, cb_], writes=[ysb_b])
                yo, yo_b = yo_r.next()
                Sc.op("gpsimd", lambda e, yo=yo, ysb=ysb, gr=gr: e.tensor_tensor(out=yo[:], in0=ysb[:], in1=gr[:], op=ALU.mult), reads=[ysb_b, gr_b], writes=[yo_b])
                Sc.dma("gpsimd", lambda e, yo=yo, c=c: e.dma_start(out=self.YR[c * 128:(c + 1) * 128, :], in_=yo[:]), reads=[yo_b], writes=[yr_b])

            prev = None
            for c in range(NT):
                cx = stage1(c)
                if prev is not None:
                    stage2(prev)
                prev = cx
            stage2(prev)
            Sc.barrier()
            Sc.emit(st)

    def phase_b3(self):
        nc = self.nc
        with contextlib.ExitStack() as st:
            sb = lambda n, s, d: st.enter_context(nc.sbuf_tensor("b3_" + n, s, d))
            ps = lambda n, s, d: st.enter_context(nc.psum_tensor("b3_" + n, s, d))
            Sc = Sched(nc, prefix="b3_")
            cb_ = Buf("const")
            identf = sb("identf", [128, 128], F32)
            ident = sb("ident", [128, 128], BF16)
            ones = sb("ones", [128, 128], BF16)
            gmem = sb("gmem", [128, D], F32)
            negh = sb("negh", [128, 1], F32)
            wkv = sb("wkv", [128, 8, 1024], BF16)
            wkv_b = Buf("wkv")
            memT = sb("memT", [128, 8, 256], BF16)
            memT_b = Buf("memT")
            kmT = sb("kmT", [128, 4, 256], BF16)
            vm = sb("vm", [128, 2, 512], BF16)
            kv_b = Buf("kv")
            Sc.dma("sync", lambda e: e.dma_start(out=identf[:], in_=self.c_ident[:, :]), writes=[cb_])
            Sc.dma("sync", lambda e: e.dma_start(out=gmem[:], in_=self.g_mem[0, :].partition_broadcast(128)), writes=[cb_])
            Sc.dma("gpsimd", lambda e: e.dma_start(out=wkv[:], in_=self.w_mem_kv[0, :, :].rearrange("(k p) c -> p k c", p=128)), writes=[wkv_b])
            Sc.op("vector", lambda e: e.tensor_copy(out=ident[:], in_=identf[:]), reads=[cb_], writes=[cb_])
            Sc.op("vector", lambda e: e.memset(ones[:], 1.0), reads=[cb_], writes=[cb_])
            Sc.op("vector", lambda e: e.memset(negh[:], -0.5), reads=[cb_], writes=[cb_])
            mt_r = Ring([(sb("mt%d" % i, [128, D], F32), Buf("mt")) for i in range(2)])
            mg_r = Ring([(sb("mg%d" % i, [128, D], BF16), Buf("mg")) for i in range(2)])
            junk = sb("junk", [128, D], BF16)
            junk_b = Buf("junk")
            mst = sb("mst", [128, 4], F32)
            mst_b = Buf("mst")
            pT = Ring([(ps("pT%d" % i, [128, 1024], BF16), Buf("pT")) for i in range(1)])
            pA = Ring([(ps("pA%d" % i, [128, 512], F32), Buf("pA")) for i in range(7)])
            Sc.op("vector", lambda e: e.memset(mst[:], 0.0), writes=[mst_b])
            for t in range(2):
                m, m_b = mt_r.next()
                Sc.dma("sync", lambda e, m=m, t=t: e.dma_start(out=m[:], in_=self.mem[t * 128:(t + 1) * 128, :]), writes=[m_b])
                Sc.op("scalar", lambda e, m=m, t=t: e.activation(out=junk[:], in_=m[:], func=AF.Square, accum_out=mst[:, t:t + 1]), reads=[m_b, mst_b], writes=[junk_b, mst_b])
                Sc.op("gpsimd", lambda e, t=t: e.tensor_scalar(out=mst[:, t:t + 1], in0=mst[:, t:t + 1], scalar1=1.0 / D, scalar2=EPS, op0=ALU.mult, op1=ALU.add), reads=[mst_b], writes=[mst_b])
                Sc.op("gpsimd", lambda e, t=t: e.tensor_tensor(out=mst[:, 2 + t:3 + t], in0=mst[:, t:t + 1], in1=negh[:, 0:1], op=ALU.pow), reads=[mst_b, cb_], writes=[mst_b])
                g, g_b = mg_r.next()
                Sc.op("vector", lambda e, g=g, m=m, t=t: e.scalar_tensor_tensor(out=g[:], in0=m[:], scalar=mst[:, 2 + t:3 + t], in1=gmem[:], op0=ALU.mult, op1=ALU.mult), reads=[m_b, mst_b, cb_], writes=[g_b])
                p, p_b = pT.next()
                for k in range(8):
                    Sc.op("tensor", lambda e, p=p, g=g, k=k: e.transpose(out=p[:, k * 128:(k + 1) * 128], in_=g[:, k * 128:(k + 1) * 128], identity=ident[:]), reads=[g_b, cb_], writes=[p_b])
                Sc.op("vector", lambda e, p=p, t=t: e.tensor_copy(out=memT[:, :, t * 128:(t + 1) * 128], in_=p[:, :].rearrange("p (k c) -> p k c", k=8)), reads=[p_b], writes=[memT_b])
            for h in range(4):
                p, p_b = pA.next()
                for k in range(8):
                    Sc.op("tensor", lambda e, p=p, h=h, k=k: e.matmul(p[:, 0:256], lhsT=wkv[:, k, h * 128:(h + 1) * 128], rhs=memT[:, k, :], start=(k == 0), stop=(k == 7)), reads=[wkv_b, memT_b], writes=[p_b])
                Sc.op("scalar", lambda e, p=p, h=h: e.activation(out=kmT[:, h, :], in_=p[:, 0:256], func=AF.Copy), reads=[p_b], writes=[kv_b])
            for t in range(2):
                p, p_b = pA.next()
                for k in range(8):
                    Sc.op("tensor", lambda e, p=p, t=t, k=k: e.matmul(p[:, :], lhsT=memT[:, k, t * 128:(t + 1) * 128], rhs=wkv[:, k, 512:1024], start=(k == 0), stop=(k == 7)), reads=[wkv_b, memT_b], writes=[p_b])
                Sc.op("scalar", lambda e, p=p, t=t: e.activation(out=vm[:, t, :], in_=p[:, :], func=AF.Copy), reads=[p_b], writes=[kv_b])
            qm_r = Ring([(sb("qm%d" % i, [128, 512], BF16), Buf("qm")) for i in range(3)])
            E_r = Ring([(sb("E%d" % i, [128, 2, 512], BF16), Buf("E")) for i in range(2)])
            R_r = Ring([(sb("R%d" % i, [128, 512], F32), Buf("R")) for i in range(2)])
            y_r = Ring([(sb("y%d" % i, [128, 512], BF16), Buf("y")) for i in range(3)])
            ymt_b = Buf("YMT")
            sc = 1.0 / float(np.sqrt(128.0))
            for tb in range(NB):
                for h in range(4):
                    q, q_b = qm_r.next()
                    Sc.dma("sync", lambda e, q=q, tb=tb, h=h: e.dma_start_transpose(out=q[:, :], in_=self.PROJ[tb * 512:(tb + 1) * 512, C_QM + h * 128:C_QM + (h + 1) * 128]), writes=[q_b])
                    E, E_b = E_r.next()
                    for mc in range(2):
                        p, p_b = pA.next()
                        Sc.op("tensor", lambda e, p=p, h=h, mc=mc, q=q: e.matmul(p[:, :], lhsT=kmT[:, h, mc * 128:(mc + 1) * 128], rhs=q[:, :], start=True, stop=True), reads=[kv_b, q_b], writes=[p_b])
                        Sc.op("scalar", lambda e, E=E, p=p, mc=mc: e.activation(out=E[:, mc, :], in_=p[:, :], func=AF.Exp, scale=sc), reads=[p_b], writes=[E_b])
                    pu, pu_b = pA.next()
                    pd, pd_b = pA.next()
                    for mc in range(2):
                        Sc.op("tensor", lambda e, pu=pu, E=E, mc=mc, h=h: e.matmul(pu[:, :], lhsT=vm[:, mc, h * 128:(h + 1) * 128], rhs=E[:, mc, :], start=(mc == 0), stop=(mc == 1)), reads=[kv_b, E_b], writes=[pu_b])
                    for mc in range(2):
                        Sc.op("tensor", lambda e, pd=pd, E=E, mc=mc: e.matmul(pd[:, :], lhsT=ones[:, :], rhs=E[:, mc, :], start=(mc == 0), stop=(mc == 1)), reads=[cb_, E_b], writes=[pd_b])
                    R, R_b = R_r.next()
                    Sc.op("vector", lambda e, R=R, pd=pd: e.reciprocal(out=R[:, :], in_=pd[:, :]), reads=[pd_b], writes=[R_b])
                    y, y_b = y_r.next()
                    Sc.op("vector", lambda e, y=y, R=R, pu=pu: e.tensor_tensor(out=y[:, :], in0=pu[:, :], in1=R[:, :], op=ALU.mult), reads=[pu_b, R_b], writes=[y_b])
                    Sc.dma("gpsimd", lambda e, y=y, h=h, tb=tb: e.dma_start(out=self.YMT[h * 128:(h + 1) * 128, tb * 512:(tb + 1) * 512], in_=y[:, :]), reads=[y_b], writes=[ymt_b])
            Sc.barrier()
            Sc.emit(st)

    def phase_c(self):
        nc = self.nc
        with contextlib.ExitStack() as st:
            sb = lambda n, s, d: st.enter_context(nc.sbuf_tensor("pc_" + n, s, d))
            ps = lambda n, s, d: st.enter_context(nc.psum_tensor("pc_" + n, s, d))
            Sc = Sched(nc, prefix="pc_")
            cb_ = Buf("const")
            w_b = Buf("w")
            identf = sb("identf", [128, 128], F32)
            ident = sb("ident", [128, 128], BF16)
            gffn = sb("gffn", [128, D], F32)
            negh = sb("negh", [128, 1], F32)
            wpa = sb("wpa", [128, 6, D], BF16)
            wpr = sb("wpr", [128, 6, D], BF16)
            wpm = sb("wpm", [128, 4, D], BF16)
            wo = sb("wo", [128, 8, D], BF16)
            Sc.dma("sync", lambda e: e.dma_start(out=identf[:], in_=self.c_ident[:, :]), writes=[cb_])
            Sc.dma("sync", lambda e: e.dma_start(out=gffn[:], in_=self.g_ffn[0, :].partition_broadcast(128)), writes=[cb_])
            Sc.dma("gpsimd", lambda e: e.dma_start(out=wpa[:], in_=self.w_pa[0, :, :].rearrange("(k p) c -> p k c", p=128)), writes=[w_b])
            Sc.dma("gpsimd", lambda e: e.dma_start(out=wpr[:], in_=self.w_pr[0, :, :].rearrange("(k p) c -> p k c", p=128)), writes=[w_b])
            Sc.dma("gpsimd", lambda e: e.dma_start(out=wpm[:], in_=self.w_pm[0, :, :].rearrange("(k p) c -> p k c", p=128)), writes=[w_b])
            Sc.dma("gpsimd", lambda e: e.dma_start(out=wo[:], in_=self.w_out[0, :, :].rearrange("(k p) c -> p k c", p=128)), writes=[w_b])
            Sc.op("vector", lambda e: e.tensor_copy(out=ident[:], in_=identf[:]), reads=[cb_], writes=[cb_])
            Sc.op("vector", lambda e: e.memset(negh[:], -0.5), reads=[cb_], writes=[cb_])
            ya_r = Ring([(sb("ya%d" % i, [128, 6, 512], BF16), Buf("ya")) for i in range(2)])
            yr_r = Ring([(sb("yr%d" % i, [128, 6, 512], BF16), Buf("yr")) for i in range(2)])
            ym_r = Ring([(sb("ym%d" % i, [128, 4, 512], BF16), Buf("ym")) for i in range(2)])
            gt_r = Ring([(sb("gt%d" % i, [128, 3, 512], BF16), Buf("gt")) for i in range(3)])
            mg_r = Ring([(sb("mg%d" % i, [128, 8, 512], BF16), Buf("mg")) for i in range(2)])
            m1_r = Ring([(sb("m1_%d" % i, [128, 512], F32), Buf("m1")) for i in range(2)])
            m2_r = Ring([(sb("m2_%d" % i, [128, 512], F32), Buf("m2")) for i in range(2)])
            m3_r = Ring([(sb("m3_%d" % i, [128, 512], F32), Buf("m3")) for i in range(2)])
            x_r = Ring([(sb("x%d" % i, [128, D], F32), Buf("x")) for i in range(2)])
            h_r = Ring([(sb("h%d" % i, [128, D], F32), Buf("h")) for i in range(2)])
            hn_r = Ring([(sb("hn%d" % i, [128, D], BF16), Buf("hn")) for i in range(2)])
            ht_r = Ring([(sb("ht%d" % i, [128, 8, 128], BF16), Buf("ht")) for i in range(2)])
            junk = sb("junk", [128, D], BF16)
            junk_b = Buf("junk")
            stt_ = sb("stat", [128, 3, NT], F32)
            st_b = [Buf("st") for _ in range(NT)]
            Sc.op("vector", lambda e: e.memset(stt_[:], 0.0), writes=st_b)
            pP = Ring([(ps("pP%d" % i, [128, 512], F32), Buf("pP")) for i in range(5)])
            pO = Ring([(ps("pO%d" % i, [128, 512], F32), Buf("pO")) for i in range(2)])
            pT = Ring([(ps("pT%d" % i, [128, 1024], BF16), Buf("pT")) for i in range(1)])
            pend = []
            h_out_b = Buf("H")
            hnt_b = Buf("HNT")
            def load_y(tb):
                cs = slice(tb * 512, (tb + 1) * 512)
                ya, ya_b = ya_r.next()
                yr, yr_b = yr_r.next()
                ym, ym_b = ym_r.next()
                Sc.dma("sync", lambda e, ya=ya, cs=cs: e.dma_start(out=ya[:], in_=self.YAT[:, cs].rearrange("(k p) s -> p k s", p=128)), writes=[ya_b])
                Sc.dma("sync", lambda e, ym=ym, cs=cs: e.dma_start(out=ym[:], in_=self.YMT[:, cs].rearrange("(k p) s -> p k s", p=128)), writes=[ym_b])
                for k in range(6):
                    Sc.dma("sync", lambda e, yr=yr, tb=tb, k=k: e.dma_start_transpose(out=yr[:, k, :], in_=self.YR[tb * 512:(tb + 1) * 512, k * 128:(k + 1) * 128]), writes=[yr_b])
                return (ya, ya_b, yr, yr_b, ym, ym_b)
            ynext = load_y(0)
            for tb in range(NB):
                cs = slice(tb * 512, (tb + 1) * 512)
                ya, ya_b, yr, yr_b, ym, ym_b = ynext
                if tb + 1 < NB:
                    ynext = load_y(tb + 1)
                mg, mg_b = mg_r.next()
                for fc in range(8):
                    gt, gt_b = gt_r.next()
                    Sc.dma("sync", lambda e, gt=gt, fc=fc, cs=cs: e.dma_start(out=gt[:], in_=self.GTT.rearrange("(i f) s -> f i s", i=3)[fc * 128:(fc + 1) * 128, :, cs]), writes=[gt_b])
                    prs = []
                    for (w, src, src_b, nk) in ((wpa, ya, ya_b, 6), (wpr, yr, yr_b, 6), (wpm, ym, ym_b, 4)):
                        p, p_b = pP.next()
                        for k in range(nk):
                            Sc.op("tensor", lambda e, p=p, w=w, src=src, k=k, fc=fc, nk=nk: e.matmul(p[:, :], lhsT=w[:, k, fc * 128:(fc + 1) * 128], rhs=src[:, k, :], start=(k == 0), stop=(k == nk - 1)), reads=[w_b, src_b], writes=[p_b])
                        prs.append((p, p_b))
                    m1, m1_b = m1_r.next()
                    m2, m2_b = m2_r.next()
                    m3, m3_b = m3_r.next()
                    for i, (m, m_b) in enumerate(((m1, m1_b), (m2, m2_b), (m3, m3_b))):
                        p, p_b = prs[i]
                        Sc.op("vector", lambda e, m=m, gt=gt, i=i, p=p: e.scalar_tensor_tensor(out=m[:, :], in0=gt[:, i, :], scalar=1.0, in1=p[:, :], op0=ALU.add, op1=ALU.mult), reads=[gt_b, p_b], writes=[m_b])
                    Sc.op("gpsimd", lambda e, m1=m1, m2=m2: e.tensor_tensor(out=m1[:, :], in0=m1[:, :], in1=m2[:, :], op=ALU.add), reads=[m1_b, m2_b], writes=[m1_b])
                    Sc.op("gpsimd", lambda e, mg=mg, m1=m1, m3=m3, fc=fc: e.tensor_tensor(out=mg[:, fc, :], in0=m1[:, :], in1=m3[:, :], op=ALU.add), reads=[m1_b, m3_b], writes=[mg_b])
                for tt in range(4):
                    t = tb * 4 + tt
                    x, x_b = x_r.next()
                    Sc.dma("sync", lambda e, x=x, t=t: e.dma_start(out=x[:], in_=self.x[t * 128:(t + 1) * 128, :]), writes=[x_b])
                    h, h_b = h_r.next()
                    for nh in range(2):
                        p, p_b = pO.next()
                        for k in range(8):
                            Sc.op("tensor", lambda e, p=p, mg=mg, k=k, tt=tt, nh=nh: e.matmul(p[:, :], lhsT=mg[:, k, tt * 128:(tt + 1) * 128], rhs=wo[:, k, nh * 512:(nh + 1) * 512], start=(k == 0), stop=(k == 7)), reads=[mg_b, w_b], writes=[p_b])
                        Sc.op("vector", lambda e, h=h, p=p, x=x, nh=nh: e.scalar_tensor_tensor(out=h[:, nh * 512:(nh + 1) * 512], in0=p[:, :], scalar=0.5, in1=x[:, nh * 512:(nh + 1) * 512], op0=ALU.mult, op1=ALU.add), reads=[p_b, x_b], writes=[h_b])
                    while pend:
                        pend.pop(0)()
                    Sc.dma("gpsimd", lambda e, h=h, t=t: e.dma_start(out=self.H[t * 128:(t + 1) * 128, :], in_=h[:]), reads=[h_b], writes=[h_out_b])
                    Sc.op("scalar", lambda e, h=h, t=t: e.activation(out=junk[:], in_=h[:], func=AF.Square, accum_out=stt_[:, 0, t:t + 1]), reads=[h_b, st_b[t]], writes=[junk_b, st_b[t]])
                    Sc.op("gpsimd", lambda e, t=t: e.tensor_scalar(out=stt_[:, 1, t:t + 1], in0=stt_[:, 0, t:t + 1], scalar1=1.0 / D, scalar2=EPS, op0=ALU.mult, op1=ALU.add), reads=[st_b[t]], writes=[st_b[t]])
                    Sc.op("gpsimd", lambda e, t=t: e.tensor_tensor(out=stt_[:, 2, t:t + 1], in0=stt_[:, 1, t:t + 1], in1=negh[:, 0:1], op=ALU.pow), reads=[st_b[t], cb_], writes=[st_b[t]])
                    hn, hn_b = hn_r.next()
                    Sc.op("vector", lambda e, hn=hn, h=h, t=t: e.scalar_tensor_tensor(out=hn[:], in0=h[:], scalar=stt_[:, 2, t:t + 1], in1=gffn[:], op0=ALU.mult, op1=ALU.mult), reads=[h_b, st_b[t], cb_], writes=[hn_b])
                    def tail(hn=hn, hn_b=hn_b, t=t):
                        p, p_b = pT.next()
                        for k in range(8):
                            Sc.op("tensor", lambda e, p=p, hn=hn, k=k: e.transpose(out=p[:, k * 128:(k + 1) * 128], in_=hn[:, k * 128:(k + 1) * 128], identity=ident[:]), reads=[hn_b, cb_], writes=[p_b])
                        ht, ht_b = ht_r.next()
                        Sc.op("scalar", lambda e, ht=ht, p=p: e.activation(out=ht[:], in_=p[:, :].rearrange("p (k c) -> p k c", k=8), func=AF.Copy), reads=[p_b], writes=[ht_b])
                        Sc.dma("gpsimd", lambda e, ht=ht, t=t: e.dma_start(out=self.HNT[:, t * 128:(t + 1) * 128].rearrange("(k p) s -> p k s", p=128), in_=ht[:]), reads=[ht_b], writes=[hnt_b])
                    pend.append(tail)
            while pend:
                pend.pop(0)()
            Sc.barrier()
            Sc.emit(st)

    def phase_d(self):
        nc = self.nc
        NF = DFF // 128
        with contextlib.ExitStack() as st:
            sb = lambda n, s, d: st.enter_context(nc.sbuf_tensor("d_" + n, s, d))
            ps = lambda n, s, d: st.enter_context(nc.psum_tensor("d_" + n, s, d))
            Sc = Sched(nc, prefix="d_")
            cb_ = Buf("const")
            hnT = sb("hnT", [128, 8, S], BF16)
            hn_b = [Buf("hnT") for _ in range(NB)]
            for tb in range(NB):
                Sc.dma("sync", lambda e, tb=tb: e.dma_start(out=hnT[:, :, tb * 512:(tb + 1) * 512], in_=self.HNT[:, tb * 512:(tb + 1) * 512].rearrange("(k p) s -> p k s", p=128)), writes=[hn_b[tb]])
            cw = sb("cw", [128, 2, 3, NF], F32)
            cbias = sb("cbias", [128, 2, NF], F32)
            for ab in range(2):
                for j in range(3):
                    Sc.dma("sync", lambda e, ab=ab, j=j: e.dma_start(out=cw[:, ab, j, :], in_=self.conv_w[0, j, ab * DFF:(ab + 1) * DFF].rearrange("(f p) -> p f", p=128), allow_slow_non_contiguous=True), writes=[cb_])
                Sc.dma("sync", lambda e, ab=ab: e.dma_start(out=cbias[:, ab, :], in_=self.conv_b[0, ab * DFF:(ab + 1) * DFF].rearrange("(f p) -> p f", p=128), allow_slow_non_contiguous=True), writes=[cb_])
            w_r = Ring([(sb("w%d" % i, [128, 8, 256], BF16), Buf("w")) for i in range(2)])
            u_r = []
            for i in range(2):
                ua = sb("ua%d" % i, [128, S + 2], F32)
                ub = sb("ub%d" % i, [128, S + 2], F32)
                bl = [Buf("u") for _ in range(NB + 1)]
                for t_ in (ua, ub):
                    Sc.op("gpsimd", lambda e, t_=t_: e.memset(t_[:, 0:1], 0.0), writes=[bl[NB]])
                    Sc.op("gpsimd", lambda e, t_=t_: e.memset(t_[:, S + 1:S + 2], 0.0), writes=[bl[NB]])
                u_r.append(((ua, ub), bl))
            u_r = Ring(u_r)
            ca = sb("ca", [128, S], F32)
            cbb = sb("cb", [128, S], F32)
            th = sb("th", [128, S], F32)
            ca_b, cbb_b, th_b = Buf("ca"), Buf("cb"), Buf("th")
            g_r = Ring([(sb("g%d" % i, [128, S], BF16), Buf("g")) for i in range(2)])
            pA = Ring([(ps("pA%d" % i, [128, 512], F32), Buf("pA")) for i in range(8)])
            gt2_b = Buf("GT2")
            for fc in range(NF):
                w, w_b = w_r.next()
                Sc.dma("gpsimd", lambda e, w=w, fc=fc: e.dma_start(out=w[:, :, 0:128], in_=self.w_up[0, :, fc * 128:(fc + 1) * 128].rearrange("(k p) c -> p k c", p=128)), writes=[w_b])
                Sc.dma("gpsimd", lambda e, w=w, fc=fc: e.dma_start(out=w[:, :, 128:256], in_=self.w_up[0, :, DFF + fc * 128:DFF + (fc + 1) * 128].rearrange("(k p) c -> p k c", p=128)), writes=[w_b])
                (ua, ub), ubl = u_r.next()
                for tb in range(NB):
                    for ab, ut in ((0, ua), (1, ub)):
                        p, p_b = pA.next()
                        for k in range(8):
                            Sc.op("tensor", lambda e, p=p, w=w, k=k, ab=ab, tb=tb: e.matmul(p[:, :], lhsT=w[:, k, ab * 128:(ab + 1) * 128], rhs=hnT[:, k, tb * 512:(tb + 1) * 512], start=(k == 0), stop=(k == 7)), reads=[w_b, hn_b[tb]], writes=[p_b])
                        Sc.op("scalar", lambda e, ut=ut, p=p, tb=tb: e.activation(out=ut[:, 1 + tb * 512:1 + (tb + 1) * 512], in_=p[:, :], func=AF.Copy), reads=[p_b], writes=[ubl[tb]])
                for ab, ut, ct, ct_b, eng in ((0, ua, ca, ca_b, "vector"), (1, ub, cbb, cbb_b, "vector")):
                    Sc.op("scalar", lambda e, ut=ut, ct=ct, ab=ab, fc=fc: e.activation(out=ct[:, :], in_=ut[:, 0:S], func=AF.Identity, bias=cbias[:, ab, fc:fc + 1], scale=cw[:, ab, 0, fc:fc + 1]), reads=ubl + [cb_], writes=[ct_b])
                    Sc.op(eng, lambda e, ut=ut, ct=ct, ab=ab, fc=fc: e.scalar_tensor_tensor(out=ct[:, :], in0=ut[:, 1:S + 1], scalar=cw[:, ab, 1, fc:fc + 1], in1=ct[:, :], op0=ALU.mult, op1=ALU.add), reads=ubl + [cb_, ct_b], writes=[ct_b])
                    Sc.op(eng, lambda e, ut=ut, ct=ct, ab=ab, fc=fc: e.scalar_tensor_tensor(out=ct[:, :], in0=ut[:, 2:S + 2], scalar=cw[:, ab, 2, fc:fc + 1], in1=ct[:, :], op0=ALU.mult, op1=ALU.add), reads=ubl + [cb_, ct_b], writes=[ct_b])
                Sc.op("scalar", lambda e: e.activation(out=th[:, :], in_=ca[:, :], func=AF.Tanh, scale=0.5), reads=[ca_b], writes=[th_b])
                Sc.op("vector", lambda e: e.scalar_tensor_tensor(out=th[:, :], in0=th[:, :], scalar=1.0, in1=ca[:, :], op0=ALU.add, op1=ALU.mult), reads=[ca_b, th_b], writes=[th_b])
                g, g_b = g_r.next()
                Sc.op("vector", lambda e, g=g: e.tensor_tensor(out=g[:, :], in0=th[:, :], in1=cbb[:, :], op=ALU.mult), reads=[th_b, cbb_b], writes=[g_b])
                Sc.dma("sync", lambda e, g=g, fc=fc: e.dma_start(out=self.GT2[fc * 128:(fc + 1) * 128, :], in_=g[:, :]), reads=[g_b], writes=[gt2_b])
            Sc.barrier()
            Sc.emit(st)

    def phase_e(self):
        nc = self.nc
        NF = DFF // 128
        with contextlib.ExitStack() as st:
            sb = lambda n, s, d: st.enter_context(nc.sbuf_tensor("e_" + n, s, d))
            ps = lambda n, s, d: st.enter_context(nc.psum_tensor("e_" + n, s, d))
            Sc = Sched(nc, prefix="e_")
            cb_ = Buf("const")
            w_b = Buf("w")
            wd = sb("wd", [128, NF, D], BF16)
            gfin = sb("gfin", [128, D], F32)
            negh = sb("negh", [128, 1], F32)
            for q4 in range(2):
                Sc.dma("gpsimd", lambda e, q4=q4: e.dma_start(out=wd[:, q4 * 11:(q4 + 1) * 11, :], in_=self.w_down[0, q4 * 11 * 128:(q4 + 1) * 11 * 128, :].rearrange("(k p) c -> p k c", p=128)), writes=[w_b])
            Sc.dma("sync", lambda e: e.dma_start(out=gfin[:], in_=self.g_final.partition_broadcast(128)), writes=[cb_])
            Sc.op("vector", lambda e: e.memset(negh[:], -0.5), reads=[cb_], writes=[cb_])
            g_r = Ring([(sb("g%d" % i, [128, NF, 512], BF16), Buf("g")) for i in range(2)])
            h_r = Ring([(sb("h%d" % i, [128, D], F32), Buf("h")) for i in range(3)])
            o_r = Ring([(sb("o%d" % i, [128, D], F32), Buf("o")) for i in range(2)])
            junk = sb("junk", [128, D], BF16)
            junk_b = Buf("junk")
            stt_ = sb("stat", [128, 3, NT], F32)
            st_b = [Buf("st") for _ in range(NT)]
            Sc.op("vector", lambda e: e.memset(stt_[:], 0.0), writes=st_b)
            pO = Ring([(ps("pO%d" % i, [128, 512], F32), Buf("pO")) for i in range(6)])
            out_b = Buf("out")
            def load_g(tb):
                g, g_b = g_r.next()
                Sc.dma("sync", lambda e, g=g, tb=tb: e.dma_start(out=g[:], in_=self.GT2[:, tb * 512:(tb + 1) * 512].rearrange("(k p) s -> p k s", p=128)), writes=[g_b])
                return g, g_b
            gnext = load_g(0)
            for tb in range(NB):
                g, g_b = gnext
                if tb + 1 < NB:
                    gnext = load_g(tb + 1)
                for tt in range(4):
                    t = tb * 4 + tt
                    h, h_b = h_r.next()
                    Sc.dma("sync", lambda e, h=h, t=t: e.dma_start(out=h[:], in_=self.H[t * 128:(t + 1) * 128, :]), writes=[h_b])
                    for nh in range(2):
                        p, p_b = pO.next()
                        for k in range(NF):
                            Sc.op("tensor", lambda e, p=p, g=g, k=k, tt=tt, nh=nh: e.matmul(p[:, :], lhsT=g[:, k, tt * 128:(tt + 1) * 128], rhs=wd[:, k, nh * 512:(nh + 1) * 512], start=(k == 0), stop=(k == NF - 1)), reads=[g_b, w_b], writes=[p_b])
                        Sc.op("vector", lambda e, h=h, p=p, nh=nh: e.scalar_tensor_tensor(out=h[:, nh * 512:(nh + 1) * 512], in0=p[:, :], scalar=0.5, in1=h[:, nh * 512:(nh + 1) * 512], op0=ALU.mult, op1=ALU.add), reads=[p_b, h_b], writes=[h_b])
                    Sc.op("scalar", lambda e, h=h, t=t: e.activation(out=junk[:], in_=h[:], func=AF.Square, accum_out=stt_[:, 0, t:t + 1]), reads=[h_b, st_b[t]], writes=[junk_b, st_b[t]])
                    Sc.op("gpsimd", lambda e, t=t: e.tensor_scalar(out=stt_[:, 1, t:t + 1], in0=stt_[:, 0, t:t + 1], scalar1=1.0 / D, scalar2=EPS, op0=ALU.mult, op1=ALU.add), reads=[st_b[t]], writes=[st_b[t]])
                    Sc.op("gpsimd", lambda e, t=t: e.tensor_tensor(out=stt_[:, 2, t:t + 1], in0=stt_[:, 1, t:t + 1], in1=negh[:, 0:1], op=ALU.pow), reads=[st_b[t], cb_], writes=[st_b[t]])
                    o, o_b = o_r.next()
                    Sc.op("vector", lambda e, o=o, h=h, t=t: e.scalar_tensor_tensor(out=o[:], in0=h[:], scalar=stt_[:, 2, t:t + 1], in1=gfin[:], op0=ALU.mult, op1=ALU.mult), reads=[h_b, st_b[t], cb_], writes=[o_b])
                    Sc.dma("sync", lambda e, o=o, t=t: e.dma_start(out=self.out[t * 128:(t + 1) * 128, :], in_=o[:]), reads=[o_b], writes=[out_b])
            Sc.barrier()
            Sc.emit(st)

    def build(self):
        for ph in ("a", "b1", "b1c", "b2", "b3", "c", "d", "e"):
            if self.phases is not None and ph not in self.phases:
                continue
            fn = getattr(self, "phase_" + ph, None)
            if fn is not None:
                fn()
            if self.stop_after == ph:
                break
        return self.nc


def host_consts():
    inv = 10000.0 ** (-np.arange(0, 64, 2, dtype=np.float32) / 64.0)
    ang = np.arange(S, dtype=np.float32)[:, None] * inv[None, :].astype(np.float32)
    c = {
        "c_cos": np.cos(ang).astype(np.float32),
        "c_sin": np.sin(ang).astype(np.float32),
        "c_ident": np.eye(128, dtype=np.float32),
    }
    kk = np.arange(128)[:, None]
    qq = np.arange(256)[None, :]
    band = ((qq >= kk) & (qq <= kk + 128))
    m = np.where(band, 0.0, -30000.0).astype(np.float32)
    am = np.zeros((128, 1024), np.float32)
    am[:, 0:256] = m
    am[:, 256:512] = m
    mf = m[64:128, 128:256]
    ml = m[0:64, 0:128]
    am[0:64, 512:640] = mf
    am[0:64, 640:768] = mf
    am[0:64, 768:896] = ml
    am[0:64, 896:1024] = ml
    c["c_amask"] = am
    r = np.zeros((128, 8, 128), np.float32)
    mm = np.arange(128)[:, None].astype(np.float32)
    nn = np.arange(128)[None, :].astype(np.float32)
    r[:, 0, :] = np.maximum(nn - mm, 0)
    r[:, 1, :] = np.maximum(mm - nn, 0)
    r[:, 2, :] = (nn >= mm) * 0.125
    r[:, 3, :] = (mm > nn) * 0.125
    r[:, 4, :] = nn + 1.0
    r[:, 5, :] = 128.0 - nn
    r[:, 6, 0] = 127.0 - np.arange(128)
    r[:, 6, 1] = np.arange(128)
    c["c_ret"] = r
    return c


def make_in_maps(inputs, n_cores=8):
    consts = host_consts()
    maps = []
    for b in range(n_cores):
        m = {"x": np.ascontiguousarray(inputs["x"][b]), "mem": np.ascontiguousarray(inputs["mem"][b])}
        for k, v in inputs.items():
            if k in ("x", "mem"):
                continue
            m[k] = np.ascontiguousarray(v)
        m.update(consts)
        maps.append(m)
    return maps


def kernel(**inputs):
    inputs = {k: np.asarray(v) for k, v in inputs.items()}
    prog = Prog()
    nc = prog.build()
    res = run_bass_kernel_spmd(nc, make_in_maps(inputs), core_ids=list(range(8)))
    return np.stack([r["out"] for r in res.results], axis=0)
```

```python
import contextlib
import numpy as np
import concourse.bass as bass
import concourse.mybir as mybir
from concourse.bass_utils import run_bass_kernel_spmd

F32 = mybir.dt.float32
BF16 = mybir.dt.bfloat16
AF = mybir.ActivationFunctionType
ALU = mybir.AluOpType
AX = mybir.AxisListType

S = 4096
D = 1024
NT = S // 128
NB = S // 512
IN_W = 5120
DFF = 2816
EPS = 1e-6
C_QA, C_KA, C_VA, C_QR, C_KR, C_VR, C_GR, C_QM = 0, 768, 1536, 2304, 2688, 3072, 3840, 4608


class Buf:
    __slots__ = ("name", "w", "r")

    def __init__(self, name):
        self.name = name
        self.w = None
        self.r = []


class Sched:
    ENG = ("tensor", "vector", "scalar", "gpsimd", "sync")

    def __init__(self, nc, n_dma_sems=32, prefix=""):
        self.nc = nc
        self.prefix = prefix
        self.lists = {e: [] for e in self.ENG}
        self.cnt = {e: 0 for e in self.ENG}
        self.known = {e: {} for e in self.ENG}
        self.ndma = n_dma_sems
        self.dma_issued = [0] * n_dma_sems
        self.dma_rr = 0

    def _need(self, eng, ev, waits):
        if ev is None:
            return
        key, val = ev
        if key == eng and eng == "tensor":
            return
        if self.known[eng].get(key, 0) >= val:
            return
        if waits.get(key, 0) < val:
            waits[key] = val

    def _deps(self, eng, reads, writes):
        waits = {}
        for b in reads:
            self._need(eng, b.w, waits)
        for b in writes:
            self._need(eng, b.w, waits)
            for ev in b.r:
                self._need(eng, ev, waits)
        for k, v in waits.items():
            self.known[eng][k] = v
        return list(waits.items())

    def op(self, eng, fn, reads=(), writes=()):
        waits = self._deps(eng, reads, writes)
        self.cnt[eng] += 1
        ev = (eng, self.cnt[eng])
        self.lists[eng].append((waits, fn, eng, 1))
        for b in reads:
            b.r.append(ev)
        for b in writes:
            b.w = ev
            b.r = []
        return ev

    def dma(self, eng, fn, reads=(), writes=()):
        i = self.dma_rr
        self.dma_rr = (self.dma_rr + 1) % self.ndma
        key = ("dma", i)
        waits = dict(self._deps(eng, reads, writes))
        prev = self.dma_issued[i]
        if prev > 0 and self.known[eng].get(key, 0) < prev:
            waits[key] = prev
            self.known[eng][key] = prev
        self.dma_issued[i] = prev + 16
        ev = (key, prev + 16)
        self.lists[eng].append((list(waits.items()), fn, key, 16))
        for b in reads:
            b.r.append(ev)
        for b in writes:
            b.w = ev
            b.r = []
        return ev

    def barrier(self):
        for e in self.ENG:
            waits = {}
            for o in self.ENG:
                if o != e and self.cnt[o] > 0:
                    self._need(e, (o, self.cnt[o]), waits)
            if e != "tensor" and self.cnt[e] > 0:
                self._need(e, (e, self.cnt[e]), waits)
            for i in range(self.ndma):
                if self.dma_issued[i] > 0:
                    self._need(e, (("dma", i), self.dma_issued[i]), waits)
            for k, v in waits.items():
                self.known[e][k] = v
            self.lists[e].append((list(waits.items()), None, None, 0))

    def emit(self, stack):
        nc = self.nc
        semmap = {}
        handles = []
        for e in self.ENG:
            semmap[e] = nc.alloc_semaphore(name=self.prefix + "s_" + e)
            handles.append(semmap[e])
        for i in range(self.ndma):
            semmap[("dma", i)] = nc.alloc_semaphore(name=self.prefix + "s_dma%d" % i)
            handles.append(semmap[("dma", i)])

        def runner(items):
            def f(e):
                for waits, fn, key, inc in items:
                    for k, v in waits:
                        e.wait_ge(semmap[k], v)
                    if fn is not None:
                        fn(e).then_inc(semmap[key], inc)
            return f
        with nc.Block() as block:
            block.tensor(runner(self.lists["tensor"]))
            block.vector(runner(self.lists["vector"]))
            block.scalar(runner(self.lists["scalar"]))
            block.gpsimd(runner(self.lists["gpsimd"]))
            block.sync(runner(self.lists["sync"]))
        nc.clear_and_free_semaphores(handles)
        nc.all_engine_barrier()


class Ring:
    def __init__(self, items):
        self.items = items
        self.i = 0

    def next(self):
        it = self.items[self.i]
        self.i = (self.i + 1) % len(self.items)
        return it


class Prog:
    def __init__(self, debug=False, stop_after=None, phases=None, ext_in=()):
        self.debug = debug
        self.stop_after = stop_after
        self.phases = phases
        self.ext_in = set(ext_in)
        nc = self.nc = bass.Bass("TRN2", target_bir_lowering=False)
        ein = lambda n, s: nc.dram_tensor(n, s, F32, kind="ExternalInput").ap()
        self.x = ein("x", [S, D])
        self.mem = ein("mem", [256, D])
        self.g_mix = ein("g_mix", [1, D])
        self.w_in = ein("w_in", [1, D, IN_W])
        self.w_mem_kv = ein("w_mem_kv", [1, D, 1024])
        self.g_mem = ein("g_mem", [1, D])
        self.dec_f = ein("ret_decay_fwd", [1, 6])
        self.dec_b = ein("ret_decay_bwd", [1, 6])
        self.g_ret = ein("g_ret", [1, 768])
        self.w_pa = ein("w_proj_attn", [1, 768, D])
        self.w_pr = ein("w_proj_ret", [1, 768, D])
        self.w_pm = ein("w_proj_mem", [1, 512, D])
        self.w_gate = ein("w_gate", [1, D, 3 * D])
        self.b_gate = ein("b_gate", [1, 3 * D])
        self.w_out = ein("w_out", [1, D, D])
        self.g_ffn = ein("g_ffn", [1, D])
        self.w_up = ein("w_up", [1, D, 2 * DFF])
        self.conv_w = ein("conv_w", [1, 3, 2 * DFF])
        self.conv_b = ein("conv_b", [1, 2 * DFF])
        self.w_down = ein("w_down", [1, DFF, D])
        self.g_final = ein("g_final", [D])
        self.c_cos = ein("c_cos", [S, 32])
        self.c_sin = ein("c_sin", [S, 32])
        self.c_ident = ein("c_ident", [128, 128])
        self.c_amask = ein("c_amask", [128, 1024])
        self.c_ret = ein("c_ret", [128, 8, 128])
        self.out = nc.dram_tensor("out", [S, D], F32, kind="ExternalOutput").ap()
        kind = "ExternalOutput" if debug else "Internal"
        scr = lambda n, s, d: nc.dram_tensor(n, s, d, kind=("ExternalInput" if n in self.ext_in else kind)).ap()
        self.PROJ = scr("PROJ", [S, IN_W], BF16)
        self.GTT = scr("GTT", [3 * D, S], BF16)
        self.UD = scr("UD", [12 * 128, S], F32)
        self.YAT = scr("YAT", [768, S], BF16)
        self.YR = scr("YR", [S, 768], BF16)
        self.YMT = scr("YMT", [512, S], BF16)
        self.H = scr("H", [S, D], F32)
        self.HNT = scr("HNT", [D, S], BF16)
        self.GT2 = scr("GT2", [DFF, S], BF16)

    def phase_a(self):
        nc = self.nc
        with contextlib.ExitStack() as st:
            sb = lambda n, s, d: st.enter_context(nc.sbuf_tensor("a_" + n, s, d))
            ps = lambda n, s, d: st.enter_context(nc.psum_tensor("a_" + n, s, d))
            Sc = Sched(nc, prefix="a_")
            xT = sb("xT", [128, 8, S], BF16)
            xT_b = [Buf("xT%d" % t) for t in range(NT)]
            xr = Ring([(sb("xr%d" % i, [128, D], F32), Buf("xr%d" % i)) for i in range(3)])
            xg = Ring([(sb("xg%d" % i, [128, D], BF16), Buf("xg%d" % i)) for i in range(2)])
            junk = sb("junk", [128, D], BF16)
            junk_b = Buf("junk")
            gmix = sb("gmix", [128, D], F32)
            gmix_b = Buf("gmix")
            ssq = sb("ssq", [128, NT], F32)
            msq = sb("msq", [128, NT], F32)
            rstd = sb("rstd", [128, NT], F32)
            negh = sb("negh", [128, 1], F32)
            st_b = [Buf("st%d" % t) for t in range(NT)]
            const_b = Buf("const")
            identf = sb("identf", [128, 128], F32)
            ident = sb("ident", [128, 128], BF16)
            cos_t = sb("cos_t", [128, NT, 32], F32)
            sin_t = sb("sin_t", [128, NT, 32], F32)
            hb = sb("hb", [128, 24], F32)
            pT = Ring([(ps("pT%d" % i, [128, 1024], BF16), Buf("pT%d" % i)) for i in range(2)])
            pA = Ring([(ps("pA%d" % i, [128, 512], F32), Buf("pA%d" % i)) for i in range(6)])
            wr = Ring([(sb("w%d" % i, [128, 8, 512], BF16), Buf("w%d" % i)) for i in range(3)])
            ob = Ring([(sb("ob%d" % i, [128, 512], BF16), Buf("ob%d" % i)) for i in range(4)])
            tA = Ring([(sb("tA%d" % i, [128, 512], F32), Buf("tA%d" % i)) for i in range(3)])
            tB = Ring([(sb("tB%d" % i, [128, 512], F32), Buf("tB%d" % i)) for i in range(3)])
            proj_b = Buf("PROJ")
            gtt_b = Buf("GTT")

            Sc.dma("sync", lambda e: e.dma_start(out=gmix[:], in_=self.g_mix[0, :].partition_broadcast(128)), writes=[gmix_b])
            Sc.dma("sync", lambda e: e.dma_start(out=identf[:], in_=self.c_ident[:, :]), writes=[const_b])
            Sc.dma("sync", lambda e: e.dma_start(out=cos_t[:], in_=self.c_cos.rearrange("(t p) c -> p t c", p=128)), writes=[const_b])
            Sc.dma("sync", lambda e: e.dma_start(out=sin_t[:], in_=self.c_sin.rearrange("(t p) c -> p t c", p=128)), writes=[const_b])
            Sc.dma("sync", lambda e: e.dma_start(out=hb[:], in_=self.b_gate[0, :].rearrange("(f p) -> p f", p=128), allow_slow_non_contiguous=True), writes=[const_b])
            Sc.op("vector", lambda e: e.tensor_copy(out=ident[:], in_=identf[:]), reads=[const_b], writes=[const_b])
            Sc.op("vector", lambda e: e.tensor_scalar(out=hb[:], in0=hb[:], scalar1=0.5, scalar2=None, op0=ALU.mult), reads=[const_b], writes=[const_b])
            Sc.op("vector", lambda e: e.memset(ssq[:], 0.0), writes=st_b)
            Sc.op("vector", lambda e: e.memset(negh[:], -0.5), writes=[const_b])

            for t in range(NT):
                xt, xt_b = xr.next()
                Sc.dma("sync", lambda e, xt=xt, t=t: e.dma_start(out=xt[:], in_=self.x[t * 128:(t + 1) * 128, :]), writes=[xt_b])
                Sc.op("scalar", lambda e, xt=xt, t=t: e.activation(out=junk[:], in_=xt[:], func=AF.Square, accum_out=ssq[:, t:t + 1]),
                      reads=[xt_b], writes=[junk_b, st_b[t]])
                Sc.op("gpsimd", lambda e, t=t: e.tensor_scalar(out=msq[:, t:t + 1], in0=ssq[:, t:t + 1], scalar1=1.0 / D, scalar2=EPS, op0=ALU.mult, op1=ALU.add),
                      reads=[st_b[t]], writes=[st_b[t]])
                Sc.op("gpsimd", lambda e, t=t: e.tensor_tensor(out=rstd[:, t:t + 1], in0=msq[:, t:t + 1], in1=negh[:, 0:1], op=ALU.pow),
                      reads=[st_b[t], const_b], writes=[st_b[t]])
                g, g_b = xg.next()
                Sc.op("vector", lambda e, g=g, xt=xt, t=t: e.scalar_tensor_tensor(out=g[:], in0=xt[:], scalar=rstd[:, t:t + 1], in1=gmix[:], op0=ALU.mult, op1=ALU.mult),
                      reads=[xt_b, st_b[t], gmix_b], writes=[g_b])
                p, p_b = pT.next()
                for k in range(8):
                    Sc.op("tensor", lambda e, p=p, g=g, k=k: e.transpose(out=p[:, k * 128:(k + 1) * 128], in_=g[:, k * 128:(k + 1) * 128], identity=ident[:]),
                          reads=[g_b, const_b], writes=[p_b])
                eng = "scalar" if t % 2 == 0 else "vector"
                dst = xT[:, :, t * 128:(t + 1) * 128]
                src = p[:, :].rearrange("p (k c) -> p k c", k=8)
                if eng == "scalar":
                    Sc.op("scalar", lambda e, dst=dst, src=src: e.activation(out=dst, in_=src, func=AF.Copy), reads=[p_b], writes=[xT_b[t]])
                else:
                    Sc.op("vector", lambda e, dst=dst, src=src: e.tensor_copy(out=dst, in_=src), reads=[p_b], writes=[xT_b[t]])

            blocks = [(C_QA, 512, "rot"), (C_QA + 512, 256, "rot"), (C_KA, 512, "rot"), (C_KA + 512, 256, "rot"),
                      (C_VA, 512, "copy"), (C_VA + 512, 256, "copy"), (C_QR, 384, "rot"), (C_KR, 384, "rot"),
                      (C_VR, 512, "copy"), (C_VR + 512, 256, "copy"), (C_GR, 512, "silu2"), (C_GR + 512, 256, "silu2"),
                      (C_QM, 512, "copy")]
            wspecs = [(self.w_in[0, :, c0:c0 + N], N) for (c0, N, kind) in blocks] + [(self.w_gate[0, :, fg * 512:(fg + 1) * 512], 512) for fg in range(6)]
            wloaded = {}

            def ensure_w(i):
                if i < len(wspecs) and i not in wloaded:
                    w, w_b = wr.next()
                    src, N = wspecs[i]
                    Sc.dma("gpsimd", lambda e, w=w, src=src, N=N: e.dma_start(out=w[:, :, 0:N], in_=src.rearrange("(k p) c -> p k c", p=128)), writes=[w_b])
                    wloaded[i] = (w, w_b)
                return wloaded.get(i)
            for bi, (c0, N, kind) in enumerate(blocks):
                w, w_b = ensure_w(bi)
                ensure_w(bi + 1)
                for t in range(NT):
                    p, p_b = pA.next()
                    for k in range(8):
                        Sc.op("tensor", lambda e, p=p, w=w, t=t, k=k, N=N: e.matmul(p[:, 0:N], lhsT=xT[:, k, t * 128:(t + 1) * 128], rhs=w[:, k, 0:N], start=(k == 0), stop=(k == 7)),
                              reads=[xT_b[t], w_b], writes=[p_b])
                    o, o_b = ob.next()
                    if kind == "copy":
                        Sc.op("scalar", lambda e, o=o, p=p, N=N: e.activation(out=o[:, 0:N], in_=p[:, 0:N], func=AF.Copy), reads=[p_b], writes=[o_b])
                    elif kind == "silu2":
                        a, a_b = tA.next()
                        Sc.op("scalar", lambda e, a=a, p=p, N=N: e.activation(out=a[:, 0:N], in_=p[:, 0:N], func=AF.Tanh, scale=0.5), reads=[p_b], writes=[a_b])
                        Sc.op("vector", lambda e, o=o, a=a, p=p, N=N: e.scalar_tensor_tensor(out=o[:, 0:N], in0=a[:, 0:N], scalar=1.0, in1=p[:, 0:N], op0=ALU.add, op1=ALU.mult),
                              reads=[a_b, p_b], writes=[o_b])
                    else:
                        H = N // 64
                        a, a_b = tA.next()
                        b, b_b = tB.next()
                        pv = p[:, 0:N].rearrange("p (h two f) -> p h two f", two=2, f=32)
                        av = a[:, 0:N].rearrange("p (h two f) -> p h two f", two=2, f=32)
                        bv = b[:, 0:N].rearrange("p (h two f) -> p h two f", two=2, f=32)
                        ov = o[:, 0:N].rearrange("p (h two f) -> p h two f", two=2, f=32)
                        cb = cos_t[:, t:t + 1, :].broadcast_to([128, H, 32])
                        sn = sin_t[:, t:t + 1, :].broadcast_to([128, H, 32])
                        x1, x2 = pv[:, :, 0, :], pv[:, :, 1, :]
                        Sc.op("vector", lambda e, av=av, x1=x1, cb=cb: e.tensor_tensor(out=av[:, :, 0, :], in0=x1, in1=cb, op=ALU.mult), reads=[p_b, const_b], writes=[a_b])
                        Sc.op("vector", lambda e, av=av, x2=x2, cb=cb: e.tensor_tensor(out=av[:, :, 1, :], in0=x2, in1=cb, op=ALU.mult), reads=[p_b, const_b], writes=[a_b])
                        Sc.op("vector", lambda e, bv=bv, x2=x2, sn=sn: e.tensor_tensor(out=bv[:, :, 0, :], in0=x2, in1=sn, op=ALU.mult), reads=[p_b, const_b], writes=[b_b])
                        Sc.op("vector", lambda e, bv=bv, x1=x1, sn=sn: e.tensor_tensor(out=bv[:, :, 1, :], in0=x1, in1=sn, op=ALU.mult), reads=[p_b, const_b], writes=[b_b])
                        Sc.op("gpsimd", lambda e, ov=ov, av=av, bv=bv: e.tensor_tensor(out=ov[:, :, 0, :], in0=av[:, :, 0, :], in1=bv[:, :, 0, :], op=ALU.subtract), reads=[a_b, b_b], writes=[o_b])
                        Sc.op("gpsimd", lambda e, ov=ov, av=av, bv=bv: e.tensor_tensor(out=ov[:, :, 1, :], in0=av[:, :, 1, :], in1=bv[:, :, 1, :], op=ALU.add), reads=[a_b, b_b], writes=[o_b])
                    Sc.dma("sync", lambda e, o=o, t=t, c0=c0, N=N: e.dma_start(out=self.PROJ[t * 128:(t + 1) * 128, c0:c0 + N], in_=o[:, 0:N]), reads=[o_b], writes=[proj_b])

            for fg in range(6):
                w, w_b = ensure_w(len(blocks) + fg)
                ensure_w(len(blocks) + fg + 1)
                for j in range(4):
                    fc = fg * 4 + j
                    for tb in range(NB):
                        p, p_b = pA.next()
                        for k in range(8):
                            Sc.op("tensor", lambda e, p=p, w=w, tb=tb, k=k, j=j: e.matmul(p[:, :], lhsT=w[:, k, j * 128:(j + 1) * 128], rhs=xT[:, k, tb * 512:(tb + 1) * 512], start=(k == 0), stop=(k == 7)),
                                  reads=xT_b[tb * 4:tb * 4 + 4] + [w_b], writes=[p_b])
                        o, o_b = ob.next()
                        Sc.op("scalar", lambda e, o=o, p=p, fc=fc: e.activation(out=o[:, :], in_=p[:, :], func=AF.Tanh, bias=hb[:, fc:fc + 1], scale=0.5), reads=[p_b, const_b], writes=[o_b])
                        Sc.dma("sync", lambda e, o=o, fc=fc, tb=tb: e.dma_start(out=self.GTT[fc * 128:(fc + 1) * 128, tb * 512:(tb + 1) * 512], in_=o[:, :]), reads=[o_b], writes=[gtt_b])
            Sc.barrier()
            Sc.emit(st)


    def phase_b1(self):
        nc = self.nc
        with contextlib.ExitStack() as st:
            sb = lambda n, s, d: st.enter_context(nc.sbuf_tensor("b1_" + n, s, d))
            ps = lambda n, s, d: st.enter_context(nc.psum_tensor("b1_" + n, s, d))
            Sc = Sched(nc, prefix="b1_")
            const_b = Buf("const")
            amf = sb("amf", [128, 1024], F32)
            am = sb("am", [128, 1024], BF16)
            identf = sb("identf", [128, 128], F32)
            ident = sb("ident", [128, 128], BF16)
            Sc.dma("sync", lambda e: e.dma_start(out=amf[:], in_=self.c_amask[:, :]), writes=[const_b])
            Sc.dma("sync", lambda e: e.dma_start(out=identf[:], in_=self.c_ident[:, :]), writes=[const_b])
            Sc.op("vector", lambda e: e.tensor_copy(out=am[:], in_=amf[:]), reads=[const_b], writes=[const_b])
            Sc.op("vector", lambda e: e.tensor_copy(out=ident[:], in_=identf[:]), reads=[const_b], writes=[const_b])
            sets = []
            for i in range(3):
                Lc = S if i == 2 else 1024
                nbc = Lc // 128
                qT = [sb("qT%d_%d" % (i, pp), [128, Lc], BF16) for pp in range(2)]
                qZ = [[sb("qZ%d_%d_%d" % (i, pp, hh), [128, Lc], BF16) for hh in range(2)] for pp in range(2)]
                qz_b = [[[Buf("qz") for _ in range(nbc)] for hh in range(2)] for pp in range(2)]
                for pp in range(2):
                    Sc.op("gpsimd", lambda e, t=qZ[pp][0]: e.memset(t[64:128, :], 0.0), writes=qz_b[pp][0])
                    Sc.op("gpsimd", lambda e, t=qZ[pp][1]: e.memset(t[0:64, :], 0.0), writes=qz_b[pp][1])
                kT = [sb("kT%d_%d" % (i, pp), [128, Lc], BF16) for pp in range(2)]
                va = sb("va%d" % i, [128, nbc + 1, 4, 128], BF16)
                q_b = [[Buf("q") for _ in range(nbc)] for pp in range(2)]
                k_b = [[Buf("k") for _ in range(nbc)] for pp in range(2)]
                v_b = [Buf("v") for _ in range(nbc + 1)]
                Sc.op("gpsimd", lambda e, va=va: e.memset(va[:, :, :, 64:128], 1.0), writes=v_b)
                sets.append((qT, kT, va, q_b, k_b, v_b, qZ, qz_b))
            pS = Ring([(ps("pS%d" % i, [128, 512], F32), Buf("pS%d" % i)) for i in range(3)])
            pU = Ring([(ps("pU%d" % i, [128, 512], F32), Buf("pU%d" % i)) for i in range(4)])
            Er = Ring([(sb("E%d" % i, [128, 512], BF16), Buf("E%d" % i)) for i in range(6)])
            us = Ring([(sb("us%d" % i, [128, 512], F32), Buf("us%d" % i)) for i in range(4)])
            ud_b = Buf("UD")
            unit = 0
            for g, dl in enumerate((1, 4, 16)):
                if g not in getattr(self, "b1_groups", (0, 1, 2)):
                    continue
                L = S // dl
                nb = L // 128
                ucurs = {0: None, 1: None}
                for r in range(dl):
                    qT, kT, va, q_b, k_b, v_b, qZ, qz_b = sets[2] if g == 0 else sets[unit % 2]
                    unit += 1
                    rows = self.PROJ.rearrange("(i r) c -> r i c", r=dl)[r]
                    vc = C_VA + g * 256
                    Sc.dma("sync", lambda e, va=va, rows=rows, vc=vc: e.dma_start(out=va[0:64, 0, :, 0:64], in_=rows[0:64, vc:vc + 256].rearrange("k (h d) -> k h d", d=64)), writes=[v_b[0]])
                    for h4 in range(4):
                        Sc.dma("sync", lambda e, va=va, rows=rows, vc=vc, nb=nb, L=L, h4=h4: e.dma_start(out=va[:, 1:nb, h4, 0:64], in_=rows[64:L - 64, vc + h4 * 64:vc + h4 * 64 + 64].rearrange("(j k) d -> k j d", k=128)), writes=v_b[1:nb])
                    Sc.dma("sync", lambda e, va=va, rows=rows, vc=vc, nb=nb, L=L: e.dma_start(out=va[0:64, nb, :, 0:64], in_=rows[L - 64:L, vc:vc + 256].rearrange("k (h d) -> k h d", d=64)), writes=[v_b[nb]])
                    for pp in range(2):
                        qc = C_QA + (g * 4 + 2 * pp) * 64
                        kc = C_KA + (g * 4 + 2 * pp) * 64
                        nbc = min(4, nb)
                        for b4 in range(0, nb, nbc):
                            rs = slice(b4 * 128, (b4 + nbc) * 128)
                            Sc.dma("sync", lambda e, dst=kT[pp], rows=rows, kc=kc, rs=rs: e.dma_start_transpose(out=dst[:, rs], in_=rows[rs, kc:kc + 128]), writes=k_b[pp][b4:b4 + nbc])
                            Sc.dma("sync", lambda e, dst=qT[pp], rows=rows, qc=qc, rs=rs: e.dma_start_transpose(out=dst[:, rs], in_=rows[rs, qc:qc + 128]), writes=q_b[pp][b4:b4 + nbc])
                        for blk in range(nb):
                            sl = slice(blk * 128, (blk + 1) * 128)
                            Sc.op("vector", lambda e, d=qZ[pp][0], s_=qT[pp], sl=sl: e.tensor_copy(out=d[0:64, sl], in_=s_[0:64, sl]), reads=[q_b[pp][blk]], writes=[qz_b[pp][0][blk]])
                            Sc.op("gpsimd", lambda e, d=qZ[pp][1], s_=qT[pp], sl=sl: e.tensor_copy(out=d[64:128, sl], in_=s_[64:128, sl]), reads=[q_b[pp][blk]], writes=[qz_b[pp][1][blk]])
                    for pp in range(2):
                        if getattr(self, "b1_stage", 9) < 1:
                            continue
                        Es = {}
                        ucur = ucurs[pp]
                        for j in range(nb + 1):
                            k0, k1 = max(0, 128 * j - 64), min(L, 128 * j + 64)
                            M = k1 - k0
                            q0, q1 = max(0, 128 * (j - 1)), min(L, 128 * (j + 1))
                            Nq = q1 - q0
                            if j == 0:
                                mk = am[0:64, 512:768]
                            elif j == nb:
                                mk = am[0:64, 768:1024]
                            else:
                                mk = am[:, 0:512]
                            kblks = sorted(set([k0 // 128, (k1 - 1) // 128]))
                            qblks = sorted(set([q0 // 128, (q1 - 1) // 128]))
                            p, p_b = pS.next()
                            for hh in range(2):
                                rd = [k_b[pp][b] for b in kblks] + [qz_b[pp][hh][b] for b in qblks]
                                Sc.op("tensor", lambda e, p=p, kt=kT[pp], qt=qZ[pp][hh], hh=hh, k0=k0, k1=k1, q0=q0, q1=q1, M=M, Nq=Nq:
                                      e.matmul(p[0:M, hh * Nq:(hh + 1) * Nq], lhsT=kt[:, k0:k1], rhs=qt[:, q0:q1], start=True, stop=False),
                                      reads=rd, writes=[p_b])
                                Sc.op("tensor", lambda e, p=p, mk=mk, M=M, Nq=Nq, hh=hh: e.matmul(p[0:M, hh * Nq:(hh + 1) * Nq], lhsT=ident[0:M, 0:M], rhs=mk[:, 0:Nq], start=False, stop=True),
                                      reads=[const_b], writes=[p_b])
                            E, E_b = Er.next()
                            Sc.op("scalar", lambda e, E=E, p=p, M=M, Nq=Nq: e.activation(out=E[0:M, 0:2 * Nq], in_=p[0:M, 0:2 * Nq], func=AF.Exp, scale=0.125), reads=[p_b], writes=[E_b])
                            Es[j] = (E, E_b, M, Nq)
                            if j == 0 or getattr(self, "b1_stage", 9) < 2:
                                continue
                            b = j - 1
                            G = r * nb + b
                            if G % 4 == 0:
                                ucur = ucurs[pp] = [pU.next(), pU.next()]
                            for hh in range(2):
                                (u, u_b) = ucur[hh]
                                col = (G % 4) * 128
                                for n_, jj in enumerate((b, b + 1)):
                                    Ej, Ej_b, Mj, Nqj = Es[jj]
                                    if jj == b:
                                        c = hh * Nqj + (128 if b >= 1 else 0)
                                    else:
                                        c = hh * Nqj
                                    hv = 2 * pp + hh
                                    Sc.op("tensor", lambda e, u=u, va=va, Ej=Ej, jj=jj, hv=hv, Mj=Mj, c=c, col=col, n_=n_:
                                          e.matmul(u[:, col:col + 128], lhsT=va[0:Mj, jj, hv, :], rhs=Ej[0:Mj, c:c + 128], start=(n_ == 0), stop=(n_ == 1)),
                                          reads=[v_b[jj], Ej_b], writes=[u_b])
                            del Es[b]
                            if G % 4 == 3:
                                for hh in range(2):
                                    (u, u_b) = ucur[hh]
                                    o, o_b = us.next()
                                    if hh == 0:
                                        Sc.op("vector", lambda e, o=o, u=u: e.tensor_copy(out=o[:], in_=u[:]), reads=[u_b], writes=[o_b])
                                    else:
                                        Sc.op("scalar", lambda e, o=o, u=u: e.activation(out=o[:], in_=u[:], func=AF.Copy), reads=[u_b], writes=[o_b])
                                    hrow = (g * 4 + 2 * pp + hh) * 128
                                    c0 = (G - 3) * 128
                                    Sc.dma("gpsimd", lambda e, o=o, hrow=hrow, c0=c0: e.dma_start(out=self.UD[hrow:hrow + 128, c0:c0 + 512], in_=o[:]), reads=[o_b], writes=[ud_b])
            Sc.barrier()
            Sc.emit(st)

    def phase_b1c(self):
        nc = self.nc
        with contextlib.ExitStack() as st:
            sb = lambda n, s, d: st.enter_context(nc.sbuf_tensor("b1c_" + n, s, d))
            Sc = Sched(nc, prefix="b1c_")
            CH = 2048
            Ut = [Ring([(sb("U%d_%d" % (g, i), [128, CH], F32), Buf("U")) for i in range(2)]) for g in range(3)]
            Dt = [Ring([(sb("D%d_%d" % (g, i), [128, CH], F32), Buf("D")) for i in range(2)]) for g in range(3)]
            Rr = Ring([(sb("R%d" % i, [128, CH], F32), Buf("R")) for i in range(2)])
            Yr = Ring([(sb("Y%d" % i, [128, CH], BF16), Buf("Y")) for i in range(3)])
            yat_b = Buf("YAT")
            for c2 in range(S // CH):
                for sp in range(2):
                    tiles = []
                    for g, dl in enumerate((1, 4, 16)):
                        L = S // dl
                        il = CH // dl
                        u, u_b = Ut[g].next()
                        d_, d_b = Dt[g].next()
                        for hh in range(2):
                            h = g * 4 + 2 * sp + hh
                            srcU = self.UD[h * 128:h * 128 + 64, :].rearrange("p (r i) -> p r i", r=dl)[:, :, c2 * il:(c2 + 1) * il]
                            srcD = self.UD[h * 128 + 64:h * 128 + 128, :].rearrange("p (r i) -> p r i", r=dl)[:, :, c2 * il:(c2 + 1) * il]
                            Sc.dma("sync", lambda e, u=u, hh=hh, srcU=srcU, dl=dl: e.dma_start(out=u[hh * 64:(hh + 1) * 64, :].rearrange("p (r i) -> p r i", r=dl), in_=srcU), writes=[u_b])
                            Sc.dma("sync", lambda e, d_=d_, hh=hh, srcD=srcD, dl=dl: e.dma_start(out=d_[hh * 64:(hh + 1) * 64, :].rearrange("p (r i) -> p r i", r=dl), in_=srcD), writes=[d_b])
                        tiles.append((u, u_b, d_, d_b, dl))
                    R, R_b = Rr.next()
                    nat = lambda t, dl: t[:, :].rearrange("p (i r) -> p i r", r=dl)
                    res = lambda t, dl: t[:, :].rearrange("p (r i) -> p i r", r=dl)
                    (u0, u0_b, d0, d0_b, _), (u1, u1_b, d1, d1_b, _), (u2, u2_b, d2, d2_b, _) = tiles
                    Sc.op("gpsimd", lambda e, R=R, d0=d0, d1=d1: e.tensor_tensor(out=nat(R, 4), in0=nat(d0, 4), in1=res(d1, 4), op=ALU.add), reads=[d0_b, d1_b], writes=[R_b])
                    Sc.op("gpsimd", lambda e, R=R, d2=d2: e.tensor_tensor(out=nat(R, 16), in0=nat(R, 16), in1=res(d2, 16), op=ALU.add), reads=[d2_b, R_b], writes=[R_b])
                    Sc.op("vector", lambda e, R=R: e.reciprocal(out=R[:, :], in_=R[:, :]), reads=[R_b], writes=[R_b])
                    for g, (u, u_b, d_, d_b, dl) in enumerate(tiles):
                        y, y_b = Yr.next()
                        eng = "vector" if g != 1 else "gpsimd"
                        Sc.op(eng, lambda e, y=y, u=u, dl=dl, R=R: e.tensor_tensor(out=nat(y, dl), in0=res(u, dl), in1=nat(R, dl), op=ALU.mult), reads=[u_b, R_b], writes=[y_b])
                        row = (g * 4 + 2 * sp) * 64
                        Sc.dma("gpsimd", lambda e, y=y, row=row, c2=c2: e.dma_start(out=self.YAT[row:row + 128, c2 * CH:(c2 + 1) * CH], in_=y[:, :]), reads=[y_b], writes=[yat_b])
            Sc.barrier()
            Sc.emit(st)

    def phase_b2(self):
        nc = self.nc
        with contextlib.ExitStack() as st:
            sb = lambda n, s, d: st.enter_context(nc.sbuf_tensor("b2_" + n, s, d))
            ps = lambda n, s, d: st.enter_context(nc.psum_tensor("b2_" + n, s, d))
            Sc = Sched(nc, prefix="b2_")
            cb_ = Buf("const")
            cret = sb("cret", [128, 8, 128], F32)
            dec = sb("dec", [128, 12], F32)
            lg = sb("lg", [128, 12], F32)
            lgp = sb("lgp", [128, 6], F32)
            Mall = sb("Mall", [128, 6, 128], F32)
            tmpm = sb("tmpm", [128, 128], F32)
            zeta = sb("zeta", [128, 12], F32)
            XiF = sb("XiF", [128, 3, 128], F32)
            XiB = sb("XiB", [128, 3, 128], F32)
            g128 = sb("g128", [128, 6], F32)
            gret = sb("gret", [128, 768], F32)
            negh = sb("negh", [128, 6], F32)
            Sc.dma("sync", lambda e: e.dma_start(out=cret[:], in_=self.c_ret[:, :, :]), writes=[cb_])
            Sc.dma("sync", lambda e: e.dma_start(out=dec[:, 0:6], in_=self.dec_f[0, :].partition_broadcast(128)), writes=[cb_])
            Sc.dma("sync", lambda e: e.dma_start(out=dec[:, 6:12], in_=self.dec_b[0, :].partition_broadcast(128)), writes=[cb_])
            Sc.dma("sync", lambda e: e.dma_start(out=gret[:], in_=self.g_ret[0, :].partition_broadcast(128)), writes=[cb_])
            C = lambda eng, fn: Sc.op(eng, fn, reads=[cb_], writes=[cb_])
            C("vector", lambda e: e.memset(negh[:], -0.5))
            C("vector", lambda e: e.tensor_scalar(out=gret[:], in0=gret[:], scalar1=0.5, scalar2=None, op0=ALU.mult))
            C("scalar", lambda e: e.activation(out=lg[:], in_=dec[:], func=AF.Exp, scale=-1.0))
            C("vector", lambda e: e.tensor_scalar(out=lg[:], in0=lg[:], scalar1=1.0, scalar2=None, op0=ALU.add))
            C("scalar", lambda e: e.activation(out=lg[:], in_=lg[:], func=AF.Ln))
            C("vector", lambda e: e.tensor_scalar(out=lg[:], in0=lg[:], scalar1=-1.0, scalar2=None, op0=ALU.mult))
            for dr in range(2):
                for hh in range(2):
                    src = lg[hh * 64:(hh + 1) * 64, dr * 6:(dr + 1) * 6].rearrange("p (a b) -> p a b", b=2)[:, :, hh]
                    C("vector", lambda e, dr=dr, hh=hh, src=src: e.tensor_copy(out=lgp[hh * 64:(hh + 1) * 64, dr * 3:(dr + 1) * 3], in_=src))
            for h in range(6):
                C("scalar", lambda e, h=h: e.activation(out=Mall[:, h, :], in_=cret[:, 0, :], func=AF.Exp, scale=lg[:, h:h + 1]))
                C("vector", lambda e, h=h: e.tensor_tensor(out=Mall[:, h, :], in0=Mall[:, h, :], in1=cret[:, 2, :], op=ALU.mult))
                C("scalar", lambda e, h=h: e.activation(out=tmpm[:], in_=cret[:, 1, :], func=AF.Exp, scale=lg[:, 6 + h:7 + h]))
                C("vector", lambda e, h=h: e.tensor_tensor(out=tmpm[:], in0=tmpm[:], in1=cret[:, 3, :], op=ALU.mult))
                C("vector", lambda e, h=h: e.tensor_tensor(out=Mall[:, h, :], in0=Mall[:, h, :], in1=tmpm[:], op=ALU.add))
                C("scalar", lambda e, h=h: e.activation(out=zeta[:, h:h + 1], in_=cret[:, 6, 0:1], func=AF.Exp, scale=lg[:, h:h + 1]))
                C("scalar", lambda e, h=h: e.activation(out=zeta[:, 6 + h:7 + h], in_=cret[:, 6, 1:2], func=AF.Exp, scale=lg[:, 6 + h:7 + h]))
            C("vector", lambda e: e.tensor_scalar(out=zeta[:], in0=zeta[:], scalar1=0.125, scalar2=None, op0=ALU.mult))
            for pp in range(3):
                C("scalar", lambda e, pp=pp: e.activation(out=XiF[:, pp, :], in_=cret[:, 4, :], func=AF.Exp, scale=lgp[:, pp:pp + 1]))
                C("scalar", lambda e, pp=pp: e.activation(out=XiB[:, pp, :], in_=cret[:, 5, :], func=AF.Exp, scale=lgp[:, 3 + pp:4 + pp]))
            C("scalar", lambda e: e.activation(out=g128[:], in_=lgp[:], func=AF.Exp, scale=128.0))

            Sall = [sb("SallF", [128, NT, 3, 128], BF16), sb("SallB", [128, NT, 3, 128], BF16)]
            sall_b = [[Buf("sf") for _ in range(NT)], [Buf("sb") for _ in range(NT)]]
            Scur = [sb("ScurF", [128, 3, 128], F32), sb("ScurB", [128, 3, 128], F32)]
            scur_b = [Buf("scf"), Buf("scb")]
            kt_r = Ring([(sb("ktok%d" % i, [128, 384], BF16), Buf("ktok")) for i in range(3)])
            vt_r = Ring([(sb("vtok%d" % i, [128, 768], BF16), Buf("vtok")) for i in range(3)])
            kz_r = Ring([(sb("kz%d" % i, [128, 6, 64], BF16), Buf("kz")) for i in range(3)])
            pkv = Ring([(ps("pkv%d" % i, [128, 512], F32), Buf("pkv")) for i in range(2)])
            for dr in range(2):
                Sc.op("vector", lambda e, dr=dr: e.memset(Scur[dr][:], 0.0), writes=[scur_b[dr]])
            for step in range(2 * NT):
                dr = step % 2
                c = (step // 2) if dr == 0 else (NT - 1 - step // 2)
                if True:
                    kt, kt_b = kt_r.next()
                    vt, vt_b = vt_r.next()
                    Sc.dma("sync", lambda e, kt=kt, c=c: e.dma_start(out=kt[:], in_=self.PROJ[c * 128:(c + 1) * 128, C_KR:C_KR + 384]), writes=[kt_b])
                    Sc.dma("sync", lambda e, vt=vt, c=c: e.dma_start(out=vt[:], in_=self.PROJ[c * 128:(c + 1) * 128, C_VR:C_VR + 768]), writes=[vt_b])
                    kz, kz_b = kz_r.next()
                    zb = zeta[:, dr * 6:(dr + 1) * 6].unsqueeze(2).broadcast_to([128, 6, 64])
                    Sc.op("gpsimd", lambda e, kz=kz, kt=kt, zb=zb: e.tensor_tensor(out=kz[:], in0=kt[:, :].rearrange("p (h d) -> p h d", d=64), in1=zb, op=ALU.mult), reads=[kt_b, cb_], writes=[kz_b])
                    p, p_b = pkv.next()
                    for h in range(6):
                        pp, hh = h // 2, h % 2
                        Sc.op("tensor", lambda e, p=p, kz=kz, vt=vt, h=h, pp=pp, hh=hh: e.matmul(p[hh * 64:(hh + 1) * 64, pp * 128:(pp + 1) * 128], lhsT=kz[:, h, :], rhs=vt[:, h * 128:(h + 1) * 128], start=True, stop=True),
                              reads=[kz_b, vt_b], writes=[p_b])
                    Sc.op("scalar", lambda e, dr=dr, c=c: e.activation(out=Sall[dr][:, c, :, :], in_=Scur[dr][:], func=AF.Copy), reads=[scur_b[dr]], writes=[sall_b[dr][c]])
                    for pp in range(3):
                        Sc.op("vector", lambda e, dr=dr, pp=pp, p=p: e.scalar_tensor_tensor(out=Scur[dr][:, pp, :], in0=Scur[dr][:, pp, :], scalar=g128[:, dr * 3 + pp:dr * 3 + pp + 1], in1=p[:, pp * 128:(pp + 1) * 128], op0=ALU.mult, op1=ALU.add),
                              reads=[p_b, scur_b[dr], cb_], writes=[scur_b[dr]])

            qt_r = Ring([(sb("qTp%d" % i, [128, 3, 128], BF16), Buf("qTp")) for i in range(3)])
            ktp_r = Ring([(sb("kTp%d" % i, [128, 3, 128], BF16), Buf("kTp")) for i in range(3)])
            gr_r = Ring([(sb("gr%d" % i, [128, 768], BF16), Buf("gr")) for i in range(3)])
            qz_r = []
            for i in range(2):
                t3 = [sb("qz%d_%d" % (i, j), [128, 6, 128], BF16) for j in range(3)]
                b3 = [Buf("qz") for j in range(3)]
                for j in range(3):
                    Sc.op("gpsimd", lambda e, t=t3[j]: e.memset(t[:], 0.0), writes=[b3[j]])
                qz_r.append((t3, b3))
            qz_r = Ring(qz_r)
            pst = Ring([(ps("pst%d" % i, [128, 512], F32), Buf("pst")) for i in range(2)])
            pyy = Ring([(ps("pyy%d" % i, [128, 512], F32), Buf("pyy")) for i in range(4)])
            A_r = Ring([(sb("A%d" % i, [128, 6, 128], BF16), Buf("A")) for i in range(2)])
            ysb_r = Ring([(sb("ysb%d" % i, [128, 768], F32), Buf("ysb")) for i in range(2)])
            ysq_r = Ring([(sb("ysq%d" % i, [128, 768], F32), Buf("ysq")) for i in range(2)])
            st_r = Ring([(sb("stat%d" % i, [128, 4, 6], F32), Buf("stat")) for i in range(2)])
            yo_r = Ring([(sb("yo%d" % i, [128, 768], BF16), Buf("yo")) for i in range(2)])
            yr_b = Buf("YR")
            def stage1(c):
                qt, qt_b = qt_r.next()
                ktp, ktp_b = ktp_r.next()
                vt, vt_b = vt_r.next()
                gr, gr_b = gr_r.next()
                for pp in range(3):
                    Sc.dma("sync", lambda e, qt=qt, c=c, pp=pp: e.dma_start_transpose(out=qt[:, pp, :], in_=self.PROJ[c * 128:(c + 1) * 128, C_QR + pp * 128:C_QR + (pp + 1) * 128]), writes=[qt_b])
                    Sc.dma("sync", lambda e, ktp=ktp, c=c, pp=pp: e.dma_start_transpose(out=ktp[:, pp, :], in_=self.PROJ[c * 128:(c + 1) * 128, C_KR + pp * 128:C_KR + (pp + 1) * 128]), writes=[ktp_b])
                Sc.dma("sync", lambda e, vt=vt, c=c: e.dma_start(out=vt[:], in_=self.PROJ[c * 128:(c + 1) * 128, C_VR:C_VR + 768]), writes=[vt_b])
                Sc.dma("sync", lambda e, gr=gr, c=c: e.dma_start(out=gr[:], in_=self.PROJ[c * 128:(c + 1) * 128, C_GR:C_GR + 768]), writes=[gr_b])
                (qz, qxf, qxb), (qz_b, qxf_b, qxb_b) = qz_r.next()
                for hh in range(2):
                    sl = slice(hh * 64, (hh + 1) * 64)
                    hv = lambda t, sl=sl, hh=hh: t[sl, :, :].rearrange("p (pp two) n -> p pp two n", two=2)[:, :, hh, :]
                    Sc.op("vector", lambda e, qz=qz, qt=qt, sl=sl, hv=hv: e.tensor_copy(out=hv(qz), in_=qt[sl, :, :]), reads=[qt_b], writes=[qz_b])
                    Sc.op("gpsimd", lambda e, qxf=qxf, qt=qt, sl=sl, hv=hv: e.tensor_tensor(out=hv(qxf), in0=qt[sl, :, :], in1=XiF[sl, :, :], op=ALU.mult), reads=[qt_b, cb_], writes=[qxf_b])
                    Sc.op("vector", lambda e, qxb=qxb, qt=qt, sl=sl, hv=hv: e.tensor_tensor(out=hv(qxb), in0=qt[sl, :, :], in1=XiB[sl, :, :], op=ALU.mult), reads=[qt_b, cb_], writes=[qxb_b])
                (s0, s0_b), (s1, s1_b) = pst.next(), pst.next()
                for h in range(6):
                    pp = h // 2
                    tgt, tb_ = (s0, s0_b) if h < 4 else (s1, s1_b)
                    col = (h % 4) * 128
                    Sc.op("tensor", lambda e, tgt=tgt, ktp=ktp, qz=qz, h=h, pp=pp, col=col: e.matmul(tgt[:, col:col + 128], lhsT=ktp[:, pp, :], rhs=qz[:, h, :], start=True, stop=True), reads=[ktp_b, qz_b], writes=[tb_])
                A, A_b = A_r.next()
                Sc.op("vector", lambda e, A=A, s0=s0: e.tensor_tensor(out=A[:, 0:4, :], in0=s0[:, :].rearrange("p (h n) -> p h n", n=128), in1=Mall[:, 0:4, :], op=ALU.mult), reads=[s0_b, cb_], writes=[A_b])
                Sc.op("vector", lambda e, A=A, s1=s1: e.tensor_tensor(out=A[:, 4:6, :], in0=s1[:, 0:256].rearrange("p (h n) -> p h n", n=128), in1=Mall[:, 4:6, :], op=ALU.mult), reads=[s1_b, cb_], writes=[A_b])
                return dict(c=c, qxf=qxf, qxb=qxb, qxf_b=qxf_b, qxb_b=qxb_b, A=A, A_b=A_b, vt=vt, vt_b=vt_b, gr=gr, gr_b=gr_b)

            def stage2(cx):
                c, qxf, qxb, qxf_b, qxb_b, A, A_b, vt, vt_b, gr, gr_b = (cx[k] for k in ("c", "qxf", "qxb", "qxf_b", "qxb_b", "A", "A_b", "vt", "vt_b", "gr", "gr_b"))
                (y0, y0_b), (y1, y1_b) = pyy.next(), pyy.next()
                for h in range(6):
                    pp = h // 2
                    tgt, tb_ = (y0, y0_b) if h < 4 else (y1, y1_b)
                    col = (h % 4) * 128
                    Sc.op("tensor", lambda e, tgt=tgt, A=A, vt=vt, h=h, col=col: e.matmul(tgt[:, col:col + 128], lhsT=A[:, h, :], rhs=vt[:, h * 128:(h + 1) * 128], start=True, stop=False), reads=[A_b, vt_b], writes=[tb_])
                    Sc.op("tensor", lambda e, tgt=tgt, qxf=qxf, h=h, pp=pp, c=c, col=col: e.matmul(tgt[:, col:col + 128], lhsT=qxf[:, h, :], rhs=Sall[0][:, c, pp, :], start=False, stop=False), reads=[qxf_b, sall_b[0][c]], writes=[tb_])
                    Sc.op("tensor", lambda e, tgt=tgt, qxb=qxb, h=h, pp=pp, c=c, col=col: e.matmul(tgt[:, col:col + 128], lhsT=qxb[:, h, :], rhs=Sall[1][:, c, pp, :], start=False, stop=True), reads=[qxb_b, sall_b[1][c]], writes=[tb_])
                ysb, ysb_b = ysb_r.next()
                ysq, ysq_b = ysq_r.next()
                stt_, stt_b = st_r.next()
                Sc.op("scalar", lambda e, ysb=ysb, y0=y0: e.activation(out=ysb[:, 0:512], in_=y0[:, :], func=AF.Copy), reads=[y0_b], writes=[ysb_b])
                Sc.op("scalar", lambda e, ysb=ysb, y1=y1: e.activation(out=ysb[:, 512:768], in_=y1[:, 0:256], func=AF.Copy), reads=[y1_b], writes=[ysb_b])
                y3 = ysb[:, :].rearrange("p (h e) -> p h e", e=128)
                Sc.op("scalar", lambda e, ysq=ysq, y0=y0: e.activation(out=ysq[:, 0:512], in_=y0[:, :], func=AF.Square), reads=[y0_b], writes=[ysq_b])
                Sc.op("scalar", lambda e, ysq=ysq, y1=y1: e.activation(out=ysq[:, 512:768], in_=y1[:, 0:256], func=AF.Square), reads=[y1_b], writes=[ysq_b])
                Sc.op("vector", lambda e, stt_=stt_, y3=y3: e.tensor_reduce(out=stt_[:, 0, :], in_=y3, axis=AX.X, op=ALU.add), reads=[ysb_b], writes=[stt_b])
                Sc.op("vector", lambda e, stt_=stt_, ysq=ysq: e.tensor_reduce(out=stt_[:, 1, :], in_=ysq[:, :].rearrange("p (h e) -> p h e", e=128), axis=AX.X, op=ALU.add), reads=[ysq_b], writes=[stt_b])
                Sc.op("gpsimd", lambda e, stt_=stt_: e.tensor_scalar(out=stt_[:, 0, :], in0=stt_[:, 0, :], scalar1=1.0 / 128, scalar2=None, op0=ALU.mult), reads=[stt_b], writes=[stt_b])
                Sc.op("gpsimd", lambda e, stt_=stt_: e.tensor_tensor(out=stt_[:, 2, :], in0=stt_[:, 0, :], in1=stt_[:, 0, :], op=ALU.mult), reads=[stt_b], writes=[stt_b])
                Sc.op("gpsimd", lambda e, stt_=stt_: e.tensor_scalar(out=stt_[:, 1, :], in0=stt_[:, 1, :], scalar1=1.0 / 128, scalar2=EPS, op0=ALU.mult, op1=ALU.add), reads=[stt_b], writes=[stt_b])
                Sc.op("gpsimd", lambda e, stt_=stt_: e.tensor_tensor(out=stt_[:, 1, :], in0=stt_[:, 1, :], in1=stt_[:, 2, :], op=ALU.subtract), reads=[stt_b], writes=[stt_b])
                Sc.op("gpsimd", lambda e, stt_=stt_: e.tensor_tensor(out=stt_[:, 3, :], in0=stt_[:, 1, :], in1=negh[:, :], op=ALU.pow), reads=[stt_b, cb_], writes=[stt_b])
                mb = stt_[:, 0, :].unsqueeze(2).broadcast_to([128, 6, 128])
                rb = stt_[:, 3, :].unsqueeze(2).broadcast_to([128, 6, 128])
                Sc.op("vector", lambda e, y3=y3, mb=mb: e.tensor_tensor(out=y3, in0=y3, in1=mb, op=ALU.subtract), reads=[stt_b, ysb_b], writes=[ysb_b])
                Sc.op("vector", lambda e, y3=y3, rb=rb: e.tensor_tensor(out=y3, in0=y3, in1=rb, op=ALU.mult), reads=[stt_b, ysb_b], writes=[ysb_b])
                Sc.op("vector", lambda e, ysb=ysb: e.tensor_tensor(out=ysb[:], in0=ysb[:], in1=gret[:], op=ALU.mult), reads=[ysb_b, cb_], writes=[ysb_b])
                yo, yo_b = yo_r.next()
                Sc.op("gpsimd", lambda e, yo=yo, ysb=ysb, gr=gr: e.tensor_tensor(out=yo[:], in0=ysb[:], in1=gr[:], op=ALU.mult), reads=[ysb_b, gr_b], writes=[yo_b])
                Sc.dma("gpsimd", lambda e, yo=yo, c=c: e.dma_start(out=self.YR[c * 128:(c + 1) * 128, :], in_=yo[:]), reads=[yo_b], writes=[yr_b])

            prev = None
            for c in range(NT):
                cx = stage1(c)
                if prev is not None:
                    stage2(prev)
                prev = cx
            stage2(prev)
            Sc.barrier()
            Sc.emit(st)

    def phase_b3(self):
        nc = self.nc
        with contextlib.ExitStack() as st:
            sb = lambda n, s, d: st.enter_context(nc.sbuf_tensor("b3_" + n, s, d))
            ps = lambda n, s, d: st.enter_context(nc.psum_tensor("b3_" + n, s, d))
            Sc = Sched(nc, prefix="b3_")
            cb_ = Buf("const")
            identf = sb("identf", [128, 128], F32)
            ident = sb("ident", [128, 128], BF16)
            ones = sb("ones", [128, 128], BF16)
            gmem = sb("gmem", [128, D], F32)
            negh = sb("negh", [128, 1], F32)
            wkv = sb("wkv", [128, 8, 1024], BF16)
            wkv_b = Buf("wkv")
            memT = sb("memT", [128, 8, 256], BF16)
            memT_b = Buf("memT")
            kmT = sb("kmT", [128, 4, 256], BF16)
            vm = sb("vm", [128, 2, 512], BF16)
            kv_b = Buf("kv")
            Sc.dma("sync", lambda e: e.dma_start(out=identf[:], in_=self.c_ident[:, :]), writes=[cb_])
            Sc.dma("sync", lambda e: e.dma_start(out=gmem[:], in_=self.g_mem[0, :].partition_broadcast(128)), writes=[cb_])
            Sc.dma("gpsimd", lambda e: e.dma_start(out=wkv[:], in_=self.w_mem_kv[0, :, :].rearrange("(k p) c -> p k c", p=128)), writes=[wkv_b])
            Sc.op("vector", lambda e: e.tensor_copy(out=ident[:], in_=identf[:]), reads=[cb_], writes=[cb_])
            Sc.op("vector", lambda e: e.memset(ones[:], 1.0), reads=[cb_], writes=[cb_])
            Sc.op("vector", lambda e: e.memset(negh[:], -0.5), reads=[cb_], writes=[cb_])
            mt_r = Ring([(sb("mt%d" % i, [128, D], F32), Buf("mt")) for i in range(2)])
            mg_r = Ring([(sb("mg%d" % i, [128, D], BF16), Buf("mg")) for i in range(2)])
            junk = sb("junk", [128, D], BF16)
            junk_b = Buf("junk")
            mst = sb("mst", [128, 4], F32)
            mst_b = Buf("mst")
            pT = Ring([(ps("pT%d" % i, [128, 1024], BF16), Buf("pT")) for i in range(1)])
            pA = Ring([(ps("pA%d" % i, [128, 512], F32), Buf("pA")) for i in range(7)])
            Sc.op("vector", lambda e: e.memset(mst[:], 0.0), writes=[mst_b])
            for t in range(2):
                m, m_b = mt_r.next()
                Sc.dma("sync", lambda e, m=m, t=t: e.dma_start(out=m[:], in_=self.mem[t * 128:(t + 1) * 128, :]), writes=[m_b])
                Sc.op("scalar", lambda e, m=m, t=t: e.activation(out=junk[:], in_=m[:], func=AF.Square, accum_out=mst[:, t:t + 1]), reads=[m_b, mst_b], writes=[junk_b, mst_b])
                Sc.op("gpsimd", lambda e, t=t: e.tensor_scalar(out=mst[:, t:t + 1], in0=mst[:, t:t + 1], scalar1=1.0 / D, scalar2=EPS, op0=ALU.mult, op1=ALU.add), reads=[mst_b], writes=[mst_b])
                Sc.op("gpsimd", lambda e, t=t: e.tensor_tensor(out=mst[:, 2 + t:3 + t], in0=mst[:, t:t + 1], in1=negh[:, 0:1], op=ALU.pow), reads=[mst_b, cb_], writes=[mst_b])
                g, g_b = mg_r.next()
                Sc.op("vector", lambda e, g=g, m=m, t=t: e.scalar_tensor_tensor(out=g[:], in0=m[:], scalar=mst[:, 2 + t:3 + t], in1=gmem[:], op0=ALU.mult, op1=ALU.mult), reads=[m_b, mst_b, cb_], writes=[g_b])
                p, p_b = pT.next()
                for k in range(8):
                    Sc.op("tensor", lambda e, p=p, g=g, k=k: e.transpose(out=p[:, k * 128:(k + 1) * 128], in_=g[:, k * 128:(k + 1) * 128], identity=ident[:]), reads=[g_b, cb_], writes=[p_b])
                Sc.op("vector", lambda e, p=p, t=t: e.tensor_copy(out=memT[:, :, t * 128:(t + 1) * 128], in_=p[:, :].rearrange("p (k c) -> p k c", k=8)), reads=[p_b], writes=[memT_b])
            for h in range(4):
                p, p_b = pA.next()
                for k in range(8):
                    Sc.op("tensor", lambda e, p=p, h=h, k=k: e.matmul(p[:, 0:256], lhsT=wkv[:, k, h * 128:(h + 1) * 128], rhs=memT[:, k, :], start=(k == 0), stop=(k == 7)), reads=[wkv_b, memT_b], writes=[p_b])
                Sc.op("scalar", lambda e, p=p, h=h: e.activation(out=kmT[:, h, :], in_=p[:, 0:256], func=AF.Copy), reads=[p_b], writes=[kv_b])
            for t in range(2):
                p, p_b = pA.next()
                for k in range(8):
                    Sc.op("tensor", lambda e, p=p, t=t, k=k: e.matmul(p[:, :], lhsT=memT[:, k, t * 128:(t + 1) * 128], rhs=wkv[:, k, 512:1024], start=(k == 0), stop=(k == 7)), reads=[wkv_b, memT_b], writes=[p_b])
                Sc.op("scalar", lambda e, p=p, t=t: e.activation(out=vm[:, t, :], in_=p[:, :], func=AF.Copy), reads=[p_b], writes=[kv_b])
            qm_r = Ring([(sb("qm%d" % i, [128, 512], BF16), Buf("qm")) for i in range(3)])
            E_r = Ring([(sb("E%d" % i, [128, 2, 512], BF16), Buf("E")) for i in range(2)])
            R_r = Ring([(sb("R%d" % i, [128, 512], F32), Buf("R")) for i in range(2)])
            y_r = Ring([(sb("y%d" % i, [128, 512], BF16), Buf("y")) for i in range(3)])
            ymt_b = Buf("YMT")
            sc = 1.0 / float(np.sqrt(128.0))
            for tb in range(NB):
                for h in range(4):
                    q, q_b = qm_r.next()
                    Sc.dma("sync", lambda e, q=q, tb=tb, h=h: e.dma_start_transpose(out=q[:, :], in_=self.PROJ[tb * 512:(tb + 1) * 512, C_QM + h * 128:C_QM + (h + 1) * 128]), writes=[q_b])
                    E, E_b = E_r.next()
                    for mc in range(2):
                        p, p_b = pA.next()
                        Sc.op("tensor", lambda e, p=p, h=h, mc=mc, q=q: e.matmul(p[:, :], lhsT=kmT[:, h, mc * 128:(mc + 1) * 128], rhs=q[:, :], start=True, stop=True), reads=[kv_b, q_b], writes=[p_b])
                        Sc.op("scalar", lambda e, E=E, p=p, mc=mc: e.activation(out=E[:, mc, :], in_=p[:, :], func=AF.Exp, scale=sc), reads=[p_b], writes=[E_b])
                    pu, pu_b = pA.next()
                    pd, pd_b = pA.next()
                    for mc in range(2):
                        Sc.op("tensor", lambda e, pu=pu, E=E, mc=mc, h=h: e.matmul(pu[:, :], lhsT=vm[:, mc, h * 128:(h + 1) * 128], rhs=E[:, mc, :], start=(mc == 0), stop=(mc == 1)), reads=[kv_b, E_b], writes=[pu_b])
                    for mc in range(2):
                        Sc.op("tensor", lambda e, pd=pd, E=E, mc=mc: e.matmul(pd[:, :], lhsT=ones[:, :], rhs=E[:, mc, :], start=(mc == 0), stop=(mc == 1)), reads=[cb_, E_b], writes=[pd_b])
                    R, R_b = R_r.next()
                    Sc.op("vector", lambda e, R=R, pd=pd: e.reciprocal(out=R[:, :], in_=pd[:, :]), reads=[pd_b], writes=[R_b])
                    y, y_b = y_r.next()
                    Sc.op("vector", lambda e, y=y, R=R, pu=pu: e.tensor_tensor(out=y[:, :], in0=pu[:, :], in1=R[:, :], op=ALU.mult), reads=[pu_b, R_b], writes=[y_b])
                    Sc.dma("gpsimd", lambda e, y=y, h=h, tb=tb: e.dma_start(out=self.YMT[h * 128:(h + 1) * 128, tb * 512:(tb + 1) * 512], in_=y[:, :]), reads=[y_b], writes=[ymt_b])
            Sc.barrier()
            Sc.emit(st)

    def phase_c(self):
        nc = self.nc
        with contextlib.ExitStack() as st:
            sb = lambda n, s, d: st.enter_context(nc.sbuf_tensor("pc_" + n, s, d))
            ps = lambda n, s, d: st.enter_context(nc.psum_tensor("pc_" + n, s, d))
            Sc = Sched(nc, prefix="pc_")
            cb_ = Buf("const")
            w_b = Buf("w")
            identf = sb("identf", [128, 128], F32)
            ident = sb("ident", [128, 128], BF16)
            gffn = sb("gffn", [128, D], F32)
            negh = sb("negh", [128, 1], F32)
            wpa = sb("wpa", [128, 6, D], BF16)
            wpr = sb("wpr", [128, 6, D], BF16)
            wpm = sb("wpm", [128, 4, D], BF16)
            wo = sb("wo", [128, 8, D], BF16)
            Sc.dma("sync", lambda e: e.dma_start(out=identf[:], in_=self.c_ident[:, :]), writes=[cb_])
            Sc.dma("sync", lambda e: e.dma_start(out=gffn[:], in_=self.g_ffn[0, :].partition_broadcast(128)), writes=[cb_])
            Sc.dma("gpsimd", lambda e: e.dma_start(out=wpa[:], in_=self.w_pa[0, :, :].rearrange("(k p) c -> p k c", p=128)), writes=[w_b])
            Sc.dma("gpsimd", lambda e: e.dma_start(out=wpr[:], in_=self.w_pr[0, :, :].rearrange("(k p) c -> p k c", p=128)), writes=[w_b])
            Sc.dma("gpsimd", lambda e: e.dma_start(out=wpm[:], in_=self.w_pm[0, :, :].rearrange("(k p) c -> p k c", p=128)), writes=[w_b])
            Sc.dma("gpsimd", lambda e: e.dma_start(out=wo[:], in_=self.w_out[0, :, :].rearrange("(k p) c -> p k c", p=128)), writes=[w_b])
            Sc.op("vector", lambda e: e.tensor_copy(out=ident[:], in_=identf[:]), reads=[cb_], writes=[cb_])
            Sc.op("vector", lambda e: e.memset(negh[:], -0.5), reads=[cb_], writes=[cb_])
            ya_r = Ring([(sb("ya%d" % i, [128, 6, 512], BF16), Buf("ya")) for i in range(2)])
            yr_r = Ring([(sb("yr%d" % i, [128, 6, 512], BF16), Buf("yr")) for i in range(2)])
            ym_r = Ring([(sb("ym%d" % i, [128, 4, 512], BF16), Buf("ym")) for i in range(2)])
            gt_r = Ring([(sb("gt%d" % i, [128, 3, 512], BF16), Buf("gt")) for i in range(3)])
            mg_r = Ring([(sb("mg%d" % i, [128, 8, 512], BF16), Buf("mg")) for i in range(2)])
            m1_r = Ring([(sb("m1_%d" % i, [128, 512], F32), Buf("m1")) for i in range(2)])
            m2_r = Ring([(sb("m2_%d" % i, [128, 512], F32), Buf("m2")) for i in range(2)])
            m3_r = Ring([(sb("m3_%d" % i, [128, 512], F32), Buf("m3")) for i in range(2)])
            x_r = Ring([(sb("x%d" % i, [128, D], F32), Buf("x")) for i in range(2)])
            h_r = Ring([(sb("h%d" % i, [128, D], F32), Buf("h")) for i in range(2)])
            hn_r = Ring([(sb("hn%d" % i, [128, D], BF16), Buf("hn")) for i in range(2)])
            ht_r = Ring([(sb("ht%d" % i, [128, 8, 128], BF16), Buf("ht")) for i in range(2)])
            junk = sb("junk", [128, D], BF16)
            junk_b = Buf("junk")
            stt_ = sb("stat", [128, 3, NT], F32)
            st_b = [Buf("st") for _ in range(NT)]
            Sc.op("vector", lambda e: e.memset(stt_[:], 0.0), writes=st_b)
            pP = Ring([(ps("pP%d" % i, [128, 512], F32), Buf("pP")) for i in range(5)])
            pO = Ring([(ps("pO%d" % i, [128, 512], F32), Buf("pO")) for i in range(2)])
            pT = Ring([(ps("pT%d" % i, [128, 1024], BF16), Buf("pT")) for i in range(1)])
            pend = []
            h_out_b = Buf("H")
            hnt_b = Buf("HNT")
            def load_y(tb):
                cs = slice(tb * 512, (tb + 1) * 512)
                ya, ya_b = ya_r.next()
                yr, yr_b = yr_r.next()
                ym, ym_b = ym_r.next()
                Sc.dma("sync", lambda e, ya=ya, cs=cs: e.dma_start(out=ya[:], in_=self.YAT[:, cs].rearrange("(k p) s -> p k s", p=128)), writes=[ya_b])
                Sc.dma("sync", lambda e, ym=ym, cs=cs: e.dma_start(out=ym[:], in_=self.YMT[:, cs].rearrange("(k p) s -> p k s", p=128)), writes=[ym_b])
                for k in range(6):
                    Sc.dma("sync", lambda e, yr=yr, tb=tb, k=k: e.dma_start_transpose(out=yr[:, k, :], in_=self.YR[tb * 512:(tb + 1) * 512, k * 128:(k + 1) * 128]), writes=[yr_b])
                return (ya, ya_b, yr, yr_b, ym, ym_b)
            ynext = load_y(0)
            for tb in range(NB):
                cs = slice(tb * 512, (tb + 1) * 512)
                ya, ya_b, yr, yr_b, ym, ym_b = ynext
                if tb + 1 < NB:
                    ynext = load_y(tb + 1)
                mg, mg_b = mg_r.next()
                for fc in range(8):
                    gt, gt_b = gt_r.next()
                    Sc.dma("sync", lambda e, gt=gt, fc=fc, cs=cs: e.dma_start(out=gt[:], in_=self.GTT.rearrange("(i f) s -> f i s", i=3)[fc * 128:(fc + 1) * 128, :, cs]), writes=[gt_b])
                    prs = []
                    for (w, src, src_b, nk) in ((wpa, ya, ya_b, 6), (wpr, yr, yr_b, 6), (wpm, ym, ym_b, 4)):
                        p, p_b = pP.next()
                        for k in range(nk):
                            Sc.op("tensor", lambda e, p=p, w=w, src=src, k=k, fc=fc, nk=nk: e.matmul(p[:, :], lhsT=w[:, k, fc * 128:(fc + 1) * 128], rhs=src[:, k, :], start=(k == 0), stop=(k == nk - 1)), reads=[w_b, src_b], writes=[p_b])
                        prs.append((p, p_b))
                    m1, m1_b = m1_r.next()
                    m2, m2_b = m2_r.next()
                    m3, m3_b = m3_r.next()
                    for i, (m, m_b) in enumerate(((m1, m1_b), (m2, m2_b), (m3, m3_b))):
                        p, p_b = prs[i]
                        Sc.op("vector", lambda e, m=m, gt=gt, i=i, p=p: e.scalar_tensor_tensor(out=m[:, :], in0=gt[:, i, :], scalar=1.0, in1=p[:, :], op0=ALU.add, op1=ALU.mult), reads=[gt_b, p_b], writes=[m_b])
                    Sc.op("gpsimd", lambda e, m1=m1, m2=m2: e.tensor_tensor(out=m1[:, :], in0=m1[:, :], in1=m2[:, :], op=ALU.add), reads=[m1_b, m2_b], writes=[m1_b])
                    Sc.op("gpsimd", lambda e, mg=mg, m1=m1, m3=m3, fc=fc: e.tensor_tensor(out=mg[:, fc, :], in0=m1[:, :], in1=m3[:, :], op=ALU.add), reads=[m1_b, m3_b], writes=[mg_b])
                for tt in range(4):
                    t = tb * 4 + tt
                    x, x_b = x_r.next()
                    Sc.dma("sync", lambda e, x=x, t=t: e.dma_start(out=x[:], in_=self.x[t * 128:(t + 1) * 128, :]), writes=[x_b])
                    h, h_b = h_r.next()
                    for nh in range(2):
                        p, p_b = pO.next()
                        for k in range(8):
                            Sc.op("tensor", lambda e, p=p, mg=mg, k=k, tt=tt, nh=nh: e.matmul(p[:, :], lhsT=mg[:, k, tt * 128:(tt + 1) * 128], rhs=wo[:, k, nh * 512:(nh + 1) * 512], start=(k == 0), stop=(k == 7)), reads=[mg_b, w_b], writes=[p_b])
                        Sc.op("vector", lambda e, h=h, p=p, x=x, nh=nh: e.scalar_tensor_tensor(out=h[:, nh * 512:(nh + 1) * 512], in0=p[:, :], scalar=0.5, in1=x[:, nh * 512:(nh + 1) * 512], op0=ALU.mult, op1=ALU.add), reads=[p_b, x_b], writes=[h_b])
                    while pend:
                        pend.pop(0)()
                    Sc.dma("gpsimd", lambda e, h=h, t=t: e.dma_start(out=self.H[t * 128:(t + 1) * 128, :], in_=h[:]), reads=[h_b], writes=[h_out_b])
                    Sc.op("scalar", lambda e, h=h, t=t: e.activation(out=junk[:], in_=h[:], func=AF.Square, accum_out=stt_[:, 0, t:t + 1]), reads=[h_b, st_b[t]], writes=[junk_b, st_b[t]])
                    Sc.op("gpsimd", lambda e, t=t: e.tensor_scalar(out=stt_[:, 1, t:t + 1], in0=stt_[:, 0, t:t + 1], scalar1=1.0 / D, scalar2=EPS, op0=ALU.mult, op1=ALU.add), reads=[st_b[t]], writes=[st_b[t]])
                    Sc.op("gpsimd", lambda e, t=t: e.tensor_tensor(out=stt_[:, 2, t:t + 1], in0=stt_[:, 1, t:t + 1], in1=negh[:, 0:1], op=ALU.pow), reads=[st_b[t], cb_], writes=[st_b[t]])
                    hn, hn_b = hn_r.next()
                    Sc.op("vector", lambda e, hn=hn, h=h, t=t: e.scalar_tensor_tensor(out=hn[:], in0=h[:], scalar=stt_[:, 2, t:t + 1], in1=gffn[:], op0=ALU.mult, op1=ALU.mult), reads=[h_b, st_b[t], cb_], writes=[hn_b])
                    def tail(hn=hn, hn_b=hn_b, t=t):
                        p, p_b = pT.next()
                        for k in range(8):
                            Sc.op("tensor", lambda e, p=p, hn=hn, k=k: e.transpose(out=p[:, k * 128:(k + 1) * 128], in_=hn[:, k * 128:(k + 1) * 128], identity=ident[:]), reads=[hn_b, cb_], writes=[p_b])
                        ht, ht_b = ht_r.next()
                        Sc.op("scalar", lambda e, ht=ht, p=p: e.activation(out=ht[:], in_=p[:, :].rearrange("p (k c) -> p k c", k=8), func=AF.Copy), reads=[p_b], writes=[ht_b])
                        Sc.dma("gpsimd", lambda e, ht=ht, t=t: e.dma_start(out=self.HNT[:, t * 128:(t + 1) * 128].rearrange("(k p) s -> p k s", p=128), in_=ht[:]), reads=[ht_b], writes=[hnt_b])
                    pend.append(tail)
            while pend:
                pend.pop(0)()
            Sc.barrier()
            Sc.emit(st)

    def phase_d(self):
        nc = self.nc
        NF = DFF // 128
        with contextlib.ExitStack() as st:
            sb = lambda n, s, d: st.enter_context(nc.sbuf_tensor("d_" + n, s, d))
            ps = lambda n, s, d: st.enter_context(nc.psum_tensor("d_" + n, s, d))
            Sc = Sched(nc, prefix="d_")
            cb_ = Buf("const")
            hnT = sb("hnT", [128, 8, S], BF16)
            hn_b = [Buf("hnT") for _ in range(NB)]
            for tb in range(NB):
                Sc.dma("sync", lambda e, tb=tb: e.dma_start(out=hnT[:, :, tb * 512:(tb + 1) * 512], in_=self.HNT[:, tb * 512:(tb + 1) * 512].rearrange("(k p) s -> p k s", p=128)), writes=[hn_b[tb]])
            cw = sb("cw", [128, 2, 3, NF], F32)
            cbias = sb("cbias", [128, 2, NF], F32)
            for ab in range(2):
                for j in range(3):
                    Sc.dma("sync", lambda e, ab=ab, j=j: e.dma_start(out=cw[:, ab, j, :], in_=self.conv_w[0, j, ab * DFF:(ab + 1) * DFF].rearrange("(f p) -> p f", p=128), allow_slow_non_contiguous=True), writes=[cb_])
                Sc.dma("sync", lambda e, ab=ab: e.dma_start(out=cbias[:, ab, :], in_=self.conv_b[0, ab * DFF:(ab + 1) * DFF].rearrange("(f p) -> p f", p=128), allow_slow_non_contiguous=True), writes=[cb_])
            w_r = Ring([(sb("w%d" % i, [128, 8, 256], BF16), Buf("w")) for i in range(2)])
            u_r = []
            for i in range(2):
                ua = sb("ua%d" % i, [128, S + 2], F32)
                ub = sb("ub%d" % i, [128, S + 2], F32)
                bl = [Buf("u") for _ in range(NB + 1)]
                for t_ in (ua, ub):
                    Sc.op("gpsimd", lambda e, t_=t_: e.memset(t_[:, 0:1], 0.0), writes=[bl[NB]])
                    Sc.op("gpsimd", lambda e, t_=t_: e.memset(t_[:, S + 1:S + 2], 0.0), writes=[bl[NB]])
                u_r.append(((ua, ub), bl))
            u_r = Ring(u_r)
            ca = sb("ca", [128, S], F32)
            cbb = sb("cb", [128, S], F32)
            th = sb("th", [128, S], F32)
            ca_b, cbb_b, th_b = Buf("ca"), Buf("cb"), Buf("th")
            g_r = Ring([(sb("g%d" % i, [128, S], BF16), Buf("g")) for i in range(2)])
            pA = Ring([(ps("pA%d" % i, [128, 512], F32), Buf("pA")) for i in range(8)])
            gt2_b = Buf("GT2")
            for fc in range(NF):
                w, w_b = w_r.next()
                Sc.dma("gpsimd", lambda e, w=w, fc=fc: e.dma_start(out=w[:, :, 0:128], in_=self.w_up[0, :, fc * 128:(fc + 1) * 128].rearrange("(k p) c -> p k c", p=128)), writes=[w_b])
                Sc.dma("gpsimd", lambda e, w=w, fc=fc: e.dma_start(out=w[:, :, 128:256], in_=self.w_up[0, :, DFF + fc * 128:DFF + (fc + 1) * 128].rearrange("(k p) c -> p k c", p=128)), writes=[w_b])
                (ua, ub), ubl = u_r.next()
                for tb in range(NB):
                    for ab, ut in ((0, ua), (1, ub)):
                        p, p_b = pA.next()
                        for k in range(8):
                            Sc.op("tensor", lambda e, p=p, w=w, k=k, ab=ab, tb=tb: e.matmul(p[:, :], lhsT=w[:, k, ab * 128:(ab + 1) * 128], rhs=hnT[:, k, tb * 512:(tb + 1) * 512], start=(k == 0), stop=(k == 7)), reads=[w_b, hn_b[tb]], writes=[p_b])
                        Sc.op("scalar", lambda e, ut=ut, p=p, tb=tb: e.activation(out=ut[:, 1 + tb * 512:1 + (tb + 1) * 512], in_=p[:, :], func=AF.Copy), reads=[p_b], writes=[ubl[tb]])
                for ab, ut, ct, ct_b, eng in ((0, ua, ca, ca_b, "vector"), (1, ub, cbb, cbb_b, "vector")):
                    Sc.op("scalar", lambda e, ut=ut, ct=ct, ab=ab, fc=fc: e.activation(out=ct[:, :], in_=ut[:, 0:S], func=AF.Identity, bias=cbias[:, ab, fc:fc + 1], scale=cw[:, ab, 0, fc:fc + 1]), reads=ubl + [cb_], writes=[ct_b])
                    Sc.op(eng, lambda e, ut=ut, ct=ct, ab=ab, fc=fc: e.scalar_tensor_tensor(out=ct[:, :], in0=ut[:, 1:S + 1], scalar=cw[:, ab, 1, fc:fc + 1], in1=ct[:, :], op0=ALU.mult, op1=ALU.add), reads=ubl + [cb_, ct_b], writes=[ct_b])
                    Sc.op(eng, lambda e, ut=ut, ct=ct, ab=ab, fc=fc: e.scalar_tensor_tensor(out=ct[:, :], in0=ut[:, 2:S + 2], scalar=cw[:, ab, 2, fc:fc + 1], in1=ct[:, :], op0=ALU.mult, op1=ALU.add), reads=ubl + [cb_, ct_b], writes=[ct_b])
                Sc.op("scalar", lambda e: e.activation(out=th[:, :], in_=ca[:, :], func=AF.Tanh, scale=0.5), reads=[ca_b], writes=[th_b])
                Sc.op("vector", lambda e: e.scalar_tensor_tensor(out=th[:, :], in0=th[:, :], scalar=1.0, in1=ca[:, :], op0=ALU.add, op1=ALU.mult), reads=[ca_b, th_b], writes=[th_b])
                g, g_b = g_r.next()
                Sc.op("vector", lambda e, g=g: e.tensor_tensor(out=g[:, :], in0=th[:, :], in1=cbb[:, :], op=ALU.mult), reads=[th_b, cbb_b], writes=[g_b])
                Sc.dma("sync", lambda e, g=g, fc=fc: e.dma_start(out=self.GT2[fc * 128:(fc + 1) * 128, :], in_=g[:, :]), reads=[g_b], writes=[gt2_b])
            Sc.barrier()
            Sc.emit(st)

    def phase_e(self):
        nc = self.nc
        NF = DFF // 128
        with contextlib.ExitStack() as st:
            sb = lambda n, s, d: st.enter_context(nc.sbuf_tensor("e_" + n, s, d))
            ps = lambda n, s, d: st.enter_context(nc.psum_tensor("e_" + n, s, d))
            Sc = Sched(nc, prefix="e_")
            cb_ = Buf("const")
            w_b = Buf("w")
            wd = sb("wd", [128, NF, D], BF16)
            gfin = sb("gfin", [128, D], F32)
            negh = sb("negh", [128, 1], F32)
            for q4 in range(2):
                Sc.dma("gpsimd", lambda e, q4=q4: e.dma_start(out=wd[:, q4 * 11:(q4 + 1) * 11, :], in_=self.w_down[0, q4 * 11 * 128:(q4 + 1) * 11 * 128, :].rearrange("(k p) c -> p k c", p=128)), writes=[w_b])
            Sc.dma("sync", lambda e: e.dma_start(out=gfin[:], in_=self.g_final.partition_broadcast(128)), writes=[cb_])
            Sc.op("vector", lambda e: e.memset(negh[:], -0.5), reads=[cb_], writes=[cb_])
            g_r = Ring([(sb("g%d" % i, [128, NF, 512], BF16), Buf("g")) for i in range(2)])
            h_r = Ring([(sb("h%d" % i, [128, D], F32), Buf("h")) for i in range(3)])
            o_r = Ring([(sb("o%d" % i, [128, D], F32), Buf("o")) for i in range(2)])
            junk = sb("junk", [128, D], BF16)
            junk_b = Buf("junk")
            stt_ = sb("stat", [128, 3, NT], F32)
            st_b = [Buf("st") for _ in range(NT)]
            Sc.op("vector", lambda e: e.memset(stt_[:], 0.0), writes=st_b)
            pO = Ring([(ps("pO%d" % i, [128, 512], F32), Buf("pO")) for i in range(6)])
            out_b = Buf("out")
            def load_g(tb):
                g, g_b = g_r.next()
                Sc.dma("sync", lambda e, g=g, tb=tb: e.dma_start(out=g[:], in_=self.GT2[:, tb * 512:(tb + 1) * 512].rearrange("(k p) s -> p k s", p=128)), writes=[g_b])
                return g, g_b
            gnext = load_g(0)
            for tb in range(NB):
                g, g_b = gnext
                if tb + 1 < NB:
                    gnext = load_g(tb + 1)
                for tt in range(4):
                    t = tb * 4 + tt
                    h, h_b = h_r.next()
                    Sc.dma("sync", lambda e, h=h, t=t: e.dma_start(out=h[:], in_=self.H[t * 128:(t + 1) * 128, :]), writes=[h_b])
                    for nh in range(2):
                        p, p_b = pO.next()
                        for k in range(NF):
                            Sc.op("tensor", lambda e, p=p, g=g, k=k, tt=tt, nh=nh: e.matmul(p[:, :], lhsT=g[:, k, tt * 128:(tt + 1) * 128], rhs=wd[:, k, nh * 512:(nh + 1) * 512], start=(k == 0), stop=(k == NF - 1)), reads=[g_b, w_b], writes=[p_b])
                        Sc.op("vector", lambda e, h=h, p=p, nh=nh: e.scalar_tensor_tensor(out=h[:, nh * 512:(nh + 1) * 512], in0=p[:, :], scalar=0.5, in1=h[:, nh * 512:(nh + 1) * 512], op0=ALU.mult, op1=ALU.add), reads=[p_b, h_b], writes=[h_b])
                    Sc.op("scalar", lambda e, h=h, t=t: e.activation(out=junk[:], in_=h[:], func=AF.Square, accum_out=stt_[:, 0, t:t + 1]), reads=[h_b, st_b[t]], writes=[junk_b, st_b[t]])
                    Sc.op("gpsimd", lambda e, t=t: e.tensor_scalar(out=stt_[:, 1, t:t + 1], in0=stt_[:, 0, t:t + 1], scalar1=1.0 / D, scalar2=EPS, op0=ALU.mult, op1=ALU.add), reads=[st_b[t]], writes=[st_b[t]])
                    Sc.op("gpsimd", lambda e, t=t: e.tensor_tensor(out=stt_[:, 2, t:t + 1], in0=stt_[:, 1, t:t + 1], in1=negh[:, 0:1], op=ALU.pow), reads=[st_b[t], cb_], writes=[st_b[t]])
                    o, o_b = o_r.next()
                    Sc.op("vector", lambda e, o=o, h=h, t=t: e.scalar_tensor_tensor(out=o[:], in0=h[:], scalar=stt_[:, 2, t:t + 1], in1=gfin[:], op0=ALU.mult, op1=ALU.mult), reads=[h_b, st_b[t], cb_], writes=[o_b])
                    Sc.dma("sync", lambda e, o=o, t=t: e.dma_start(out=self.out[t * 128:(t + 1) * 128, :], in_=o[:]), reads=[o_b], writes=[out_b])
            Sc.barrier()
            Sc.emit(st)

    def build(self):
        for ph in ("a", "b1", "b1c", "b2", "b3", "c", "d", "e"):
            if self.phases is not None and ph not in self.phases:
                continue
            fn = getattr(self, "phase_" + ph, None)
            if fn is not None:
                fn()
            if self.stop_after == ph:
                break
        return self.nc


def host_consts():
    inv = 10000.0 ** (-np.arange(0, 64, 2, dtype=np.float32) / 64.0)
    ang = np.arange(S, dtype=np.float32)[:, None] * inv[None, :].astype(np.float32)
    c = {
        "c_cos": np.cos(ang).astype(np.float32),
        "c_sin": np.sin(ang).astype(np.float32),
        "c_ident": np.eye(128, dtype=np.float32),
    }
    kk = np.arange(128)[:, None]
    qq = np.arange(256)[None, :]
    band = ((qq >= kk) & (qq <= kk + 128))
    m = np.where(band, 0.0, -30000.0).astype(np.float32)
    am = np.zeros((128, 1024), np.float32)
    am[:, 0:256] = m
    am[:, 256:512] = m
    mf = m[64:128, 128:256]
    ml = m[0:64, 0:128]
    am[0:64, 512:640] = mf
    am[0:64, 640:768] = mf
    am[0:64, 768:896] = ml
    am[0:64, 896:1024] = ml
    c["c_amask"] = am
    r = np.zeros((128, 8, 128), np.float32)
    mm = np.arange(128)[:, None].astype(np.float32)
    nn = np.arange(128)[None, :].astype(np.float32)
    r[:, 0, :] = np.maximum(nn - mm, 0)
    r[:, 1, :] = np.maximum(mm - nn, 0)
    r[:, 2, :] = (nn >= mm) * 0.125
    r[:, 3, :] = (mm > nn) * 0.125
    r[:, 4, :] = nn + 1.0
    r[:, 5, :] = 128.0 - nn
    r[:, 6, 0] = 127.0 - np.arange(128)
    r[:, 6, 1] = np.arange(128)
    c["c_ret"] = r
    return c


def make_in_maps(inputs, n_cores=8):
    consts = host_consts()
    maps = []
    for b in range(n_cores):
        m = {"x": np.ascontiguousarray(inputs["x"][b]), "mem": np.ascontiguousarray(inputs["mem"][b])}
        for k, v in inputs.items():
            if k in ("x", "mem"):
                continue
            m[k] = np.ascontiguousarray(v)
        m.update(consts)
        maps.append(m)
    return maps


def kernel(**inputs):
    inputs = {k: np.asarray(v) for k, v in inputs.items()}
    prog = Prog()
    nc = prog.build()
    res = run_bass_kernel_spmd(nc, make_in_maps(inputs), core_ids=list(range(8)))
    return np.stack([r["out"] for r in res.results], axis=0)
```

```python
import contextlib
import numpy as np
import concourse.bass as bass
import concourse.mybir as mybir
from concourse.bass_utils import run_bass_kernel_spmd

F32 = mybir.dt.float32
BF16 = mybir.dt.bfloat16
AF = mybir.ActivationFunctionType
ALU = mybir.AluOpType
AX = mybir.AxisListType

S = 4096
D = 1024
NT = S // 128
NB = S // 512
IN_W = 5120
DFF = 2816
EPS = 1e-6
C_QA, C_KA, C_VA, C_QR, C_KR, C_VR, C_GR, C_QM = 0, 768, 1536, 2304, 2688, 3072, 3840, 4608


class Buf:
    __slots__ = ("name", "w", "r")

    def __init__(self, name):
        self.name = name
        self.w = None
        self.r = []


class Sched:
    ENG = ("tensor", "vector", "scalar", "gpsimd", "sync")

    def __init__(self, nc, n_dma_sems=32, prefix=""):
        self.nc = nc
        self.prefix = prefix
        self.lists = {e: [] for e in self.ENG}
        self.cnt = {e: 0 for e in self.ENG}
        self.known = {e: {} for e in self.ENG}
        self.ndma = n_dma_sems
        self.dma_issued = [0] * n_dma_sems
        self.dma_rr = 0

    def _need(self, eng, ev, waits):
        if ev is None:
            return
        key, val = ev
        if key == eng and eng == "tensor":
            return
        if self.known[eng].get(key, 0) >= val:
            return
        if waits.get(key, 0) < val:
            waits[key] = val

    def _deps(self, eng, reads, writes):
        waits = {}
        for b in reads:
            self._need(eng, b.w, waits)
        for b in writes:
            self._need(eng, b.w, waits)
            for ev in b.r:
                self._need(eng, ev, waits)
        for k, v in waits.items():
            self.known[eng][k] = v
        return list(waits.items())

    def op(self, eng, fn, reads=(), writes=()):
        waits = self._deps(eng, reads, writes)
        self.cnt[eng] += 1
        ev = (eng, self.cnt[eng])
        self.lists[eng].append((waits, fn, eng, 1))
        for b in reads:
            b.r.append(ev)
        for b in writes:
            b.w = ev
            b.r = []
        return ev

    def dma(self, eng, fn, reads=(), writes=()):
        i = self.dma_rr
        self.dma_rr = (self.dma_rr + 1) % self.ndma
        key = ("dma", i)
        waits = dict(self._deps(eng, reads, writes))
        prev = self.dma_issued[i]
        if prev > 0 and self.known[eng].get(key, 0) < prev:
            waits[key] = prev
            self.known[eng][key] = prev
        self.dma_issued[i] = prev + 16
        ev = (key, prev + 16)
        self.lists[eng].append((list(waits.items()), fn, key, 16))
        for b in reads:
            b.r.append(ev)
        for b in writes:
            b.w = ev
            b.r = []
        return ev

    def barrier(self):
        for e in self.ENG:
            waits = {}
            for o in self.ENG:
                if o != e and self.cnt[o] > 0:
                    self._need(e, (o, self.cnt[o]), waits)
            if e != "tensor" and self.cnt[e] > 0:
                self._need(e, (e, self.cnt[e]), waits)
            for i in range(self.ndma):
                if self.dma_issued[i] > 0:
                    self._need(e, (("dma", i), self.dma_issued[i]), waits)
            for k, v in waits.items():
                self.known[e][k] = v
            self.lists[e].append((list(waits.items()), None, None, 0))

    def emit(self, stack):
        nc = self.nc
        semmap = {}
        handles = []
        for e in self.ENG:
            semmap[e] = nc.alloc_semaphore(name=self.prefix + "s_" + e)
            handles.append(semmap[e])
        for i in range(self.ndma):
            semmap[("dma", i)] = nc.alloc_semaphore(name=self.prefix + "s_dma%d" % i)
            handles.append(semmap[("dma", i)])

        def runner(items):
            def f(e):
                for waits, fn, key, inc in items:
                    for k, v in waits:
                        e.wait_ge(semmap[k], v)
                    if fn is not None:
                        fn(e).then_inc(semmap[key], inc)
            return f
        with nc.Block() as block:
            block.tensor(runner(self.lists["tensor"]))
            block.vector(runner(self.lists["vector"]))
            block.scalar(runner(self.lists["scalar"]))
            block.gpsimd(runner(self.lists["gpsimd"]))
            block.sync(runner(self.lists["sync"]))
        nc.clear_and_free_semaphores(handles)
        nc.all_engine_barrier()


class Ring:
    def __init__(self, items):
        self.items = items
        self.i = 0

    def next(self):
        it = self.items[self.i]
        self.i = (self.i + 1) % len(self.items)
        return it


class Prog:
    def __init__(self, debug=False, stop_after=None, phases=None, ext_in=()):
        self.debug = debug
        self.stop_after = stop_after
        self.phases = phases
        self.ext_in = set(ext_in)
        nc = self.nc = bass.Bass("TRN2", target_bir_lowering=False)
        ein = lambda n, s: nc.dram_tensor(n, s, F32, kind="ExternalInput").ap()
        self.x = ein("x", [S, D])
        self.mem = ein("mem", [256, D])
        self.g_mix = ein("g_mix", [1, D])
        self.w_in = ein("w_in", [1, D, IN_W])
        self.w_mem_kv = ein("w_mem_kv", [1, D, 1024])
        self.g_mem = ein("g_mem", [1, D])
        self.dec_f = ein("ret_decay_fwd", [1, 6])
        self.dec_b = ein("ret_decay_bwd", [1, 6])
        self.g_ret = ein("g_ret", [1, 768])
        self.w_pa = ein("w_proj_attn", [1, 768, D])
        self.w_pr = ein("w_proj_ret", [1, 768, D])
        self.w_pm = ein("w_proj_mem", [1, 512, D])
        self.w_gate = ein("w_gate", [1, D, 3 * D])
        self.b_gate = ein("b_gate", [1, 3 * D])
        self.w_out = ein("w_out", [1, D, D])
        self.g_ffn = ein("g_ffn", [1, D])
        self.w_up = ein("w_up", [1, D, 2 * DFF])
        self.conv_w = ein("conv_w", [1, 3, 2 * DFF])
        self.conv_b = ein("conv_b", [1, 2 * DFF])
        self.w_down = ein("w_down", [1, DFF, D])
        self.g_final = ein("g_final", [D])
        self.c_cos = ein("c_cos", [S, 32])
        self.c_sin = ein("c_sin", [S, 32])
        self.c_ident = ein("c_ident", [128, 128])
        self.c_amask = ein("c_amask", [128, 1024])
        self.c_ret = ein("c_ret", [128, 8, 128])
        self.out = nc.dram_tensor("out", [S, D], F32, kind="ExternalOutput").ap()
        kind = "ExternalOutput" if debug else "Internal"
        scr = lambda n, s, d: nc.dram_tensor(n, s, d, kind=("ExternalInput" if n in self.ext_in else kind)).ap()
        self.PROJ = scr("PROJ", [S, IN_W], BF16)
        self.GTT = scr("GTT", [3 * D, S], BF16)
        self.UD = scr("UD", [12 * 128, S], F32)
        self.YAT = scr("YAT", [768, S], BF16)
        self.YR = scr("YR", [S, 768], BF16)
        self.YMT = scr("YMT", [512, S], BF16)
        self.H = scr("H", [S, D], F32)
        self.HNT = scr("HNT", [D, S], BF16)
        self.GT2 = scr("GT2", [DFF, S], BF16)

    def phase_a(self):
        nc = self.nc
        with contextlib.ExitStack() as st:
            sb = lambda n, s, d: st.enter_context(nc.sbuf_tensor("a_" + n, s, d))
            ps = lambda n, s, d: st.enter_context(nc.psum_tensor("a_" + n, s, d))
            Sc = Sched(nc, prefix="a_")
            xT = sb("xT", [128, 8, S], BF16)
            xT_b = [Buf("xT%d" % t) for t in range(NT)]
            xr = Ring([(sb("xr%d" % i, [128, D], F32), Buf("xr%d" % i)) for i in range(3)])
            xg = Ring([(sb("xg%d" % i, [128, D], BF16), Buf("xg%d" % i)) for i in range(2)])
            junk = sb("junk", [128, D], BF16)
            junk_b = Buf("junk")
            gmix = sb("gmix", [128, D], F32)
            gmix_b = Buf("gmix")
            ssq = sb("ssq", [128, NT], F32)
            msq = sb("msq", [128, NT], F32)
            rstd = sb("rstd", [128, NT], F32)
            negh = sb("negh", [128, 1], F32)
            st_b = [Buf("st%d" % t) for t in range(NT)]
            const_b = Buf("const")
            identf = sb("identf", [128, 128], F32)
            ident = sb("ident", [128, 128], BF16)
            cos_t = sb("cos_t", [128, NT, 32], F32)
            sin_t = sb("sin_t", [128, NT, 32], F32)
            hb = sb("hb", [128, 24], F32)
            pT = Ring([(ps("pT%d" % i, [128, 1024], BF16), Buf("pT%d" % i)) for i in range(2)])
            pA = Ring([(ps("pA%d" % i, [128, 512], F32), Buf("pA%d" % i)) for i in range(6)])
            wr = Ring([(sb("w%d" % i, [128, 8, 512], BF16), Buf("w%d" % i)) for i in range(3)])
            ob = Ring([(sb("ob%d" % i, [128, 512], BF16), Buf("ob%d" % i)) for i in range(4)])
            tA = Ring([(sb("tA%d" % i, [128, 512], F32), Buf("tA%d" % i)) for i in range(3)])
            tB = Ring([(sb("tB%d" % i, [128, 512], F32), Buf("tB%d" % i)) for i in range(3)])
            proj_b = Buf("PROJ")
            gtt_b = Buf("GTT")

            Sc.dma("sync", lambda e: e.dma_start(out=gmix[:], in_=self.g_mix[0, :].partition_broadcast(128)), writes=[gmix_b])
            Sc.dma("sync", lambda e: e.dma_start(out=identf[:], in_=self.c_ident[:, :]), writes=[const_b])
            Sc.dma("sync", lambda e: e.dma_start(out=cos_t[:], in_=self.c_cos.rearrange("(t p) c -> p t c", p=128)), writes=[const_b])
            Sc.dma("sync", lambda e: e.dma_start(out=sin_t[:], in_=self.c_sin.rearrange("(t p) c -> p t c", p=128)), writes=[const_b])
            Sc.dma("sync", lambda e: e.dma_start(out=hb[:], in_=self.b_gate[0, :].rearrange("(f p) -> p f", p=128), allow_slow_non_contiguous=True), writes=[const_b])
            Sc.op("vector", lambda e: e.tensor_copy(out=ident[:], in_=identf[:]), reads=[const_b], writes=[const_b])
            Sc.op("vector", lambda e: e.tensor_scalar(out=hb[:], in0=hb[:], scalar1=0.5, scalar2=None, op0=ALU.mult), reads=[const_b], writes=[const_b])
            Sc.op("vector", lambda e: e.memset(ssq[:], 0.0), writes=st_b)
            Sc.op("vector", lambda e: e.memset(negh[:], -0.5), writes=[const_b])

            for t in range(NT):
                xt, xt_b = xr.next()
                Sc.dma("sync", lambda e, xt=xt, t=t: e.dma_start(out=xt[:], in_=self.x[t * 128:(t + 1) * 128, :]), writes=[xt_b])
                Sc.op("scalar", lambda e, xt=xt, t=t: e.activation(out=junk[:], in_=xt[:], func=AF.Square, accum_out=ssq[:, t:t + 1]),
                      reads=[xt_b], writes=[junk_b, st_b[t]])
                Sc.op("gpsimd", lambda e, t=t: e.tensor_scalar(out=msq[:, t:t + 1], in0=ssq[:, t:t + 1], scalar1=1.0 / D, scalar2=EPS, op0=ALU.mult, op1=ALU.add),
                      reads=[st_b[t]], writes=[st_b[t]])
                Sc.op("gpsimd", lambda e, t=t: e.tensor_tensor(out=rstd[:, t:t + 1], in0=msq[:, t:t + 1], in1=negh[:, 0:1], op=ALU.pow),
                      reads=[st_b[t], const_b], writes=[st_b[t]])
                g, g_b = xg.next()
                Sc.op("vector", lambda e, g=g, xt=xt, t=t: e.scalar_tensor_tensor(out=g[:], in0=xt[:], scalar=rstd[:, t:t + 1], in1=gmix[:], op0=ALU.mult, op1=ALU.mult),
                      reads=[xt_b, st_b[t], gmix_b], writes=[g_b])
                p, p_b = pT.next()
                for k in range(8):
                    Sc.op("tensor", lambda e, p=p, g=g, k=k: e.transpose(out=p[:, k * 128:(k + 1) * 128], in_=g[:, k * 128:(k + 1) * 128], identity=ident[:]),
                          reads=[g_b, const_b], writes=[p_b])
                eng = "scalar" if t % 2 == 0 else "vector"
                dst = xT[:, :, t * 128:(t + 1) * 128]
                src = p[:, :].rearrange("p (k c) -> p k c", k=8)
                if eng == "scalar":
                    Sc.op("scalar", lambda e, dst=dst, src=src: e.activation(out=dst, in_=src, func=AF.Copy), reads=[p_b], writes=[xT_b[t]])
                else:
                    Sc.op("vector", lambda e, dst=dst, src=src: e.tensor_copy(out=dst, in_=src), reads=[p_b], writes=[xT_b[t]])

            blocks = [(C_QA, 512, "rot"), (C_QA + 512, 256, "rot"), (C_KA, 512, "rot"), (C_KA + 512, 256, "rot"),
                      (C_VA, 512, "copy"), (C_VA + 512, 256, "copy"), (C_QR, 384, "rot"), (C_KR, 384, "rot"),
                      (C_VR, 512, "copy"), (C_VR + 512, 256, "copy"), (C_GR, 512, "silu2"), (C_GR + 512, 256, "silu2"),
                      (C_QM, 512, "copy")]
            wspecs = [(self.w_in[0, :, c0:c0 + N], N) for (c0, N, kind) in blocks] + [(self.w_gate[0, :, fg * 512:(fg + 1) * 512], 512) for fg in range(6)]
            wloaded = {}

            def ensure_w(i):
                if i < len(wspecs) and i not in wloaded:
                    w, w_b = wr.next()
                    src, N = wspecs[i]
                    Sc.dma("gpsimd", lambda e, w=w, src=src, N=N: e.dma_start(out=w[:, :, 0:N], in_=src.rearrange("(k p) c -> p k c", p=128)), writes=[w_b])
                    wloaded[i] = (w, w_b)
                return wloaded.get(i)
            for bi, (c0, N, kind) in enumerate(blocks):
                w, w_b = ensure_w(bi)
                ensure_w(bi + 1)
                for t in range(NT):
                    p, p_b = pA.next()
                    for k in range(8):
                        Sc.op("tensor", lambda e, p=p, w=w, t=t, k=k, N=N: e.matmul(p[:, 0:N], lhsT=xT[:, k, t * 128:(t + 1) * 128], rhs=w[:, k, 0:N], start=(k == 0), stop=(k == 7)),
                              reads=[xT_b[t], w_b], writes=[p_b])
                    o, o_b = ob.next()
                    if kind == "copy":
                        Sc.op("scalar", lambda e, o=o, p=p, N=N: e.activation(out=o[:, 0:N], in_=p[:, 0:N], func=AF.Copy), reads=[p_b], writes=[o_b])
                    elif kind == "silu2":
                        a, a_b = tA.next()
                        Sc.op("scalar", lambda e, a=a, p=p, N=N: e.activation(out=a[:, 0:N], in_=p[:, 0:N], func=AF.Tanh, scale=0.5), reads=[p_b], writes=[a_b])
                        Sc.op("vector", lambda e, o=o, a=a, p=p, N=N: e.scalar_tensor_tensor(out=o[:, 0:N], in0=a[:, 0:N], scalar=1.0, in1=p[:, 0:N], op0=ALU.add, op1=ALU.mult),
                              reads=[a_b, p_b], writes=[o_b])
                    else:
                        H = N // 64
                        a, a_b = tA.next()
                        b, b_b = tB.next()
                        pv = p[:, 0:N].rearrange("p (h two f) -> p h two f", two=2, f=32)
                        av = a[:, 0:N].rearrange("p (h two f) -> p h two f", two=2, f=32)
                        bv = b[:, 0:N].rearrange("p (h two f) -> p h two f", two=2, f=32)
                        ov = o[:, 0:N].rearrange("p (h two f) -> p h two f", two=2, f=32)
                        cb = cos_t[:, t:t + 1, :].broadcast_to([128, H, 32])
                        sn = sin_t[:, t:t + 1, :].broadcast_to([128, H, 32])
                        x1, x2 = pv[:, :, 0, :], pv[:, :, 1, :]
                        Sc.op("vector", lambda e, av=av, x1=x1, cb=cb: e.tensor_tensor(out=av[:, :, 0, :], in0=x1, in1=cb, op=ALU.mult), reads=[p_b, const_b], writes=[a_b])
                        Sc.op("vector", lambda e, av=av, x2=x2, cb=cb: e.tensor_tensor(out=av[:, :, 1, :], in0=x2, in1=cb, op=ALU.mult), reads=[p_b, const_b], writes=[a_b])
                        Sc.op("vector", lambda e, bv=bv, x2=x2, sn=sn: e.tensor_tensor(out=bv[:, :, 0, :], in0=x2, in1=sn, op=ALU.mult), reads=[p_b, const_b], writes=[b_b])
                        Sc.op("vector", lambda e, bv=bv, x1=x1, sn=sn: e.tensor_tensor(out=bv[:, :, 1, :], in0=x1, in1=sn, op=ALU.mult), reads=[p_b, const_b], writes=[b_b])
                        Sc.op("gpsimd", lambda e, ov=ov, av=av, bv=bv: e.tensor_tensor(out=ov[:, :, 0, :], in0=av[:, :, 0, :], in1=bv[:, :, 0, :], op=ALU.subtract), reads=[a_b, b_b], writes=[o_b])
                        Sc.op("gpsimd", lambda e, ov=ov, av=av, bv=bv: e.tensor_tensor(out=ov[:, :, 1, :], in0=av[:, :, 1, :], in1=bv[:, :, 1, :], op=ALU.add), reads=[a_b, b_b], writes=[o_b])
                    Sc.dma("sync", lambda e, o=o, t=t, c0=c0, N=N: e.dma_start(out=self.PROJ[t * 128:(t + 1) * 128, c0:c0 + N], in_=o[:, 0:N]), reads=[o_b], writes=[proj_b])

            for fg in range(6):
                w, w_b = ensure_w(len(blocks) + fg)
                ensure_w(len(blocks) + fg + 1)
                for j in range(4):
                    fc = fg * 4 + j
                    for tb in range(NB):
                        p, p_b = pA.next()
                        for k in range(8):
                            Sc.op("tensor", lambda e, p=p, w=w, tb=tb, k=k, j=j: e.matmul(p[:, :], lhsT=w[:, k, j * 128:(j + 1) * 128], rhs=xT[:, k, tb * 512:(tb + 1) * 512], start=(k == 0), stop=(k == 7)),
                                  reads=xT_b[tb * 4:tb * 4 + 4] + [w_b], writes=[p_b])
                        o, o_b = ob.next()
                        Sc.op("scalar", lambda e, o=o, p=p, fc=fc: e.activation(out=o[:, :], in_=p[:, :], func=AF.Tanh, bias=hb[:, fc:fc + 1], scale=0.5), reads=[p_b, const_b], writes=[o_b])
                        Sc.dma("sync", lambda e, o=o, fc=fc, tb=tb: e.dma_start(out=self.GTT[fc * 128:(fc + 1) * 128, tb * 512:(tb + 1) * 512], in_=o[:, :]), reads=[o_b], writes=[gtt_b])
            Sc.barrier()
            Sc.emit(st)


    def phase_b1(self):
        nc = self.nc
        with contextlib.ExitStack() as st:
            sb = lambda n, s, d: st.enter_context(nc.sbuf_tensor("b1_" + n, s, d))
            ps = lambda n, s, d: st.enter_context(nc.psum_tensor("b1_" + n, s, d))
            Sc = Sched(nc, prefix="b1_")
            const_b = Buf("const")
            amf = sb("amf", [128, 1024], F32)
            am = sb("am", [128, 1024], BF16)
            identf = sb("identf", [128, 128], F32)
            ident = sb("ident", [128, 128], BF16)
            Sc.dma("sync", lambda e: e.dma_start(out=amf[:], in_=self.c_amask[:, :]), writes=[const_b])
            Sc.dma("sync", lambda e: e.dma_start(out=identf[:], in_=self.c_ident[:, :]), writes=[const_b])
            Sc.op("vector", lambda e: e.tensor_copy(out=am[:], in_=amf[:]), reads=[const_b], writes=[const_b])
            Sc.op("vector", lambda e: e.tensor_copy(out=ident[:], in_=identf[:]), reads=[const_b], writes=[const_b])
            sets = []
            for i in range(3):
                Lc = S if i == 2 else 1024
                nbc = Lc // 128
                qT = [sb("qT%d_%d" % (i, pp), [128, Lc], BF16) for pp in range(2)]
                qZ = [[sb("qZ%d_%d_%d" % (i, pp, hh), [128, Lc], BF16) for hh in range(2)] for pp in range(2)]
                qz_b = [[[Buf("qz") for _ in range(nbc)] for hh in range(2)] for pp in range(2)]
                for pp in range(2):
                    Sc.op("gpsimd", lambda e, t=qZ[pp][0]: e.memset(t[64:128, :], 0.0), writes=qz_b[pp][0])
                    Sc.op("gpsimd", lambda e, t=qZ[pp][1]: e.memset(t[0:64, :], 0.0), writes=qz_b[pp][1])
                kT = [sb("kT%d_%d" % (i, pp), [128, Lc], BF16) for pp in range(2)]
                va = sb("va%d" % i, [128, nbc + 1, 4, 128], BF16)
                q_b = [[Buf("q") for _ in range(nbc)] for pp in range(2)]
                k_b = [[Buf("k") for _ in range(nbc)] for pp in range(2)]
                v_b = [Buf("v") for _ in range(nbc + 1)]
                Sc.op("gpsimd", lambda e, va=va: e.memset(va[:, :, :, 64:128], 1.0), writes=v_b)
                sets.append((qT, kT, va, q_b, k_b, v_b, qZ, qz_b))
            pS = Ring([(ps("pS%d" % i, [128, 512], F32), Buf("pS%d" % i)) for i in range(3)])
            pU = Ring([(ps("pU%d" % i, [128, 512], F32), Buf("pU%d" % i)) for i in range(4)])
            Er = Ring([(sb("E%d" % i, [128, 512], BF16), Buf("E%d" % i)) for i in range(6)])
            us = Ring([(sb("us%d" % i, [128, 512], F32), Buf("us%d" % i)) for i in range(4)])
            ud_b = Buf("UD")
            unit = 0
            for g, dl in enumerate((1, 4, 16)):
                if g not in getattr(self, "b1_groups", (0, 1, 2)):
                    continue
                L = S // dl
                nb = L // 128
                ucurs = {0: None, 1: None}
                for r in range(dl):
                    qT, kT, va, q_b, k_b, v_b, qZ, qz_b = sets[2] if g == 0 else sets[unit % 2]
                    unit += 1
                    rows = self.PROJ.rearrange("(i r) c -> r i c", r=dl)[r]
                    vc = C_VA + g * 256
                    Sc.dma("sync", lambda e, va=va, rows=rows, vc=vc: e.dma_start(out=va[0:64, 0, :, 0:64], in_=rows[0:64, vc:vc + 256].rearrange("k (h d) -> k h d", d=64)), writes=[v_b[0]])
                    for h4 in range(4):
                        Sc.dma("sync", lambda e, va=va, rows=rows, vc=vc, nb=nb, L=L, h4=h4: e.dma_start(out=va[:, 1:nb, h4, 0:64], in_=rows[64:L - 64, vc + h4 * 64:vc + h4 * 64 + 64].rearrange("(j k) d -> k j d", k=128)), writes=v_b[1:nb])
                    Sc.dma("sync", lambda e, va=va, rows=rows, vc=vc, nb=nb, L=L: e.dma_start(out=va[0:64, nb, :, 0:64], in_=rows[L - 64:L, vc:vc + 256].rearrange("k (h d) -> k h d", d=64)), writes=[v_b[nb]])
                    for pp in range(2):
                        qc = C_QA + (g * 4 + 2 * pp) * 64
                        kc = C_KA + (g * 4 + 2 * pp) * 64
                        nbc = min(4, nb)
                        for b4 in range(0, nb, nbc):
                            rs = slice(b4 * 128, (b4 + nbc) * 128)
                            Sc.dma("sync", lambda e, dst=kT[pp], rows=rows, kc=kc, rs=rs: e.dma_start_transpose(out=dst[:, rs], in_=rows[rs, kc:kc + 128]), writes=k_b[pp][b4:b4 + nbc])
                            Sc.dma("sync", lambda e, dst=qT[pp], rows=rows, qc=qc, rs=rs: e.dma_start_transpose(out=dst[:, rs], in_=rows[rs, qc:qc + 128]), writes=q_b[pp][b4:b4 + nbc])
                        for blk in range(nb):
                            sl = slice(blk * 128, (blk + 1) * 128)
                            Sc.op("vector", lambda e, d=qZ[pp][0], s_=qT[pp], sl=sl: e.tensor_copy(out=d[0:64, sl], in_=s_[0:64, sl]), reads=[q_b[pp][blk]], writes=[qz_b[pp][0][blk]])
                            Sc.op("gpsimd", lambda e, d=qZ[pp][1], s_=qT[pp], sl=sl: e.tensor_copy(out=d[64:128, sl], in_=s_[64:128, sl]), reads=[q_b[pp][blk]], writes=[qz_b[pp][1][blk]])
                    for pp in range(2):
                        if getattr(self, "b1_stage", 9) < 1:
                            continue
                        Es = {}
                        ucur = ucurs[pp]
                        for j in range(nb + 1):
                            k0, k1 = max(0, 128 * j - 64), min(L, 128 * j + 64)
                            M = k1 - k0
                            q0, q1 = max(0, 128 * (j - 1)), min(L, 128 * (j + 1))
                            Nq = q1 - q0
                            if j == 0:
                                mk = am[0:64, 512:768]
                            elif j == nb:
                                mk = am[0:64, 768:1024]
                            else:
                                mk = am[:, 0:512]
                            kblks = sorted(set([k0 // 128, (k1 - 1) // 128]))
                            qblks = sorted(set([q0 // 128, (q1 - 1) // 128]))
                            p, p_b = pS.next()
                            for hh in range(2):
                                rd = [k_b[pp][b] for b in kblks] + [qz_b[pp][hh][b] for b in qblks]
                                Sc.op("tensor", lambda e, p=p, kt=kT[pp], qt=qZ[pp][hh], hh=hh, k0=k0, k1=k1, q0=q0, q1=q1, M=M, Nq=Nq:
                                      e.matmul(p[0:M, hh * Nq:(hh + 1) * Nq], lhsT=kt[:, k0:k1], rhs=qt[:, q0:q1], start=True, stop=False),
                                      reads=rd, writes=[p_b])
                                Sc.op("tensor", lambda e, p=p, mk=mk, M=M, Nq=Nq, hh=hh: e.matmul(p[0:M, hh * Nq:(hh + 1) * Nq], lhsT=ident[0:M, 0:M], rhs=mk[:, 0:Nq], start=False, stop=True),
                                      reads=[const_b], writes=[p_b])
                            E, E_b = Er.next()
                            Sc.op("scalar", lambda e, E=E, p=p, M=M, Nq=Nq: e.activation(out=E[0:M, 0:2 * Nq], in_=p[0:M, 0:2 * Nq], func=AF.Exp, scale=0.125), reads=[p_b], writes=[E_b])
                            Es[j] = (E, E_b, M, Nq)

                            def pv(b, pp=pp, Es=Es):
                                G = r * nb + b
                                if G % 4 == 0:
                                    ucurs[pp] = [pU.next(), pU.next()]
                                for hh in range(2):
                                    (u, u_b) = ucurs[pp][hh]
                                    col = (G % 4) * 128
                                    for n_, jj in enumerate((b, b + 1)):
                                        Ej, Ej_b, Mj, Nqj = Es[jj]
                                        if jj == b:
                                            c = hh * Nqj + (128 if b >= 1 else 0)
                                        else:
                                            c = hh * Nqj
                                        hv = 2 * pp + hh
                                        Sc.op("tensor", lambda e, u=u, va=va, Ej=Ej, jj=jj, hv=hv, Mj=Mj, c=c, col=col, n_=n_:
                                              e.matmul(u[:, col:col + 128], lhsT=va[0:Mj, jj, hv, :], rhs=Ej[0:Mj, c:c + 128], start=(n_ == 0), stop=(n_ == 1)),
                                              reads=[v_b[jj], Ej_b], writes=[u_b])
                                del Es[b]
                                if G % 4 == 3:
                                    for hh in range(2):
                                        (u, u_b) = ucurs[pp][hh]
                                        o, o_b = us.next()
                                        if hh == 0:
                                            Sc.op("vector", lambda e, o=o, u=u: e.tensor_copy(out=o[:], in_=u[:]), reads=[u_b], writes=[o_b])
                                        else:
                                            Sc.op("scalar", lambda e, o=o, u=u: e.activation(out=o[:], in_=u[:], func=AF.Copy), reads=[u_b], writes=[o_b])
                                        hrow = (g * 4 + 2 * pp + hh) * 128
                                        c0 = (G - 3) * 128
                                        Sc.dma("gpsimd", lambda e, o=o, hrow=hrow, c0=c0: e.dma_start(out=self.UD[hrow:hrow + 128, c0:c0 + 512], in_=o[:]), reads=[o_b], writes=[ud_b])

                            if j >= 2:
                                pv(j - 2)
                        pv(nb - 1)
            Sc.barrier()
            Sc.emit(st)

    def phase_b1c(self):
        nc = self.nc
        with contextlib.ExitStack() as st:
            sb = lambda n, s, d: st.enter_context(nc.sbuf_tensor("b1c_" + n, s, d))
            Sc = Sched(nc, prefix="b1c_")
            CH = 2048
            Ut = [Ring([(sb("U%d_%d" % (g, i), [128, CH], F32), Buf("U")) for i in range(2)]) for g in range(3)]
            Dt = [Ring([(sb("D%d_%d" % (g, i), [128, CH], F32), Buf("D")) for i in range(2)]) for g in range(3)]
            Rr = Ring([(sb("R%d" % i, [128, CH], F32), Buf("R")) for i in range(2)])
            Yr = Ring([(sb("Y%d" % i, [128, CH], BF16), Buf("Y")) for i in range(3)])
            yat_b = Buf("YAT")
            for c2 in range(S // CH):
                for sp in range(2):
                    tiles = []
                    for g, dl in enumerate((1, 4, 16)):
                        L = S // dl
                        il = CH // dl
                        u, u_b = Ut[g].next()
                        d_, d_b = Dt[g].next()
                        for hh in range(2):
                            h = g * 4 + 2 * sp + hh
                            srcU = self.UD[h * 128:h * 128 + 64, :].rearrange("p (r i) -> p r i", r=dl)[:, :, c2 * il:(c2 + 1) * il]
                            srcD = self.UD[h * 128 + 64:h * 128 + 128, :].rearrange("p (r i) -> p r i", r=dl)[:, :, c2 * il:(c2 + 1) * il]
                            Sc.dma("sync", lambda e, u=u, hh=hh, srcU=srcU, dl=dl: e.dma_start(out=u[hh * 64:(hh + 1) * 64, :].rearrange("p (r i) -> p r i", r=dl), in_=srcU), writes=[u_b])
                            Sc.dma("sync", lambda e, d_=d_, hh=hh, srcD=srcD, dl=dl: e.dma_start(out=d_[hh * 64:(hh + 1) * 64, :].rearrange("p (r i) -> p r i", r=dl), in_=srcD), writes=[d_b])
                        tiles.append((u, u_b, d_, d_b, dl))
                    R, R_b = Rr.next()
                    nat = lambda t, dl: t[:, :].rearrange("p (i r) -> p i r", r=dl)
                    res = lambda t, dl: t[:, :].rearrange("p (r i) -> p i r", r=dl)
                    (u0, u0_b, d0, d0_b, _), (u1, u1_b, d1, d1_b, _), (u2, u2_b, d2, d2_b, _) = tiles
                    Sc.op("gpsimd", lambda e, R=R, d0=d0, d1=d1: e.tensor_tensor(out=nat(R, 4), in0=nat(d0, 4), in1=res(d1, 4), op=ALU.add), reads=[d0_b, d1_b], writes=[R_b])
                    Sc.op("gpsimd", lambda e, R=R, d2=d2: e.tensor_tensor(out=nat(R, 16), in0=nat(R, 16), in1=res(d2, 16), op=ALU.add), reads=[d2_b, R_b], writes=[R_b])
                    Sc.op("vector", lambda e, R=R: e.reciprocal(out=R[:, :], in_=R[:, :]), reads=[R_b], writes=[R_b])
                    for g, (u, u_b, d_, d_b, dl) in enumerate(tiles):
                        y, y_b = Yr.next()
                        eng = "vector" if g != 1 else "gpsimd"
                        Sc.op(eng, lambda e, y=y, u=u, dl=dl, R=R: e.tensor_tensor(out=nat(y, dl), in0=res(u, dl), in1=nat(R, dl), op=ALU.mult), reads=[u_b, R_b], writes=[y_b])
                        row = (g * 4 + 2 * sp) * 64
                        Sc.dma("gpsimd", lambda e, y=y, row=row, c2=c2: e.dma_start(out=self.YAT[row:row + 128, c2 * CH:(c2 + 1) * CH], in_=y[:, :]), reads=[y_b], writes=[yat_b])
            Sc.barrier()
            Sc.emit(st)

    def phase_b2(self):
        nc = self.nc
        with contextlib.ExitStack() as st:
            sb = lambda n, s, d: st.enter_context(nc.sbuf_tensor("b2_" + n, s, d))
            ps = lambda n, s, d: st.enter_context(nc.psum_tensor("b2_" + n, s, d))
            Sc = Sched(nc, prefix="b2_")
            cb_ = Buf("const")
            cret = sb("cret", [128, 8, 128], F32)
            dec = sb("dec", [128, 12], F32)
            lg = sb("lg", [128, 12], F32)
            lgp = sb("lgp", [128, 6], F32)
            Mall = sb("Mall", [128, 6, 128], F32)
            tmpm = sb("tmpm", [128, 128], F32)
            zeta = sb("zeta", [128, 12], F32)
            XiF = sb("XiF", [128, 3, 128], F32)
            XiB = sb("XiB", [128, 3, 128], F32)
            g128 = sb("g128", [128, 6], F32)
            gret = sb("gret", [128, 768], F32)
            negh = sb("negh", [128, 6], F32)
            Sc.dma("sync", lambda e: e.dma_start(out=cret[:], in_=self.c_ret[:, :, :]), writes=[cb_])
            Sc.dma("sync", lambda e: e.dma_start(out=dec[:, 0:6], in_=self.dec_f[0, :].partition_broadcast(128)), writes=[cb_])
            Sc.dma("sync", lambda e: e.dma_start(out=dec[:, 6:12], in_=self.dec_b[0, :].partition_broadcast(128)), writes=[cb_])
            Sc.dma("sync", lambda e: e.dma_start(out=gret[:], in_=self.g_ret[0, :].partition_broadcast(128)), writes=[cb_])
            C = lambda eng, fn: Sc.op(eng, fn, reads=[cb_], writes=[cb_])
            C("vector", lambda e: e.memset(negh[:], -0.5))
            C("vector", lambda e: e.tensor_scalar(out=gret[:], in0=gret[:], scalar1=0.5, scalar2=None, op0=ALU.mult))
            C("scalar", lambda e: e.activation(out=lg[:], in_=dec[:], func=AF.Exp, scale=-1.0))
            C("vector", lambda e: e.tensor_scalar(out=lg[:], in0=lg[:], scalar1=1.0, scalar2=None, op0=ALU.add))
            C("scalar", lambda e: e.activation(out=lg[:], in_=lg[:], func=AF.Ln))
            C("vector", lambda e: e.tensor_scalar(out=lg[:], in0=lg[:], scalar1=-1.0, scalar2=None, op0=ALU.mult))
            for dr in range(2):
                for hh in range(2):
                    src = lg[hh * 64:(hh + 1) * 64, dr * 6:(dr + 1) * 6].rearrange("p (a b) -> p a b", b=2)[:, :, hh]
                    C("vector", lambda e, dr=dr, hh=hh, src=src: e.tensor_copy(out=lgp[hh * 64:(hh + 1) * 64, dr * 3:(dr + 1) * 3], in_=src))
            for h in range(6):
                C("scalar", lambda e, h=h: e.activation(out=Mall[:, h, :], in_=cret[:, 0, :], func=AF.Exp, scale=lg[:, h:h + 1]))
                C("vector", lambda e, h=h: e.tensor_tensor(out=Mall[:, h, :], in0=Mall[:, h, :], in1=cret[:, 2, :], op=ALU.mult))
                C("scalar", lambda e, h=h: e.activation(out=tmpm[:], in_=cret[:, 1, :], func=AF.Exp, scale=lg[:, 6 + h:7 + h]))
                C("vector", lambda e, h=h: e.tensor_tensor(out=tmpm[:], in0=tmpm[:], in1=cret[:, 3, :], op=ALU.mult))
                C("vector", lambda e, h=h: e.tensor_tensor(out=Mall[:, h, :], in0=Mall[:, h, :], in1=tmpm[:], op=ALU.add))
                C("scalar", lambda e, h=h: e.activation(out=zeta[:, h:h + 1], in_=cret[:, 6, 0:1], func=AF.Exp, scale=lg[:, h:h + 1]))
                C("scalar", lambda e, h=h: e.activation(out=zeta[:, 6 + h:7 + h], in_=cret[:, 6, 1:2], func=AF.Exp, scale=lg[:, 6 + h:7 + h]))
            C("vector", lambda e: e.tensor_scalar(out=zeta[:], in0=zeta[:], scalar1=0.125, scalar2=None, op0=ALU.mult))
            for pp in range(3):
                C("scalar", lambda e, pp=pp: e.activation(out=XiF[:, pp, :], in_=cret[:, 4, :], func=AF.Exp, scale=lgp[:, pp:pp + 1]))
                C("scalar", lambda e, pp=pp: e.activation(out=XiB[:, pp, :], in_=cret[:, 5, :], func=AF.Exp, scale=lgp[:, 3 + pp:4 + pp]))
            C("scalar", lambda e: e.activation(out=g128[:], in_=lgp[:], func=AF.Exp, scale=128.0))

            Sall = [sb("SallF", [128, NT, 3, 128], BF16), sb("SallB", [128, NT, 3, 128], BF16)]
            sall_b = [[Buf("sf") for _ in range(NT)], [Buf("sb") for _ in range(NT)]]
            Scur = [sb("ScurF", [128, 3, 128], F32), sb("ScurB", [128, 3, 128], F32)]
            scur_b = [Buf("scf"), Buf("scb")]
            kt_r = Ring([(sb("ktok%d" % i, [128, 384], BF16), Buf("ktok")) for i in range(3)])
            vt_r = Ring([(sb("vtok%d" % i, [128, 768], BF16), Buf("vtok")) for i in range(3)])
            kz_r = Ring([(sb("kz%d" % i, [128, 6, 64], BF16), Buf("kz")) for i in range(3)])
            pkv = Ring([(ps("pkv%d" % i, [128, 512], F32), Buf("pkv")) for i in range(2)])
            for dr in range(2):
                Sc.op("vector", lambda e, dr=dr: e.memset(Scur[dr][:], 0.0), writes=[scur_b[dr]])
            for step in range(2 * NT):
                dr = step % 2
                c = (step // 2) if dr == 0 else (NT - 1 - step // 2)
                if True:
                    kt, kt_b = kt_r.next()
                    vt, vt_b = vt_r.next()
                    Sc.dma("sync", lambda e, kt=kt, c=c: e.dma_start(out=kt[:], in_=self.PROJ[c * 128:(c + 1) * 128, C_KR:C_KR + 384]), writes=[kt_b])
                    Sc.dma("sync", lambda e, vt=vt, c=c: e.dma_start(out=vt[:], in_=self.PROJ[c * 128:(c + 1) * 128, C_VR:C_VR + 768]), writes=[vt_b])
                    kz, kz_b = kz_r.next()
                    zb = zeta[:, dr * 6:(dr + 1) * 6].unsqueeze(2).broadcast_to([128, 6, 64])
                    Sc.op("gpsimd", lambda e, kz=kz, kt=kt, zb=zb: e.tensor_tensor(out=kz[:], in0=kt[:, :].rearrange("p (h d) -> p h d", d=64), in1=zb, op=ALU.mult), reads=[kt_b, cb_], writes=[kz_b])
                    p, p_b = pkv.next()
                    for h in range(6):
                        pp, hh = h // 2, h % 2
                        Sc.op("tensor", lambda e, p=p, kz=kz, vt=vt, h=h, pp=pp, hh=hh: e.matmul(p[hh * 64:(hh + 1) * 64, pp * 128:(pp + 1) * 128], lhsT=kz[:, h, :], rhs=vt[:, h * 128:(h + 1) * 128], start=True, stop=True),
                              reads=[kz_b, vt_b], writes=[p_b])
                    Sc.op("scalar", lambda e, dr=dr, c=c: e.activation(out=Sall[dr][:, c, :, :], in_=Scur[dr][:], func=AF.Copy), reads=[scur_b[dr]], writes=[sall_b[dr][c]])
                    for pp in range(3):
                        Sc.op("vector", lambda e, dr=dr, pp=pp, p=p: e.scalar_tensor_tensor(out=Scur[dr][:, pp, :], in0=Scur[dr][:, pp, :], scalar=g128[:, dr * 3 + pp:dr * 3 + pp + 1], in1=p[:, pp * 128:(pp + 1) * 128], op0=ALU.mult, op1=ALU.add),
                              reads=[p_b, scur_b[dr], cb_], writes=[scur_b[dr]])

            qt_r = Ring([(sb("qTp%d" % i, [128, 3, 128], BF16), Buf("qTp")) for i in range(3)])
            ktp_r = Ring([(sb("kTp%d" % i, [128, 3, 128], BF16), Buf("kTp")) for i in range(3)])
            gr_r = Ring([(sb("gr%d" % i, [128, 768], BF16), Buf("gr")) for i in range(3)])
            qz_r = []
            for i in range(2):
                t3 = [sb("qz%d_%d" % (i, j), [128, 6, 128], BF16) for j in range(3)]
                b3 = [Buf("qz") for j in range(3)]
                for j in range(3):
                    Sc.op("gpsimd", lambda e, t=t3[j]: e.memset(t[:], 0.0), writes=[b3[j]])
                qz_r.append((t3, b3))
            qz_r = Ring(qz_r)
            pst = Ring([(ps("pst%d" % i, [128, 512], F32), Buf("pst")) for i in range(2)])
            pyy = Ring([(ps("pyy%d" % i, [128, 512], F32), Buf("pyy")) for i in range(4)])
            A_r = Ring([(sb("A%d" % i, [128, 6, 128], BF16), Buf("A")) for i in range(2)])
            ysb_r = Ring([(sb("ysb%d" % i, [128, 768], F32), Buf("ysb")) for i in range(2)])
            ysq_r = Ring([(sb("ysq%d" % i, [128, 768], F32), Buf("ysq")) for i in range(2)])
            st_r = Ring([(sb("stat%d" % i, [128, 4, 6], F32), Buf("stat")) for i in range(2)])
            yo_r = Ring([(sb("yo%d" % i, [128, 768], BF16), Buf("yo")) for i in range(2)])
            yr_b = Buf("YR")
            def stage1(c):
                qt, qt_b = qt_r.next()
                ktp, ktp_b = ktp_r.next()
                vt, vt_b = vt_r.next()
                gr, gr_b = gr_r.next()
                for pp in range(3):
                    Sc.dma("sync", lambda e, qt=qt, c=c, pp=pp: e.dma_start_transpose(out=qt[:, pp, :], in_=self.PROJ[c * 128:(c + 1) * 128, C_QR + pp * 128:C_QR + (pp + 1) * 128]), writes=[qt_b])
                    Sc.dma("sync", lambda e, ktp=ktp, c=c, pp=pp: e.dma_start_transpose(out=ktp[:, pp, :], in_=self.PROJ[c * 128:(c + 1) * 128, C_KR + pp * 128:C_KR + (pp + 1) * 128]), writes=[ktp_b])
                Sc.dma("sync", lambda e, vt=vt, c=c: e.dma_start(out=vt[:], in_=self.PROJ[c * 128:(c + 1) * 128, C_VR:C_VR + 768]), writes=[vt_b])
                Sc.dma("sync", lambda e, gr=gr, c=c: e.dma_start(out=gr[:], in_=self.PROJ[c * 128:(c + 1) * 128, C_GR:C_GR + 768]), writes=[gr_b])
                (qz, qxf, qxb), (qz_b, qxf_b, qxb_b) = qz_r.next()
                for hh in range(2):
                    sl = slice(hh * 64, (hh + 1) * 64)
                    hv = lambda t, sl=sl, hh=hh: t[sl, :, :].rearrange("p (pp two) n -> p pp two n", two=2)[:, :, hh, :]
                    Sc.op("vector", lambda e, qz=qz, qt=qt, sl=sl, hv=hv: e.tensor_copy(out=hv(qz), in_=qt[sl, :, :]), reads=[qt_b], writes=[qz_b])
                    Sc.op("gpsimd", lambda e, qxf=qxf, qt=qt, sl=sl, hv=hv: e.tensor_tensor(out=hv(qxf), in0=qt[sl, :, :], in1=XiF[sl, :, :], op=ALU.mult), reads=[qt_b, cb_], writes=[qxf_b])
                    Sc.op("vector", lambda e, qxb=qxb, qt=qt, sl=sl, hv=hv: e.tensor_tensor(out=hv(qxb), in0=qt[sl, :, :], in1=XiB[sl, :, :], op=ALU.mult), reads=[qt_b, cb_], writes=[qxb_b])
                (s0, s0_b), (s1, s1_b) = pst.next(), pst.next()
                for h in range(6):
                    pp = h // 2
                    tgt, tb_ = (s0, s0_b) if h < 4 else (s1, s1_b)
                    col = (h % 4) * 128
                    Sc.op("tensor", lambda e, tgt=tgt, ktp=ktp, qz=qz, h=h, pp=pp, col=col: e.matmul(tgt[:, col:col + 128], lhsT=ktp[:, pp, :], rhs=qz[:, h, :], start=True, stop=True), reads=[ktp_b, qz_b], writes=[tb_])
                A, A_b = A_r.next()
                Sc.op("vector", lambda e, A=A, s0=s0: e.tensor_tensor(out=A[:, 0:4, :], in0=s0[:, :].rearrange("p (h n) -> p h n", n=128), in1=Mall[:, 0:4, :], op=ALU.mult), reads=[s0_b, cb_], writes=[A_b])
                Sc.op("vector", lambda e, A=A, s1=s1: e.tensor_tensor(out=A[:, 4:6, :], in0=s1[:, 0:256].rearrange("p (h n) -> p h n", n=128), in1=Mall[:, 4:6, :], op=ALU.mult), reads=[s1_b, cb_], writes=[A_b])
                return dict(c=c, qxf=qxf, qxb=qxb, qxf_b=qxf_b, qxb_b=qxb_b, A=A, A_b=A_b, vt=vt, vt_b=vt_b, gr=gr, gr_b=gr_b)

            def stage2(cx):
                c, qxf, qxb, qxf_b, qxb_b, A, A_b, vt, vt_b, gr, gr_b = (cx[k] for k in ("c", "qxf", "qxb", "qxf_b", "qxb_b", "A", "A_b", "vt", "vt_b", "gr", "gr_b"))
                (y0, y0_b), (y1, y1_b) = pyy.next(), pyy.next()
                for h in range(6):
                    pp = h // 2
                    tgt, tb_ = (y0, y0_b) if h < 4 else (y1, y1_b)
                    col = (h % 4) * 128
                    Sc.op("tensor", lambda e, tgt=tgt, A=A, vt=vt, h=h, col=col: e.matmul(tgt[:, col:col + 128], lhsT=A[:, h, :], rhs=vt[:, h * 128:(h + 1) * 128], start=True, stop=False), reads=[A_b, vt_b], writes=[tb_])
                    Sc.op("tensor", lambda e, tgt=tgt, qxf=qxf, h=h, pp=pp, c=c, col=col: e.matmul(tgt[:, col:col + 128], lhsT=qxf[:, h, :], rhs=Sall[0][:, c, pp, :], start=False, stop=False), reads=[qxf_b, sall_b[0][c]], writes=[tb_])
                    Sc.op("tensor", lambda e, tgt=tgt, qxb=qxb, h=h, pp=pp, c=c, col=col: e.matmul(tgt[:, col:col + 128], lhsT=qxb[:, h, :], rhs=Sall[1][:, c, pp, :], start=False, stop=True), reads=[qxb_b, sall_b[1][c]], writes=[tb_])
                ysb, ysb_b = ysb_r.next()
                ysq, ysq_b = ysq_r.next()
                stt_, stt_b = st_r.next()
                Sc.op("scalar", lambda e, ysb=ysb, y0=y0: e.activation(out=ysb[:, 0:512], in_=y0[:, :], func=AF.Copy), reads=[y0_b], writes=[ysb_b])
                Sc.op("scalar", lambda e, ysb=ysb, y1=y1: e.activation(out=ysb[:, 512:768], in_=y1[:, 0:256], func=AF.Copy), reads=[y1_b], writes=[ysb_b])
                y3 = ysb[:, :].rearrange("p (h e) -> p h e", e=128)
                Sc.op("scalar", lambda e, ysq=ysq, y0=y0: e.activation(out=ysq[:, 0:512], in_=y0[:, :], func=AF.Square), reads=[y0_b], writes=[ysq_b])
                Sc.op("scalar", lambda e, ysq=ysq, y1=y1: e.activation(out=ysq[:, 512:768], in_=y1[:, 0:256], func=AF.Square), reads=[y1_b], writes=[ysq_b])
                Sc.op("vector", lambda e, stt_=stt_, y3=y3: e.tensor_reduce(out=stt_[:, 0, :], in_=y3, axis=AX.X, op=ALU.add), reads=[ysb_b], writes=[stt_b])
                Sc.op("vector", lambda e, stt_=stt_, ysq=ysq: e.tensor_reduce(out=stt_[:, 1, :], in_=ysq[:, :].rearrange("p (h e) -> p h e", e=128), axis=AX.X, op=ALU.add), reads=[ysq_b], writes=[stt_b])
                Sc.op("gpsimd", lambda e, stt_=stt_: e.tensor_scalar(out=stt_[:, 0, :], in0=stt_[:, 0, :], scalar1=1.0 / 128, scalar2=None, op0=ALU.mult), reads=[stt_b], writes=[stt_b])
                Sc.op("gpsimd", lambda e, stt_=stt_: e.tensor_tensor(out=stt_[:, 2, :], in0=stt_[:, 0, :], in1=stt_[:, 0, :], op=ALU.mult), reads=[stt_b], writes=[stt_b])
                Sc.op("gpsimd", lambda e, stt_=stt_: e.tensor_scalar(out=stt_[:, 1, :], in0=stt_[:, 1, :], scalar1=1.0 / 128, scalar2=EPS, op0=ALU.mult, op1=ALU.add), reads=[stt_b], writes=[stt_b])
                Sc.op("gpsimd", lambda e, stt_=stt_: e.tensor_tensor(out=stt_[:, 1, :], in0=stt_[:, 1, :], in1=stt_[:, 2, :], op=ALU.subtract), reads=[stt_b], writes=[stt_b])
                Sc.op("gpsimd", lambda e, stt_=stt_: e.tensor_tensor(out=stt_[:, 3, :], in0=stt_[:, 1, :], in1=negh[:, :], op=ALU.pow), reads=[stt_b, cb_], writes=[stt_b])
                mb = stt_[:, 0, :].unsqueeze(2).broadcast_to([128, 6, 128])
                rb = stt_[:, 3, :].unsqueeze(2).broadcast_to([128, 6, 128])
                Sc.op("vector", lambda e, y3=y3, mb=mb: e.tensor_tensor(out=y3, in0=y3, in1=mb, op=ALU.subtract), reads=[stt_b, ysb_b], writes=[ysb_b])
                Sc.op("vector", lambda e, y3=y3, rb=rb: e.tensor_tensor(out=y3, in0=y3, in1=rb, op=ALU.mult), reads=[stt_b, ysb_b], writes=[ysb_b])
                Sc.op("vector", lambda e, ysb=ysb: e.tensor_tensor(out=ysb[:], in0=ysb[:], in1=gret[:], op=ALU.mult), reads=[ysb_b, cb_], writes=[ysb_b])
                yo, yo_b = yo_r.next()
                Sc.op("gpsimd", lambda e, yo=yo, ysb=ysb, gr=gr: e.tensor_tensor(out=yo[:], in0=ysb[:], in1=gr[:], op=ALU.mult), reads=[ysb_b, gr_b], writes=[yo_b])
                Sc.dma("gpsimd", lambda e, yo=yo, c=c: e.dma_start(out=self.YR[c * 128:(c + 1) * 128, :], in_=yo[:]), reads=[yo_b], writes=[yr_b])

            prev = None
            for c in range(NT):
                cx = stage1(c)
                if prev is not None:
                    stage2(prev)
                prev = cx
            stage2(prev)
            Sc.barrier()
            Sc.emit(st)

    def phase_b3(self):
        nc = self.nc
        with contextlib.ExitStack() as st:
            sb = lambda n, s, d: st.enter_context(nc.sbuf_tensor("b3_" + n, s, d))
            ps = lambda n, s, d: st.enter_context(nc.psum_tensor("b3_" + n, s, d))
            Sc = Sched(nc, prefix="b3_")
            cb_ = Buf("const")
            identf = sb("identf", [128, 128], F32)
            ident = sb("ident", [128, 128], BF16)
            ones = sb("ones", [128, 128], BF16)
            gmem = sb("gmem", [128, D], F32)
            negh = sb("negh", [128, 1], F32)
            wkv = sb("wkv", [128, 8, 1024], BF16)
            wkv_b = Buf("wkv")
            memT = sb("memT", [128, 8, 256], BF16)
            memT_b = Buf("memT")
            kmT = sb("kmT", [128, 4, 256], BF16)
            vm = sb("vm", [128, 2, 512], BF16)
            kv_b = Buf("kv")
            Sc.dma("sync", lambda e: e.dma_start(out=identf[:], in_=self.c_ident[:, :]), writes=[cb_])
            Sc.dma("sync", lambda e: e.dma_start(out=gmem[:], in_=self.g_mem[0, :].partition_broadcast(128)), writes=[cb_])
            Sc.dma("gpsimd", lambda e: e.dma_start(out=wkv[:], in_=self.w_mem_kv[0, :, :].rearrange("(k p) c -> p k c", p=128)), writes=[wkv_b])
            Sc.op("vector", lambda e: e.tensor_copy(out=ident[:], in_=identf[:]), reads=[cb_], writes=[cb_])
            Sc.op("vector", lambda e: e.memset(ones[:], 1.0), reads=[cb_], writes=[cb_])
            Sc.op("vector", lambda e: e.memset(negh[:], -0.5), reads=[cb_], writes=[cb_])
            mt_r = Ring([(sb("mt%d" % i, [128, D], F32), Buf("mt")) for i in range(2)])
            mg_r = Ring([(sb("mg%d" % i, [128, D], BF16), Buf("mg")) for i in range(2)])
            junk = sb("junk", [128, D], BF16)
            junk_b = Buf("junk")
            mst = sb("mst", [128, 4], F32)
            mst_b = Buf("mst")
            pT = Ring([(ps("pT%d" % i, [128, 1024], BF16), Buf("pT")) for i in range(1)])
            pA = Ring([(ps("pA%d" % i, [128, 512], F32), Buf("pA")) for i in range(7)])
            Sc.op("vector", lambda e: e.memset(mst[:], 0.0), writes=[mst_b])
            for t in range(2):
                m, m_b = mt_r.next()
                Sc.dma("sync", lambda e, m=m, t=t: e.dma_start(out=m[:], in_=self.mem[t * 128:(t + 1) * 128, :]), writes=[m_b])
                Sc.op("scalar", lambda e, m=m, t=t: e.activation(out=junk[:], in_=m[:], func=AF.Square, accum_out=mst[:, t:t + 1]), reads=[m_b, mst_b], writes=[junk_b, mst_b])
                Sc.op("gpsimd", lambda e, t=t: e.tensor_scalar(out=mst[:, t:t + 1], in0=mst[:, t:t + 1], scalar1=1.0 / D, scalar2=EPS, op0=ALU.mult, op1=ALU.add), reads=[mst_b], writes=[mst_b])
                Sc.op("gpsimd", lambda e, t=t: e.tensor_tensor(out=mst[:, 2 + t:3 + t], in0=mst[:, t:t + 1], in1=negh[:, 0:1], op=ALU.pow), reads=[mst_b, cb_], writes=[mst_b])
                g, g_b = mg_r.next()
                Sc.op("vector", lambda e, g=g, m=m, t=t: e.scalar_tensor_tensor(out=g[:], in0=m[:], scalar=mst[:, 2 + t:3 + t], in1=gmem[:], op0=ALU.mult, op1=ALU.mult), reads=[m_b, mst_b, cb_], writes=[g_b])
                p, p_b = pT.next()
                for k in range(8):
                    Sc.op("tensor", lambda e, p=p, g=g, k=k: e.transpose(out=p[:, k * 128:(k + 1) * 128], in_=g[:, k * 128:(k + 1) * 128], identity=ident[:]), reads=[g_b, cb_], writes=[p_b])
                Sc.op("vector", lambda e, p=p, t=t: e.tensor_copy(out=memT[:, :, t * 128:(t + 1) * 128], in_=p[:, :].rearrange("p (k c) -> p k c", k=8)), reads=[p_b], writes=[memT_b])
            for h in range(4):
                p, p_b = pA.next()
                for k in range(8):
                    Sc.op("tensor", lambda e, p=p, h=h, k=k: e.matmul(p[:, 0:256], lhsT=wkv[:, k, h * 128:(h + 1) * 128], rhs=memT[:, k, :], start=(k == 0), stop=(k == 7)), reads=[wkv_b, memT_b], writes=[p_b])
                Sc.op("scalar", lambda e, p=p, h=h: e.activation(out=kmT[:, h, :], in_=p[:, 0:256], func=AF.Copy), reads=[p_b], writes=[kv_b])
            for t in range(2):
                p, p_b = pA.next()
                for k in range(8):
                    Sc.op("tensor", lambda e, p=p, t=t, k=k: e.matmul(p[:, :], lhsT=memT[:, k, t * 128:(t + 1) * 128], rhs=wkv[:, k, 512:1024], start=(k == 0), stop=(k == 7)), reads=[wkv_b, memT_b], writes=[p_b])
                Sc.op("scalar", lambda e, p=p, t=t: e.activation(out=vm[:, t, :], in_=p[:, :], func=AF.Copy), reads=[p_b], writes=[kv_b])
            qm_r = Ring([(sb("qm%d" % i, [128, 512], BF16), Buf("qm")) for i in range(3)])
            E_r = Ring([(sb("E%d" % i, [128, 2, 512], BF16), Buf("E")) for i in range(2)])
            R_r = Ring([(sb("R%d" % i, [128, 512], F32), Buf("R")) for i in range(2)])
            y_r = Ring([(sb("y%d" % i, [128, 512], BF16), Buf("y")) for i in range(3)])
            ymt_b = Buf("YMT")
            sc = 1.0 / float(np.sqrt(128.0))
            for tb in range(NB):
                for h in range(4):
                    q, q_b = qm_r.next()
                    Sc.dma("sync", lambda e, q=q, tb=tb, h=h: e.dma_start_transpose(out=q[:, :], in_=self.PROJ[tb * 512:(tb + 1) * 512, C_QM + h * 128:C_QM + (h + 1) * 128]), writes=[q_b])
                    E, E_b = E_r.next()
                    for mc in range(2):
                        p, p_b = pA.next()
                        Sc.op("tensor", lambda e, p=p, h=h, mc=mc, q=q: e.matmul(p[:, :], lhsT=kmT[:, h, mc * 128:(mc + 1) * 128], rhs=q[:, :], start=True, stop=True), reads=[kv_b, q_b], writes=[p_b])
                        Sc.op("scalar", lambda e, E=E, p=p, mc=mc: e.activation(out=E[:, mc, :], in_=p[:, :], func=AF.Exp, scale=sc), reads=[p_b], writes=[E_b])
                    pu, pu_b = pA.next()
                    pd, pd_b = pA.next()
                    for mc in range(2):
                        Sc.op("tensor", lambda e, pu=pu, E=E, mc=mc, h=h: e.matmul(pu[:, :], lhsT=vm[:, mc, h * 128:(h + 1) * 128], rhs=E[:, mc, :], start=(mc == 0), stop=(mc == 1)), reads=[kv_b, E_b], writes=[pu_b])
                    for mc in range(2):
                        Sc.op("tensor", lambda e, pd=pd, E=E, mc=mc: e.matmul(pd[:, :], lhsT=ones[:, :], rhs=E[:, mc, :], start=(mc == 0), stop=(mc == 1)), reads=[cb_, E_b], writes=[pd_b])
                    R, R_b = R_r.next()
                    Sc.op("vector", lambda e, R=R, pd=pd: e.reciprocal(out=R[:, :], in_=pd[:, :]), reads=[pd_b], writes=[R_b])
                    y, y_b = y_r.next()
                    Sc.op("vector", lambda e, y=y, R=R, pu=pu: e.tensor_tensor(out=y[:, :], in0=pu[:, :], in1=R[:, :], op=ALU.mult), reads=[pu_b, R_b], writes=[y_b])
                    Sc.dma("gpsimd", lambda e, y=y, h=h, tb=tb: e.dma_start(out=self.YMT[h * 128:(h + 1) * 128, tb * 512:(tb + 1) * 512], in_=y[:, :]), reads=[y_b], writes=[ymt_b])
            Sc.barrier()
            Sc.emit(st)

    def phase_c(self):
        nc = self.nc
        with contextlib.ExitStack() as st:
            sb = lambda n, s, d: st.enter_context(nc.sbuf_tensor("pc_" + n, s, d))
            ps = lambda n, s, d: st.enter_context(nc.psum_tensor("pc_" + n, s, d))
            Sc = Sched(nc, prefix="pc_")
            cb_ = Buf("const")
            w_b = Buf("w")
            identf = sb("identf", [128, 128], F32)
            ident = sb("ident", [128, 128], BF16)
            gffn = sb("gffn", [128, D], F32)
            negh = sb("negh", [128, 1], F32)
            wpa = sb("wpa", [128, 6, D], BF16)
            wpr = sb("wpr", [128, 6, D], BF16)
            wpm = sb("wpm", [128, 4, D], BF16)
            wo = sb("wo", [128, 8, D], BF16)
            Sc.dma("sync", lambda e: e.dma_start(out=identf[:], in_=self.c_ident[:, :]), writes=[cb_])
            Sc.dma("sync", lambda e: e.dma_start(out=gffn[:], in_=self.g_ffn[0, :].partition_broadcast(128)), writes=[cb_])
            Sc.dma("gpsimd", lambda e: e.dma_start(out=wpa[:], in_=self.w_pa[0, :, :].rearrange("(k p) c -> p k c", p=128)), writes=[w_b])
            Sc.dma("gpsimd", lambda e: e.dma_start(out=wpr[:], in_=self.w_pr[0, :, :].rearrange("(k p) c -> p k c", p=128)), writes=[w_b])
            Sc.dma("gpsimd", lambda e: e.dma_start(out=wpm[:], in_=self.w_pm[0, :, :].rearrange("(k p) c -> p k c", p=128)), writes=[w_b])
            Sc.dma("gpsimd", lambda e: e.dma_start(out=wo[:], in_=self.w_out[0, :, :].rearrange("(k p) c -> p k c", p=128)), writes=[w_b])
            Sc.op("vector", lambda e: e.tensor_copy(out=ident[:], in_=identf[:]), reads=[cb_], writes=[cb_])
            Sc.op("vector", lambda e: e.memset(negh[:], -0.5), reads=[cb_], writes=[cb_])
            ya_r = Ring([(sb("ya%d" % i, [128, 6, 512], BF16), Buf("ya")) for i in range(2)])
            yr_r = Ring([(sb("yr%d" % i, [128, 6, 512], BF16), Buf("yr")) for i in range(2)])
            ym_r = Ring([(sb("ym%d" % i, [128, 4, 512], BF16), Buf("ym")) for i in range(2)])
            gt_r = Ring([(sb("gt%d" % i, [128, 3, 512], BF16), Buf("gt")) for i in range(3)])
            mg_r = Ring([(sb("mg%d" % i, [128, 8, 512], BF16), Buf("mg")) for i in range(2)])
            m1_r = Ring([(sb("m1_%d" % i, [128, 512], F32), Buf("m1")) for i in range(2)])
            m2_r = Ring([(sb("m2_%d" % i, [128, 512], F32), Buf("m2")) for i in range(2)])
            m3_r = Ring([(sb("m3_%d" % i, [128, 512], F32), Buf("m3")) for i in range(2)])
            x_r = Ring([(sb("x%d" % i, [128, D], F32), Buf("x")) for i in range(2)])
            h_r = Ring([(sb("h%d" % i, [128, D], F32), Buf("h")) for i in range(2)])
            hn_r = Ring([(sb("hn%d" % i, [128, D], BF16), Buf("hn")) for i in range(2)])
            ht_r = Ring([(sb("ht%d" % i, [128, 8, 128], BF16), Buf("ht")) for i in range(2)])
            junk = sb("junk", [128, D], BF16)
            junk_b = Buf("junk")
            stt_ = sb("stat", [128, 3, NT], F32)
            st_b = [Buf("st") for _ in range(NT)]
            Sc.op("vector", lambda e: e.memset(stt_[:], 0.0), writes=st_b)
            pP = Ring([(ps("pP%d" % i, [128, 512], F32), Buf("pP")) for i in range(5)])
            pO = Ring([(ps("pO%d" % i, [128, 512], F32), Buf("pO")) for i in range(2)])
            pT = Ring([(ps("pT%d" % i, [128, 1024], BF16), Buf("pT")) for i in range(1)])
            pend = []
            h_out_b = Buf("H")
            hnt_b = Buf("HNT")
            def load_y(tb):
                cs = slice(tb * 512, (tb + 1) * 512)
                ya, ya_b = ya_r.next()
                yr, yr_b = yr_r.next()
                ym, ym_b = ym_r.next()
                Sc.dma("sync", lambda e, ya=ya, cs=cs: e.dma_start(out=ya[:], in_=self.YAT[:, cs].rearrange("(k p) s -> p k s", p=128)), writes=[ya_b])
                Sc.dma("sync", lambda e, ym=ym, cs=cs: e.dma_start(out=ym[:], in_=self.YMT[:, cs].rearrange("(k p) s -> p k s", p=128)), writes=[ym_b])
                for k in range(6):
                    Sc.dma("sync", lambda e, yr=yr, tb=tb, k=k: e.dma_start_transpose(out=yr[:, k, :], in_=self.YR[tb * 512:(tb + 1) * 512, k * 128:(k + 1) * 128]), writes=[yr_b])
                return (ya, ya_b, yr, yr_b, ym, ym_b)
            ynext = load_y(0)
            for tb in range(NB):
                cs = slice(tb * 512, (tb + 1) * 512)
                ya, ya_b, yr, yr_b, ym, ym_b = ynext
                if tb + 1 < NB:
                    ynext = load_y(tb + 1)
                mg, mg_b = mg_r.next()
                for fc in range(8):
                    gt, gt_b = gt_r.next()
                    Sc.dma("sync", lambda e, gt=gt, fc=fc, cs=cs: e.dma_start(out=gt[:], in_=self.GTT.rearrange("(i f) s -> f i s", i=3)[fc * 128:(fc + 1) * 128, :, cs]), writes=[gt_b])
                    prs = []
                    for (w, src, src_b, nk) in ((wpa, ya, ya_b, 6), (wpr, yr, yr_b, 6), (wpm, ym, ym_b, 4)):
                        p, p_b = pP.next()
                        for k in range(nk):
                            Sc.op("tensor", lambda e, p=p, w=w, src=src, k=k, fc=fc, nk=nk: e.matmul(p[:, :], lhsT=w[:, k, fc * 128:(fc + 1) * 128], rhs=src[:, k, :], start=(k == 0), stop=(k == nk - 1)), reads=[w_b, src_b], writes=[p_b])
                        prs.append((p, p_b))
                    m1, m1_b = m1_r.next()
                    m2, m2_b = m2_r.next()
                    m3, m3_b = m3_r.next()
                    for i, (m, m_b) in enumerate(((m1, m1_b), (m2, m2_b), (m3, m3_b))):
                        p, p_b = prs[i]
                        Sc.op("vector", lambda e, m=m, gt=gt, i=i, p=p: e.scalar_tensor_tensor(out=m[:, :], in0=gt[:, i, :], scalar=1.0, in1=p[:, :], op0=ALU.add, op1=ALU.mult), reads=[gt_b, p_b], writes=[m_b])
                    Sc.op("gpsimd", lambda e, m1=m1, m2=m2: e.tensor_tensor(out=m1[:, :], in0=m1[:, :], in1=m2[:, :], op=ALU.add), reads=[m1_b, m2_b], writes=[m1_b])
                    Sc.op("gpsimd", lambda e, mg=mg, m1=m1, m3=m3, fc=fc: e.tensor_tensor(out=mg[:, fc, :], in0=m1[:, :], in1=m3[:, :], op=ALU.add), reads=[m1_b, m3_b], writes=[mg_b])
                for tt in range(4):
                    t = tb * 4 + tt
                    x, x_b = x_r.next()
                    Sc.dma("sync", lambda e, x=x, t=t: e.dma_start(out=x[:], in_=self.x[t * 128:(t + 1) * 128, :]), writes=[x_b])
                    h, h_b = h_r.next()
                    for nh in range(2):
                        p, p_b = pO.next()
                        for k in range(8):
                            Sc.op("tensor", lambda e, p=p, mg=mg, k=k, tt=tt, nh=nh: e.matmul(p[:, :], lhsT=mg[:, k, tt * 128:(tt + 1) * 128], rhs=wo[:, k, nh * 512:(nh + 1) * 512], start=(k == 0), stop=(k == 7)), reads=[mg_b, w_b], writes=[p_b])
                        Sc.op("vector", lambda e, h=h, p=p, x=x, nh=nh: e.scalar_tensor_tensor(out=h[:, nh * 512:(nh + 1) * 512], in0=p[:, :], scalar=0.5, in1=x[:, nh * 512:(nh + 1) * 512], op0=ALU.mult, op1=ALU.add), reads=[p_b, x_b], writes=[h_b])
                    while pend:
                        pend.pop(0)()
                    Sc.dma("gpsimd", lambda e, h=h, t=t: e.dma_start(out=self.H[t * 128:(t + 1) * 128, :], in_=h[:]), reads=[h_b], writes=[h_out_b])
                    Sc.op("scalar", lambda e, h=h, t=t: e.activation(out=junk[:], in_=h[:], func=AF.Square, accum_out=stt_[:, 0, t:t + 1]), reads=[h_b, st_b[t]], writes=[junk_b, st_b[t]])
                    Sc.op("gpsimd", lambda e, t=t: e.tensor_scalar(out=stt_[:, 1, t:t + 1], in0=stt_[:, 0, t:t + 1], scalar1=1.0 / D, scalar2=EPS, op0=ALU.mult, op1=ALU.add), reads=[st_b[t]], writes=[st_b[t]])
                    Sc.op("gpsimd", lambda e, t=t: e.tensor_tensor(out=stt_[:, 2, t:t + 1], in0=stt_[:, 1, t:t + 1], in1=negh[:, 0:1], op=ALU.pow), reads=[st_b[t], cb_], writes=[st_b[t]])
                    hn, hn_b = hn_r.next()
                    Sc.op("vector", lambda e, hn=hn, h=h, t=t: e.scalar_tensor_tensor(out=hn[:], in0=h[:], scalar=stt_[:, 2, t:t + 1], in1=gffn[:], op0=ALU.mult, op1=ALU.mult), reads=[h_b, st_b[t], cb_], writes=[hn_b])
                    def tail(hn=hn, hn_b=hn_b, t=t):
                        p, p_b = pT.next()
                        for k in range(8):
                            Sc.op("tensor", lambda e, p=p, hn=hn, k=k: e.transpose(out=p[:, k * 128:(k + 1) * 128], in_=hn[:, k * 128:(k + 1) * 128], identity=ident[:]), reads=[hn_b, cb_], writes=[p_b])
                        ht, ht_b = ht_r.next()
                        Sc.op("scalar", lambda e, ht=ht, p=p: e.activation(out=ht[:], in_=p[:, :].rearrange("p (k c) -> p k c", k=8), func=AF.Copy), reads=[p_b], writes=[ht_b])
                        Sc.dma("gpsimd", lambda e, ht=ht, t=t: e.dma_start(out=self.HNT[:, t * 128:(t + 1) * 128].rearrange("(k p) s -> p k s", p=128), in_=ht[:]), reads=[ht_b], writes=[hnt_b])
                    pend.append(tail)
            while pend:
                pend.pop(0)()
            Sc.barrier()
            Sc.emit(st)

    def phase_d(self):
        nc = self.nc
        NF = DFF // 128
        with contextlib.ExitStack() as st:
            sb = lambda n, s, d: st.enter_context(nc.sbuf_tensor("d_" + n, s, d))
            ps = lambda n, s, d: st.enter_context(nc.psum_tensor("d_" + n, s, d))
            Sc = Sched(nc, prefix="d_")
            cb_ = Buf("const")
            hnT = sb("hnT", [128, 8, S], BF16)
            hn_b = [Buf("hnT") for _ in range(NB)]
            for tb in range(NB):
                Sc.dma("sync", lambda e, tb=tb: e.dma_start(out=hnT[:, :, tb * 512:(tb + 1) * 512], in_=self.HNT[:, tb * 512:(tb + 1) * 512].rearrange("(k p) s -> p k s", p=128)), writes=[hn_b[tb]])
            cw = sb("cw", [128, 2, 3, NF], F32)
            cbias = sb("cbias", [128, 2, NF], F32)
            for ab in range(2):
                for j in range(3):
                    Sc.dma("sync", lambda e, ab=ab, j=j: e.dma_start(out=cw[:, ab, j, :], in_=self.conv_w[0, j, ab * DFF:(ab + 1) * DFF].rearrange("(f p) -> p f", p=128), allow_slow_non_contiguous=True), writes=[cb_])
                Sc.dma("sync", lambda e, ab=ab: e.dma_start(out=cbias[:, ab, :], in_=self.conv_b[0, ab * DFF:(ab + 1) * DFF].rearrange("(f p) -> p f", p=128), allow_slow_non_contiguous=True), writes=[cb_])
            w_r = Ring([(sb("w%d" % i, [128, 8, 256], BF16), Buf("w")) for i in range(2)])
            u_r = []
            for i in range(2):
                ua = sb("ua%d" % i, [128, S + 2], F32)
                ub = sb("ub%d" % i, [128, S + 2], F32)
                bl = [Buf("u") for _ in range(NB + 1)]
                for t_ in (ua, ub):
                    Sc.op("gpsimd", lambda e, t_=t_: e.memset(t_[:, 0:1], 0.0), writes=[bl[NB]])
                    Sc.op("gpsimd", lambda e, t_=t_: e.memset(t_[:, S + 1:S + 2], 0.0), writes=[bl[NB]])
                u_r.append(((ua, ub), bl))
            u_r = Ring(u_r)
            ca = sb("ca", [128, S], F32)
            cbb = sb("cb", [128, S], F32)
            th = sb("th", [128, S], F32)
            ca_b, cbb_b, th_b = Buf("ca"), Buf("cb"), Buf("th")
            g_r = Ring([(sb("g%d" % i, [128, S], BF16), Buf("g")) for i in range(2)])
            pA = Ring([(ps("pA%d" % i, [128, 512], F32), Buf("pA")) for i in range(8)])
            gt2_b = Buf("GT2")
            for fc in range(NF):
                w, w_b = w_r.next()
                Sc.dma("gpsimd", lambda e, w=w, fc=fc: e.dma_start(out=w[:, :, 0:128], in_=self.w_up[0, :, fc * 128:(fc + 1) * 128].rearrange("(k p) c -> p k c", p=128)), writes=[w_b])
                Sc.dma("gpsimd", lambda e, w=w, fc=fc: e.dma_start(out=w[:, :, 128:256], in_=self.w_up[0, :, DFF + fc * 128:DFF + (fc + 1) * 128].rearrange("(k p) c -> p k c", p=128)), writes=[w_b])
                (ua, ub), ubl = u_r.next()
                for tb in range(NB):
                    for ab, ut in ((0, ua), (1, ub)):
                        p, p_b = pA.next()
                        for k in range(8):
                            Sc.op("tensor", lambda e, p=p, w=w, k=k, ab=ab, tb=tb: e.matmul(p[:, :], lhsT=w[:, k, ab * 128:(ab + 1) * 128], rhs=hnT[:, k, tb * 512:(tb + 1) * 512], start=(k == 0), stop=(k == 7)), reads=[w_b, hn_b[tb]], writes=[p_b])
                        Sc.op("scalar", lambda e, ut=ut, p=p, tb=tb: e.activation(out=ut[:, 1 + tb * 512:1 + (tb + 1) * 512], in_=p[:, :], func=AF.Copy), reads=[p_b], writes=[ubl[tb]])
                for ab, ut, ct, ct_b, eng in ((0, ua, ca, ca_b, "vector"), (1, ub, cbb, cbb_b, "vector")):
                    Sc.op("scalar", lambda e, ut=ut, ct=ct, ab=ab, fc=fc: e.activation(out=ct[:, :], in_=ut[:, 0:S], func=AF.Identity, bias=cbias[:, ab, fc:fc + 1], scale=cw[:, ab, 0, fc:fc + 1]), reads=ubl + [cb_], writes=[ct_b])
                    Sc.op(eng, lambda e, ut=ut, ct=ct, ab=ab, fc=fc: e.scalar_tensor_tensor(out=ct[:, :], in0=ut[:, 1:S + 1], scalar=cw[:, ab, 1, fc:fc + 1], in1=ct[:, :], op0=ALU.mult, op1=ALU.add), reads=ubl + [cb_, ct_b], writes=[ct_b])
                    Sc.op(eng, lambda e, ut=ut, ct=ct, ab=ab, fc=fc: e.scalar_tensor_tensor(out=ct[:, :], in0=ut[:, 2:S + 2], scalar=cw[:, ab, 2, fc:fc + 1], in1=ct[:, :], op0=ALU.mult, op1=ALU.add), reads=ubl + [cb_, ct_b], writes=[ct_b])
                Sc.op("scalar", lambda e: e.activation(out=th[:, :], in_=ca[:, :], func=AF.Tanh, scale=0.5), reads=[ca_b], writes=[th_b])
                Sc.op("vector", lambda e: e.scalar_tensor_tensor(out=th[:, :], in0=th[:, :], scalar=1.0, in1=ca[:, :], op0=ALU.add, op1=ALU.mult), reads=[ca_b, th_b], writes=[th_b])
                g, g_b = g_r.next()
                Sc.op("vector", lambda e, g=g: e.tensor_tensor(out=g[:, :], in0=th[:, :], in1=cbb[:, :], op=ALU.mult), reads=[th_b, cbb_b], writes=[g_b])
                Sc.dma("sync", lambda e, g=g, fc=fc: e.dma_start(out=self.GT2[fc * 128:(fc + 1) * 128, :], in_=g[:, :]), reads=[g_b], writes=[gt2_b])
            Sc.barrier()
            Sc.emit(st)

    def phase_e(self):
        nc = self.nc
        NF = DFF // 128
        with contextlib.ExitStack() as st:
            sb = lambda n, s, d: st.enter_context(nc.sbuf_tensor("e_" + n, s, d))
            ps = lambda n, s, d: st.enter_context(nc.psum_tensor("e_" + n, s, d))
            Sc = Sched(nc, prefix="e_")
            cb_ = Buf("const")
            w_b = Buf("w")
            wd = sb("wd", [128, NF, D], BF16)
            gfin = sb("gfin", [128, D], F32)
            negh = sb("negh", [128, 1], F32)
            for q4 in range(2):
                Sc.dma("gpsimd", lambda e, q4=q4: e.dma_start(out=wd[:, q4 * 11:(q4 + 1) * 11, :], in_=self.w_down[0, q4 * 11 * 128:(q4 + 1) * 11 * 128, :].rearrange("(k p) c -> p k c", p=128)), writes=[w_b])
            Sc.dma("sync", lambda e: e.dma_start(out=gfin[:], in_=self.g_final.partition_broadcast(128)), writes=[cb_])
            Sc.op("vector", lambda e: e.memset(negh[:], -0.5), reads=[cb_], writes=[cb_])
            g_r = Ring([(sb("g%d" % i, [128, NF, 512], BF16), Buf("g")) for i in range(2)])
            h_r = Ring([(sb("h%d" % i, [128, D], F32), Buf("h")) for i in range(3)])
            o_r = Ring([(sb("o%d" % i, [128, D], F32), Buf("o")) for i in range(2)])
            junk = sb("junk", [128, D], BF16)
            junk_b = Buf("junk")
            stt_ = sb("stat", [128, 3, NT], F32)
            st_b = [Buf("st") for _ in range(NT)]
            Sc.op("vector", lambda e: e.memset(stt_[:], 0.0), writes=st_b)
            pO = Ring([(ps("pO%d" % i, [128, 512], F32), Buf("pO")) for i in range(6)])
            out_b = Buf("out")
            def load_g(tb):
                g, g_b = g_r.next()
                Sc.dma("sync", lambda e, g=g, tb=tb: e.dma_start(out=g[:], in_=self.GT2[:, tb * 512:(tb + 1) * 512].rearrange("(k p) s -> p k s", p=128)), writes=[g_b])
                return g, g_b
            gnext = load_g(0)
            for tb in range(NB):
                g, g_b = gnext
                if tb + 1 < NB:
                    gnext = load_g(tb + 1)
                for tt in range(4):
                    t = tb * 4 + tt
                    h, h_b = h_r.next()
                    Sc.dma("sync", lambda e, h=h, t=t: e.dma_start(out=h[:], in_=self.H[t * 128:(t + 1) * 128, :]), writes=[h_b])
                    for nh in range(2):
                        p, p_b = pO.next()
                        for k in range(NF):
                            Sc.op("tensor", lambda e, p=p, g=g, k=k, tt=tt, nh=nh: e.matmul(p[:, :], lhsT=g[:, k, tt * 128:(tt + 1) * 128], rhs=wd[:, k, nh * 512:(nh + 1) * 512], start=(k == 0), stop=(k == NF - 1)), reads=[g_b, w_b], writes=[p_b])
                        Sc.op("vector", lambda e, h=h, p=p, nh=nh: e.scalar_tensor_tensor(out=h[:, nh * 512:(nh + 1) * 512], in0=p[:, :], scalar=0.5, in1=h[:, nh * 512:(nh + 1) * 512], op0=ALU.mult, op1=ALU.add), reads=[p_b, h_b], writes=[h_b])
                    Sc.op("scalar", lambda e, h=h, t=t: e.activation(out=junk[:], in_=h[:], func=AF.Square, accum_out=stt_[:, 0, t:t + 1]), reads=[h_b, st_b[t]], writes=[junk_b, st_b[t]])
                    Sc.op("gpsimd", lambda e, t=t: e.tensor_scalar(out=stt_[:, 1, t:t + 1], in0=stt_[:, 0, t:t + 1], scalar1=1.0 / D, scalar2=EPS, op0=ALU.mult, op1=ALU.add), reads=[st_b[t]], writes=[st_b[t]])
                    Sc.op("gpsimd", lambda e, t=t: e.tensor_tensor(out=stt_[:, 2, t:t + 1], in0=stt_[:, 1, t:t + 1], in1=negh[:, 0:1], op=ALU.pow), reads=[st_b[t], cb_], writes=[st_b[t]])
                    o, o_b = o_r.next()
                    Sc.op("vector", lambda e, o=o, h=h, t=t: e.scalar_tensor_tensor(out=o[:], in0=h[:], scalar=stt_[:, 2, t:t + 1], in1=gfin[:], op0=ALU.mult, op1=ALU.mult), reads=[h_b, st_b[t], cb_], writes=[o_b])
                    Sc.dma("sync", lambda e, o=o, t=t: e.dma_start(out=self.out[t * 128:(t + 1) * 128, :], in_=o[:]), reads=[o_b], writes=[out_b])
            Sc.barrier()
            Sc.emit(st)

    def build(self):
        for ph in ("a", "b1", "b1c", "b2", "b3", "c", "d", "e"):
            if self.phases is not None and ph not in self.phases:
                continue
            fn = getattr(self, "phase_" + ph, None)
            if fn is not None:
                fn()
            if self.stop_after == ph:
                break
        return self.nc


def host_consts():
    inv = 10000.0 ** (-np.arange(0, 64, 2, dtype=np.float32) / 64.0)
    ang = np.arange(S, dtype=np.float32)[:, None] * inv[None, :].astype(np.float32)
    c = {
        "c_cos": np.cos(ang).astype(np.float32),
        "c_sin": np.sin(ang).astype(np.float32),
        "c_ident": np.eye(128, dtype=np.float32),
    }
    kk = np.arange(128)[:, None]
    qq = np.arange(256)[None, :]
    band = ((qq >= kk) & (qq <= kk + 128))
    m = np.where(band, 0.0, -30000.0).astype(np.float32)
    am = np.zeros((128, 1024), np.float32)
    am[:, 0:256] = m
    am[:, 256:512] = m
    mf = m[64:128, 128:256]
    ml = m[0:64, 0:128]
    am[0:64, 512:640] = mf
    am[0:64, 640:768] = mf
    am[0:64, 768:896] = ml
    am[0:64, 896:1024] = ml
    c["c_amask"] = am
    r = np.zeros((128, 8, 128), np.float32)
    mm = np.arange(128)[:, None].astype(np.float32)
    nn = np.arange(128)[None, :].astype(np.float32)
    r[:, 0, :] = np.maximum(nn - mm, 0)
    r[:, 1, :] = np.maximum(mm - nn, 0)
    r[:, 2, :] = (nn >= mm) * 0.125
    r[:, 3, :] = (mm > nn) * 0.125
    r[:, 4, :] = nn + 1.0
    r[:, 5, :] = 128.0 - nn
    r[:, 6, 0] = 127.0 - np.arange(128)
    r[:, 6, 1] = np.arange(128)
    c["c_ret"] = r
    return c


def make_in_maps(inputs, n_cores=8):
    consts = host_consts()
    maps = []
    for b in range(n_cores):
        m = {"x": np.ascontiguousarray(inputs["x"][b]), "mem": np.ascontiguousarray(inputs["mem"][b])}
        for k, v in inputs.items():
            if k in ("x", "mem"):
                continue
            m[k] = np.ascontiguousarray(v)
        m.update(consts)
        maps.append(m)
    return maps


def kernel(**inputs):
    inputs = {k: np.asarray(v) for k, v in inputs.items()}
    prog = Prog()
    nc = prog.build()
    res = run_bass_kernel_spmd(nc, make_in_maps(inputs), core_ids=list(range(8)))
    return np.stack([r["out"] for r in res.results], axis=0)
```
